# Optimizing a Trainium2 kernel written in Bass

```python
import math
import jax
import jax.numpy as jnp
from jax import lax
import numpy as np

D_MODEL = 1024
BATCH = 16
SEQ = 256
DEPTH = 4
DEC_BATCH = 8
DEC_SEQ = 1024
PAST_LEN = 256

GRID_W = 64
N_MIXERS = 2
N_GDN = (DEPTH + 1) // 2
N_NA = DEPTH // 2
GDN_HEADS = 8
GDN_DK = 128
GDN_DV = 128
GDN_QK_W = GDN_HEADS * GDN_DK
GDN_W = GDN_HEADS * GDN_DV
GDN_CONV = 3
GDN_CHUNK = 64
NA_HEADS = 16
NA_DH = D_MODEL // NA_HEADS
NA_W = NA_HEADS * NA_DH
NA_KR_MAX = 8
NA_KC = 16
CTX_QBLOCK = 128
D_FF = 2816
FFN_CONV = 3
N_MOD = 6
EPS = 1e-6
NEG_INF = -1e30

kernel_name = 'hybrid_gdn_natten_diffusion_step'


def rmsnorm(x, g):
    xf = x.astype(jnp.float32)
    y = xf * lax.rsqrt(jnp.mean(xf * xf, axis=-1, keepdims=True) + EPS)
    return y.astype(x.dtype) * g


def l2norm(x):
    xf = x.astype(jnp.float32)
    return xf * lax.rsqrt(jnp.sum(xf * xf, axis=-1, keepdims=True) + EPS)


def dwconv_centred(x, w):
    k, ch = w.shape
    p = k // 2
    return lax.conv_general_dilated(x, w.reshape(k, 1, ch).astype(x.dtype), window_strides=(1,),
                                    padding=((p, p),), dimension_numbers=('NWC', 'WIO', 'NWC'),
                                    feature_group_count=ch)


def modulation(cvec, w, b):
    m = jax.nn.silu(cvec) @ w + b
    return [t[:, None, :] for t in jnp.split(m, N_MOD, axis=-1)]


def chunk_gated_delta(q, k, v, g, beta, s0):
    f32 = jnp.float32
    B, T, H, DK = q.shape
    DV = v.shape[-1]
    C = GDN_CHUNK
    N = T // C

    def chunks(t):
        t = t.astype(f32).reshape((B, N, C, H) + t.shape[3:])
        return jnp.moveaxis(t, (1, 3), (0, 2))

    qc = chunks(q) * DK ** -0.5
    kc = chunks(k)
    vc = chunks(v)
    gc = jnp.cumsum(chunks(g), axis=-1)
    bc = chunks(beta)
    idx = jnp.arange(C)
    lower = idx[:, None] >= idx[None, :]
    strict = idx[:, None] > idx[None, :]
    diff = gc[..., :, None] - gc[..., None, :]
    decay = jnp.where(lower, jnp.exp(jnp.where(lower, diff, 0.0)), 0.0)
    kb = kc * bc[..., None]
    a_mat = jnp.where(strict, jnp.einsum('nbhid,nbhjd->nbhij', kb, kc) * decay, 0.0)
    eye = jnp.broadcast_to(jnp.eye(C, dtype=f32), a_mat.shape)
    t_mat = lax.linalg.triangular_solve(eye + a_mat, eye, left_side=True, lower=True)
    u = jnp.einsum('nbhij,nbhjd->nbhid', t_mat, vc * bc[..., None])
    w = jnp.einsum('nbhij,nbhjd->nbhid', t_mat, kb * jnp.exp(gc)[..., None])

    def step(s, inp):
        qi, ki, ui, wi, gi, di = inp
        v_new = ui - jnp.einsum('bhcd,bhde->bhce', wi, s)
        intra = jnp.einsum('bhcd,bhjd->bhcj', qi, ki) * di
        o = (jnp.einsum('bhcd,bhde->bhce', qi * jnp.exp(gi)[..., None], s)
             + jnp.einsum('bhcj,bhje->bhce', intra, v_new))
        g_last = gi[..., -1]
        s = (s * jnp.exp(g_last)[..., None, None]
             + jnp.einsum('bhcd,bhce->bhde', ki * jnp.exp(g_last[..., None] - gi)[..., None], v_new))
        return s, o

    s_fin, o = lax.scan(step, s0.astype(f32), (qc, kc, u, w, gc, decay))
    o = jnp.moveaxis(o, (0, 2), (1, 3)).reshape(B, T, H, DV)
    return o, s_fin.astype(s0.dtype)


def gdn_mixer(h, s0, w_in, conv_w, a_log, dt_bias, norm_g, w_out):
    B, T, _ = h.shape
    proj = h @ w_in
    qkv, z, a, b = jnp.split(proj, [2 * GDN_QK_W + GDN_W, 2 * GDN_QK_W + 2 * GDN_W,
                                    2 * GDN_QK_W + 2 * GDN_W + 2 * GDN_HEADS], axis=-1)
    qkv = jax.nn.silu(dwconv_centred(qkv, conv_w))
    q, k, v = jnp.split(qkv, [GDN_QK_W, 2 * GDN_QK_W], axis=-1)
    q = l2norm(q.reshape(B, T, GDN_HEADS, GDN_DK))
    k = l2norm(k.reshape(B, T, GDN_HEADS, GDN_DK))
    v = v.reshape(B, T, GDN_HEADS, GDN_DV)
    a = a.reshape(B, T, 2, GDN_HEADS).astype(jnp.float32)
    b = b.reshape(B, T, 2, GDN_HEADS).astype(jnp.float32)
    g = -jnp.exp(a_log.astype(jnp.float32)) * jax.nn.softplus(a + dt_bias.astype(jnp.float32))
    beta = jax.nn.sigmoid(b)
    o_f, s_f = chunk_gated_delta(q, k, v, g[:, :, 0], beta[:, :, 0], s0[:, 0])
    rev = lambda t: jnp.flip(t, axis=1)
    o_b, s_b = chunk_gated_delta(rev(q), rev(k), rev(v), rev(g[:, :, 1]), rev(beta[:, :, 1]), s0[:, 1])
    o = (o_f + rev(o_b)).astype(h.dtype)
    o = rmsnorm(o, norm_g) * jax.nn.silu(z.reshape(B, T, GDN_HEADS, GDN_DV))
    return o.reshape(B, T, GDN_W) @ w_out, jnp.stack([s_f, s_b], axis=1)


def na_project(h, w_qkv):
    B, T, _ = h.shape
    q, k, v = jnp.split(h @ w_qkv, 3, axis=-1)
    shp = (B, T, NA_HEADS, NA_DH)
    return q.reshape(shp), k.reshape(shp), v.reshape(shp)


def context_attention(q, k, v):
    B, T, H, Dh = q.shape
    nb = T // CTX_QBLOCK
    qb = jnp.moveaxis(q.reshape(B, nb, CTX_QBLOCK, H, Dh), 1, 0)

    def block(qi):
        s = jnp.einsum('bqhd,bkhd->bhqk', qi, k).astype(jnp.float32) * Dh ** -0.5
        p = jax.nn.softmax(s, axis=-1).astype(v.dtype)
        return jnp.einsum('bhqk,bkhd->bqhd', p, v)

    o = lax.map(block, qb)
    return jnp.moveaxis(o, 0, 1).reshape(B, T, H * Dh)


def neighbourhood_attention(q, k, v, k_ctx, v_ctx, rel_bias):
    B, T, H, Dh = q.shape
    rows = T // GRID_W
    kr = min(NA_KR_MAX, rows)
    scale = Dh ** -0.5
    qg = q.reshape(B, rows, GRID_W, H, Dh)
    kg = k.reshape(B, rows, GRID_W, H, Dh)
    vg = v.reshape(B, rows, GRID_W, H, Dh)
    col = jnp.arange(GRID_W)
    col_start = jnp.clip(col - NA_KC // 2, 0, GRID_W - NA_KC)
    col_mask = (col[None, :] >= col_start[:, None]) & (col[None, :] < col_start[:, None] + NA_KC)
    col_idx = jnp.clip(col[None, :] - col[:, None] + NA_KC - 1, 0, 2 * NA_KC - 2)

    def row_block(args):
        r, q_r = args
        rs = jnp.clip(r - kr // 2, 0, rows - kr)
        k_blk = lax.dynamic_slice_in_dim(kg, rs, kr, axis=1)
        v_blk = lax.dynamic_slice_in_dim(vg, rs, kr, axis=1)
        row_idx = rs + jnp.arange(kr) - r + NA_KR_MAX - 1
        bias = rel_bias[:, row_idx][:, :, col_idx]
        s_loc = (jnp.einsum('bqhd,brkhd->bhqrk', q_r, k_blk).astype(jnp.float32) * scale
                 + jnp.transpose(bias, (0, 2, 1, 3))[None].astype(jnp.float32))
        s_loc = jnp.where(col_mask[:, None, :], s_loc, NEG_INF)
        s_ctx = jnp.einsum('bqhd,bchd->bhqc', q_r, k_ctx).astype(jnp.float32) * scale
        p = jax.nn.softmax(jnp.concatenate([s_loc.reshape(B, H, GRID_W, kr * GRID_W), s_ctx], axis=-1),
                           axis=-1).astype(v.dtype)
        p_loc = p[..., :kr * GRID_W].reshape(B, H, GRID_W, kr, GRID_W)
        p_ctx = p[..., kr * GRID_W:]
        return (jnp.einsum('bhqrk,brkhd->bqhd', p_loc, v_blk)
                + jnp.einsum('bhqc,bchd->bqhd', p_ctx, v_ctx))

    o = lax.map(row_block, (jnp.arange(rows), jnp.moveaxis(qg, 1, 0)))
    return jnp.moveaxis(o, 0, 1).reshape(B, T, H * Dh)


def conv_ffn(h, w_up, conv_w, conv_b, w_down):
    u = dwconv_centred(h @ w_up, conv_w) + conv_b
    val, gate = jnp.split(u, 2, axis=-1)
    return (jax.nn.silu(gate) * val) @ w_down


def setup_inputs(seed: int = 0) -> dict:
    key = jax.random.key(seed)
    ks = jax.random.split(key, 26)
    f32 = jnp.float32

    def nrm(k, shape, scale):
        return jax.random.normal(k, shape, f32) * scale

    d_in = 2 * GDN_QK_W + 2 * GDN_W + 4 * GDN_HEADS
    dt = jnp.exp(jax.random.uniform(ks[13], (N_GDN, 2, GDN_HEADS), f32, math.log(1e-3), math.log(1e-1)))
    return {
        'x_prompt': nrm(ks[0], (BATCH, SEQ, D_MODEL), 1.0),
        'x_sample': nrm(ks[1], (DEC_BATCH, DEC_SEQ, D_MODEL), 1.0),
        'state_gdn': nrm(ks[2], (DEC_BATCH, N_GDN, 2, GDN_HEADS, GDN_DK, GDN_DV), 0.3),
        'cache_k': nrm(ks[3], (DEC_BATCH, N_NA, PAST_LEN, NA_HEADS, NA_DH), 1.0),
        'cache_v': nrm(ks[4], (DEC_BATCH, N_NA, PAST_LEN, NA_HEADS, NA_DH), 1.0),
        'c': nrm(ks[5], (DEC_BATCH, D_MODEL), 1.0),
        'c_ctx': nrm(ks[6], (D_MODEL,), 1.0),
        'w_ada': nrm(ks[7], (DEPTH, D_MODEL, N_MOD * D_MODEL), 0.5 * D_MODEL ** -0.5),
        'b_ada': nrm(ks[8], (DEPTH, N_MOD * D_MODEL), 0.02),
        'norm1_g': 1.0 + nrm(ks[9], (DEPTH, D_MODEL), 0.02),
        'norm2_g': 1.0 + nrm(ks[10], (DEPTH, D_MODEL), 0.02),
        'gdn_w_in': nrm(ks[11], (N_GDN, D_MODEL, d_in), D_MODEL ** -0.5),
        'gdn_conv_w': nrm(ks[12], (N_GDN, GDN_CONV, 2 * GDN_QK_W + GDN_W), GDN_CONV ** -0.5),
        'gdn_a_log': jnp.log(jax.random.uniform(ks[14], (N_GDN, 2, GDN_HEADS), f32, 1.0, 16.0)),
        'gdn_dt_bias': dt + jnp.log(-jnp.expm1(-dt)),
        'gdn_norm_g': 1.0 + nrm(ks[15], (N_GDN, GDN_DV), 0.02),
        'gdn_w_out': nrm(ks[16], (N_GDN, GDN_W, D_MODEL), GDN_W ** -0.5),
        'na_w_qkv': nrm(ks[17], (N_NA, D_MODEL, 3 * NA_W), D_MODEL ** -0.5),
        'na_rel_bias': nrm(ks[18], (N_NA, NA_HEADS, 2 * NA_KR_MAX - 1, 2 * NA_KC - 1), 0.1),
        'na_w_out': nrm(ks[19], (N_NA, NA_W, D_MODEL), NA_W ** -0.5),
        'ffn_w_up': nrm(ks[20], (DEPTH, D_MODEL, 2 * D_FF), D_MODEL ** -0.5),
        'ffn_conv_w': nrm(ks[21], (DEPTH, FFN_CONV, 2 * D_FF), FFN_CONV ** -0.5),
        'ffn_conv_b': nrm(ks[22], (DEPTH, 2 * D_FF), 0.02),
        'ffn_w_down': nrm(ks[23], (DEPTH, D_FF, D_MODEL), D_FF ** -0.5),
        'final_g': 1.0 + nrm(ks[24], (D_MODEL,), 0.02),
    }


def reference(x_prompt, x_sample, state_gdn, cache_k, cache_v, c, c_ctx, w_ada, b_ada, norm1_g, norm2_g,
              gdn_w_in, gdn_conv_w, gdn_a_log, gdn_dt_bias, gdn_norm_g, gdn_w_out,
              na_w_qkv, na_rel_bias, na_w_out, ffn_w_up, ffn_conv_w, ffn_conv_b, ffn_w_down, final_g):
    xp, xs = x_prompt, x_sample
    gdn_states, na_keys, na_vals = [], [], []
    for l in range(DEPTH):
        j = l // N_MIXERS
        sh1p, sc1p, g1p, sh2p, sc2p, g2p = modulation(c_ctx[None, :], w_ada[l], b_ada[l])
        sh1s, sc1s, g1s, sh2s, sc2s, g2s = modulation(c, w_ada[l], b_ada[l])
        hp = rmsnorm(xp, norm1_g[l]) * (1.0 + sc1p) + sh1p
        hs = rmsnorm(xs, norm1_g[l]) * (1.0 + sc1s) + sh1s
        if l % N_MIXERS == 0:
            zero_state = jnp.zeros((xp.shape[0], 2, GDN_HEADS, GDN_DK, GDN_DV), xp.dtype)
            mp, st = gdn_mixer(hp, zero_state, gdn_w_in[j], gdn_conv_w[j], gdn_a_log[j], gdn_dt_bias[j],
                               gdn_norm_g[j], gdn_w_out[j])
            ms, _ = gdn_mixer(hs, state_gdn[:, j], gdn_w_in[j], gdn_conv_w[j], gdn_a_log[j], gdn_dt_bias[j],
                              gdn_norm_g[j], gdn_w_out[j])
            gdn_states.append(st)
        else:
            qp, kp, vp = na_project(hp, na_w_qkv[j])
            mp = context_attention(qp, kp, vp) @ na_w_out[j]
            qs, kls, vls = na_project(hs, na_w_qkv[j])
            ms = neighbourhood_attention(qs, kls, vls, cache_k[:, j], cache_v[:, j], na_rel_bias[j]) @ na_w_out[j]
            na_keys.append(kp)
            na_vals.append(vp)
        xp = xp + g1p * mp
        xs = xs + g1s * ms
        xp = xp + g2p * conv_ffn(rmsnorm(xp, norm2_g[l]) * (1.0 + sc2p) + sh2p,
                                 ffn_w_up[l], ffn_conv_w[l], ffn_conv_b[l], ffn_w_down[l])
        xs = xs + g2s * conv_ffn(rmsnorm(xs, norm2_g[l]) * (1.0 + sc2s) + sh2s,
                                 ffn_w_up[l], ffn_conv_w[l], ffn_conv_b[l], ffn_w_down[l])
    y_prompt = rmsnorm(xp, final_g)
    y_sample = rmsnorm(xs, final_g)
    new_state_gdn = jnp.stack(gdn_states, axis=1)
    new_cache_k = jnp.stack(na_keys, axis=1)
    new_cache_v = jnp.stack(na_vals, axis=1)
    return (y_prompt, y_sample, new_state_gdn, new_cache_k, new_cache_v)
```

```python
import numpy as np
import concourse.bass as bass
import concourse.mybir as mybir
from concourse.bass_utils import run_bass_kernel_spmd

F32 = mybir.dt.float32
F32R = mybir.dt.float32r
BF16 = mybir.dt.bfloat16
AF = mybir.ActivationFunctionType
ALU = mybir.AluOpType
AX = mybir.AxisListType

D = 1024
KC = 8
DEPTH = 4
NP_TOK = 512
NS_TOK = 1024
NT = NP_TOK + NS_TOK
TB = 512
NTB = NT // TB
D_FF = 2816
NFC = D_FF // 128
GDN_DIN = 4128
EPS = 1e-6
SEQS = [(0, 256), (256, 256), (512, 1024)]


class Buf:
    __slots__ = ("name", "t", "psum", "lastw", "readers", "dma_readers", "root")

    def __init__(self, name, t, psum=False):
        self.name = name
        self.t = t
        self.psum = psum
        self.lastw = None
        self.readers = {}
        self.dma_readers = []
        self.root = self

    def alias(self, ap, name=None):
        b = Buf(name or self.name, ap, self.psum)
        b.root = self.root
        return b

    def __getitem__(self, idx):
        return self.t[idx]

    def sub(self, name=None):
        return Buf(name or self.name, self.t, self.psum)


class Op:
    __slots__ = ("eng", "fn", "reads", "writes", "dma", "waits", "sig", "sem", "sigval", "deps", "n")


class _Rec:
    def __init__(self):
        self.call = None

    def __getattr__(self, name):
        def f(*args, **kwargs):
            assert self.call is None
            self.call = (name, args, kwargs)
            return None
        return f


def _bind(fn):
    r = _Rec()
    fn(r)
    name, args, kwargs = r.call
    return lambda e: getattr(e, name)(*args, **kwargs)


class Prog:
    ENGS = ("pe", "act", "dve", "pool", "sp")

    def __init__(self, nc, n_dma_ring=8):
        self.nc = nc
        self.ops = []
        self.nring = n_dma_ring
        self.sbuf_bytes = 0

    def sb(self, name, shape, dtype):
        t = self.nc.alloc_sbuf_tensor(name, list(shape), dtype)
        return Buf(name, t)

    def ps(self, name, shape, dtype=F32):
        t = self.nc.alloc_psum_tensor(name, list(shape), dtype)
        return Buf(name, t, psum=True)

    def add(self, eng, fn, reads=(), writes=(), dma=False):
        op = Op()
        op.eng = eng
        op.fn = _bind(fn)
        op.reads = [b.root for b in reads if b is not None]
        op.writes = [b.root for b in writes if b is not None]
        op.dma = dma
        op.waits = []
        op.sig = dma
        op.sem = None
        op.sigval = 0
        op.n = len(self.ops)
        self.ops.append(op)
        return op

    def fence(self):
        op = Op()
        op.eng = None
        op.n = len(self.ops)
        op.dma = False
        op.sig = False
        self.ops.append(op)

    def view(self, name, ap, psum=False):
        return Buf(name, ap, psum)

    def dma(self, out_ap, in_ap, reads=(), writes=(), q="sp", **kw):
        return self.add(q, lambda e: e.dma_start(out=out_ap, in_=in_ap, **kw), reads, writes, dma=True)

    def resolve(self):
        last_on = {}
        dma_since = []
        pending = {}
        real_ops = []
        for op in self.ops:
            if op.eng is None:
                snap = (dict(last_on), list(dma_since))
                dma_since = []
                for e in self.ENGS:
                    pending[e] = snap
                continue
            real_ops.append(op)
            wdeps = []
            rdeps = []
            for b in op.reads:
                if b.lastw is not None:
                    wdeps.append(b.lastw)
                if b.psum:
                    for e, r in b.readers.items():
                        if e != op.eng:
                            rdeps.append(r)
            for b in op.writes:
                if b.lastw is not None:
                    wdeps.append(b.lastw)
                rdeps.extend(b.readers.values())
                rdeps.extend(b.dma_readers)
            for b in op.writes:
                b.lastw = op
                b.readers = {}
                b.dma_readers = []
            for b in op.reads:
                if op.dma:
                    b.dma_readers.append(op)
                else:
                    b.readers[op.eng] = op
            deps = {}
            for d in wdeps:
                if d is op:
                    continue
                if d.dma or op.dma:
                    deps[d.n] = d
                elif d.eng == op.eng:
                    if op.eng != "pe":
                        deps[d.n] = d
                else:
                    deps[d.n] = d
            for d in rdeps:
                if d is op:
                    continue
                if d.dma or op.dma:
                    deps[d.n] = d
                elif d.eng != op.eng:
                    deps[d.n] = d
            if pending.get(op.eng) is not None:
                lo, dl = pending[op.eng]
                pending[op.eng] = None
                for e2, d in lo.items():
                    if e2 == op.eng and e2 == "pe" and not op.dma:
                        continue
                    deps[d.n] = d
                for d in dl:
                    deps[d.n] = d
            op.deps = list(deps.values())
            for d in op.deps:
                d.sig = True
            if op.dma:
                dma_since.append(op)
            else:
                last_on[op.eng] = op
        self.ops = real_ops

    def emit(self):
        nc = self.nc
        self.resolve()
        streams = {e: [] for e in self.ENGS}
        for op in self.ops:
            streams[op.eng].append(op)
        import contextlib
        with contextlib.ExitStack() as es:
            esem = {e: es.enter_context(nc.semaphore("s_" + e)) for e in self.ENGS}
            rings = {e: [es.enter_context(nc.semaphore("d_%s%d" % (e, i))) for i in range(self.nring)]
                     for e in ("sp", "pool", "act")}
            fin = es.enter_context(nc.semaphore("fin"))
            cnt = {e: 0 for e in self.ENGS}
            ringcnt = {e: [0] * self.nring for e in rings}
            ringpos = {e: 0 for e in rings}
            ring_prev = {}
            for op in self.ops:
                if op.dma:
                    r = ringpos[op.eng] % self.nring
                    ringpos[op.eng] += 1
                    ringcnt[op.eng][r] += 1
                    op.sem = rings[op.eng][r]
                    op.sigval = 16 * ringcnt[op.eng][r]
                    if ringcnt[op.eng][r] > 1:
                        ring_prev[op.n] = (op.sem, op.sigval - 16)
                elif op.sig:
                    cnt[op.eng] += 1
                    op.sem = esem[op.eng]
                    op.sigval = cnt[op.eng]
            waited = {e: {} for e in self.ENGS}
            for e in self.ENGS:
                for op in streams[e]:
                    need = {}
                    for d in op.deps:
                        k = id(d.sem)
                        if k not in need or need[k][1] < d.sigval:
                            need[k] = (d.sem, d.sigval)
                    if op.n in ring_prev:
                        s, v = ring_prev[op.n]
                        k = id(s)
                        if k not in need or need[k][1] < v:
                            need[k] = (s, v)
                    for k, (s, v) in need.items():
                        if waited[e].get(k, 0) >= v:
                            continue
                        waited[e][k] = v
                        op.waits.append((s, v))
            final_waits = []
            for e in rings:
                for r in range(self.nring):
                    if ringcnt[e][r] > 0:
                        final_waits.append((rings[e][r], 16 * ringcnt[e][r]))

            def run_stream(ename, eng):
                for op in streams[ename]:
                    for s, v in op.waits:
                        eng.wait_ge(s, v)
                    ins = op.fn(eng)
                    if op.sig:
                        ins.then_inc(op.sem, 16 if op.dma else 1)
                if ename == "sp":
                    for s, v in final_waits:
                        eng.wait_ge(s, v)

            with nc.Block() as block:
                @block.tensor
                def _(eng):
                    run_stream("pe", eng)

                @block.scalar
                def _(eng):
                    run_stream("act", eng)

                @block.vector
                def _(eng):
                    run_stream("dve", eng)

                @block.gpsimd
                def _(eng):
                    run_stream("pool", eng)

                @block.sync
                def _(eng):
                    run_stream("sp", eng)


class Cfg:
    n_layers = DEPTH
    do_gdn = True
    do_na = True


def build(cfg=Cfg):
    nc = bass.Bass("TRN2", target_bir_lowering=False)
    P = Prog(nc)

    def din(name, shape):
        return nc.dram_tensor(name, list(shape), F32, kind="ExternalInput").ap()

    def dout(name, shape):
        return nc.dram_tensor(name, list(shape), F32, kind="ExternalOutput").ap()

    x_in = din("x_in", [NT, D])
    state_gdn = din("state_gdn", [2, 2, 8, 128, 128])
    cache_k = din("cache_k", [2, 256, 1024])
    cache_v = din("cache_v", [2, 256, 1024])
    cvec = din("cvec", [2, D])
    w_ada = din("w_ada", [DEPTH, 48, 128, 1024])
    b_ada = din("b_ada", [DEPTH, 6 * D])
    norm1_g = din("norm1_g", [DEPTH, D])
    norm2_g = din("norm2_g", [DEPTH, D])
    gdn_w_in = din("gdn_w_in", [2, 8, 2, 128, 2048])
    gdn_w_ab = din("gdn_w_ab", [2, 128, 256])
    gdn_conv_w = din("gdn_conv_w", [2, 3, 3072])
    gdn_a_log = din("gdn_a_log", [2, 16])
    gdn_dt_bias = din("gdn_dt_bias", [2, 16])
    gdn_norm_g = din("gdn_norm_g", [2, 128])
    gdn_w_out = din("gdn_w_out", [2, D, D])
    na_w_qkv = din("na_w_qkv", [2, 8, 128, 3072])
    na_rel_bias = din("na_rel_bias", [2, 16 * 15, 31])
    na_w_out = din("na_w_out", [2, 8, 128, 1024])
    ffn_w_up = din("ffn_w_up", [DEPTH, 22, 128, 2048])
    ffn_conv_w = din("ffn_conv_w", [DEPTH, 3, 2 * D_FF])
    ffn_conv_b = din("ffn_conv_b", [DEPTH, 2 * D_FF])
    ffn_w_down = din("ffn_w_down", [DEPTH, 2, 8, 128, 1408])
    final_g = din("final_g", [D])
    consts = din("consts", [128, 1536])

    y_out = dout("y_out", [NT, D])
    st_out = dout("st_out", [2, 2, 2, 8, 128, 128])
    ck_out = dout("ck_out", [2, 2, 256, 1024])
    cv_out = dout("cv_out", [2, 2, 256, 1024])

    X = P.sb("X", [128, KC, NT], F32)
    Xb = {(kc, tb): X.sub("X%d_%d" % (kc, tb)) for kc in range(KC) for tb in range(NTB)}
    H = P.sb("H", [128, KC, NT], BF16)
    Hb = {(kc, tb): H.sub("H%d_%d" % (kc, tb)) for kc in range(KC) for tb in range(NTB)}
    ARENA = P.sb("ARENA", [128, 13824], F32)
    CHAIN = nc.alloc_sbuf_tensor("CHAIN", [128, 3072], F32R)
    WREG = P.sb("WREG", [128, 3584], F32)

    def carve(region, byte_off, dtype, shape, name):
        esz = 2 if dtype == BF16 else 4
        n = 1
        for d_ in shape[1:]:
            n *= d_
        base = region.t.bitcast(dtype) if dtype != F32 else region.t
        ap = base[:, byte_off // esz: byte_off // esz + n]
        if len(shape) == 3:
            ap = ap.rearrange("p (a b) -> p a b", a=shape[1])
        elif len(shape) == 4:
            ap = ap.rearrange("p (a b c) -> p a b c", a=shape[1], b=shape[2])
        return P.view(name, ap)

    CONST = P.sb("CONST", [128, 1536], F32)
    ident = CONST
    ONESB = P.sb("ONESB", [128, 128], BF16)
    IDB = P.sb("IDB", [128, 128], BF16)

    PSALL = nc.alloc_psum_tensor("psall", [128, 4096], F32)
    PS = [P.view("ps%d" % i, PSALL[:, i * 512:(i + 1) * 512], psum=True) for i in range(8)]
    ps_rr = [0]

    def psum():
        b = PS[ps_rr[0] % 8]
        ps_rr[0] += 1
        return b

    P.dma(CONST[:, :], consts[:, :], writes=[CONST])
    P.add("dve", lambda e: e.memset(ONESB[:, :], 1.0), writes=[ONESB])
    P.add("dve", lambda e: e.tensor_copy(IDB[:, :], CONST[:, 0:128]), reads=[CONST], writes=[IDB])

    XT = [carve(ARENA, i * 4096, F32, [128, D], "XT%d" % i) for i in range(2)]
    for blk in range(NT // 128):
        xt = XT[blk % 2]
        P.dma(xt[:, :], x_in[blk * 128:(blk + 1) * 128, :], writes=[xt])
        tb = (blk * 128) // TB
        for half in range(2):
            pb = psum()
            for j in range(4):
                kc = half * 4 + j
                P.add("pe", lambda e, pb=pb, xt=xt, kc=kc, j=j: e.transpose(
                    pb[:, j * 128:(j + 1) * 128], xt[:, kc * 128:(kc + 1) * 128], CONST[:, 0:128]),
                    reads=[xt, CONST], writes=[pb])
            wr = [Xb[(half * 4 + j, tb)] for j in range(4)]
            eng = "act" if half == 0 else "dve"
            if eng == "act":
                P.add("act", lambda e, pb=pb, half=half, blk=blk: e.copy(
                    X[:, half * 4:half * 4 + 4, blk * 128:(blk + 1) * 128],
                    pb[:, :].rearrange("p (j t) -> p j t", j=4)), reads=[pb], writes=wr)
            else:
                P.add("dve", lambda e, pb=pb, half=half, blk=blk: e.tensor_copy(
                    X[:, half * 4:half * 4 + 4, blk * 128:(blk + 1) * 128],
                    pb[:, :].rearrange("p (j t) -> p j t", j=4)), reads=[pb], writes=wr)

    STG = [P.sb("STG%d" % i, [128, 128], F32) for i in range(2)]
    stg_rr = [0]

    def load_fm(dst, dst_flat_ap, src_rows_ap, R):
        r0 = 0
        while r0 < R:
            r = min(128, R - r0)
            stg = STG[stg_rr[0] % 2]
            stg_rr[0] += 1
            P.dma(stg[0:r, :], src_rows_ap[r0:r0 + r, :], writes=[stg])
            pb = psum()
            P.add("pe", lambda e, pb=pb, stg=stg, r=r: e.transpose(pb[:, 0:r], stg[0:r, :], CONST[0:r, 0:r]),
                  reads=[stg, CONST], writes=[pb])
            P.add("dve", lambda e, pb=pb, r=r, r0=r0: e.tensor_copy(dst_flat_ap[:, r0:r0 + r], pb[:, 0:r]),
                  reads=[pb], writes=[dst])
            r0 += r

    G1 = P.sb("G1", [128, DEPTH, KC], F32)
    G2 = P.sb("G2", [128, DEPTH, KC], F32)
    BADA = P.sb("BADA", [128, DEPTH, 48], F32)
    CV = P.sb("CV", [128, 2, KC], F32)
    SCV = P.sb("SCV", [128, 2, KC], BF16)
    load_fm(G1, G1[:, :, :].rearrange("p l k -> p (l k)"), norm1_g.rearrange("l (k f) -> (l k) f", f=128), 32)
    load_fm(G2, G2[:, :, :].rearrange("p l k -> p (l k)"), norm2_g.rearrange("l (k f) -> (l k) f", f=128), 32)
    load_fm(BADA, BADA[:, :, :].rearrange("p l k -> p (l k)"), b_ada.rearrange("l (k f) -> (l k) f", f=128), 192)
    load_fm(CV, CV[:, :, :].rearrange("p v k -> p (v k)"), cvec.rearrange("v (k f) -> (v k) f", f=128), 16)
    P.add("act", lambda e: e.activation(SCV[:, :, :], CV[:, :, :], AF.Silu), reads=[CV], writes=[SCV])

    MOD = [P.sb("MOD%d" % l, [128, 48, 2], F32) for l in range(DEPTH)]
    A1 = [P.sb("A1_%d" % l, [128, KC, 2], F32) for l in range(DEPTH)]
    A2 = [P.sb("A2_%d" % l, [128, KC, 2], F32) for l in range(DEPTH)]
    WA = [P.sb("WA%d" % i, [128, KC, 128], BF16) for i in range(3)]
    wa_rr = [0]

    MODg = [[MOD[l].sub("MOD%d_%d" % (l, g)) for g in range(6)] for l in range(DEPTH)]

    class WStream:
        def __init__(self, slots, n, issue):
            self.slots, self.n, self.issue, self.nxt = slots, n, issue, 0

        def need(self, i):
            k = len(self.slots)
            while self.nxt <= min(i + k - 1, self.n - 1):
                self.issue(self.nxt, self.slots[self.nxt % k])
                self.nxt += 1
            return self.slots[i % k]

    def mod_gen(l):
        ws = WStream(WA, 48, lambda i, wa: P.dma(wa[:, :, :].rearrange("p k n -> p (k n)"), w_ada[l, i, :, :],
                                                 writes=[wa], q="pool"))
        for n in range(48):
            wa = ws.need(n)
            pm = PS[6 + mod_bank[0] % 2]
            mod_bank[0] += 1
            for kc in range(KC):
                P.add("pe", lambda e: e.matmul(pm[:, 0:2], wa[:, kc, :], SCV[:, :, kc],
                                               start=(kc == 0), stop=(kc == KC - 1)), reads=[wa, SCV], writes=[pm])
            P.add("dve", lambda e: e.tensor_scalar(MOD[l][:, n, :], pm[:, 0:2], BADA[:, l, n:n + 1], None, ALU.add),
                  reads=[pm, BADA], writes=[MODg[l][n // 8]])
            if n == 15:
                P.add("dve", lambda e: e.scalar_tensor_tensor(
                    A1[l][:, :, :], MOD[l][:, 8:16, :], 1.0, G1[:, l, :].unsqueeze(2).to_broadcast([128, KC, 2]),
                    ALU.add, ALU.mult), reads=[MODg[l][1], G1], writes=[A1[l]])
            if n == 39:
                P.add("dve", lambda e: e.scalar_tensor_tensor(
                    A2[l][:, :, :], MOD[l][:, 32:40, :], 1.0, G2[:, l, :].unsqueeze(2).to_broadcast([128, KC, 2]),
                    ALU.add, ALU.mult), reads=[MODg[l][4], G2], writes=[A2[l]])
            yield n

    mod_state = {"gen": None}
    mod_bank = [0]

    def pump(k):
        g = mod_state["gen"]
        if g is None:
            return
        for _ in range(k):
            try:
                next(g)
            except StopIteration:
                mod_state["gen"] = None
                return

    SQ = [P.sb("SQ%d" % i, [128, TB], BF16) for i in range(2)]
    RSTD = [P.sb("RSTD%d" % i, [128, TB], F32) for i in range(2)]
    TMPN = [P.sb("TMPN%d" % i, [128, TB], F32) for i in range(2)]
    EPSC = P.sb("EPSC", [128, 1], F32)
    P.add("dve", lambda e: e.memset(EPSC[:, :], EPS), writes=[EPSC])
    nrr = [0]

    def emit_norm_mod(l, which):
        A = A1[l] if which == 1 else A2[l]
        shoff = 0 if which == 1 else 24
        for tb in range(NTB):
            v = 0 if tb == 0 else 1
            rstd = RSTD[nrr[0] % 2]
            nrr[0] += 1
            sl = slice(tb * TB, (tb + 1) * TB)
            pb = psum()
            for kc in range(KC):
                sq = SQ[kc % 2]
                P.add("act", lambda e, sq=sq, sl=sl, kc=kc: e.activation(
                    sq[:, :], X[:, kc, sl], AF.Square), reads=[Xb[(kc, tb)]], writes=[sq])
                P.add("pe", lambda e, pb=pb, sq=sq, kc=kc: e.matmul(
                    pb[:, :], ONESB[:, :], sq[:, :], start=(kc == 0), stop=(kc == KC - 1)),
                    reads=[sq, ONESB], writes=[pb])
            P.add("act", lambda e, pb=pb, rstd=rstd: e.activation(
                rstd[:, :], pb[:, :], AF.Ln, bias=EPSC[:, 0:1], scale=1.0 / D),
                reads=[pb, EPSC], writes=[rstd])
            P.add("act", lambda e, rstd=rstd: e.activation(rstd[:, :], rstd[:, :], AF.Exp, scale=-0.5), reads=[rstd], writes=[rstd])
            for kc in range(KC):
                tmp = TMPN[kc % 2]
                P.add("dve", lambda e, tmp=tmp, kc=kc, sl=sl, rstd=rstd: e.tensor_tensor(
                    tmp[:, :], X[:, kc, sl], rstd[:, :], ALU.mult),
                    reads=[Xb[(kc, tb)], rstd], writes=[tmp])
                P.add("act", lambda e, tmp=tmp, kc=kc, sl=sl, v=v, A=A, l=l, shoff=shoff: e.activation(
                    H[:, kc, sl], tmp[:, :], AF.Identity,
                    bias=MOD[l][:, shoff + kc, v:v + 1], scale=A[:, kc, v:v + 1]),
                    reads=[tmp, A, MODg[l][shoff // 8]], writes=[Hb[(kc, tb)]])

    def resid_from_psum(pb, l, gate_off, n, tb, ncols=TB, col0=0):
        v = 0 if tb == 0 else 1
        sl = slice(tb * TB + col0, tb * TB + col0 + ncols)
        P.add("dve", lambda e: e.scalar_tensor_tensor(
            X[:, n, sl], pb[:, 0:ncols], MOD[l][:, gate_off + n, v:v + 1], X[:, n, sl], ALU.mult, ALU.add),
            reads=[pb, MODg[l][gate_off // 8], Xb[(n, tb)]], writes=[Xb[(n, tb)]])

    WUP = [carve(WREG, i * 4096, BF16, [128, KC, 2, 128], "WUP%d" % i) for i in range(2)]
    wup_rr = [0]
    CT = [P.sb("CT%d" % i, [128, NT], F32) for i in range(4)]
    ct_rr = [0]
    NFH = NFC // 2
    GB = carve(ARENA, 0, BF16, [128, NFH, NT], "GB")
    GBb = {(i, tb): GB.sub("GB%d_%d" % (i, tb)) for i in range(NFH) for tb in range(NTB)}
    FCW = P.sb("FCW", [128, DEPTH, 3, 44], F32)
    FCB = P.sb("FCB", [128, DEPTH, 44], F32)
    load_fm(FCW, FCW[:, :, :, :].rearrange("p l t k -> p (l t k)"),
            ffn_conv_w.rearrange("l t (k f) -> (l t k) f", f=128), DEPTH * 3 * 44)
    load_fm(FCB, FCB[:, :, :].rearrange("p l k -> p (l k)"), ffn_conv_b.rearrange("l (k f) -> (l k) f", f=128),
            DEPTH * 44)
    WDN = [carve(WREG, 8192 + i * 2816, BF16, [128, NFH, 128], "WDN%d" % i) for i in range(2)]
    wdn_rr = [0]
    grp_rr = [0]
    dn_rr = [0]

    def bank_group():
        g = grp_rr[0] % 2
        grp_rr[0] += 1
        return g * 3

    def conv_from_psum(b0, w0, w1, w2, bias, ct, ctb):
        pbs = [PS[b0], PS[b0 + 1], PS[b0 + 2]]
        samp = PSALL[:, (b0 + 1) * 512:(b0 + 3) * 512]
        P.add("act", lambda e: e.activation(ct[:, 0:512], PS[b0][:, :], AF.Identity, bias=bias, scale=w1),
              reads=[pbs[0], FCWb], writes=[ctb])
        P.add("act", lambda e: e.activation(ct[:, 512:NT], samp, AF.Identity, bias=bias, scale=w1),
              reads=[pbs[1], pbs[2], FCWb], writes=[ctb])
        for (t0, ln) in SEQS:
            if t0 < 512:
                src = lambda a, b_: PS[b0][:, a:b_]
                rd = [pbs[0]]
            else:
                src = lambda a, b_: PSALL[:, (b0 + 1) * 512 + a - 512:(b0 + 1) * 512 + b_ - 512]
                rd = [pbs[1], pbs[2]]
            P.add("dve", lambda e, src=src, t0=t0, ln=ln: e.scalar_tensor_tensor(
                ct[:, t0 + 1:t0 + ln], src(t0, t0 + ln - 1), w0, ct[:, t0 + 1:t0 + ln], ALU.mult, ALU.add),
                reads=rd + [FCWb, ctb], writes=[ctb])
            P.add("dve", lambda e, src=src, t0=t0, ln=ln: e.scalar_tensor_tensor(
                ct[:, t0:t0 + ln - 1], src(t0 + 1, t0 + ln), w2, ct[:, t0:t0 + ln - 1], ALU.mult, ALU.add),
                reads=rd + [FCWb, ctb], writes=[ctb])

    FCWb = FCW

    def emit_ffn(l):
        wus = WStream(WUP, NFC, lambda i, wu: P.dma(wu[:, :, :, :].rearrange("p k g n -> p (k g n)"),
                                                    ffn_w_up[l, i, :, :], writes=[wu], q="pool"))
        wds = WStream(WDN, 2 * KC, lambda i, wd: P.dma(wd[:, :, :].rearrange("p i n -> p (i n)"),
                                                       ffn_w_down[l, i // KC, i % KC, :, :], writes=[wd], q="pool"))
        for hf in range(2):
            for piece in range(hf * NFH, (hf + 1) * NFH):
                wu = wus.need(piece)
                if piece == (hf + 1) * NFH - 1:
                    wds.need(hf * KC)
                i = piece
                cts = []
                for g in range(2):
                    b0 = bank_group()
                    for tb in range(NTB):
                        pb = PS[b0 + tb]
                        for kc in range(KC):
                            P.add("pe", lambda e: e.matmul(
                                pb[:, :], wu[:, kc, g, :], H[:, kc, tb * TB:(tb + 1) * TB],
                                start=(kc == 0), stop=(kc == KC - 1)),
                                reads=[wu, Hb[(kc, tb)]], writes=[pb])
                    ct = CT[ct_rr[0] % 4]
                    ct_rr[0] += 1
                    ch = g * NFC + i
                    conv_from_psum(b0, FCW[:, l, 0, ch:ch + 1], FCW[:, l, 1, ch:ch + 1], FCW[:, l, 2, ch:ch + 1],
                                   FCB[:, l, ch:ch + 1], ct, ct)
                    cts.append(ct)
                ctv, ctg = cts
                pump(2)
                P.add("act", lambda e: e.activation(ctg[:, :], ctg[:, :], AF.Silu), reads=[ctg], writes=[ctg])
                P.add("pool", lambda e: e.tensor_tensor(GB[:, i - hf * NFH, :], ctv[:, :], ctg[:, :], ALU.mult),
                      reads=[ctv, ctg], writes=[GBb[(i - hf * NFH, tb)] for tb in range(NTB)])
            for piece in range(KC):
                wd = wds.need(hf * KC + piece)
                n = piece
                for tb in range(NTB):
                    pb = PS[dn_rr[0] % 6]
                    dn_rr[0] += 1
                    for i in range(NFH):
                        P.add("pe", lambda e: e.matmul(
                            pb[:, :], wd[:, i, :], GB[:, i, tb * TB:(tb + 1) * TB],
                            start=(i == 0), stop=(i == NFH - 1)),
                            reads=[wd, GBb[(i, tb)]], writes=[pb])
                    resid_from_psum(pb, l, 40, n, tb)
                pump(2)

    WO = [carve(WREG, i * 2048, BF16, [128, KC, 128], "WO%d" % i) for i in range(2)]
    wo_rr = [0]

    def emit_wout(l, w_dram, OT, OTb):
        wos = WStream(WO, KC, lambda i, wo: P.dma(wo[:, :, :].rearrange("p k n -> p (k n)"), w_dram[i, :, :],
                                                  writes=[wo], q="pool"))
        for n in range(KC):
            wo = wos.need(n)
            pump(1)
            for tb in range(NTB):
                pb = psum()
                for kc in range(KC):
                    P.add("pe", lambda e, pb=pb, wo=wo, kc=kc, tb=tb: e.matmul(
                        pb[:, :], wo[:, kc, :], OT[:, kc, tb * TB:(tb + 1) * TB],
                        start=(kc == 0), stop=(kc == KC - 1)), reads=[wo, OTb[kc]], writes=[pb])
                resid_from_psum(pb, l, 16, n, tb)

    rb_t = nc.dram_tensor("rbpad", [480, 127], F32)
    RBPAD = P.view("rbpad", rb_t.ap())
    J2B = P.sb("J2B", [128, 64], BF16)
    P.add("dve", lambda e: e.tensor_copy(J2B[:, :], CONST[:, 128:192]), reads=[CONST], writes=[J2B])

    def emit_rbpad():
        RBP = carve(ARENA, 16384, F32, [128, 4, 127], "RBP")
        P.add("pool", lambda e: e.memset(RBP[:, :, :], 0.0), writes=[RBP])
        P.dma(RBP[0:120, :, 48:79], na_rel_bias.rearrange("j (p a) f -> p (j a) f", a=2)[:, :, :]
              if False else na_rel_bias.rearrange("j r f -> (j r) f").rearrange("(p a) f -> p a f", a=4),
              reads=[], writes=[RBP], allow_slow_non_contiguous=True)
        P.dma(rb_t.ap().rearrange("(p a) f -> p a f", a=4), RBP[0:120, :, :], reads=[RBP], writes=[RBPAD])

    def psum_of(lst, st):
        b = PS[lst[st[0] % len(lst)]]
        st[0] += 1
        return b

    def emit_na(l):
        j = l // 2
        OT = carve(ARENA, 0, BF16, [128, KC, NT], "OT")
        OTb = [OT.sub("OT%d" % c) for c in range(KC)]
        QZ = [carve(ARENA, 24576 + i * 3072, BF16, [128, NT], "QZ%d" % i) for i in range(2)]
        KT = carve(ARENA, 30720, BF16, [128, NT], "KT")
        VT = carve(ARENA, 33792, BF16, [128, 12, 128], "VT")
        VTS = carve(ARENA, 36864, BF16, [128, 7, 128], "VTS")
        KCT = carve(ARENA, 38656, BF16, [128, 256], "KCT")
        VC = carve(ARENA, 39168, BF16, [128, 2, 128], "VC")
        PTC = carve(ARENA, 39680, BF16, [128, 2, 1024], "PTC")
        PTL = [carve(ARENA, 43776 + i * 512, BF16, [128, 4, 64], "PTL%d" % i) for i in range(2)]
        PTP = [carve(ARENA, 44800 + i * 1024, BF16, [128, 2, 256], "PTP%d" % i) for i in range(2)]
        HKR = carve(ARENA, 46848, BF16, [128, 14, 2, 64], "HKR")
        HKZ = carve(ARENA, 50432, BF16, [128, 14, 2, 64], "HKZ")
        CKS = TMPN[0].alias(TMPN[0].t[:, :].rearrange("p (a b) -> p a b", a=4))
        CVS = TMPN[1].alias(TMPN[1].t[:, :].rearrange("p (a b) -> p a b", a=4))
        RD = RSTD[0].alias(RSTD[0].t[:, :])
        KCS = RSTD[1].alias(RSTD[1].t[:, 0:256].rearrange("p (a b) -> p a b", a=2))
        WQ = [carve(WREG, i * 6144, BF16, [128, KC, 3, 128], "WQ%d" % i) for i in range(2)]
        lo = [0]
        hi = [0]
        LO = [0, 1, 2, 3]
        HI = [4, 5, 6, 7]
        P.add("pool", lambda e: e.memset(QZ[0][64:128, :], 0.0), writes=[QZ[0]])
        P.add("pool", lambda e: e.memset(QZ[1][0:64, :], 0.0), writes=[QZ[1]])
        P.add("pool", lambda e: e.memset(HKZ[:, :, :, :], 0.0), writes=[HKZ])
        ptl_rr = [0]
        ptp_rr = [0]
        wqs = WStream(WQ, KC, lambda i, wq: P.dma(wq[:, :, :, :].rearrange("p k g n -> p (k g n)"),
                                                  na_w_qkv[j, i, :, :], writes=[wq], q="pool"))
        for c in range(KC):
            wq = wqs.need(c)
            pump(3)
            for tb in range(NTB):
                sl = slice(tb * TB, (tb + 1) * TB)
                pb = psum_of(LO, lo)
                for kc in range(KC):
                    P.add("pe", lambda e, pb=pb, kc=kc, sl=sl: e.matmul(
                        pb[:, :], wq[:, kc, 0, :], H[:, kc, sl], start=(kc == 0), stop=(kc == KC - 1)),
                        reads=[wq, Hb[(kc, tb)]], writes=[pb])
                P.add("act", lambda e, pb=pb, sl=sl: e.activation(QZ[0][0:64, sl], pb[0:64, :], AF.Copy, scale=0.125),
                      reads=[pb], writes=[QZ[0]])
                P.add("act", lambda e, pb=pb, sl=sl: e.activation(QZ[1][64:128, sl], pb[64:128, :], AF.Copy, scale=0.125),
                      reads=[pb], writes=[QZ[1]])
                pb = psum_of(LO, lo)
                for kc in range(KC):
                    P.add("pe", lambda e, pb=pb, kc=kc, sl=sl: e.matmul(
                        pb[:, :], wq[:, kc, 1, :], H[:, kc, sl], start=(kc == 0), stop=(kc == KC - 1)),
                        reads=[wq, Hb[(kc, tb)]], writes=[pb])
                P.add("dve", lambda e, pb=pb, sl=sl: e.tensor_copy(KT[:, sl], pb[:, :]), reads=[pb], writes=[KT])
            for g in range(3):
                pb = psum_of(LO, lo)
                for b in range(4):
                    blk = g * 4 + b
                    for kc in range(KC):
                        P.add("pe", lambda e, pb=pb, kc=kc, b=b, blk=blk: e.matmul(
                            pb[:, b * 128:(b + 1) * 128], H[:, kc, blk * 128:(blk + 1) * 128], wq[:, kc, 2, :],
                            start=(kc == 0), stop=(kc == KC - 1)),
                            reads=[wq, Hb[(kc, blk // 4)]], writes=[pb])
                P.add("dve", lambda e, pb=pb, g=g: e.tensor_copy(
                    VT[:, g * 4:(g + 1) * 4, :], pb[:, :].rearrange("p (b f) -> p b f", b=4)),
                    reads=[pb], writes=[VT])
                if g == 0:
                    P.add("act", lambda e, pb=pb: e.copy(CVS[:, :, :], pb[:, :].rearrange("p (b f) -> p b f", b=4)),
                          reads=[pb], writes=[CVS])
                    for sq in range(2):
                        P.dma(cv_out[sq, j, :, c * 128:(c + 1) * 128].rearrange("(b p) f -> p b f", p=128),
                              CVS[:, sq * 2:sq * 2 + 2, :], reads=[CVS])
            pb = psum_of(LO, lo)
            for b in range(4):
                for kc in range(KC):
                    P.add("pe", lambda e, pb=pb, kc=kc, b=b: e.matmul(
                        pb[:, b * 128:(b + 1) * 128], H[:, kc, b * 128:(b + 1) * 128], wq[:, kc, 1, :],
                        start=(kc == 0), stop=(kc == KC - 1)), reads=[wq, Hb[(kc, 0)]], writes=[pb])
            P.add("act", lambda e, pb=pb: e.copy(CKS[:, :, :], pb[:, :].rearrange("p (b f) -> p b f", b=4)),
                  reads=[pb], writes=[CKS])
            for sq in range(2):
                P.dma(ck_out[sq, j, :, c * 128:(c + 1) * 128].rearrange("(b p) f -> p b f", p=128),
                      CKS[:, sq * 2:sq * 2 + 2, :], reads=[CKS])
            for g in range(2):
                pb = psum_of(LO, lo)
                nb = 4 if g == 0 else 3
                for b in range(nb):
                    m = g * 4 + b
                    t0 = 512 + 64 + 128 * m
                    for kc in range(KC):
                        P.add("pe", lambda e, pb=pb, kc=kc, b=b, t0=t0: e.matmul(
                            pb[:, b * 128:(b + 1) * 128], H[:, kc, t0:t0 + 128], wq[:, kc, 2, :],
                            start=(kc == 0), stop=(kc == KC - 1)),
                            reads=[wq, Hb[(kc, 1)], Hb[(kc, 2)]], writes=[pb])
                P.add("dve", lambda e, pb=pb, g=g, nb=nb: e.tensor_copy(
                    VTS[:, g * 4:g * 4 + nb, :], pb[:, 0:nb * 128].rearrange("p (b f) -> p b f", b=nb)),
                    reads=[pb], writes=[VTS])
            P.dma(KCS[:, :, :], cache_k[j, :, c * 128:(c + 1) * 128].rearrange("(b p) f -> p b f", p=128),
                  writes=[KCS])
            pb = psum_of(LO, lo)
            for b in range(2):
                P.add("pe", lambda e, pb=pb, b=b: e.transpose(pb[:, b * 128:(b + 1) * 128], KCS[:, b, :], CONST[:, 0:128]),
                      reads=[KCS, CONST], writes=[pb])
            P.add("act", lambda e, pb=pb: e.copy(KCT[:, :], pb[:, 0:256]), reads=[pb], writes=[KCT])
            P.dma(VC[:, :, :], cache_v[j, :, c * 128:(c + 1) * 128].rearrange("(b p) f -> p b f", p=128),
                  writes=[VC], q="pool")
            for sq in range(2):
                pbo = psum_of(HI, hi)
                for hh in range(2):
                    hs = slice(hh * 64, hh * 64 + 64)
                    ptp = PTP[ptp_rr[0] % 2]
                    ptp_rr[0] += 1
                    pbs = psum_of(LO, lo)
                    for kb in range(2):
                        P.add("pe", lambda e, pbs=pbs, kb=kb, hh=hh, sq=sq: e.matmul(
                            pbs[:, kb * 256:(kb + 1) * 256], KT[:, sq * 256 + kb * 128:sq * 256 + (kb + 1) * 128],
                            QZ[hh][:, sq * 256:(sq + 1) * 256], start=True, stop=True),
                            reads=[KT, QZ[hh]], writes=[pbs])
                    P.add("act", lambda e, pbs=pbs, ptp=ptp: e.activation(
                        ptp[:, :, :], pbs[:, :].rearrange("p (b q) -> p b q", b=2), AF.Exp),
                        reads=[pbs], writes=[ptp])
                    for kb in range(2):
                        P.add("pe", lambda e, kb=kb, hs=hs, ptp=ptp, sq=sq: e.matmul(
                            pbo[hs, 0:256], VT[:, sq * 2 + kb, hs], ptp[:, kb, :], start=(kb == 0), stop=(kb == 1)),
                            reads=[VT, ptp], writes=[pbo])
                    for kb in range(2):
                        P.add("pe", lambda e, kb=kb, hs=hs, ptp=ptp: e.matmul(
                            pbo[hs, 256:512], ONESB[:, 0:64], ptp[:, kb, :], start=(kb == 0), stop=(kb == 1)),
                            reads=[ONESB, ptp], writes=[pbo])
                P.add("act", lambda e, pbo=pbo: e.activation(RD[:, 0:256], pbo[:, 256:512], AF.Ln), reads=[pbo], writes=[RD])
                P.add("act", lambda e: e.activation(RD[:, 0:256], RD[:, 0:256], AF.Exp, scale=-1.0), reads=[RD], writes=[RD])
                P.add("dve", lambda e, pbo=pbo, sq=sq: e.tensor_tensor(
                    OT[:, c, sq * 256:(sq + 1) * 256], pbo[:, 0:256], RD[:, 0:256], ALU.mult),
                    reads=[pbo, RD], writes=[OTb[c]])
            pbo_s = [PS[4], PS[5]]
            pbd_s = [PS[6], PS[7]]
            for hh in range(2):
                h = 2 * c + hh
                hs = slice(hh * 64, hh * 64 + 64)
                for u in range(2):
                    src = bass.AP(rb_t, (j * 240 + h * 15 + u) * 127, [[1, 64], [127, 14], [1, 64]])
                    P.dma(HKR[0:64, :, u, :], src, reads=[RBPAD], writes=[HKR], q="pool")
                P.add("pool", lambda e: e.tensor_tensor(
                    HKZ[0:64, :, :, :].rearrange("p a u k -> p (a u) k"),
                    HKR[0:64, :, :, :].rearrange("p a u k -> p (a u) k"),
                    CONST[0:64, 192:256].unsqueeze(1).to_broadcast([64, 28, 64]), ALU.add),
                    reads=[HKR, CONST], writes=[HKZ])
                for kb in range(2):
                    for qb in range(2):
                        pbs = psum_of(LO, lo)
                        P.add("pe", lambda e, pbs=pbs, kb=kb, qb=qb, hh=hh: e.matmul(
                            pbs[:, :], KCT[:, kb * 128:(kb + 1) * 128], QZ[hh][:, 512 + qb * 512:512 + (qb + 1) * 512],
                            start=True, stop=True), reads=[KCT, QZ[hh]], writes=[pbs])
                        P.add("act", lambda e, pbs=pbs, kb=kb, qb=qb: e.activation(
                            PTC[:, kb, qb * 512:(qb + 1) * 512], pbs[:, :], AF.Exp), reads=[pbs], writes=[PTC])
                for r in range(16):
                    rs = min(max(r - 4, 0), 8)
                    ptl = PTL[ptl_rr[0] % 2]
                    ptl_rr[0] += 1
                    pbs = psum_of(LO, lo)
                    qsl = slice(512 + 64 * r, 512 + 64 * r + 64)
                    for i in range(4):
                        kr = rs + 2 * i
                        ri = kr - r + 7
                        k0 = 512 + 64 * kr
                        P.add("pe", lambda e, pbs=pbs, i=i, k0=k0, qsl=qsl, hh=hh: e.matmul(
                            pbs[:, i * 64:(i + 1) * 64], KT[:, k0:k0 + 128], QZ[hh][:, qsl], start=True, stop=False),
                            reads=[KT, QZ[hh]], writes=[pbs])
                        P.add("pe", lambda e, pbs=pbs, i=i, ri=ri: e.matmul(
                            pbs[:, i * 64:(i + 1) * 64], HKZ[:, ri, :, :].rearrange("p u k -> p (u k)"), J2B[:, :],
                            start=False, stop=True), reads=[HKZ, J2B], writes=[pbs])
                    P.add("act", lambda e, pbs=pbs, ptl=ptl: e.activation(
                        ptl[:, :, :], pbs[:, 0:256].rearrange("p (i q) -> p i q", i=4), AF.Exp),
                        reads=[pbs], writes=[ptl])
                    pbo = pbo_s[r // 8]
                    pbd = pbd_s[r // 8]
                    osl = slice((r % 8) * 64, (r % 8) * 64 + 64)
                    for pbx, isden in ((pbo, False), (pbd, True)):
                        for i in range(4):
                            kr = rs + 2 * i
                            if kr % 2 == 0:
                                vv = VT[:, 4 + kr // 2, hs]
                                vb = VT
                            else:
                                vv = VTS[:, (kr - 1) // 2, hs]
                                vb = VTS
                            lhs = ONESB[:, 0:64] if isden else vv
                            P.add("pe", lambda e, pbx=pbx, lhs=lhs, i=i, ptl=ptl, osl=osl, hs=hs: e.matmul(
                                pbx[hs, osl], lhs, ptl[:, i, :], start=(i == 0), stop=False),
                                reads=[vb, ONESB, ptl], writes=[pbx])
                        for kb in range(2):
                            lhs = ONESB[:, 0:64] if isden else VC[:, kb, hs]
                            P.add("pe", lambda e, pbx=pbx, lhs=lhs, kb=kb, osl=osl, hs=hs, r=r: e.matmul(
                                pbx[hs, osl], lhs, PTC[:, kb, 64 * r:64 * r + 64], start=False, stop=(kb == 1)),
                                reads=[VC, ONESB, PTC], writes=[pbx])
            for half in range(2):
                P.add("act", lambda e, half=half: e.activation(RD[:, :], pbd_s[half][:, :], AF.Ln),
                      reads=[pbd_s[half]], writes=[RD])
                P.add("act", lambda e: e.activation(RD[:, :], RD[:, :], AF.Exp, scale=-1.0), reads=[RD], writes=[RD])
                P.add("dve", lambda e, half=half: e.tensor_tensor(
                    OT[:, c, 512 + half * 512:512 + (half + 1) * 512], pbo_s[half][:, :], RD[:, :], ALU.mult),
                    reads=[pbo_s[half], RD], writes=[OTb[c]])
        P.fence()
        emit_wout(l, na_w_out[j], OT, OTb)


    C_I = CONST[:, 0:128]
    C_TRIF = CONST[:, 256:384]
    C_TRIB = CONST[:, 384:512]
    C_EVEN = CONST[:, 512:640]
    C_ODD = CONST[:, 640:768]
    C_MINC = [CONST[:, 768:896], CONST[:, 1024:1152]]
    C_MSTR = [CONST[:, 896:1024], CONST[:, 1152:1280]]
    C_ONES = CONST[:, 1280:1408]
    C_NEG1 = CONST[:, 1408:1536]
    GCW = P.sb("GCW", [128, 2, 3, 24], F32)
    load_fm(GCW, GCW[:, :, :, :].rearrange("p j t k -> p (j t k)"),
            gdn_conv_w.rearrange("j t (k f) -> (j t k) f", f=128), 144)
    GNG = P.sb("GNG", [128, 2], F32)
    load_fm(GNG, GNG[:, :], gdn_norm_g, 2)
    ZEROC = P.sb("ZEROC", [128, 1], F32)
    P.add("dve", lambda e: e.memset(ZEROC[:, :], 0.0), writes=[ZEROC])
    ev_rr = [0]

    def evac(dst_ap, dst_bufs, src_ap, pbs, scale=None):
        use_act = True
        ev_rr[0] += 1
        if use_act:
            if scale is None:
                P.add("act", lambda e: e.copy(dst_ap, src_ap), reads=pbs, writes=dst_bufs)
            else:
                P.add("act", lambda e: e.activation(dst_ap, src_ap, AF.Copy, scale=scale), reads=pbs, writes=dst_bufs)
        else:
            if scale is None:
                P.add("dve", lambda e: e.tensor_copy(dst_ap, src_ap), reads=pbs, writes=dst_bufs)
            else:
                P.add("dve", lambda e: e.tensor_scalar(dst_ap, src_ap, scale, None, ALU.mult), reads=pbs, writes=dst_bufs)

    def emit_gdn(l):
        j = l // 2
        off = [0]

        def AR(dtype, shape, name):
            esz = 2 if dtype == BF16 else 4
            n = esz
            for d_ in shape[1:]:
                n *= d_
            v = carve(ARENA, off[0], dtype, shape, name)
            off[0] += (n + 63) // 64 * 64
            assert off[0] <= 55296, off[0]
            return v

        TK = [AR(F32, [128, 12, 16], "TK%d" % i) for i in range(12)]
        GTOK, GC, BETA, GCB, EGC, BEG, GLE, GLO, EDE, EDO, TMPA, TMPB = TK
        AB = P.view("AB", ARENA.t[:, (10 * 768) // 4:(12 * 768) // 4].rearrange("p (b c) -> p b c", b=12))
        QNb = AR(BF16, [128, NT], "QNb")
        KNb = AR(BF16, [128, NT], "KNb")
        OTh = AR(BF16, [128, NT], "OTh")
        BS = []
        BSR = []
        for d_ in range(2):
            row, rowr = [], []
            for i in range(3):
                k_ = d_ * 3 + i
                apr = CHAIN[:, k_ * 512:(k_ + 1) * 512].rearrange("p (a b) -> p a b", a=4)
                apf = CHAIN.bitcast(F32)[:, k_ * 512:(k_ + 1) * 512].rearrange("p (a b) -> p a b", a=4)
                row.append(P.view("BS%d_%d" % (d_, i), apf))
                rowr.append(apr)
            BS.append(row)
            BSR.append(rowr)
        TTbs = [AR(BF16, [128, 4, 128], "TTb%d" % d_) for d_ in range(2)]
        VBs = [AR(BF16, [128, 4, 128], "VB%d" % d_) for d_ in range(2)]
        KBGs = [AR(BF16, [128, 4, 128], "KBG%d" % d_) for d_ in range(2)]
        F = [b_.alias(b_.t[:, :].rearrange("p (a b) -> p a b", a=4)) for b_ in (RSTD[0], RSTD[1], TMPN[0], TMPN[1])]
        TC = F
        SETS = []
        for i in range(4):
            SETS.append(dict(U=AR(BF16, [128, 4, 128], "U%d" % i), NWT=AR(BF16, [128, 4, 128], "NWT%d" % i),
                             PT=AR(BF16, [128, 4, 128], "PT%d" % i), KD=[AR(BF16, [128, 4, 128], "KDe%d" % i),
                                                                        AR(BF16, [128, 4, 128], "KDo%d" % i)],
                             QG=AR(BF16, [128, 4, 128], "QG%d" % i)))
        CH = []
        for i in range(4):
            CH.append(dict(S=AR(F32, [128, 128], "S%d" % i), Sb=AR(BF16, [128, 128], "Sb%d" % i),
                           VN=AR(BF16, [128, 128], "VN%d" % i)))
        WAB = AR(BF16, [128, KC, 32], "WAB")
        DTB16 = AR(F32, [128, 16], "DTB16")
        NEGA = AR(F32, [128, 16], "NEGA")
        WI = [carve(WREG, 4096 + i * 4096, BF16, [128, KC, 2, 128], "WI%d" % i) for i in range(2)]
        WOH = [carve(WREG, i * 2048, BF16, [128, D], "WOH%d" % i) for i in range(2)]
        CTQ, CTK, CTV, CTZ = CT
        OF = CTQ
        OFb = [OF.sub("OF%d" % b) for b in range(12)]
        for ch in CH:
            P.add("pool", lambda e, ch=ch: e.memset(ch["VN"][:, :], 0.0), writes=[ch["VN"]])

        P.dma(WAB[:, :, :].rearrange("p k n -> p (k n)"), gdn_w_ab[j, :, :], writes=[WAB], q="pool")
        P.dma(DTB16[:, :], gdn_dt_bias[j:j + 1, :].to_broadcast([128, 16]), writes=[DTB16])
        P.dma(NEGA[:, :], gdn_a_log[j:j + 1, :].to_broadcast([128, 16]), writes=[NEGA])
        P.add("act", lambda e: e.activation(NEGA[:, :], NEGA[:, :], AF.Exp), reads=[NEGA], writes=[NEGA])
        pb = psum()
        for blk in range(12):
            for kc in range(KC):
                P.add("pe", lambda e, pb=pb, blk=blk, kc=kc: e.matmul(
                    pb[:, blk * 32:(blk + 1) * 32], H[:, kc, blk * 128:(blk + 1) * 128], WAB[:, kc, :],
                    start=(kc == 0), stop=(kc == KC - 1)), reads=[WAB, Hb[(kc, blk // 4)]], writes=[pb])
        P.add("dve", lambda e, pb=pb: e.tensor_copy(AB[:, :, :], pb[:, 0:384].rearrange("p (b c) -> p b c", b=12)),
              reads=[pb], writes=[TMPA, TMPB])
        bc16 = lambda t: t[:, :].unsqueeze(1).to_broadcast([128, 12, 16])
        P.add("dve", lambda e: e.tensor_tensor(GTOK[:, :, :], AB[:, :, 0:16], bc16(DTB16), ALU.add),
              reads=[TMPA, TMPB, DTB16], writes=[GTOK])
        P.add("act", lambda e: e.activation(GTOK[:, :, :], GTOK[:, :, :], AF.Exp), reads=[GTOK], writes=[GTOK])
        P.add("dve", lambda e: e.tensor_scalar(GTOK[:, :, :], GTOK[:, :, :], 1.0, None, ALU.add), reads=[GTOK], writes=[GTOK])
        P.add("act", lambda e: e.activation(GTOK[:, :, :], GTOK[:, :, :], AF.Ln), reads=[GTOK], writes=[GTOK])
        P.add("dve", lambda e: e.scalar_tensor_tensor(GTOK[:, :, :], GTOK[:, :, :], -1.0, bc16(NEGA), ALU.mult, ALU.mult),
              reads=[GTOK, NEGA], writes=[GTOK])
        P.add("act", lambda e: e.activation(BETA[:, :, :], AB[:, :, 16:32], AF.Exp, scale=-1.0),
              reads=[TMPA, TMPB], writes=[BETA])
        P.add("dve", lambda e: e.tensor_scalar(BETA[:, :, :], BETA[:, :, :], 1.0, None, ALU.add), reads=[BETA], writes=[BETA])
        P.add("act", lambda e: e.activation(GCB[:, :, :], BETA[:, :, :], AF.Ln), reads=[BETA], writes=[GCB])
        P.add("dve", lambda e: e.reciprocal(BETA[:, :, :], BETA[:, :, :]), reads=[BETA], writes=[BETA])
        pb = psum()
        for blk in range(12):
            for d in range(2):
                tri = C_TRIF if d == 0 else C_TRIB
                P.add("pe", lambda e, pb=pb, blk=blk, d=d, tri=tri: e.matmul(
                    pb[:, blk * 16 + d * 8:blk * 16 + d * 8 + 8], tri, GTOK[:, blk, d * 8:d * 8 + 8],
                    start=True, stop=True), reads=[CONST, GTOK], writes=[pb])
        P.add("dve", lambda e, pb=pb: e.tensor_copy(GC[:, :, :], pb[:, 0:192].rearrange("p (b c) -> p b c", b=12)),
              reads=[pb], writes=[GC])
        for (dst, cm) in ((GLE, C_EVEN), (GLO, C_ODD)):
            pb = psum()
            for blk in range(12):
                P.add("pe", lambda e, pb=pb, blk=blk, cm=cm: e.matmul(
                    pb[:, blk * 16:(blk + 1) * 16], cm, GTOK[:, blk, :], start=True, stop=True),
                    reads=[CONST, GTOK], writes=[pb])
            P.add("dve", lambda e, pb=pb, dst=dst: e.tensor_copy(
                dst[:, :, :], pb[:, 0:192].rearrange("p (b c) -> p b c", b=12)), reads=[pb], writes=[dst])
        P.add("dve", lambda e: e.tensor_tensor(GCB[:, :, :], GC[:, :, :], GCB[:, :, :], ALU.subtract),
              reads=[GC, GCB], writes=[GCB])
        P.add("act", lambda e: e.activation(EGC[:, :, :], GC[:, :, :], AF.Exp), reads=[GC], writes=[EGC])
        P.add("dve", lambda e: e.tensor_tensor(BEG[:, :, :], BETA[:, :, :], EGC[:, :, :], ALU.mult),
              reads=[BETA, EGC], writes=[BEG])
        for (dst, gl, mcol) in ((EDE, GLE, C_EVEN[:, 0:1]), (EDO, GLO, C_ODD[:, 0:1])):
            P.add("dve", lambda e, dst=dst, gl=gl: e.tensor_tensor(dst[:, :, :], gl[:, :, :], GC[:, :, :], ALU.subtract),
                  reads=[gl, GC], writes=[dst])
            P.add("dve", lambda e, dst=dst: e.tensor_scalar(dst[:, :, :], dst[:, :, :], 0.0, None, ALU.min),
                  reads=[dst], writes=[dst])
            P.add("act", lambda e, dst=dst: e.activation(dst[:, :, :], dst[:, :, :], AF.Exp), reads=[dst], writes=[dst])
            P.add("dve", lambda e, dst=dst, mcol=mcol: e.tensor_scalar(dst[:, :, :], dst[:, :, :], mcol, None, ALU.mult),
                  reads=[dst, CONST], writes=[dst])
        P.add("act", lambda e: e.activation(GLE[:, :, :], GLE[:, :, :], AF.Exp), reads=[GLE], writes=[GLE])
        P.add("act", lambda e: e.activation(GLO[:, :, :], GLO[:, :, :], AF.Exp), reads=[GLO], writes=[GLO])
        EGL = [GLE, GLO]
        ED = [EDE, EDO]
        NGC = TMPA
        P.add("dve", lambda e: e.tensor_scalar(NGC[:, :, :], GC[:, :, :], -1.0, None, ALU.mult), reads=[GC], writes=[NGC])

        wis_ = WStream(WI, 16, lambda i, wi: P.dma(wi[:, :, :, :].rearrange("p k g n -> p (k g n)"),
                                                   gdn_w_in[j, i // 2, i % 2, :, :], writes=[wi], q="pool"))
        whs_ = WStream(WOH, 8, lambda i, woh: P.dma(woh[:, :], gdn_w_out[j, i * 128:(i + 1) * 128, :],
                                                    writes=[woh], q="pool"))
        for h in range(8):
            pump(3)
            for pi in range(2):
                wi = wis_.need(h * 2 + pi)
                for g in range(2):
                    t = pi * 2 + g
                    b0 = bank_group()
                    for tb in range(NTB):
                        pbk = PS[b0 + tb]
                        for kc in range(KC):
                            P.add("pe", lambda e, pbk=pbk, wi=wi, kc=kc, g=g, tb=tb: e.matmul(
                                pbk[:, :], wi[:, kc, g, :], H[:, kc, tb * TB:(tb + 1) * TB],
                                start=(kc == 0), stop=(kc == KC - 1)), reads=[wi, Hb[(kc, tb)]], writes=[pbk])
                    ct = CT[t]
                    if t < 3:
                        ch = t * 8 + h
                        conv_from_psum(b0, GCW[:, j, 0, ch:ch + 1], GCW[:, j, 1, ch:ch + 1], GCW[:, j, 2, ch:ch + 1],
                                       ZEROC[:, 0:1], ct, ct)
                        P.add("act", lambda e, ct=ct: e.activation(ct[:, :], ct[:, :], AF.Silu), reads=[ct], writes=[ct])
                    else:
                        P.add("act", lambda e, ct=ct, b0=b0: e.activation(ct[:, 0:512], PS[b0][:, :], AF.Silu),
                              reads=[PS[b0]], writes=[ct])
                        P.add("act", lambda e, ct=ct, b0=b0: e.activation(
                            ct[:, 512:NT], PSALL[:, (b0 + 1) * 512:(b0 + 3) * 512], AF.Silu),
                            reads=[PS[b0 + 1], PS[b0 + 2]], writes=[ct])
            for (ct, dstb, sc, keep) in ((CTQ, QNb, 128.0 ** -0.5, False), (CTK, KNb, 1.0, True)):
                for tb in range(NTB):
                    sl = slice(tb * TB, (tb + 1) * TB)
                    sqf = TC[tb % 2]
                    rs = TC[2 + tb % 2]
                    sqv = sqf[:, :, :].rearrange("p a b -> p (a b)")
                    rsv = rs[:, :, :].rearrange("p a b -> p (a b)")
                    P.add("act", lambda e, ct=ct, sl=sl, sqv=sqv: e.activation(sqv, ct[:, sl], AF.Square),
                          reads=[ct], writes=[sqf])
                    pbk = psum()
                    P.add("pe", lambda e, pbk=pbk, sqv=sqv: e.matmul(pbk[:, :], C_ONES, sqv, start=True, stop=True),
                          reads=[CONST, sqf], writes=[pbk])
                    P.add("act", lambda e, pbk=pbk, rsv=rsv: e.activation(rsv, pbk[:, :], AF.Ln, bias=EPSC[:, 0:1], scale=1.0),
                          reads=[pbk, EPSC], writes=[rs])
                    P.add("act", lambda e, rsv=rsv: e.activation(rsv, rsv, AF.Exp, scale=-0.5), reads=[rs], writes=[rs])
                    if keep:
                        P.add("pool", lambda e, ct=ct, sl=sl, rsv=rsv: e.tensor_tensor(ct[:, sl], ct[:, sl], rsv, ALU.mult),
                              reads=[ct, rs], writes=[ct])
                        P.add("act", lambda e, ct=ct, sl=sl, dstb=dstb: e.copy(dstb[:, sl], ct[:, sl]),
                              reads=[ct], writes=[dstb])
                    else:
                        P.add("pool", lambda e, ct=ct, sl=sl, rsv=rsv, dstb=dstb: e.tensor_tensor(
                            dstb[:, sl], ct[:, sl], rsv, ALU.mult), reads=[ct, rs], writes=[dstb])
            P.add("pool", lambda e: e.memset(OF[:, :], 0.0), writes=[OF] + OFb)

            f4 = lambda T_: T_[:, :, :].rearrange("p a b -> p (a b)")
            v4 = lambda pb_: pb_[:, :].rearrange("p (b f) -> p b f", b=4)
            Ibc = C_I.unsqueeze(1).to_broadcast([128, 4, 128])

            def pre_dir(g, d, st, pk, pv, pg, pq):
                cd = d * 8 + h
                blks = slice(g * 4, g * 4 + 4)
                tsl = slice(g * 512, (g + 1) * 512)
                col = lambda T_: T_[:, blks, cd:cd + 1].to_broadcast([128, 4, 128])
                B = BS[d]
                Fa, Fb = F[2 * d], F[2 * d + 1]
                VB, KBG = VBs[d], KBGs[d]
                P.add("dve", lambda e: e.tensor_tensor(VB[:, :, :], v4(pv), col(BETA), ALU.mult),
                      reads=[pv, BETA], writes=[VB])
                P.add("dve", lambda e: e.tensor_tensor(KBG[:, :, :], v4(pk), col(BEG), ALU.mult),
                      reads=[pk, BEG], writes=[KBG])
                for u in range(2):
                    P.add("dve", lambda e: e.tensor_tensor(st["KD"][u][:, :, :], v4(pk), col(ED[u]), ALU.mult),
                          reads=[pk, ED[u]], writes=[st["KD"][u]])
                yield
                P.add("pool", lambda e: e.tensor_tensor(Fa[:, :, :], Ibc, col(EGC), ALU.mult),
                      reads=[CONST, EGC], writes=[Fa])
                pr = psum()
                for b in range(4):
                    P.add("pe", lambda e: e.matmul(pr[:, b * 128:(b + 1) * 128], C_ONES, Fa[:, b, :],
                                                   start=True, stop=True), reads=[CONST, Fa], writes=[pr])
                P.add("dve", lambda e: e.scalar_tensor_tensor(f4(st["QG"]), pr[:, :], 128.0 ** -0.5, QNb[:, tsl],
                                                              ALU.mult, ALU.mult), reads=[pr, QNb], writes=[st["QG"]])
                yield
                P.add("pool", lambda e: e.tensor_tensor(Fa[:, :, :], Ibc, col(GC), ALU.mult),
                      reads=[CONST, GC], writes=[Fa])
                P.add("pool", lambda e: e.tensor_tensor(Fb[:, :, :], Ibc, col(GCB), ALU.mult),
                      reads=[CONST, GCB], writes=[Fb])
                pa = psum()
                pbb = psum()
                for (pz, dg, msk) in ((pa, Fa, C_MINC[d]), (pbb, Fb, C_MSTR[d])):
                    for b in range(4):
                        o_ = pz[:, b * 128:(b + 1) * 128]
                        P.add("pe", lambda e: e.matmul(o_, C_ONES, dg[:, b, :], start=True, stop=False),
                              reads=[CONST, dg], writes=[pz])
                        P.add("pe", lambda e: e.matmul(o_, C_I, msk, start=False, stop=True),
                              reads=[CONST], writes=[pz])
                BT, Bs_, M_ = B
                BTr, Bsr, Mr = [x_.t for x_ in B]
                r4 = lambda ap_: ap_[:, :, :].rearrange("p a b -> p (a b)")
                for b in range(4):
                    ngc = NGC[:, g * 4 + b, cd:cd + 1]
                    P.add("act", lambda e: e.activation(Mr[:, b, :], pa[:, b * 128:(b + 1) * 128], AF.Exp, bias=ngc),
                          reads=[pa, NGC], writes=[M_])
                    P.add("act", lambda e: e.activation(BTr[:, b, :], pbb[:, b * 128:(b + 1) * 128], AF.Exp, bias=ngc),
                          reads=[pbb, NGC], writes=[BT])
                yield
                pg = psum()
                pq = psum()
                for b in range(4):
                    c0 = g * 512 + b * 128
                    P.add("pe", lambda e: e.matmul(pg[:, b * 128:(b + 1) * 128], KNb[:, c0:c0 + 128], KNb[:, c0:c0 + 128],
                                                   start=True, stop=True), reads=[KNb], writes=[pg])
                for b in range(4):
                    c0 = g * 512 + b * 128
                    P.add("pe", lambda e: e.matmul(pq[:, b * 128:(b + 1) * 128], KNb[:, c0:c0 + 128], QNb[:, c0:c0 + 128],
                                                   start=True, stop=True), reads=[KNb, QNb], writes=[pq])
                P.add("dve", lambda e: e.scalar_tensor_tensor(f4(st["PT"]), pq[:, :], 128.0 ** -0.5, f4(M_),
                                                              ALU.mult, ALU.mult), reads=[pq, M_], writes=[st["PT"]])
                P.add("dve", lambda e: e.scalar_tensor_tensor(r4(BTr), pg[:, :], -1.0, f4(BT), ALU.mult, ALU.mult),
                      reads=[pg, BT], writes=[BT])
                yield
                pt_ = psum()
                for b in range(4):
                    P.add("pe", lambda e: e.transpose(pt_[:, b * 128:(b + 1) * 128], BT[:, b, :], C_I),
                          reads=[BT, CONST], writes=[pt_])
                evac(r4(Bsr), [Bs_], pt_[:, :], [pt_])
                P.add("pool", lambda e: e.tensor_tensor(Mr[:, :, :], BT[:, :, :], Ibc, ALU.add),
                      reads=[BT, CONST], writes=[M_])
                yield
                TTb = TTbs[d]
                for k in range(5):
                    if k < 4:
                        px = psum()
                        for b in range(4):
                            P.add("pe", lambda e: e.matmul(px[:, b * 128:(b + 1) * 128], Bsr[:, b, :], BTr[:, b, :],
                                                           start=True, stop=True), reads=[Bs_, BT], writes=[px])
                    py = psum()
                    for b in range(4):
                        P.add("pe", lambda e: e.matmul(py[:, b * 128:(b + 1) * 128], BTr[:, b, :], Bsr[:, b, :],
                                                       start=True, stop=True), reads=[Bs_, BT], writes=[py])
                    if k < 4:
                        evac(r4(BTr), [BT], px[:, :], [px])
                    evac(r4(Bsr), [Bs_], py[:, :], [py])
                    yield
                    pm_ = psum()
                    for b in range(4):
                        o_ = pm_[:, b * 128:(b + 1) * 128]
                        P.add("pe", lambda e: e.matmul(o_, Bsr[:, b, :], Mr[:, b, :], start=True, stop=True),
                              reads=[Bs_, M_], writes=[pm_])
                    if k < 4:
                        P.add("dve", lambda e: e.tensor_tensor(r4(Mr), pm_[:, :], f4(M_), ALU.add),
                              reads=[pm_, M_], writes=[M_])
                    else:
                        P.add("dve", lambda e: e.tensor_tensor(f4(TTb), pm_[:, :], f4(M_), ALU.add),
                              reads=[pm_, M_], writes=[TTb])
                    yield
                pu = psum()
                pw = psum()
                for b in range(4):
                    P.add("pe", lambda e: e.matmul(pu[:, b * 128:(b + 1) * 128], TTb[:, b, :], VB[:, b, :],
                                                   start=True, stop=True), reads=[TTb, VB], writes=[pu])
                for b in range(4):
                    P.add("pe", lambda e: e.matmul(pw[:, b * 128:(b + 1) * 128], KBG[:, b, :], TTb[:, b, :],
                                                   start=True, stop=True), reads=[TTb, KBG], writes=[pw])
                evac(f4(st["U"]), [st["U"]], pu[:, :], [pu])
                evac(f4(st["NWT"]), [st["NWT"]], pw[:, :], [pw], scale=-1.0)
                yield

            def precompute2_gen(g, sts):
                pk = psum()
                pv = psum()
                for b in range(4):
                    c0 = g * 512 + b * 128
                    P.add("pe", lambda e: e.transpose(pk[:, b * 128:(b + 1) * 128], CTK[:, c0:c0 + 128], C_I),
                          reads=[CTK, CONST], writes=[pk])
                for b in range(4):
                    c0 = g * 512 + b * 128
                    P.add("pe", lambda e: e.transpose(pv[:, b * 128:(b + 1) * 128], CTV[:, c0:c0 + 128], C_I),
                          reads=[CTV, CONST], writes=[pv])
                gens = [pre_dir(g, d_, sts[d_], pk, pv, None, None) for d_ in range(2)]
                while gens:
                    for gi in list(gens):
                        try:
                            next(gi)
                        except StopIteration:
                            gens.remove(gi)
                    yield

            def interleave(gen, rounds, every):
                i = 0
                r = 0
                for _ in gen:
                    i += 1
                    if i % every == 0 and r < len(rounds):
                        rounds[r]()
                        r += 1
                while r < len(rounds):
                    rounds[r]()
                    r += 1

            def scan_step(ch, st, d, blk, u):
                cd = d * 8 + h
                bi = blk % 4
                rsl = slice(u * 64, u * 64 + 64)
                S, Sb, VN = ch["S"], ch["Sb"], ch["VN"]
                p1 = psum()
                P.add("pe", lambda e: e.matmul(p1[rsl, 0:128], st["NWT"][:, bi, rsl], Sb[:, :], start=True, stop=True),
                      reads=[st["NWT"], Sb], writes=[p1])
                P.add("dve", lambda e: e.tensor_tensor(VN[rsl, :], st["U"][rsl, bi, :], p1[rsl, 0:128], ALU.add),
                      reads=[st["U"], p1], writes=[VN])
                p2 = psum()
                P.add("pe", lambda e: e.matmul(p2[:, 0:64], Sb[:, :], st["QG"][:, bi, rsl], start=True, stop=False),
                      reads=[Sb, st["QG"]], writes=[p2])
                P.add("pe", lambda e: e.matmul(p2[:, 0:64], VN[:, :], st["PT"][:, bi, rsl], start=False, stop=True),
                      reads=[VN, st["PT"]], writes=[p2])
                c0 = blk * 128 + u * 64
                P.add("dve", lambda e: e.tensor_tensor(OF[:, c0:c0 + 64], OF[:, c0:c0 + 64], p2[:, 0:64], ALU.add),
                      reads=[OFb[blk], p2], writes=[OFb[blk]])
                p3 = psum()
                P.add("pe", lambda e: e.matmul(p3[:, 0:128], st["KD"][u][:, bi, :], VN[:, :], start=True, stop=True),
                      reads=[st["KD"][u], VN], writes=[p3])
                P.add("dve", lambda e: e.scalar_tensor_tensor(Sb[:, :], S[:, :], EGL[u][:, blk, cd:cd + 1], p3[:, 0:128],
                                                              ALU.mult, ALU.add), reads=[S, EGL[u], p3], writes=[Sb])
                P.add("dve", lambda e: e.scalar_tensor_tensor(S[:, :], S[:, :], EGL[u][:, blk, cd:cd + 1], p3[:, 0:128],
                                                              ALU.mult, ALU.add), reads=[S, EGL[u], p3], writes=[S])

            def chain_steps(d, blocks):
                steps = [(b, u) for b in blocks for u in range(2)]
                return steps if d == 0 else steps[::-1]

            def init_chain(ch, d, seq):
                if seq < 2:
                    P.add("pool", lambda e: e.memset(ch["S"][:, :], 0.0), writes=[ch["S"]])
                else:
                    P.dma(ch["S"][:, :], state_gdn[j, d, h, :, :], writes=[ch["S"]])
                P.add("act", lambda e: e.copy(ch["Sb"][:, :], ch["S"][:, :]), reads=[ch["S"]], writes=[ch["Sb"]])

            for _ in precompute2_gen(0, [SETS[2], SETS[3]]):
                pass
            chains = []
            for d in range(2):
                for seq in range(2):
                    ch = CH[d * 2 + seq]
                    init_chain(ch, d, seq)
                    chains.append((ch, SETS[2 + d], d, seq, chain_steps(d, [2 * seq, 2 * seq + 1])))

            def roundA(si):
                for (ch, st, d, seq, steps) in chains:
                    scan_step(ch, st, d, steps[si][0], steps[si][1])

            interleave(precompute2_gen(1, [SETS[0], SETS[1]]), [lambda si=si: roundA(si) for si in range(4)], 3)
            for (ch, st, d, seq, steps) in chains:
                P.dma(st_out[seq, j, d, h, :, :], ch["S"][:, :], reads=[ch["S"]])
            chB = [CH[0], CH[1]]
            stepsB = [chain_steps(d, list(range(4, 12))) for d in range(2)]
            for d in range(2):
                init_chain(chB[d], d, 2)

            def stepB(d, si):
                blk, u = stepsB[d][si]
                scan_step(chB[d], SETS[(blk // 4 - 1) * 2 + d], d, blk, u)

            interleave(precompute2_gen(2, [SETS[2], SETS[3]]), [lambda si=si: stepB(0, si) for si in range(8)], 2)
            for si in range(16):
                stepB(1, si)
                if si < 8:
                    stepB(0, 8 + si)

            for tb in range(NTB):
                sl = slice(tb * TB, (tb + 1) * TB)
                sqf = TC[tb % 2]
                rs = TC[2 + tb % 2]
                sqv = sqf[:, :, :].rearrange("p a b -> p (a b)")
                rsv = rs[:, :, :].rearrange("p a b -> p (a b)")
                P.add("act", lambda e, sl=sl, sqv=sqv: e.activation(sqv, OF[:, sl], AF.Square),
                      reads=OFb[tb * 4:tb * 4 + 4] + [OF], writes=[sqf])
                pbk = psum()
                P.add("pe", lambda e, pbk=pbk, sqv=sqv: e.matmul(pbk[:, :], C_ONES, sqv, start=True, stop=True),
                      reads=[CONST, sqf], writes=[pbk])
                P.add("act", lambda e, pbk=pbk, rsv=rsv: e.activation(rsv, pbk[:, :], AF.Ln, bias=EPSC[:, 0:1], scale=1.0 / 128),
                      reads=[pbk, EPSC], writes=[rs])
                P.add("act", lambda e, rsv=rsv: e.activation(rsv, rsv, AF.Exp, scale=-0.5), reads=[rs], writes=[rs])
                P.add("pool", lambda e, sl=sl, rsv=rsv: e.tensor_tensor(rsv, OF[:, sl], rsv, ALU.mult),
                      reads=OFb[tb * 4:tb * 4 + 4] + [rs, OF], writes=[rs])
                P.add("dve", lambda e, sl=sl, rsv=rsv: e.scalar_tensor_tensor(
                    OTh[:, sl], rsv, GNG[:, j:j + 1], CTZ[:, sl], ALU.mult, ALU.mult), reads=[rs, GNG, CTZ], writes=[OTh])
            woh = whs_.need(h)
            for n in range(KC):
                for tb in range(NTB):
                    pbk = psum()
                    P.add("pe", lambda e, pbk=pbk, n=n, tb=tb, woh=woh: e.matmul(
                        pbk[:, :], woh[:, n * 128:(n + 1) * 128], OTh[:, tb * TB:(tb + 1) * TB], start=True, stop=True),
                        reads=[woh, OTh], writes=[pbk])
                    resid_from_psum(pbk, l, 16, n, tb)

    P.fence()
    emit_rbpad()
    mod_state["gen"] = mod_gen(0)
    pump(24)
    for l in range(cfg.n_layers):
        emit_norm_mod(l, 1)
        P.fence()
        if l % 2 == 1 and cfg.do_na:
            emit_na(l)
        if l % 2 == 0 and cfg.do_gdn:
            emit_gdn(l)
        pump(48)
        emit_norm_mod(l, 2)
        P.fence()
        if l + 1 < cfg.n_layers:
            mod_state["gen"] = mod_gen(l + 1)
        emit_ffn(l)
        pump(48)
    P.fence()

    FG = carve(ARENA, 8192, F32, [128, D], "FG")
    P.dma(FG[:, :], final_g.rearrange("(o d) -> o d", o=1).to_broadcast([128, D]), writes=[FG])
    YT = [carve(ARENA, i * 4096, F32, [128, D], "YT%d" % i) for i in range(2)]
    YSQ = carve(ARENA, 12288, F32, [128, D], "YSQ")
    SS = [P.sb("SS%d" % i, [128, 1], F32) for i in range(2)]
    for blk in range(NT // 128):
        yt = YT[blk % 2]
        ss = SS[blk % 2]
        tb = (blk * 128) // TB
        for half in range(2):
            pb = psum()
            for j in range(4):
                kc = half * 4 + j
                P.add("pe", lambda e, pb=pb, kc=kc, j=j, blk=blk: e.transpose(
                    pb[:, j * 128:(j + 1) * 128], X[:, kc, blk * 128:(blk + 1) * 128], CONST[:, 0:128]),
                    reads=[Xb[(kc, tb)], CONST], writes=[pb])
            P.add("act", lambda e, pb=pb, yt=yt, half=half: e.copy(yt[:, half * 512:(half + 1) * 512], pb[:, :]),
                  reads=[pb], writes=[yt])
        P.add("act", lambda e, yt=yt, ss=ss: e.activation(YSQ[:, :], yt[:, :], AF.Square, accum_out=ss[:, 0:1]),
              reads=[yt], writes=[YSQ, ss])
        P.add("act", lambda e, ss=ss: e.activation(ss[:, :], ss[:, :], AF.Ln, bias=EPSC[:, 0:1], scale=1.0 / D),
              reads=[ss, EPSC], writes=[ss])
        P.add("act", lambda e, ss=ss: e.activation(ss[:, :], ss[:, :], AF.Exp, scale=-0.5), reads=[ss], writes=[ss])
        P.add("dve", lambda e, yt=yt, ss=ss: e.scalar_tensor_tensor(
            yt[:, :], yt[:, :], ss[:, 0:1], FG[:, :], ALU.mult, ALU.mult), reads=[yt, ss, FG], writes=[yt])
        P.dma(y_out[blk * 128:(blk + 1) * 128, :], yt[:, :], reads=[yt])

    P.emit()
    return nc


def make_consts():
    c = np.zeros((128, 1536), np.float32)
    c[:, 0:128] = np.eye(128, dtype=np.float32)
    J = np.zeros((64, 64), np.float32)
    J[np.arange(64), 63 - np.arange(64)] = 1.0
    c[0:64, 128:192] = J
    c[64:128, 128:192] = J
    qc = 63 - np.arange(64)[:, None]
    kc = np.arange(64)[None, :]
    cs = np.clip(qc - 8, 0, 48)
    inside = (kc >= cs) & (kc < cs + 16)
    c[0:64, 192:256] = np.where(inside, 0.0, -30000.0)
    jj = np.arange(128)[:, None]
    ii = np.arange(128)[None, :]
    same = (jj // 64) == (ii // 64)
    c[:, 256:384] = (same & (jj <= ii)).astype(np.float32)
    c[:, 384:512] = (same & (jj >= ii)).astype(np.float32)
    c[:, 512:640] = (jj < 64).astype(np.float32) * np.ones((1, 128), np.float32)
    c[:, 640:768] = (jj >= 64).astype(np.float32) * np.ones((1, 128), np.float32)
    NEG = -30000.0
    c[:, 768:896] = np.where(same & (jj <= ii), 0.0, NEG)
    c[:, 896:1024] = np.where(same & (jj < ii), 0.0, NEG)
    c[:, 1024:1152] = np.where(same & (jj >= ii), 0.0, NEG)
    c[:, 1152:1280] = np.where(same & (jj > ii), 0.0, NEG)
    c[:, 1280:1408] = 1.0
    c[:, 1408:1536] = -1.0
    return c


_NC_CACHE = {}


def kernel(x_prompt, x_sample, state_gdn, cache_k, cache_v, c, c_ctx, w_ada, b_ada, norm1_g, norm2_g,
           gdn_w_in, gdn_conv_w, gdn_a_log, gdn_dt_bias, gdn_norm_g, gdn_w_out,
           na_w_qkv, na_rel_bias, na_w_out, ffn_w_up, ffn_conv_w, ffn_conv_b, ffn_w_down, final_g, _cfg=Cfg, _cores=8):
    f = lambda a: np.ascontiguousarray(np.asarray(a, dtype=np.float32))
    nc = build(_cfg)
    consts = make_consts()
    c_ = np.ascontiguousarray
    w_ada_t = c_(f(w_ada).reshape(4, 8, 128, 48, 128).transpose(0, 3, 2, 1, 4)).reshape(4, 48, 128, 1024)
    w_up_t = c_(f(ffn_w_up).reshape(4, 8, 128, 2, 22, 128).transpose(0, 4, 2, 1, 3, 5)).reshape(4, 22, 128, 2048)
    w_dn_t = c_(f(ffn_w_down).reshape(4, 2, 11, 128, 8, 128).transpose(0, 1, 4, 3, 2, 5)).reshape(4, 2, 8, 128, 1408)
    qkv_t = c_(f(na_w_qkv).reshape(2, 8, 128, 3, 8, 128).transpose(0, 4, 2, 1, 3, 5)).reshape(2, 8, 128, 3072)
    nwo_t = c_(f(na_w_out).reshape(2, 8, 128, 8, 128).transpose(0, 3, 2, 1, 4)).reshape(2, 8, 128, 1024)
    gin = f(gdn_w_in)
    gin_t = c_(gin[:, :, :4096].reshape(2, 8, 128, 2, 2, 8, 128).transpose(0, 5, 3, 2, 1, 4, 6)).reshape(2, 8, 2, 128, 2048)
    gab_t = c_(gin[:, :, 4096:4128].reshape(2, 8, 128, 32).transpose(0, 2, 1, 3)).reshape(2, 128, 256)
    shared = {
        "w_ada": w_ada_t, "b_ada": f(b_ada), "norm1_g": f(norm1_g), "norm2_g": f(norm2_g),
        "gdn_w_in": gin_t, "gdn_w_ab": gab_t, "gdn_conv_w": f(gdn_conv_w),
        "gdn_a_log": f(gdn_a_log).reshape(2, 16), "gdn_dt_bias": f(gdn_dt_bias).reshape(2, 16),
        "gdn_norm_g": f(gdn_norm_g), "gdn_w_out": f(gdn_w_out),
        "na_w_qkv": qkv_t, "na_rel_bias": f(na_rel_bias).reshape(2, 240, 31), "na_w_out": nwo_t,
        "ffn_w_up": w_up_t, "ffn_conv_w": f(ffn_conv_w), "ffn_conv_b": f(ffn_conv_b),
        "ffn_w_down": w_dn_t, "final_g": f(final_g), "consts": consts,
    }
    xp = f(x_prompt)
    xs = f(x_sample)
    in_maps = []
    for i in range(_cores):
        m = dict(shared)
        m["x_in"] = np.concatenate([xp[2 * i].reshape(256, D), xp[2 * i + 1].reshape(256, D), xs[i]], axis=0)
        m["state_gdn"] = f(state_gdn[i])
        m["cache_k"] = f(cache_k[i]).reshape(2, 256, 1024)
        m["cache_v"] = f(cache_v[i]).reshape(2, 256, 1024)
        m["cvec"] = np.stack([f(c_ctx), f(c[i])], axis=0)
        in_maps.append(m)
    res = run_bass_kernel_spmd(nc, in_maps, core_ids=list(range(_cores)))
    R = res.results
    y_prompt = np.zeros((16, 256, D), np.float32)
    y_sample = np.zeros((8, 1024, D), np.float32)
    new_state = np.zeros((16, 2, 2, 8, 128, 128), np.float32)
    new_k = np.zeros((16, 2, 256, 16, 64), np.float32)
    new_v = np.zeros((16, 2, 256, 16, 64), np.float32)
    for i in range(_cores):
        y = R[i]["y_out"]
        y_prompt[2 * i] = y[0:256]
        y_prompt[2 * i + 1] = y[256:512]
        y_sample[i] = y[512:]
        new_state[2 * i:2 * i + 2] = R[i]["st_out"]
        new_k[2 * i:2 * i + 2] = R[i]["ck_out"].reshape(2, 2, 256, 16, 64)
        new_v[2 * i:2 * i + 2] = R[i]["cv_out"].reshape(2, 2, 256, 16, 64)
    return (y_prompt, y_sample, new_state, new_k, new_v)
```

```python
import numpy as np
import concourse.bass as bass
import concourse.mybir as mybir
from concourse.bass_utils import run_bass_kernel_spmd

F32 = mybir.dt.float32
F32R = mybir.dt.float32r
BF16 = mybir.dt.bfloat16
AF = mybir.ActivationFunctionType
ALU = mybir.AluOpType
AX = mybir.AxisListType

D = 1024
KC = 8
DEPTH = 4
NP_TOK = 512
NS_TOK = 1024
NT = NP_TOK + NS_TOK
TB = 512
NTB = NT // TB
D_FF = 2816
NFC = D_FF // 128
GDN_DIN = 4128
EPS = 1e-6
SEQS = [(0, 256), (256, 256), (512, 1024)]


class Buf:
    __slots__ = ("name", "t", "psum", "lastw", "readers", "dma_readers", "root")

    def __init__(self, name, t, psum=False):
        self.name = name
        self.t = t
        self.psum = psum
        self.lastw = None
        self.readers = {}
        self.dma_readers = []
        self.root = self

    def alias(self, ap, name=None):
        b = Buf(name or self.name, ap, self.psum)
        b.root = self.root
        return b

    def __getitem__(self, idx):
        return self.t[idx]

    def sub(self, name=None):
        return Buf(name or self.name, self.t, self.psum)


class Op:
    __slots__ = ("eng", "fn", "reads", "writes", "dma", "waits", "sig", "sem", "sigval", "deps", "n")


class _Rec:
    def __init__(self):
        self.call = None

    def __getattr__(self, name):
        def f(*args, **kwargs):
            assert self.call is None
            self.call = (name, args, kwargs)
            return None
        return f


def _bind(fn):
    r = _Rec()
    fn(r)
    name, args, kwargs = r.call
    return lambda e: getattr(e, name)(*args, **kwargs)


class Prog:
    ENGS = ("pe", "act", "dve", "pool", "sp")

    def __init__(self, nc, n_dma_ring=8):
        self.nc = nc
        self.ops = []
        self.nring = n_dma_ring
        self.sbuf_bytes = 0

    def sb(self, name, shape, dtype):
        t = self.nc.alloc_sbuf_tensor(name, list(shape), dtype)
        return Buf(name, t)

    def ps(self, name, shape, dtype=F32):
        t = self.nc.alloc_psum_tensor(name, list(shape), dtype)
        return Buf(name, t, psum=True)

    def add(self, eng, fn, reads=(), writes=(), dma=False):
        op = Op()
        op.eng = eng
        op.fn = _bind(fn)
        op.reads = [b.root for b in reads if b is not None]
        op.writes = [b.root for b in writes if b is not None]
        op.dma = dma
        op.waits = []
        op.sig = dma
        op.sem = None
        op.sigval = 0
        op.n = len(self.ops)
        self.ops.append(op)
        return op

    def fence(self):
        op = Op()
        op.eng = None
        op.n = len(self.ops)
        op.dma = False
        op.sig = False
        self.ops.append(op)

    def view(self, name, ap, psum=False):
        return Buf(name, ap, psum)

    def dma(self, out_ap, in_ap, reads=(), writes=(), q="sp", **kw):
        return self.add(q, lambda e: e.dma_start(out=out_ap, in_=in_ap, **kw), reads, writes, dma=True)

    def resolve(self):
        last_on = {}
        dma_since = []
        pending = {}
        real_ops = []
        for op in self.ops:
            if op.eng is None:
                snap = (dict(last_on), list(dma_since))
                dma_since = []
                for e in self.ENGS:
                    pending[e] = snap
                continue
            real_ops.append(op)
            wdeps = []
            rdeps = []
            for b in op.reads:
                if b.lastw is not None:
                    wdeps.append(b.lastw)
                if b.psum:
                    for e, r in b.readers.items():
                        if e != op.eng:
                            rdeps.append(r)
            for b in op.writes:
                if b.lastw is not None:
                    wdeps.append(b.lastw)
                rdeps.extend(b.readers.values())
                rdeps.extend(b.dma_readers)
            for b in op.writes:
                b.lastw = op
                b.readers = {}
                b.dma_readers = []
            for b in op.reads:
                if op.dma:
                    b.dma_readers.append(op)
                else:
                    b.readers[op.eng] = op
            deps = {}
            for d in wdeps:
                if d is op:
                    continue
                if d.dma or op.dma:
                    deps[d.n] = d
                elif d.eng == op.eng:
                    if op.eng != "pe":
                        deps[d.n] = d
                else:
                    deps[d.n] = d
            for d in rdeps:
                if d is op:
                    continue
                if d.dma or op.dma:
                    deps[d.n] = d
                elif d.eng != op.eng:
                    deps[d.n] = d
            if pending.get(op.eng) is not None:
                lo, dl = pending[op.eng]
                pending[op.eng] = None
                for e2, d in lo.items():
                    if e2 == op.eng and e2 == "pe" and not op.dma:
                        continue
                    deps[d.n] = d
                for d in dl:
                    deps[d.n] = d
            op.deps = list(deps.values())
            for d in op.deps:
                d.sig = True
            if op.dma:
                dma_since.append(op)
            else:
                last_on[op.eng] = op
        self.ops = real_ops

    def emit(self):
        nc = self.nc
        self.resolve()
        streams = {e: [] for e in self.ENGS}
        for op in self.ops:
            streams[op.eng].append(op)
        import contextlib
        with contextlib.ExitStack() as es:
            esem = {e: es.enter_context(nc.semaphore("s_" + e)) for e in self.ENGS}
            rings = {e: [es.enter_context(nc.semaphore("d_%s%d" % (e, i))) for i in range(self.nring)]
                     for e in ("sp", "pool", "act")}
            fin = es.enter_context(nc.semaphore("fin"))
            cnt = {e: 0 for e in self.ENGS}
            ringcnt = {e: [0] * self.nring for e in rings}
            ringpos = {e: 0 for e in rings}
            ring_prev = {}
            for op in self.ops:
                if op.dma:
                    r = ringpos[op.eng] % self.nring
                    ringpos[op.eng] += 1
                    ringcnt[op.eng][r] += 1
                    op.sem = rings[op.eng][r]
                    op.sigval = 16 * ringcnt[op.eng][r]
                    if ringcnt[op.eng][r] > 1:
                        ring_prev[op.n] = (op.sem, op.sigval - 16)
                elif op.sig:
                    cnt[op.eng] += 1
                    op.sem = esem[op.eng]
                    op.sigval = cnt[op.eng]
            waited = {e: {} for e in self.ENGS}
            for e in self.ENGS:
                for op in streams[e]:
                    need = {}
                    for d in op.deps:
                        k = id(d.sem)
                        if k not in need or need[k][1] < d.sigval:
                            need[k] = (d.sem, d.sigval)
                    if op.n in ring_prev:
                        s, v = ring_prev[op.n]
                        k = id(s)
                        if k not in need or need[k][1] < v:
                            need[k] = (s, v)
                    for k, (s, v) in need.items():
                        if waited[e].get(k, 0) >= v:
                            continue
                        waited[e][k] = v
                        op.waits.append((s, v))
            final_waits = []
            for e in rings:
                for r in range(self.nring):
                    if ringcnt[e][r] > 0:
                        final_waits.append((rings[e][r], 16 * ringcnt[e][r]))

            def run_stream(ename, eng):
                for op in streams[ename]:
                    for s, v in op.waits:
                        eng.wait_ge(s, v)
                    ins = op.fn(eng)
                    if op.sig:
                        ins.then_inc(op.sem, 16 if op.dma else 1)
                if ename == "sp":
                    for s, v in final_waits:
                        eng.wait_ge(s, v)

            with nc.Block() as block:
                @block.tensor
                def _(eng):
                    run_stream("pe", eng)

                @block.scalar
                def _(eng):
                    run_stream("act", eng)

                @block.vector
                def _(eng):
                    run_stream("dve", eng)

                @block.gpsimd
                def _(eng):
                    run_stream("pool", eng)

                @block.sync
                def _(eng):
                    run_stream("sp", eng)


class Cfg:
    n_layers = DEPTH
    do_gdn = True
    do_na = True


def build(cfg=Cfg):
    nc = bass.Bass("TRN2", target_bir_lowering=False)
    P = Prog(nc)

    def din(name, shape):
        return nc.dram_tensor(name, list(shape), F32, kind="ExternalInput").ap()

    def dout(name, shape):
        return nc.dram_tensor(name, list(shape), F32, kind="ExternalOutput").ap()

    x_in = din("x_in", [NT, D])
    state_gdn = din("state_gdn", [2, 2, 8, 128, 128])
    cache_k = din("cache_k", [2, 256, 1024])
    cache_v = din("cache_v", [2, 256, 1024])
    cvec = din("cvec", [2, D])
    w_ada = din("w_ada", [DEPTH, 48, 128, 1024])
    b_ada = din("b_ada", [DEPTH, 6 * D])
    norm1_g = din("norm1_g", [DEPTH, D])
    norm2_g = din("norm2_g", [DEPTH, D])
    gdn_w_in = din("gdn_w_in", [2, 8, 2, 128, 2048])
    gdn_w_ab = din("gdn_w_ab", [2, 128, 256])
    gdn_conv_w = din("gdn_conv_w", [2, 3, 3072])
    gdn_a_log = din("gdn_a_log", [2, 16])
    gdn_dt_bias = din("gdn_dt_bias", [2, 16])
    gdn_norm_g = din("gdn_norm_g", [2, 128])
    gdn_w_out = din("gdn_w_out", [2, D, D])
    na_w_qkv = din("na_w_qkv", [2, 8, 128, 3072])
    na_rel_bias = din("na_rel_bias", [2, 16 * 15, 31])
    na_w_out = din("na_w_out", [2, 8, 128, 1024])
    ffn_w_up = din("ffn_w_up", [DEPTH, 22, 128, 2048])
    ffn_conv_w = din("ffn_conv_w", [DEPTH, 3, 2 * D_FF])
    ffn_conv_b = din("ffn_conv_b", [DEPTH, 2 * D_FF])
    ffn_w_down = din("ffn_w_down", [DEPTH, 2, 8, 128, 1408])
    final_g = din("final_g", [D])
    consts = din("consts", [128, 1536])

    y_out = dout("y_out", [NT, D])
    st_out = dout("st_out", [2, 2, 2, 8, 128, 128])
    ck_out = dout("ck_out", [2, 2, 256, 1024])
    cv_out = dout("cv_out", [2, 2, 256, 1024])

    X = P.sb("X", [128, KC, NT], F32)
    Xb = {(kc, tb): X.sub("X%d_%d" % (kc, tb)) for kc in range(KC) for tb in range(NTB)}
    H = P.sb("H", [128, KC, NT], BF16)
    Hb = {(kc, tb): H.sub("H%d_%d" % (kc, tb)) for kc in range(KC) for tb in range(NTB)}
    ARENA = P.sb("ARENA", [128, 13824], F32)
    CHAIN = nc.alloc_sbuf_tensor("CHAIN", [128, 3072], F32R)
    WREG = P.sb("WREG", [128, 3584], F32)

    def carve(region, byte_off, dtype, shape, name):
        esz = 2 if dtype == BF16 else 4
        n = 1
        for d_ in shape[1:]:
            n *= d_
        base = region.t.bitcast(dtype) if dtype != F32 else region.t
        ap = base[:, byte_off // esz: byte_off // esz + n]
        if len(shape) == 3:
            ap = ap.rearrange("p (a b) -> p a b", a=shape[1])
        elif len(shape) == 4:
            ap = ap.rearrange("p (a b c) -> p a b c", a=shape[1], b=shape[2])
        return P.view(name, ap)

    CONST = P.sb("CONST", [128, 1536], F32)
    ident = CONST
    ONESB = P.sb("ONESB", [128, 128], BF16)
    IDB = P.sb("IDB", [128, 128], BF16)

    PSALL = nc.alloc_psum_tensor("psall", [128, 4096], F32)
    PS = [P.view("ps%d" % i, PSALL[:, i * 512:(i + 1) * 512], psum=True) for i in range(8)]
    ps_rr = [0]

    def psum():
        b = PS[ps_rr[0] % 8]
        ps_rr[0] += 1
        return b

    P.dma(CONST[:, :], consts[:, :], writes=[CONST])
    P.add("dve", lambda e: e.memset(ONESB[:, :], 1.0), writes=[ONESB])
    P.add("dve", lambda e: e.tensor_copy(IDB[:, :], CONST[:, 0:128]), reads=[CONST], writes=[IDB])

    XT = [carve(ARENA, i * 4096, F32, [128, D], "XT%d" % i) for i in range(2)]
    for blk in range(NT // 128):
        xt = XT[blk % 2]
        P.dma(xt[:, :], x_in[blk * 128:(blk + 1) * 128, :], writes=[xt])
        tb = (blk * 128) // TB
        for half in range(2):
            pb = psum()
            for j in range(4):
                kc = half * 4 + j
                P.add("pe", lambda e, pb=pb, xt=xt, kc=kc, j=j: e.transpose(
                    pb[:, j * 128:(j + 1) * 128], xt[:, kc * 128:(kc + 1) * 128], CONST[:, 0:128]),
                    reads=[xt, CONST], writes=[pb])
            wr = [Xb[(half * 4 + j, tb)] for j in range(4)]
            eng = "act" if half == 0 else "dve"
            if eng == "act":
                P.add("act", lambda e, pb=pb, half=half, blk=blk: e.copy(
                    X[:, half * 4:half * 4 + 4, blk * 128:(blk + 1) * 128],
                    pb[:, :].rearrange("p (j t) -> p j t", j=4)), reads=[pb], writes=wr)
            else:
                P.add("dve", lambda e, pb=pb, half=half, blk=blk: e.tensor_copy(
                    X[:, half * 4:half * 4 + 4, blk * 128:(blk + 1) * 128],
                    pb[:, :].rearrange("p (j t) -> p j t", j=4)), reads=[pb], writes=wr)

    STG = [P.sb("STG%d" % i, [128, 128], F32) for i in range(2)]
    stg_rr = [0]

    def load_fm(dst, dst_flat_ap, src_rows_ap, R):
        r0 = 0
        while r0 < R:
            r = min(128, R - r0)
            stg = STG[stg_rr[0] % 2]
            stg_rr[0] += 1
            P.dma(stg[0:r, :], src_rows_ap[r0:r0 + r, :], writes=[stg])
            pb = psum()
            P.add("pe", lambda e, pb=pb, stg=stg, r=r: e.transpose(pb[:, 0:r], stg[0:r, :], CONST[0:r, 0:r]),
                  reads=[stg, CONST], writes=[pb])
            P.add("dve", lambda e, pb=pb, r=r, r0=r0: e.tensor_copy(dst_flat_ap[:, r0:r0 + r], pb[:, 0:r]),
                  reads=[pb], writes=[dst])
            r0 += r

    G1 = P.sb("G1", [128, DEPTH, KC], F32)
    G2 = P.sb("G2", [128, DEPTH, KC], F32)
    BADA = P.sb("BADA", [128, DEPTH, 48], F32)
    CV = P.sb("CV", [128, 2, KC], F32)
    SCV = P.sb("SCV", [128, 2, KC], BF16)
    load_fm(G1, G1[:, :, :].rearrange("p l k -> p (l k)"), norm1_g.rearrange("l (k f) -> (l k) f", f=128), 32)
    load_fm(G2, G2[:, :, :].rearrange("p l k -> p (l k)"), norm2_g.rearrange("l (k f) -> (l k) f", f=128), 32)
    load_fm(BADA, BADA[:, :, :].rearrange("p l k -> p (l k)"), b_ada.rearrange("l (k f) -> (l k) f", f=128), 192)
    load_fm(CV, CV[:, :, :].rearrange("p v k -> p (v k)"), cvec.rearrange("v (k f) -> (v k) f", f=128), 16)
    P.add("act", lambda e: e.activation(SCV[:, :, :], CV[:, :, :], AF.Silu), reads=[CV], writes=[SCV])

    MOD = [P.sb("MOD%d" % l, [128, 48, 2], F32) for l in range(DEPTH)]
    A1 = [P.sb("A1_%d" % l, [128, KC, 2], F32) for l in range(DEPTH)]
    A2 = [P.sb("A2_%d" % l, [128, KC, 2], F32) for l in range(DEPTH)]
    WA = [P.sb("WA%d" % i, [128, KC, 128], BF16) for i in range(3)]
    wa_rr = [0]

    MODg = [[MOD[l].sub("MOD%d_%d" % (l, g)) for g in range(6)] for l in range(DEPTH)]

    class WStream:
        def __init__(self, slots, n, issue):
            self.slots, self.n, self.issue, self.nxt = slots, n, issue, 0

        def need(self, i):
            k = len(self.slots)
            while self.nxt <= min(i + k - 1, self.n - 1):
                self.issue(self.nxt, self.slots[self.nxt % k])
                self.nxt += 1
            return self.slots[i % k]

    def mod_gen(l):
        ws = WStream(WA, 48, lambda i, wa: P.dma(wa[:, :, :].rearrange("p k n -> p (k n)"), w_ada[l, i, :, :],
                                                 writes=[wa], q="pool"))
        for n in range(48):
            wa = ws.need(n)
            pm = PS[6 + mod_bank[0] % 2]
            mod_bank[0] += 1
            for kc in range(KC):
                P.add("pe", lambda e: e.matmul(pm[:, 0:2], wa[:, kc, :], SCV[:, :, kc],
                                               start=(kc == 0), stop=(kc == KC - 1)), reads=[wa, SCV], writes=[pm])
            P.add("dve", lambda e: e.tensor_scalar(MOD[l][:, n, :], pm[:, 0:2], BADA[:, l, n:n + 1], None, ALU.add),
                  reads=[pm, BADA], writes=[MODg[l][n // 8]])
            if n == 15:
                P.add("dve", lambda e: e.scalar_tensor_tensor(
                    A1[l][:, :, :], MOD[l][:, 8:16, :], 1.0, G1[:, l, :].unsqueeze(2).to_broadcast([128, KC, 2]),
                    ALU.add, ALU.mult), reads=[MODg[l][1], G1], writes=[A1[l]])
            if n == 39:
                P.add("dve", lambda e: e.scalar_tensor_tensor(
                    A2[l][:, :, :], MOD[l][:, 32:40, :], 1.0, G2[:, l, :].unsqueeze(2).to_broadcast([128, KC, 2]),
                    ALU.add, ALU.mult), reads=[MODg[l][4], G2], writes=[A2[l]])
            yield n

    mod_state = {"gen": None}
    mod_bank = [0]

    def pump(k):
        g = mod_state["gen"]
        if g is None:
            return
        for _ in range(k):
            try:
                next(g)
            except StopIteration:
                mod_state["gen"] = None
                return

    SQ = [P.sb("SQ%d" % i, [128, TB], BF16) for i in range(2)]
    RSTD = [P.sb("RSTD%d" % i, [128, TB], F32) for i in range(2)]
    TMPN = [P.sb("TMPN%d" % i, [128, TB], F32) for i in range(2)]
    EPSC = P.sb("EPSC", [128, 1], F32)
    P.add("dve", lambda e: e.memset(EPSC[:, :], EPS), writes=[EPSC])
    nrr = [0]

    def emit_norm_mod(l, which):
        A = A1[l] if which == 1 else A2[l]
        shoff = 0 if which == 1 else 24
        for tb in range(NTB):
            v = 0 if tb == 0 else 1
            rstd = RSTD[nrr[0] % 2]
            nrr[0] += 1
            sl = slice(tb * TB, (tb + 1) * TB)
            pb = psum()
            for kc in range(KC):
                sq = SQ[kc % 2]
                P.add("act", lambda e, sq=sq, sl=sl, kc=kc: e.activation(
                    sq[:, :], X[:, kc, sl], AF.Square), reads=[Xb[(kc, tb)]], writes=[sq])
                P.add("pe", lambda e, pb=pb, sq=sq, kc=kc: e.matmul(
                    pb[:, :], ONESB[:, :], sq[:, :], start=(kc == 0), stop=(kc == KC - 1)),
                    reads=[sq, ONESB], writes=[pb])
            P.add("act", lambda e, pb=pb, rstd=rstd: e.activation(
                rstd[:, :], pb[:, :], AF.Ln, bias=EPSC[:, 0:1], scale=1.0 / D),
                reads=[pb, EPSC], writes=[rstd])
            P.add("act", lambda e, rstd=rstd: e.activation(rstd[:, :], rstd[:, :], AF.Exp, scale=-0.5), reads=[rstd], writes=[rstd])
            for kc in range(KC):
                tmp = TMPN[kc % 2]
                P.add("dve", lambda e, tmp=tmp, kc=kc, sl=sl, rstd=rstd: e.tensor_tensor(
                    tmp[:, :], X[:, kc, sl], rstd[:, :], ALU.mult),
                    reads=[Xb[(kc, tb)], rstd], writes=[tmp])
                P.add("act", lambda e, tmp=tmp, kc=kc, sl=sl, v=v, A=A, l=l, shoff=shoff: e.activation(
                    H[:, kc, sl], tmp[:, :], AF.Identity,
                    bias=MOD[l][:, shoff + kc, v:v + 1], scale=A[:, kc, v:v + 1]),
                    reads=[tmp, A, MODg[l][shoff // 8]], writes=[Hb[(kc, tb)]])

    def resid_from_psum(pb, l, gate_off, n, tb, ncols=TB, col0=0):
        v = 0 if tb == 0 else 1
        sl = slice(tb * TB + col0, tb * TB + col0 + ncols)
        P.add("dve", lambda e: e.scalar_tensor_tensor(
            X[:, n, sl], pb[:, 0:ncols], MOD[l][:, gate_off + n, v:v + 1], X[:, n, sl], ALU.mult, ALU.add),
            reads=[pb, MODg[l][gate_off // 8], Xb[(n, tb)]], writes=[Xb[(n, tb)]])

    WUP = [carve(WREG, i * 4096, BF16, [128, KC, 2, 128], "WUP%d" % i) for i in range(2)]
    wup_rr = [0]
    CT = [P.sb("CT%d" % i, [128, NT], F32) for i in range(4)]
    ct_rr = [0]
    NFH = NFC // 2
    GB = carve(ARENA, 0, BF16, [128, NFH, NT], "GB")
    GBb = {(i, tb): GB.sub("GB%d_%d" % (i, tb)) for i in range(NFH) for tb in range(NTB)}
    FCW = P.sb("FCW", [128, DEPTH, 3, 44], F32)
    FCB = P.sb("FCB", [128, DEPTH, 44], F32)
    load_fm(FCW, FCW[:, :, :, :].rearrange("p l t k -> p (l t k)"),
            ffn_conv_w.rearrange("l t (k f) -> (l t k) f", f=128), DEPTH * 3 * 44)
    load_fm(FCB, FCB[:, :, :].rearrange("p l k -> p (l k)"), ffn_conv_b.rearrange("l (k f) -> (l k) f", f=128),
            DEPTH * 44)
    WDN = [carve(WREG, 8192 + i * 2816, BF16, [128, NFH, 128], "WDN%d" % i) for i in range(2)]
    wdn_rr = [0]
    grp_rr = [0]
    dn_rr = [0]

    def bank_group():
        g = grp_rr[0] % 2
        grp_rr[0] += 1
        return g * 3

    def conv_from_psum(b0, w0, w1, w2, bias, ct, ctb):
        pbs = [PS[b0], PS[b0 + 1], PS[b0 + 2]]
        samp = PSALL[:, (b0 + 1) * 512:(b0 + 3) * 512]
        P.add("act", lambda e: e.activation(ct[:, 0:512], PS[b0][:, :], AF.Identity, bias=bias, scale=w1),
              reads=[pbs[0], FCWb], writes=[ctb])
        P.add("act", lambda e: e.activation(ct[:, 512:NT], samp, AF.Identity, bias=bias, scale=w1),
              reads=[pbs[1], pbs[2], FCWb], writes=[ctb])
        for (t0, ln) in SEQS:
            if t0 < 512:
                src = lambda a, b_: PS[b0][:, a:b_]
                rd = [pbs[0]]
            else:
                src = lambda a, b_: PSALL[:, (b0 + 1) * 512 + a - 512:(b0 + 1) * 512 + b_ - 512]
                rd = [pbs[1], pbs[2]]
            P.add("dve", lambda e, src=src, t0=t0, ln=ln: e.scalar_tensor_tensor(
                ct[:, t0 + 1:t0 + ln], src(t0, t0 + ln - 1), w0, ct[:, t0 + 1:t0 + ln], ALU.mult, ALU.add),
                reads=rd + [FCWb, ctb], writes=[ctb])
            P.add("dve", lambda e, src=src, t0=t0, ln=ln: e.scalar_tensor_tensor(
                ct[:, t0:t0 + ln - 1], src(t0 + 1, t0 + ln), w2, ct[:, t0:t0 + ln - 1], ALU.mult, ALU.add),
                reads=rd + [FCWb, ctb], writes=[ctb])

    FCWb = FCW

    def emit_ffn(l):
        wus = WStream(WUP, NFC, lambda i, wu: P.dma(wu[:, :, :, :].rearrange("p k g n -> p (k g n)"),
                                                    ffn_w_up[l, i, :, :], writes=[wu], q="pool"))
        wds = WStream(WDN, 2 * KC, lambda i, wd: P.dma(wd[:, :, :].rearrange("p i n -> p (i n)"),
                                                       ffn_w_down[l, i // KC, i % KC, :, :], writes=[wd], q="pool"))
        for hf in range(2):
            for piece in range(hf * NFH, (hf + 1) * NFH):
                wu = wus.need(piece)
                if piece == (hf + 1) * NFH - 1:
                    wds.need(hf * KC)
                i = piece
                cts = []
                for g in range(2):
                    b0 = bank_group()
                    for tb in range(NTB):
                        pb = PS[b0 + tb]
                        for kc in range(KC):
                            P.add("pe", lambda e: e.matmul(
                                pb[:, :], wu[:, kc, g, :], H[:, kc, tb * TB:(tb + 1) * TB],
                                start=(kc == 0), stop=(kc == KC - 1)),
                                reads=[wu, Hb[(kc, tb)]], writes=[pb])
                    ct = CT[ct_rr[0] % 4]
                    ct_rr[0] += 1
                    ch = g * NFC + i
                    conv_from_psum(b0, FCW[:, l, 0, ch:ch + 1], FCW[:, l, 1, ch:ch + 1], FCW[:, l, 2, ch:ch + 1],
                                   FCB[:, l, ch:ch + 1], ct, ct)
                    cts.append(ct)
                ctv, ctg = cts
                pump(2)
                P.add("act", lambda e: e.activation(ctg[:, :], ctg[:, :], AF.Silu), reads=[ctg], writes=[ctg])
                P.add("pool", lambda e: e.tensor_tensor(GB[:, i - hf * NFH, :], ctv[:, :], ctg[:, :], ALU.mult),
                      reads=[ctv, ctg], writes=[GBb[(i - hf * NFH, tb)] for tb in range(NTB)])
            for piece in range(KC):
                wd = wds.need(hf * KC + piece)
                n = piece
                for tb in range(NTB):
                    pb = PS[dn_rr[0] % 6]
                    dn_rr[0] += 1
                    for i in range(NFH):
                        P.add("pe", lambda e: e.matmul(
                            pb[:, :], wd[:, i, :], GB[:, i, tb * TB:(tb + 1) * TB],
                            start=(i == 0), stop=(i == NFH - 1)),
                            reads=[wd, GBb[(i, tb)]], writes=[pb])
                    resid_from_psum(pb, l, 40, n, tb)
                pump(2)

    WO = [carve(WREG, i * 2048, BF16, [128, KC, 128], "WO%d" % i) for i in range(2)]
    wo_rr = [0]

    def emit_wout(l, w_dram, OT, OTb):
        wos = WStream(WO, KC, lambda i, wo: P.dma(wo[:, :, :].rearrange("p k n -> p (k n)"), w_dram[i, :, :],
                                                  writes=[wo], q="pool"))
        for n in range(KC):
            wo = wos.need(n)
            pump(1)
            for tb in range(NTB):
                pb = psum()
                for kc in range(KC):
                    P.add("pe", lambda e, pb=pb, wo=wo, kc=kc, tb=tb: e.matmul(
                        pb[:, :], wo[:, kc, :], OT[:, kc, tb * TB:(tb + 1) * TB],
                        start=(kc == 0), stop=(kc == KC - 1)), reads=[wo, OTb[kc]], writes=[pb])
                resid_from_psum(pb, l, 16, n, tb)

    rb_t = nc.dram_tensor("rbpad", [480, 127], F32)
    RBPAD = P.view("rbpad", rb_t.ap())
    J2B = P.sb("J2B", [128, 64], BF16)
    P.add("dve", lambda e: e.tensor_copy(J2B[:, :], CONST[:, 128:192]), reads=[CONST], writes=[J2B])

    def emit_rbpad():
        RBP = carve(ARENA, 16384, F32, [128, 4, 127], "RBP")
        P.add("pool", lambda e: e.memset(RBP[:, :, :], 0.0), writes=[RBP])
        P.dma(RBP[0:120, :, 48:79], na_rel_bias.rearrange("j (p a) f -> p (j a) f", a=2)[:, :, :]
              if False else na_rel_bias.rearrange("j r f -> (j r) f").rearrange("(p a) f -> p a f", a=4),
              reads=[], writes=[RBP], allow_slow_non_contiguous=True)
        P.dma(rb_t.ap().rearrange("(p a) f -> p a f", a=4), RBP[0:120, :, :], reads=[RBP], writes=[RBPAD])

    def psum_of(lst, st):
        b = PS[lst[st[0] % len(lst)]]
        st[0] += 1
        return b

    def emit_na(l):
        j = l // 2
        OT = carve(ARENA, 0, BF16, [128, KC, NT], "OT")
        OTb = [OT.sub("OT%d" % c) for c in range(KC)]
        QZ = [carve(ARENA, 24576 + i * 3072, BF16, [128, NT], "QZ%d" % i) for i in range(2)]
        KT = carve(ARENA, 30720, BF16, [128, NT], "KT")
        VT = carve(ARENA, 33792, BF16, [128, 12, 128], "VT")
        VTS = carve(ARENA, 36864, BF16, [128, 7, 128], "VTS")
        KCT = carve(ARENA, 38656, BF16, [128, 256], "KCT")
        VC = carve(ARENA, 39168, BF16, [128, 2, 128], "VC")
        PTC = carve(ARENA, 39680, BF16, [128, 2, 1024], "PTC")
        PTL = [carve(ARENA, 43776 + i * 512, BF16, [128, 4, 64], "PTL%d" % i) for i in range(2)]
        PTP = [carve(ARENA, 44800 + i * 1024, BF16, [128, 2, 256], "PTP%d" % i) for i in range(2)]
        HKR = carve(ARENA, 46848, BF16, [128, 14, 2, 64], "HKR")
        HKZ = carve(ARENA, 50432, BF16, [128, 14, 2, 64], "HKZ")
        CKS = TMPN[0].alias(TMPN[0].t[:, :].rearrange("p (a b) -> p a b", a=4))
        CVS = TMPN[1].alias(TMPN[1].t[:, :].rearrange("p (a b) -> p a b", a=4))
        RD = RSTD[0].alias(RSTD[0].t[:, :])
        KCS = RSTD[1].alias(RSTD[1].t[:, 0:256].rearrange("p (a b) -> p a b", a=2))
        WQ = [carve(WREG, i * 6144, BF16, [128, KC, 3, 128], "WQ%d" % i) for i in range(2)]
        lo = [0]
        hi = [0]
        LO = [0, 1, 2, 3]
        HI = [4, 5, 6, 7]
        P.add("pool", lambda e: e.memset(QZ[0][64:128, :], 0.0), writes=[QZ[0]])
        P.add("pool", lambda e: e.memset(QZ[1][0:64, :], 0.0), writes=[QZ[1]])
        P.add("pool", lambda e: e.memset(HKZ[:, :, :, :], 0.0), writes=[HKZ])
        ptl_rr = [0]
        ptp_rr = [0]
        wqs = WStream(WQ, KC, lambda i, wq: P.dma(wq[:, :, :, :].rearrange("p k g n -> p (k g n)"),
                                                  na_w_qkv[j, i, :, :], writes=[wq], q="pool"))
        for c in range(KC):
            wq = wqs.need(c)
            pump(3)
            for tb in range(NTB):
                sl = slice(tb * TB, (tb + 1) * TB)
                pb = psum_of(LO, lo)
                for kc in range(KC):
                    P.add("pe", lambda e, pb=pb, kc=kc, sl=sl: e.matmul(
                        pb[:, :], wq[:, kc, 0, :], H[:, kc, sl], start=(kc == 0), stop=(kc == KC - 1)),
                        reads=[wq, Hb[(kc, tb)]], writes=[pb])
                P.add("act", lambda e, pb=pb, sl=sl: e.activation(QZ[0][0:64, sl], pb[0:64, :], AF.Copy, scale=0.125),
                      reads=[pb], writes=[QZ[0]])
                P.add("act", lambda e, pb=pb, sl=sl: e.activation(QZ[1][64:128, sl], pb[64:128, :], AF.Copy, scale=0.125),
                      reads=[pb], writes=[QZ[1]])
                pb = psum_of(LO, lo)
                for kc in range(KC):
                    P.add("pe", lambda e, pb=pb, kc=kc, sl=sl: e.matmul(
                        pb[:, :], wq[:, kc, 1, :], H[:, kc, sl], start=(kc == 0), stop=(kc == KC - 1)),
                        reads=[wq, Hb[(kc, tb)]], writes=[pb])
                P.add("dve", lambda e, pb=pb, sl=sl: e.tensor_copy(KT[:, sl], pb[:, :]), reads=[pb], writes=[KT])
            for g in range(3):
                pb = psum_of(LO, lo)
                for b in range(4):
                    blk = g * 4 + b
                    for kc in range(KC):
                        P.add("pe", lambda e, pb=pb, kc=kc, b=b, blk=blk: e.matmul(
                            pb[:, b * 128:(b + 1) * 128], H[:, kc, blk * 128:(blk + 1) * 128], wq[:, kc, 2, :],
                            start=(kc == 0), stop=(kc == KC - 1)),
                            reads=[wq, Hb[(kc, blk // 4)]], writes=[pb])
                P.add("dve", lambda e, pb=pb, g=g: e.tensor_copy(
                    VT[:, g * 4:(g + 1) * 4, :], pb[:, :].rearrange("p (b f) -> p b f", b=4)),
                    reads=[pb], writes=[VT])
                if g == 0:
                    P.add("act", lambda e, pb=pb: e.copy(CVS[:, :, :], pb[:, :].rearrange("p (b f) -> p b f", b=4)),
                          reads=[pb], writes=[CVS])
                    for sq in range(2):
                        P.dma(cv_out[sq, j, :, c * 128:(c + 1) * 128].rearrange("(b p) f -> p b f", p=128),
                              CVS[:, sq * 2:sq * 2 + 2, :], reads=[CVS])
            pb = psum_of(LO, lo)
            for b in range(4):
                for kc in range(KC):
                    P.add("pe", lambda e, pb=pb, kc=kc, b=b: e.matmul(
                        pb[:, b * 128:(b + 1) * 128], H[:, kc, b * 128:(b + 1) * 128], wq[:, kc, 1, :],
                        start=(kc == 0), stop=(kc == KC - 1)), reads=[wq, Hb[(kc, 0)]], writes=[pb])
            P.add("act", lambda e, pb=pb: e.copy(CKS[:, :, :], pb[:, :].rearrange("p (b f) -> p b f", b=4)),
                  reads=[pb], writes=[CKS])
            for sq in range(2):
                P.dma(ck_out[sq, j, :, c * 128:(c + 1) * 128].rearrange("(b p) f -> p b f", p=128),
                      CKS[:, sq * 2:sq * 2 + 2, :], reads=[CKS])
            for g in range(2):
                pb = psum_of(LO, lo)
                nb = 4 if g == 0 else 3
                for b in range(nb):
                    m = g * 4 + b
                    t0 = 512 + 64 + 128 * m
                    for kc in range(KC):
                        P.add("pe", lambda e, pb=pb, kc=kc, b=b, t0=t0: e.matmul(
                            pb[:, b * 128:(b + 1) * 128], H[:, kc, t0:t0 + 128], wq[:, kc, 2, :],
                            start=(kc == 0), stop=(kc == KC - 1)),
                            reads=[wq, Hb[(kc, 1)], Hb[(kc, 2)]], writes=[pb])
                P.add("dve", lambda e, pb=pb, g=g, nb=nb: e.tensor_copy(
                    VTS[:, g * 4:g * 4 + nb, :], pb[:, 0:nb * 128].rearrange("p (b f) -> p b f", b=nb)),
                    reads=[pb], writes=[VTS])
            P.dma(KCS[:, :, :], cache_k[j, :, c * 128:(c + 1) * 128].rearrange("(b p) f -> p b f", p=128),
                  writes=[KCS])
            pb = psum_of(LO, lo)
            for b in range(2):
                P.add("pe", lambda e, pb=pb, b=b: e.transpose(pb[:, b * 128:(b + 1) * 128], KCS[:, b, :], CONST[:, 0:128]),
                      reads=[KCS, CONST], writes=[pb])
            P.add("act", lambda e, pb=pb: e.copy(KCT[:, :], pb[:, 0:256]), reads=[pb], writes=[KCT])
            P.dma(VC[:, :, :], cache_v[j, :, c * 128:(c + 1) * 128].rearrange("(b p) f -> p b f", p=128),
                  writes=[VC], q="pool")
            for sq in range(2):
                pbo = psum_of(HI, hi)
                for hh in range(2):
                    hs = slice(hh * 64, hh * 64 + 64)
                    ptp = PTP[ptp_rr[0] % 2]
                    ptp_rr[0] += 1
                    pbs = psum_of(LO, lo)
                    for kb in range(2):
                        P.add("pe", lambda e, pbs=pbs, kb=kb, hh=hh, sq=sq: e.matmul(
                            pbs[:, kb * 256:(kb + 1) * 256], KT[:, sq * 256 + kb * 128:sq * 256 + (kb + 1) * 128],
                            QZ[hh][:, sq * 256:(sq + 1) * 256], start=True, stop=True),
                            reads=[KT, QZ[hh]], writes=[pbs])
                    P.add("act", lambda e, pbs=pbs, ptp=ptp: e.activation(
                        ptp[:, :, :], pbs[:, :].rearrange("p (b q) -> p b q", b=2), AF.Exp),
                        reads=[pbs], writes=[ptp])
                    for kb in range(2):
                        P.add("pe", lambda e, kb=kb, hs=hs, ptp=ptp, sq=sq: e.matmul(
                            pbo[hs, 0:256], VT[:, sq * 2 + kb, hs], ptp[:, kb, :], start=(kb == 0), stop=(kb == 1)),
                            reads=[VT, ptp], writes=[pbo])
                    for kb in range(2):
                        P.add("pe", lambda e, kb=kb, hs=hs, ptp=ptp: e.matmul(
                            pbo[hs, 256:512], ONESB[:, 0:64], ptp[:, kb, :], start=(kb == 0), stop=(kb == 1)),
                            reads=[ONESB, ptp], writes=[pbo])
                P.add("act", lambda e, pbo=pbo: e.activation(RD[:, 0:256], pbo[:, 256:512], AF.Ln), reads=[pbo], writes=[RD])
                P.add("act", lambda e: e.activation(RD[:, 0:256], RD[:, 0:256], AF.Exp, scale=-1.0), reads=[RD], writes=[RD])
                P.add("dve", lambda e, pbo=pbo, sq=sq: e.tensor_tensor(
                    OT[:, c, sq * 256:(sq + 1) * 256], pbo[:, 0:256], RD[:, 0:256], ALU.mult),
                    reads=[pbo, RD], writes=[OTb[c]])
            pbo_s = [PS[4], PS[5]]
            pbd_s = [PS[6], PS[7]]
            for hh in range(2):
                h = 2 * c + hh
                hs = slice(hh * 64, hh * 64 + 64)
                for u in range(2):
                    src = bass.AP(rb_t, (j * 240 + h * 15 + u) * 127, [[1, 64], [127, 14], [1, 64]])
                    P.dma(HKR[0:64, :, u, :], src, reads=[RBPAD], writes=[HKR], q="pool")
                P.add("pool", lambda e: e.tensor_tensor(
                    HKZ[0:64, :, :, :].rearrange("p a u k -> p (a u) k"),
                    HKR[0:64, :, :, :].rearrange("p a u k -> p (a u) k"),
                    CONST[0:64, 192:256].unsqueeze(1).to_broadcast([64, 28, 64]), ALU.add),
                    reads=[HKR, CONST], writes=[HKZ])
                for kb in range(2):
                    for qb in range(2):
                        pbs = psum_of(LO, lo)
                        P.add("pe", lambda e, pbs=pbs, kb=kb, qb=qb, hh=hh: e.matmul(
                            pbs[:, :], KCT[:, kb * 128:(kb + 1) * 128], QZ[hh][:, 512 + qb * 512:512 + (qb + 1) * 512],
                            start=True, stop=True), reads=[KCT, QZ[hh]], writes=[pbs])
                        P.add("act", lambda e, pbs=pbs, kb=kb, qb=qb: e.activation(
                            PTC[:, kb, qb * 512:(qb + 1) * 512], pbs[:, :], AF.Exp), reads=[pbs], writes=[PTC])
                for r in range(16):
                    rs = min(max(r - 4, 0), 8)
                    ptl = PTL[ptl_rr[0] % 2]
                    ptl_rr[0] += 1
                    pbs = psum_of(LO, lo)
                    qsl = slice(512 + 64 * r, 512 + 64 * r + 64)
                    for i in range(4):
                        kr = rs + 2 * i
                        ri = kr - r + 7
                        k0 = 512 + 64 * kr
                        P.add("pe", lambda e, pbs=pbs, i=i, k0=k0, qsl=qsl, hh=hh: e.matmul(
                            pbs[:, i * 64:(i + 1) * 64], KT[:, k0:k0 + 128], QZ[hh][:, qsl], start=True, stop=False),
                            reads=[KT, QZ[hh]], writes=[pbs])
                        P.add("pe", lambda e, pbs=pbs, i=i, ri=ri: e.matmul(
                            pbs[:, i * 64:(i + 1) * 64], HKZ[:, ri, :, :].rearrange("p u k -> p (u k)"), J2B[:, :],
                            start=False, stop=True), reads=[HKZ, J2B], writes=[pbs])
                    P.add("act", lambda e, pbs=pbs, ptl=ptl: e.activation(
                        ptl[:, :, :], pbs[:, 0:256].rearrange("p (i q) -> p i q", i=4), AF.Exp),
                        reads=[pbs], writes=[ptl])
                    pbo = pbo_s[r // 8]
                    pbd = pbd_s[r // 8]
                    osl = slice((r % 8) * 64, (r % 8) * 64 + 64)
                    for pbx, isden in ((pbo, False), (pbd, True)):
                        for i in range(4):
                            kr = rs + 2 * i
                            if kr % 2 == 0:
                                vv = VT[:, 4 + kr // 2, hs]
                                vb = VT
                            else:
                                vv = VTS[:, (kr - 1) // 2, hs]
                                vb = VTS
                            lhs = ONESB[:, 0:64] if isden else vv
                            P.add("pe", lambda e, pbx=pbx, lhs=lhs, i=i, ptl=ptl, osl=osl, hs=hs: e.matmul(
                                pbx[hs, osl], lhs, ptl[:, i, :], start=(i == 0), stop=False),
                                reads=[vb, ONESB, ptl], writes=[pbx])
                        for kb in range(2):
                            lhs = ONESB[:, 0:64] if isden else VC[:, kb, hs]
                            P.add("pe", lambda e, pbx=pbx, lhs=lhs, kb=kb, osl=osl, hs=hs, r=r: e.matmul(
                                pbx[hs, osl], lhs, PTC[:, kb, 64 * r:64 * r + 64], start=False, stop=(kb == 1)),
                                reads=[VC, ONESB, PTC], writes=[pbx])
            for half in range(2):
                P.add("act", lambda e, half=half: e.activation(RD[:, :], pbd_s[half][:, :], AF.Ln),
                      reads=[pbd_s[half]], writes=[RD])
                P.add("act", lambda e: e.activation(RD[:, :], RD[:, :], AF.Exp, scale=-1.0), reads=[RD], writes=[RD])
                P.add("dve", lambda e, half=half: e.tensor_tensor(
                    OT[:, c, 512 + half * 512:512 + (half + 1) * 512], pbo_s[half][:, :], RD[:, :], ALU.mult),
                    reads=[pbo_s[half], RD], writes=[OTb[c]])
        P.fence()
        emit_wout(l, na_w_out[j], OT, OTb)


    C_I = CONST[:, 0:128]
    C_TRIF = CONST[:, 256:384]
    C_TRIB = CONST[:, 384:512]
    C_EVEN = CONST[:, 512:640]
    C_ODD = CONST[:, 640:768]
    C_MINC = [CONST[:, 768:896], CONST[:, 1024:1152]]
    C_MSTR = [CONST[:, 896:1024], CONST[:, 1152:1280]]
    C_ONES = CONST[:, 1280:1408]
    C_NEG1 = CONST[:, 1408:1536]
    GCW = P.sb("GCW", [128, 2, 3, 24], F32)
    load_fm(GCW, GCW[:, :, :, :].rearrange("p j t k -> p (j t k)"),
            gdn_conv_w.rearrange("j t (k f) -> (j t k) f", f=128), 144)
    GNG = P.sb("GNG", [128, 2], F32)
    load_fm(GNG, GNG[:, :], gdn_norm_g, 2)
    ZEROC = P.sb("ZEROC", [128, 1], F32)
    P.add("dve", lambda e: e.memset(ZEROC[:, :], 0.0), writes=[ZEROC])
    ev_rr = [0]

    def evac(dst_ap, dst_bufs, src_ap, pbs, scale=None):
        use_act = True
        ev_rr[0] += 1
        if use_act:
            if scale is None:
                P.add("act", lambda e: e.copy(dst_ap, src_ap), reads=pbs, writes=dst_bufs)
            else:
                P.add("act", lambda e: e.activation(dst_ap, src_ap, AF.Copy, scale=scale), reads=pbs, writes=dst_bufs)
        else:
            if scale is None:
                P.add("dve", lambda e: e.tensor_copy(dst_ap, src_ap), reads=pbs, writes=dst_bufs)
            else:
                P.add("dve", lambda e: e.tensor_scalar(dst_ap, src_ap, scale, None, ALU.mult), reads=pbs, writes=dst_bufs)

    def emit_gdn(l):
        j = l // 2
        off = [0]

        def AR(dtype, shape, name):
            esz = 2 if dtype == BF16 else 4
            n = esz
            for d_ in shape[1:]:
                n *= d_
            v = carve(ARENA, off[0], dtype, shape, name)
            off[0] += (n + 63) // 64 * 64
            assert off[0] <= 55296, off[0]
            return v

        TK = [AR(F32, [128, 12, 16], "TK%d" % i) for i in range(12)]
        GTOK, GC, BETA, GCB, EGC, BEG, GLE, GLO, EDE, EDO, TMPA, TMPB = TK
        AB = P.view("AB", ARENA.t[:, (10 * 768) // 4:(12 * 768) // 4].rearrange("p (b c) -> p b c", b=12))
        QNb = AR(BF16, [128, NT], "QNb")
        KNb = AR(BF16, [128, NT], "KNb")
        OTh = AR(BF16, [128, NT], "OTh")
        BS = []
        BSR = []
        for d_ in range(2):
            row, rowr = [], []
            for i in range(3):
                k_ = d_ * 3 + i
                apr = CHAIN[:, k_ * 512:(k_ + 1) * 512].rearrange("p (a b) -> p a b", a=4)
                apf = CHAIN.bitcast(F32)[:, k_ * 512:(k_ + 1) * 512].rearrange("p (a b) -> p a b", a=4)
                row.append(P.view("BS%d_%d" % (d_, i), apf))
                rowr.append(apr)
            BS.append(row)
            BSR.append(rowr)
        TTbs = [AR(BF16, [128, 4, 128], "TTb%d" % d_) for d_ in range(2)]
        VBs = [AR(BF16, [128, 4, 128], "VB%d" % d_) for d_ in range(2)]
        KBGs = [AR(BF16, [128, 4, 128], "KBG%d" % d_) for d_ in range(2)]
        F = [b_.alias(b_.t[:, :].rearrange("p (a b) -> p a b", a=4)) for b_ in (RSTD[0], RSTD[1], TMPN[0], TMPN[1])]
        TC = F
        SETS = []
        for i in range(4):
            SETS.append(dict(U=AR(BF16, [128, 4, 128], "U%d" % i), NWT=AR(BF16, [128, 4, 128], "NWT%d" % i),
                             PT=AR(BF16, [128, 4, 128], "PT%d" % i), KD=[AR(BF16, [128, 4, 128], "KDe%d" % i),
                                                                        AR(BF16, [128, 4, 128], "KDo%d" % i)],
                             QG=AR(BF16, [128, 4, 128], "QG%d" % i)))
        CH = []
        for i in range(4):
            CH.append(dict(S=AR(F32, [128, 128], "S%d" % i), Sb=AR(BF16, [128, 128], "Sb%d" % i),
                           VN=AR(BF16, [128, 128], "VN%d" % i)))
        WAB = AR(BF16, [128, KC, 32], "WAB")
        DTB16 = AR(F32, [128, 16], "DTB16")
        NEGA = AR(F32, [128, 16], "NEGA")
        WI = [carve(WREG, 4096 + i * 4096, BF16, [128, KC, 2, 128], "WI%d" % i) for i in range(2)]
        WOH = [carve(WREG, i * 2048, BF16, [128, D], "WOH%d" % i) for i in range(2)]
        CTQ, CTK, CTV, CTZ = CT
        OF = CTQ
        OFb = [OF.sub("OF%d" % b) for b in range(12)]
        for ch in CH:
            P.add("pool", lambda e, ch=ch: e.memset(ch["VN"][:, :], 0.0), writes=[ch["VN"]])

        P.dma(WAB[:, :, :].rearrange("p k n -> p (k n)"), gdn_w_ab[j, :, :], writes=[WAB], q="pool")
        P.dma(DTB16[:, :], gdn_dt_bias[j:j + 1, :].to_broadcast([128, 16]), writes=[DTB16])
        P.dma(NEGA[:, :], gdn_a_log[j:j + 1, :].to_broadcast([128, 16]), writes=[NEGA])
        P.add("act", lambda e: e.activation(NEGA[:, :], NEGA[:, :], AF.Exp), reads=[NEGA], writes=[NEGA])
        pb = psum()
        for blk in range(12):
            for kc in range(KC):
                P.add("pe", lambda e, pb=pb, blk=blk, kc=kc: e.matmul(
                    pb[:, blk * 32:(blk + 1) * 32], H[:, kc, blk * 128:(blk + 1) * 128], WAB[:, kc, :],
                    start=(kc == 0), stop=(kc == KC - 1)), reads=[WAB, Hb[(kc, blk // 4)]], writes=[pb])
        P.add("dve", lambda e, pb=pb: e.tensor_copy(AB[:, :, :], pb[:, 0:384].rearrange("p (b c) -> p b c", b=12)),
              reads=[pb], writes=[TMPA, TMPB])
        bc16 = lambda t: t[:, :].unsqueeze(1).to_broadcast([128, 12, 16])
        P.add("dve", lambda e: e.tensor_tensor(GTOK[:, :, :], AB[:, :, 0:16], bc16(DTB16), ALU.add),
              reads=[TMPA, TMPB, DTB16], writes=[GTOK])
        P.add("act", lambda e: e.activation(GTOK[:, :, :], GTOK[:, :, :], AF.Exp), reads=[GTOK], writes=[GTOK])
        P.add("dve", lambda e: e.tensor_scalar(GTOK[:, :, :], GTOK[:, :, :], 1.0, None, ALU.add), reads=[GTOK], writes=[GTOK])
        P.add("act", lambda e: e.activation(GTOK[:, :, :], GTOK[:, :, :], AF.Ln), reads=[GTOK], writes=[GTOK])
        P.add("dve", lambda e: e.scalar_tensor_tensor(GTOK[:, :, :], GTOK[:, :, :], -1.0, bc16(NEGA), ALU.mult, ALU.mult),
              reads=[GTOK, NEGA], writes=[GTOK])
        P.add("act", lambda e: e.activation(BETA[:, :, :], AB[:, :, 16:32], AF.Exp, scale=-1.0),
              reads=[TMPA, TMPB], writes=[BETA])
        P.add("dve", lambda e: e.tensor_scalar(BETA[:, :, :], BETA[:, :, :], 1.0, None, ALU.add), reads=[BETA], writes=[BETA])
        P.add("act", lambda e: e.activation(GCB[:, :, :], BETA[:, :, :], AF.Ln), reads=[BETA], writes=[GCB])
        P.add("dve", lambda e: e.reciprocal(BETA[:, :, :], BETA[:, :, :]), reads=[BETA], writes=[BETA])
        pb = psum()
        for blk in range(12):
            for d in range(2):
                tri = C_TRIF if d == 0 else C_TRIB
                P.add("pe", lambda e, pb=pb, blk=blk, d=d, tri=tri: e.matmul(
                    pb[:, blk * 16 + d * 8:blk * 16 + d * 8 + 8], tri, GTOK[:, blk, d * 8:d * 8 + 8],
                    start=True, stop=True), reads=[CONST, GTOK], writes=[pb])
        P.add("dve", lambda e, pb=pb: e.tensor_copy(GC[:, :, :], pb[:, 0:192].rearrange("p (b c) -> p b c", b=12)),
              reads=[pb], writes=[GC])
        for (dst, cm) in ((GLE, C_EVEN), (GLO, C_ODD)):
            pb = psum()
            for blk in range(12):
                P.add("pe", lambda e, pb=pb, blk=blk, cm=cm: e.matmul(
                    pb[:, blk * 16:(blk + 1) * 16], cm, GTOK[:, blk, :], start=True, stop=True),
                    reads=[CONST, GTOK], writes=[pb])
            P.add("dve", lambda e, pb=pb, dst=dst: e.tensor_copy(
                dst[:, :, :], pb[:, 0:192].rearrange("p (b c) -> p b c", b=12)), reads=[pb], writes=[dst])
        P.add("dve", lambda e: e.tensor_tensor(GCB[:, :, :], GC[:, :, :], GCB[:, :, :], ALU.subtract),
              reads=[GC, GCB], writes=[GCB])
        P.add("act", lambda e: e.activation(EGC[:, :, :], GC[:, :, :], AF.Exp), reads=[GC], writes=[EGC])
        P.add("dve", lambda e: e.tensor_tensor(BEG[:, :, :], BETA[:, :, :], EGC[:, :, :], ALU.mult),
              reads=[BETA, EGC], writes=[BEG])
        for (dst, gl, mcol) in ((EDE, GLE, C_EVEN[:, 0:1]), (EDO, GLO, C_ODD[:, 0:1])):
            P.add("dve", lambda e, dst=dst, gl=gl: e.tensor_tensor(dst[:, :, :], gl[:, :, :], GC[:, :, :], ALU.subtract),
                  reads=[gl, GC], writes=[dst])
            P.add("dve", lambda e, dst=dst: e.tensor_scalar(dst[:, :, :], dst[:, :, :], 0.0, None, ALU.min),
                  reads=[dst], writes=[dst])
            P.add("act", lambda e, dst=dst: e.activation(dst[:, :, :], dst[:, :, :], AF.Exp), reads=[dst], writes=[dst])
            P.add("dve", lambda e, dst=dst, mcol=mcol: e.tensor_scalar(dst[:, :, :], dst[:, :, :], mcol, None, ALU.mult),
                  reads=[dst, CONST], writes=[dst])
        P.add("act", lambda e: e.activation(GLE[:, :, :], GLE[:, :, :], AF.Exp), reads=[GLE], writes=[GLE])
        P.add("act", lambda e: e.activation(GLO[:, :, :], GLO[:, :, :], AF.Exp), reads=[GLO], writes=[GLO])
        EGL = [GLE, GLO]
        ED = [EDE, EDO]
        NGC = TMPA
        P.add("dve", lambda e: e.tensor_scalar(NGC[:, :, :], GC[:, :, :], -1.0, None, ALU.mult), reads=[GC], writes=[NGC])

        wis_ = WStream(WI, 16, lambda i, wi: P.dma(wi[:, :, :, :].rearrange("p k g n -> p (k g n)"),
                                                   gdn_w_in[j, i // 2, i % 2, :, :], writes=[wi], q="pool"))
        whs_ = WStream(WOH, 8, lambda i, woh: P.dma(woh[:, :], gdn_w_out[j, i * 128:(i + 1) * 128, :],
                                                    writes=[woh], q="pool"))
        PRE_BANKS, SCAN_BANKS = [0, 1, 2, 3, 4], [5, 6, 7]
        pre_rr, scan_rr = [0], [0]
        for h in range(8):
            pump(3)
            for pi in range(2):
                wi = wis_.need(h * 2 + pi)
                for g in range(2):
                    t = pi * 2 + g
                    b0 = bank_group()
                    for tb in range(NTB):
                        pbk = PS[b0 + tb]
                        for kc in range(KC):
                            P.add("pe", lambda e, pbk=pbk, wi=wi, kc=kc, g=g, tb=tb: e.matmul(
                                pbk[:, :], wi[:, kc, g, :], H[:, kc, tb * TB:(tb + 1) * TB],
                                start=(kc == 0), stop=(kc == KC - 1)), reads=[wi, Hb[(kc, tb)]], writes=[pbk])
                    ct = CT[t]
                    if t < 3:
                        ch = t * 8 + h
                        conv_from_psum(b0, GCW[:, j, 0, ch:ch + 1], GCW[:, j, 1, ch:ch + 1], GCW[:, j, 2, ch:ch + 1],
                                       ZEROC[:, 0:1], ct, ct)
                        P.add("act", lambda e, ct=ct: e.activation(ct[:, :], ct[:, :], AF.Silu), reads=[ct], writes=[ct])
                    else:
                        P.add("act", lambda e, ct=ct, b0=b0: e.activation(ct[:, 0:512], PS[b0][:, :], AF.Silu),
                              reads=[PS[b0]], writes=[ct])
                        P.add("act", lambda e, ct=ct, b0=b0: e.activation(
                            ct[:, 512:NT], PSALL[:, (b0 + 1) * 512:(b0 + 3) * 512], AF.Silu),
                            reads=[PS[b0 + 1], PS[b0 + 2]], writes=[ct])
            for (ct, dstb, sc, keep) in ((CTQ, QNb, 128.0 ** -0.5, False), (CTK, KNb, 1.0, True)):
                for tb in range(NTB):
                    sl = slice(tb * TB, (tb + 1) * TB)
                    sqf = TC[tb % 2]
                    rs = TC[2 + tb % 2]
                    sqv = sqf[:, :, :].rearrange("p a b -> p (a b)")
                    rsv = rs[:, :, :].rearrange("p a b -> p (a b)")
                    P.add("act", lambda e, ct=ct, sl=sl, sqv=sqv: e.activation(sqv, ct[:, sl], AF.Square),
                          reads=[ct], writes=[sqf])
                    pbk = psum()
                    P.add("pe", lambda e, pbk=pbk, sqv=sqv: e.matmul(pbk[:, :], C_ONES, sqv, start=True, stop=True),
                          reads=[CONST, sqf], writes=[pbk])
                    P.add("act", lambda e, pbk=pbk, rsv=rsv: e.activation(rsv, pbk[:, :], AF.Ln, bias=EPSC[:, 0:1], scale=1.0),
                          reads=[pbk, EPSC], writes=[rs])
                    P.add("act", lambda e, rsv=rsv: e.activation(rsv, rsv, AF.Exp, scale=-0.5), reads=[rs], writes=[rs])
                    if keep:
                        P.add("pool", lambda e, ct=ct, sl=sl, rsv=rsv: e.tensor_tensor(ct[:, sl], ct[:, sl], rsv, ALU.mult),
                              reads=[ct, rs], writes=[ct])
                        P.add("act", lambda e, ct=ct, sl=sl, dstb=dstb: e.copy(dstb[:, sl], ct[:, sl]),
                              reads=[ct], writes=[dstb])
                    else:
                        P.add("pool", lambda e, ct=ct, sl=sl, rsv=rsv, dstb=dstb: e.tensor_tensor(
                            dstb[:, sl], ct[:, sl], rsv, ALU.mult), reads=[ct, rs], writes=[dstb])
            P.add("pool", lambda e: e.memset(OF[:, :], 0.0), writes=[OF] + OFb)

            f4 = lambda T_: T_[:, :, :].rearrange("p a b -> p (a b)")
            v4 = lambda pb_: pb_[:, :].rearrange("p (b f) -> p b f", b=4)
            Ibc = C_I.unsqueeze(1).to_broadcast([128, 4, 128])

            def pre_dir(g, d, st, pk, pv, pg, pq):
                cd = d * 8 + h
                blks = slice(g * 4, g * 4 + 4)
                tsl = slice(g * 512, (g + 1) * 512)
                col = lambda T_: T_[:, blks, cd:cd + 1].to_broadcast([128, 4, 128])
                B = BS[d]
                Fa, Fb = F[2 * d], F[2 * d + 1]
                VB, KBG = VBs[d], KBGs[d]
                P.add("dve", lambda e: e.tensor_tensor(VB[:, :, :], v4(pv), col(BETA), ALU.mult),
                      reads=[pv, BETA], writes=[VB])
                P.add("dve", lambda e: e.tensor_tensor(KBG[:, :, :], v4(pk), col(BEG), ALU.mult),
                      reads=[pk, BEG], writes=[KBG])
                for u in range(2):
                    P.add("dve", lambda e: e.tensor_tensor(st["KD"][u][:, :, :], v4(pk), col(ED[u]), ALU.mult),
                          reads=[pk, ED[u]], writes=[st["KD"][u]])
                yield
                P.add("pool", lambda e: e.tensor_tensor(Fa[:, :, :], Ibc, col(EGC), ALU.mult),
                      reads=[CONST, EGC], writes=[Fa])
                pr = psum_of(PRE_BANKS, pre_rr)
                for b in range(4):
                    P.add("pe", lambda e: e.matmul(pr[:, b * 128:(b + 1) * 128], C_ONES, Fa[:, b, :],
                                                   start=True, stop=True), reads=[CONST, Fa], writes=[pr])
                P.add("dve", lambda e: e.scalar_tensor_tensor(f4(st["QG"]), pr[:, :], 128.0 ** -0.5, QNb[:, tsl],
                                                              ALU.mult, ALU.mult), reads=[pr, QNb], writes=[st["QG"]])
                yield
                P.add("pool", lambda e: e.tensor_tensor(Fa[:, :, :], Ibc, col(GC), ALU.mult),
                      reads=[CONST, GC], writes=[Fa])
                P.add("pool", lambda e: e.tensor_tensor(Fb[:, :, :], Ibc, col(GCB), ALU.mult),
                      reads=[CONST, GCB], writes=[Fb])
                pa = psum_of(PRE_BANKS, pre_rr)
                pbb = psum_of(PRE_BANKS, pre_rr)
                for (pz, dg, msk) in ((pa, Fa, C_MINC[d]), (pbb, Fb, C_MSTR[d])):
                    for b in range(4):
                        o_ = pz[:, b * 128:(b + 1) * 128]
                        P.add("pe", lambda e: e.matmul(o_, C_ONES, dg[:, b, :], start=True, stop=False),
                              reads=[CONST, dg], writes=[pz])
                        P.add("pe", lambda e: e.matmul(o_, C_I, msk, start=False, stop=True),
                              reads=[CONST], writes=[pz])
                BT, Bs_, M_ = B
                BTr, Bsr, Mr = [x_.t for x_ in B]
                r4 = lambda ap_: ap_[:, :, :].rearrange("p a b -> p (a b)")
                for b in range(4):
                    ngc = NGC[:, g * 4 + b, cd:cd + 1]
                    P.add("act", lambda e: e.activation(Mr[:, b, :], pa[:, b * 128:(b + 1) * 128], AF.Exp, bias=ngc),
                          reads=[pa, NGC], writes=[M_])
                    P.add("act", lambda e: e.activation(BTr[:, b, :], pbb[:, b * 128:(b + 1) * 128], AF.Exp, bias=ngc),
                          reads=[pbb, NGC], writes=[BT])
                yield
                pg = psum_of(PRE_BANKS, pre_rr)
                pq = psum_of(PRE_BANKS, pre_rr)
                for b in range(4):
                    c0 = g * 512 + b * 128
                    P.add("pe", lambda e: e.matmul(pg[:, b * 128:(b + 1) * 128], KNb[:, c0:c0 + 128], KNb[:, c0:c0 + 128],
                                                   start=True, stop=True), reads=[KNb], writes=[pg])
                for b in range(4):
                    c0 = g * 512 + b * 128
                    P.add("pe", lambda e: e.matmul(pq[:, b * 128:(b + 1) * 128], KNb[:, c0:c0 + 128], QNb[:, c0:c0 + 128],
                                                   start=True, stop=True), reads=[KNb, QNb], writes=[pq])
                P.add("dve", lambda e: e.scalar_tensor_tensor(f4(st["PT"]), pq[:, :], 128.0 ** -0.5, f4(M_),
                                                              ALU.mult, ALU.mult), reads=[pq, M_], writes=[st["PT"]])
                P.add("dve", lambda e: e.scalar_tensor_tensor(r4(BTr), pg[:, :], -1.0, f4(BT), ALU.mult, ALU.mult),
                      reads=[pg, BT], writes=[BT])
                yield
                pt_ = psum_of(PRE_BANKS, pre_rr)
                for b in range(4):
                    P.add("pe", lambda e: e.transpose(pt_[:, b * 128:(b + 1) * 128], BT[:, b, :], C_I),
                          reads=[BT, CONST], writes=[pt_])
                evac(r4(Bsr), [Bs_], pt_[:, :], [pt_])
                P.add("pool", lambda e: e.tensor_tensor(Mr[:, :, :], BT[:, :, :], Ibc, ALU.add),
                      reads=[BT, CONST], writes=[M_])
                yield
                TTb = TTbs[d]
                for k in range(5):
                    if k < 4:
                        px = psum_of(PRE_BANKS, pre_rr)
                        for b in range(4):
                            P.add("pe", lambda e: e.matmul(px[:, b * 128:(b + 1) * 128], Bsr[:, b, :], BTr[:, b, :],
                                                           start=True, stop=True), reads=[Bs_, BT], writes=[px])
                    py = psum_of(PRE_BANKS, pre_rr)
                    for b in range(4):
                        P.add("pe", lambda e: e.matmul(py[:, b * 128:(b + 1) * 128], BTr[:, b, :], Bsr[:, b, :],
                                                       start=True, stop=True), reads=[Bs_, BT], writes=[py])
                    if k < 4:
                        evac(r4(BTr), [BT], px[:, :], [px])
                    evac(r4(Bsr), [Bs_], py[:, :], [py])
                    yield
                    pm_ = psum_of(PRE_BANKS, pre_rr)
                    for b in range(4):
                        o_ = pm_[:, b * 128:(b + 1) * 128]
                        P.add("pe", lambda e: e.matmul(o_, Bsr[:, b, :], Mr[:, b, :], start=True, stop=True),
                              reads=[Bs_, M_], writes=[pm_])
                    if k < 4:
                        P.add("dve", lambda e: e.tensor_tensor(r4(Mr), pm_[:, :], f4(M_), ALU.add),
                              reads=[pm_, M_], writes=[M_])
                    else:
                        P.add("dve", lambda e: e.tensor_tensor(f4(TTb), pm_[:, :], f4(M_), ALU.add),
                              reads=[pm_, M_], writes=[TTb])
                    yield
                pu = psum_of(PRE_BANKS, pre_rr)
                pw = psum_of(PRE_BANKS, pre_rr)
                for b in range(4):
                    P.add("pe", lambda e: e.matmul(pu[:, b * 128:(b + 1) * 128], TTb[:, b, :], VB[:, b, :],
                                                   start=True, stop=True), reads=[TTb, VB], writes=[pu])
                for b in range(4):
                    P.add("pe", lambda e: e.matmul(pw[:, b * 128:(b + 1) * 128], KBG[:, b, :], TTb[:, b, :],
                                                   start=True, stop=True), reads=[TTb, KBG], writes=[pw])
                evac(f4(st["U"]), [st["U"]], pu[:, :], [pu])
                evac(f4(st["NWT"]), [st["NWT"]], pw[:, :], [pw], scale=-1.0)
                yield

            def precompute2_gen(g, sts):
                pk = psum_of(PRE_BANKS, pre_rr)
                pv = psum_of(PRE_BANKS, pre_rr)
                for b in range(4):
                    c0 = g * 512 + b * 128
                    P.add("pe", lambda e: e.transpose(pk[:, b * 128:(b + 1) * 128], CTK[:, c0:c0 + 128], C_I),
                          reads=[CTK, CONST], writes=[pk])
                for b in range(4):
                    c0 = g * 512 + b * 128
                    P.add("pe", lambda e: e.transpose(pv[:, b * 128:(b + 1) * 128], CTV[:, c0:c0 + 128], C_I),
                          reads=[CTV, CONST], writes=[pv])
                gens = [pre_dir(g, d_, sts[d_], pk, pv, None, None) for d_ in range(2)]
                while gens:
                    for gi in list(gens):
                        try:
                            next(gi)
                        except StopIteration:
                            gens.remove(gi)
                    yield

            def interleave(gen, rounds, every):
                i = 0
                r = 0
                for _ in gen:
                    i += 1
                    if i % every == 0 and r < len(rounds):
                        rounds[r]()
                        r += 1
                while r < len(rounds):
                    rounds[r]()
                    r += 1

            def scan_step(ch, st, d, blk, u):
                cd = d * 8 + h
                bi = blk % 4
                rsl = slice(u * 64, u * 64 + 64)
                S, Sb, VN = ch["S"], ch["Sb"], ch["VN"]
                p1 = psum_of(SCAN_BANKS, scan_rr)
                P.add("pe", lambda e: e.matmul(p1[rsl, 0:128], st["NWT"][:, bi, rsl], Sb[:, :], start=True, stop=True),
                      reads=[st["NWT"], Sb], writes=[p1])
                P.add("dve", lambda e: e.tensor_tensor(VN[rsl, :], st["U"][rsl, bi, :], p1[rsl, 0:128], ALU.add),
                      reads=[st["U"], p1], writes=[VN])
                p2 = psum_of(SCAN_BANKS, scan_rr)
                P.add("pe", lambda e: e.matmul(p2[:, 0:64], Sb[:, :], st["QG"][:, bi, rsl], start=True, stop=False),
                      reads=[Sb, st["QG"]], writes=[p2])
                P.add("pe", lambda e: e.matmul(p2[:, 0:64], VN[:, :], st["PT"][:, bi, rsl], start=False, stop=True),
                      reads=[VN, st["PT"]], writes=[p2])
                c0 = blk * 128 + u * 64
                P.add("dve", lambda e: e.tensor_tensor(OF[:, c0:c0 + 64], OF[:, c0:c0 + 64], p2[:, 0:64], ALU.add),
                      reads=[OFb[blk], p2], writes=[OFb[blk]])
                p3 = psum_of(SCAN_BANKS, scan_rr)
                P.add("pe", lambda e: e.matmul(p3[:, 0:128], st["KD"][u][:, bi, :], VN[:, :], start=True, stop=True),
                      reads=[st["KD"][u], VN], writes=[p3])
                P.add("dve", lambda e: e.scalar_tensor_tensor(Sb[:, :], S[:, :], EGL[u][:, blk, cd:cd + 1], p3[:, 0:128],
                                                              ALU.mult, ALU.add), reads=[S, EGL[u], p3], writes=[Sb])
                P.add("dve", lambda e: e.scalar_tensor_tensor(S[:, :], S[:, :], EGL[u][:, blk, cd:cd + 1], p3[:, 0:128],
                                                              ALU.mult, ALU.add), reads=[S, EGL[u], p3], writes=[S])

            def chain_steps(d, blocks):
                steps = [(b, u) for b in blocks for u in range(2)]
                return steps if d == 0 else steps[::-1]

            def init_chain(ch, d, seq):
                if seq < 2:
                    P.add("pool", lambda e: e.memset(ch["S"][:, :], 0.0), writes=[ch["S"]])
                else:
                    P.dma(ch["S"][:, :], state_gdn[j, d, h, :, :], writes=[ch["S"]])
                P.add("act", lambda e: e.copy(ch["Sb"][:, :], ch["S"][:, :]), reads=[ch["S"]], writes=[ch["Sb"]])

            for _ in precompute2_gen(0, [SETS[2], SETS[3]]):
                pass
            chains = []
            for d in range(2):
                for seq in range(2):
                    ch = CH[d * 2 + seq]
                    init_chain(ch, d, seq)
                    chains.append((ch, SETS[2 + d], d, seq, chain_steps(d, [2 * seq, 2 * seq + 1])))

            def roundA(si):
                for (ch, st, d, seq, steps) in chains:
                    scan_step(ch, st, d, steps[si][0], steps[si][1])

            interleave(precompute2_gen(1, [SETS[0], SETS[1]]), [lambda si=si: roundA(si) for si in range(4)], 3)
            for (ch, st, d, seq, steps) in chains:
                P.dma(st_out[seq, j, d, h, :, :], ch["S"][:, :], reads=[ch["S"]])
            chB = [CH[0], CH[1]]
            stepsB = [chain_steps(d, list(range(4, 12))) for d in range(2)]
            for d in range(2):
                init_chain(chB[d], d, 2)

            def stepB(d, si):
                blk, u = stepsB[d][si]
                scan_step(chB[d], SETS[(blk // 4 - 1) * 2 + d], d, blk, u)

            interleave(precompute2_gen(2, [SETS[2], SETS[3]]), [lambda si=si: stepB(0, si) for si in range(8)], 2)
            for si in range(16):
                stepB(1, si)
                if si < 8:
                    stepB(0, 8 + si)

            for tb in range(NTB):
                sl = slice(tb * TB, (tb + 1) * TB)
                sqf = TC[tb % 2]
                rs = TC[2 + tb % 2]
                sqv = sqf[:, :, :].rearrange("p a b -> p (a b)")
                rsv = rs[:, :, :].rearrange("p a b -> p (a b)")
                P.add("act", lambda e, sl=sl, sqv=sqv: e.activation(sqv, OF[:, sl], AF.Square),
                      reads=OFb[tb * 4:tb * 4 + 4] + [OF], writes=[sqf])
                pbk = psum()
                P.add("pe", lambda e, pbk=pbk, sqv=sqv: e.matmul(pbk[:, :], C_ONES, sqv, start=True, stop=True),
                      reads=[CONST, sqf], writes=[pbk])
                P.add("act", lambda e, pbk=pbk, rsv=rsv: e.activation(rsv, pbk[:, :], AF.Ln, bias=EPSC[:, 0:1], scale=1.0 / 128),
                      reads=[pbk, EPSC], writes=[rs])
                P.add("act", lambda e, rsv=rsv: e.activation(rsv, rsv, AF.Exp, scale=-0.5), reads=[rs], writes=[rs])
                P.add("pool", lambda e, sl=sl, rsv=rsv: e.tensor_tensor(rsv, OF[:, sl], rsv, ALU.mult),
                      reads=OFb[tb * 4:tb * 4 + 4] + [rs, OF], writes=[rs])
                P.add("dve", lambda e, sl=sl, rsv=rsv: e.scalar_tensor_tensor(
                    OTh[:, sl], rsv, GNG[:, j:j + 1], CTZ[:, sl], ALU.mult, ALU.mult), reads=[rs, GNG, CTZ], writes=[OTh])
            woh = whs_.need(h)
            for n in range(KC):
                for tb in range(NTB):
                    pbk = psum()
                    P.add("pe", lambda e, pbk=pbk, n=n, tb=tb, woh=woh: e.matmul(
                        pbk[:, :], woh[:, n * 128:(n + 1) * 128], OTh[:, tb * TB:(tb + 1) * TB], start=True, stop=True),
                        reads=[woh, OTh], writes=[pbk])
                    resid_from_psum(pbk, l, 16, n, tb)

    P.fence()
    emit_rbpad()
    mod_state["gen"] = mod_gen(0)
    pump(24)
    for l in range(cfg.n_layers):
        emit_norm_mod(l, 1)
        P.fence()
        if l % 2 == 1 and cfg.do_na:
            emit_na(l)
        if l % 2 == 0 and cfg.do_gdn:
            emit_gdn(l)
        pump(48)
        emit_norm_mod(l, 2)
        P.fence()
        if l + 1 < cfg.n_layers:
            mod_state["gen"] = mod_gen(l + 1)
        emit_ffn(l)
        pump(48)
    P.fence()

    FG = carve(ARENA, 8192, F32, [128, D], "FG")
    P.dma(FG[:, :], final_g.rearrange("(o d) -> o d", o=1).to_broadcast([128, D]), writes=[FG])
    YT = [carve(ARENA, i * 4096, F32, [128, D], "YT%d" % i) for i in range(2)]
    YSQ = carve(ARENA, 12288, F32, [128, D], "YSQ")
    SS = [P.sb("SS%d" % i, [128, 1], F32) for i in range(2)]
    for blk in range(NT // 128):
        yt = YT[blk % 2]
        ss = SS[blk % 2]
        tb = (blk * 128) // TB
        for half in range(2):
            pb = psum()
            for j in range(4):
                kc = half * 4 + j
                P.add("pe", lambda e, pb=pb, kc=kc, j=j, blk=blk: e.transpose(
                    pb[:, j * 128:(j + 1) * 128], X[:, kc, blk * 128:(blk + 1) * 128], CONST[:, 0:128]),
                    reads=[Xb[(kc, tb)], CONST], writes=[pb])
            P.add("act", lambda e, pb=pb, yt=yt, half=half: e.copy(yt[:, half * 512:(half + 1) * 512], pb[:, :]),
                  reads=[pb], writes=[yt])
        P.add("act", lambda e, yt=yt, ss=ss: e.activation(YSQ[:, :], yt[:, :], AF.Square, accum_out=ss[:, 0:1]),
              reads=[yt], writes=[YSQ, ss])
        P.add("act", lambda e, ss=ss: e.activation(ss[:, :], ss[:, :], AF.Ln, bias=EPSC[:, 0:1], scale=1.0 / D),
              reads=[ss, EPSC], writes=[ss])
        P.add("act", lambda e, ss=ss: e.activation(ss[:, :], ss[:, :], AF.Exp, scale=-0.5), reads=[ss], writes=[ss])
        P.add("dve", lambda e, yt=yt, ss=ss: e.scalar_tensor_tensor(
            yt[:, :], yt[:, :], ss[:, 0:1], FG[:, :], ALU.mult, ALU.mult), reads=[yt, ss, FG], writes=[yt])
        P.dma(y_out[blk * 128:(blk + 1) * 128, :], yt[:, :], reads=[yt])

    P.emit()
    return nc


def make_consts():
    c = np.zeros((128, 1536), np.float32)
    c[:, 0:128] = np.eye(128, dtype=np.float32)
    J = np.zeros((64, 64), np.float32)
    J[np.arange(64), 63 - np.arange(64)] = 1.0
    c[0:64, 128:192] = J
    c[64:128, 128:192] = J
    qc = 63 - np.arange(64)[:, None]
    kc = np.arange(64)[None, :]
    cs = np.clip(qc - 8, 0, 48)
    inside = (kc >= cs) & (kc < cs + 16)
    c[0:64, 192:256] = np.where(inside, 0.0, -30000.0)
    jj = np.arange(128)[:, None]
    ii = np.arange(128)[None, :]
    same = (jj // 64) == (ii // 64)
    c[:, 256:384] = (same & (jj <= ii)).astype(np.float32)
    c[:, 384:512] = (same & (jj >= ii)).astype(np.float32)
    c[:, 512:640] = (jj < 64).astype(np.float32) * np.ones((1, 128), np.float32)
    c[:, 640:768] = (jj >= 64).astype(np.float32) * np.ones((1, 128), np.float32)
    NEG = -30000.0
    c[:, 768:896] = np.where(same & (jj <= ii), 0.0, NEG)
    c[:, 896:1024] = np.where(same & (jj < ii), 0.0, NEG)
    c[:, 1024:1152] = np.where(same & (jj >= ii), 0.0, NEG)
    c[:, 1152:1280] = np.where(same & (jj > ii), 0.0, NEG)
    c[:, 1280:1408] = 1.0
    c[:, 1408:1536] = -1.0
    return c


_NC_CACHE = {}


def kernel(x_prompt, x_sample, state_gdn, cache_k, cache_v, c, c_ctx, w_ada, b_ada, norm1_g, norm2_g,
           gdn_w_in, gdn_conv_w, gdn_a_log, gdn_dt_bias, gdn_norm_g, gdn_w_out,
           na_w_qkv, na_rel_bias, na_w_out, ffn_w_up, ffn_conv_w, ffn_conv_b, ffn_w_down, final_g, _cfg=Cfg, _cores=8):
    f = lambda a: np.ascontiguousarray(np.asarray(a, dtype=np.float32))
    nc = build(_cfg)
    consts = make_consts()
    c_ = np.ascontiguousarray
    w_ada_t = c_(f(w_ada).reshape(4, 8, 128, 48, 128).transpose(0, 3, 2, 1, 4)).reshape(4, 48, 128, 1024)
    w_up_t = c_(f(ffn_w_up).reshape(4, 8, 128, 2, 22, 128).transpose(0, 4, 2, 1, 3, 5)).reshape(4, 22, 128, 2048)
    w_dn_t = c_(f(ffn_w_down).reshape(4, 2, 11, 128, 8, 128).transpose(0, 1, 4, 3, 2, 5)).reshape(4, 2, 8, 128, 1408)
    qkv_t = c_(f(na_w_qkv).reshape(2, 8, 128, 3, 8, 128).transpose(0, 4, 2, 1, 3, 5)).reshape(2, 8, 128, 3072)
    nwo_t = c_(f(na_w_out).reshape(2, 8, 128, 8, 128).transpose(0, 3, 2, 1, 4)).reshape(2, 8, 128, 1024)
    gin = f(gdn_w_in)
    gin_t = c_(gin[:, :, :4096].reshape(2, 8, 128, 2, 2, 8, 128).transpose(0, 5, 3, 2, 1, 4, 6)).reshape(2, 8, 2, 128, 2048)
    gab_t = c_(gin[:, :, 4096:4128].reshape(2, 8, 128, 32).transpose(0, 2, 1, 3)).reshape(2, 128, 256)
    shared = {
        "w_ada": w_ada_t, "b_ada": f(b_ada), "norm1_g": f(norm1_g), "norm2_g": f(norm2_g),
        "gdn_w_in": gin_t, "gdn_w_ab": gab_t, "gdn_conv_w": f(gdn_conv_w),
        "gdn_a_log": f(gdn_a_log).reshape(2, 16), "gdn_dt_bias": f(gdn_dt_bias).reshape(2, 16),
        "gdn_norm_g": f(gdn_norm_g), "gdn_w_out": f(gdn_w_out),
        "na_w_qkv": qkv_t, "na_rel_bias": f(na_rel_bias).reshape(2, 240, 31), "na_w_out": nwo_t,
        "ffn_w_up": w_up_t, "ffn_conv_w": f(ffn_conv_w), "ffn_conv_b": f(ffn_conv_b),
        "ffn_w_down": w_dn_t, "final_g": f(final_g), "consts": consts,
    }
    xp = f(x_prompt)
    xs = f(x_sample)
    in_maps = []
    for i in range(_cores):
        m = dict(shared)
        m["x_in"] = np.concatenate([xp[2 * i].reshape(256, D), xp[2 * i + 1].reshape(256, D), xs[i]], axis=0)
        m["state_gdn"] = f(state_gdn[i])
        m["cache_k"] = f(cache_k[i]).reshape(2, 256, 1024)
        m["cache_v"] = f(cache_v[i]).reshape(2, 256, 1024)
        m["cvec"] = np.stack([f(c_ctx), f(c[i])], axis=0)
        in_maps.append(m)
    res = run_bass_kernel_spmd(nc, in_maps, core_ids=list(range(_cores)))
    R = res.results
    y_prompt = np.zeros((16, 256, D), np.float32)
    y_sample = np.zeros((8, 1024, D), np.float32)
    new_state = np.zeros((16, 2, 2, 8, 128, 128), np.float32)
    new_k = np.zeros((16, 2, 256, 16, 64), np.float32)
    new_v = np.zeros((16, 2, 256, 16, 64), np.float32)
    for i in range(_cores):
        y = R[i]["y_out"]
        y_prompt[2 * i] = y[0:256]
        y_prompt[2 * i + 1] = y[256:512]
        y_sample[i] = y[512:]
        new_state[2 * i:2 * i + 2] = R[i]["st_out"]
        new_k[2 * i:2 * i + 2] = R[i]["ck_out"].reshape(2, 2, 256, 16, 64)
        new_v[2 * i:2 * i + 2] = R[i]["cv_out"].reshape(2, 2, 256, 16, 64)
    return (y_prompt, y_sample, new_state, new_k, new_v)
```

```python
import numpy as np
import concourse.bass as bass
import concourse.mybir as mybir
from concourse.bass_utils import run_bass_kernel_spmd

F32 = mybir.dt.float32
F32R = mybir.dt.float32r
BF16 = mybir.dt.bfloat16
AF = mybir.ActivationFunctionType
ALU = mybir.AluOpType
AX = mybir.AxisListType

D = 1024
KC = 8
DEPTH = 4
NP_TOK = 512
NS_TOK = 1024
NT = NP_TOK + NS_TOK
TB = 512
NTB = NT // TB
D_FF = 2816
NFC = D_FF // 128
GDN_DIN = 4128
EPS = 1e-6
SEQS = [(0, 256), (256, 256), (512, 1024)]


class Buf:
    __slots__ = ("name", "t", "psum", "lastw", "readers", "dma_readers", "root")

    def __init__(self, name, t, psum=False):
        self.name = name
        self.t = t
        self.psum = psum
        self.lastw = None
        self.readers = {}
        self.dma_readers = []
        self.root = self

    def alias(self, ap, name=None):
        b = Buf(name or self.name, ap, self.psum)
        b.root = self.root
        return b

    def __getitem__(self, idx):
        return self.t[idx]

    def sub(self, name=None):
        return Buf(name or self.name, self.t, self.psum)


class Op:
    __slots__ = ("eng", "fn", "reads", "writes", "dma", "waits", "sig", "sem", "sigval", "deps", "n")


class _Rec:
    def __init__(self):
        self.call = None

    def __getattr__(self, name):
        def f(*args, **kwargs):
            assert self.call is None
            self.call = (name, args, kwargs)
            return None
        return f


def _bind(fn):
    r = _Rec()
    fn(r)
    name, args, kwargs = r.call
    return lambda e: getattr(e, name)(*args, **kwargs)


class Prog:
    ENGS = ("pe", "act", "dve", "pool", "sp")

    def __init__(self, nc, n_dma_ring=8):
        self.nc = nc
        self.ops = []
        self.nring = n_dma_ring
        self.sbuf_bytes = 0

    def sb(self, name, shape, dtype):
        t = self.nc.alloc_sbuf_tensor(name, list(shape), dtype)
        return Buf(name, t)

    def ps(self, name, shape, dtype=F32):
        t = self.nc.alloc_psum_tensor(name, list(shape), dtype)
        return Buf(name, t, psum=True)

    def add(self, eng, fn, reads=(), writes=(), dma=False):
        op = Op()
        op.eng = eng
        op.fn = _bind(fn)
        op.reads = [b.root for b in reads if b is not None]
        op.writes = [b.root for b in writes if b is not None]
        op.dma = dma
        op.waits = []
        op.sig = dma
        op.sem = None
        op.sigval = 0
        op.n = len(self.ops)
        self.ops.append(op)
        return op

    def fence(self):
        op = Op()
        op.eng = None
        op.n = len(self.ops)
        op.dma = False
        op.sig = False
        self.ops.append(op)

    def view(self, name, ap, psum=False):
        return Buf(name, ap, psum)

    def dma(self, out_ap, in_ap, reads=(), writes=(), q="sp", **kw):
        return self.add(q, lambda e: e.dma_start(out=out_ap, in_=in_ap, **kw), reads, writes, dma=True)

    def resolve(self):
        last_on = {}
        dma_since = []
        pending = {}
        real_ops = []
        for op in self.ops:
            if op.eng is None:
                snap = (dict(last_on), list(dma_since))
                dma_since = []
                for e in self.ENGS:
                    pending[e] = snap
                continue
            real_ops.append(op)
            wdeps = []
            rdeps = []
            for b in op.reads:
                if b.lastw is not None:
                    wdeps.append(b.lastw)
                if b.psum:
                    for e, r in b.readers.items():
                        if e != op.eng:
                            rdeps.append(r)
            for b in op.writes:
                if b.lastw is not None:
                    wdeps.append(b.lastw)
                rdeps.extend(b.readers.values())
                rdeps.extend(b.dma_readers)
            for b in op.writes:
                b.lastw = op
                b.readers = {}
                b.dma_readers = []
            for b in op.reads:
                if op.dma:
                    b.dma_readers.append(op)
                else:
                    b.readers[op.eng] = op
            deps = {}
            for d in wdeps:
                if d is op:
                    continue
                if d.dma or op.dma:
                    deps[d.n] = d
                elif d.eng == op.eng:
                    if op.eng != "pe":
                        deps[d.n] = d
                else:
                    deps[d.n] = d
            for d in rdeps:
                if d is op:
                    continue
                if d.dma or op.dma:
                    deps[d.n] = d
                elif d.eng != op.eng:
                    deps[d.n] = d
            if pending.get(op.eng) is not None:
                lo, dl = pending[op.eng]
                pending[op.eng] = None
                for e2, d in lo.items():
                    if e2 == op.eng and e2 == "pe" and not op.dma:
                        continue
                    deps[d.n] = d
                for d in dl:
                    deps[d.n] = d
            op.deps = list(deps.values())
            for d in op.deps:
                d.sig = True
            if op.dma:
                dma_since.append(op)
            else:
                last_on[op.eng] = op
        self.ops = real_ops

    def emit(self):
        nc = self.nc
        self.resolve()
        streams = {e: [] for e in self.ENGS}
        for op in self.ops:
            streams[op.eng].append(op)
        import contextlib
        with contextlib.ExitStack() as es:
            esem = {e: es.enter_context(nc.semaphore("s_" + e)) for e in self.ENGS}
            rings = {e: [es.enter_context(nc.semaphore("d_%s%d" % (e, i))) for i in range(self.nring)]
                     for e in ("sp", "pool", "act")}
            fin = es.enter_context(nc.semaphore("fin"))
            cnt = {e: 0 for e in self.ENGS}
            ringcnt = {e: [0] * self.nring for e in rings}
            ringpos = {e: 0 for e in rings}
            ring_prev = {}
            for op in self.ops:
                if op.dma:
                    r = ringpos[op.eng] % self.nring
                    ringpos[op.eng] += 1
                    ringcnt[op.eng][r] += 1
                    op.sem = rings[op.eng][r]
                    op.sigval = 16 * ringcnt[op.eng][r]
                    if ringcnt[op.eng][r] > 1:
                        ring_prev[op.n] = (op.sem, op.sigval - 16)
                elif op.sig:
                    cnt[op.eng] += 1
                    op.sem = esem[op.eng]
                    op.sigval = cnt[op.eng]
            waited = {e: {} for e in self.ENGS}
            for e in self.ENGS:
                for op in streams[e]:
                    need = {}
                    for d in op.deps:
                        k = id(d.sem)
                        if k not in need or need[k][1] < d.sigval:
                            need[k] = (d.sem, d.sigval)
                    if op.n in ring_prev:
                        s, v = ring_prev[op.n]
                        k = id(s)
                        if k not in need or need[k][1] < v:
                            need[k] = (s, v)
                    for k, (s, v) in need.items():
                        if waited[e].get(k, 0) >= v:
                            continue
                        waited[e][k] = v
                        op.waits.append((s, v))
            final_waits = []
            for e in rings:
                for r in range(self.nring):
                    if ringcnt[e][r] > 0:
                        final_waits.append((rings[e][r], 16 * ringcnt[e][r]))

            def run_stream(ename, eng):
                for op in streams[ename]:
                    for s, v in op.waits:
                        eng.wait_ge(s, v)
                    ins = op.fn(eng)
                    if op.sig:
                        ins.then_inc(op.sem, 16 if op.dma else 1)
                if ename == "sp":
                    for s, v in final_waits:
                        eng.wait_ge(s, v)

            with nc.Block() as block:
                @block.tensor
                def _(eng):
                    run_stream("pe", eng)

                @block.scalar
                def _(eng):
                    run_stream("act", eng)

                @block.vector
                def _(eng):
                    run_stream("dve", eng)

                @block.gpsimd
                def _(eng):
                    run_stream("pool", eng)

                @block.sync
                def _(eng):
                    run_stream("sp", eng)


class Cfg:
    n_layers = DEPTH
    do_gdn = True
    do_na = True


def build(cfg=Cfg):
    nc = bass.Bass("TRN2", target_bir_lowering=False)
    P = Prog(nc)

    def din(name, shape):
        return nc.dram_tensor(name, list(shape), F32, kind="ExternalInput").ap()

    def dout(name, shape):
        return nc.dram_tensor(name, list(shape), F32, kind="ExternalOutput").ap()

    x_in = din("x_in", [NT, D])
    state_gdn = din("state_gdn", [2, 2, 8, 128, 128])
    cache_k = din("cache_k", [2, 256, 1024])
    cache_v = din("cache_v", [2, 256, 1024])
    cvec = din("cvec", [2, D])
    w_ada = din("w_ada", [DEPTH, 48, 128, 1024])
    b_ada = din("b_ada", [DEPTH, 6 * D])
    norm1_g = din("norm1_g", [DEPTH, D])
    norm2_g = din("norm2_g", [DEPTH, D])
    gdn_w_in = din("gdn_w_in", [2, 8, 2, 128, 2048])
    gdn_w_ab = din("gdn_w_ab", [2, 128, 256])
    gdn_conv_w = din("gdn_conv_w", [2, 3, 3072])
    gdn_a_log = din("gdn_a_log", [2, 16])
    gdn_dt_bias = din("gdn_dt_bias", [2, 16])
    gdn_norm_g = din("gdn_norm_g", [2, 128])
    gdn_w_out = din("gdn_w_out", [2, D, D])
    na_w_qkv = din("na_w_qkv", [2, 8, 128, 3072])
    na_rel_bias = din("na_rel_bias", [2, 16 * 15, 31])
    na_w_out = din("na_w_out", [2, 8, 128, 1024])
    ffn_w_up = din("ffn_w_up", [DEPTH, 22, 128, 2048])
    ffn_conv_w = din("ffn_conv_w", [DEPTH, 3, 2 * D_FF])
    ffn_conv_b = din("ffn_conv_b", [DEPTH, 2 * D_FF])
    ffn_w_down = din("ffn_w_down", [DEPTH, 2, 8, 128, 1408])
    final_g = din("final_g", [D])
    consts = din("consts", [128, 1536])

    y_out = dout("y_out", [NT, D])
    st_out = dout("st_out", [2, 2, 2, 8, 128, 128])
    ck_out = dout("ck_out", [2, 2, 256, 1024])
    cv_out = dout("cv_out", [2, 2, 256, 1024])

    X = P.sb("X", [128, KC, NT], F32)
    Xb = {(kc, tb): X.sub("X%d_%d" % (kc, tb)) for kc in range(KC) for tb in range(NTB)}
    H = P.sb("H", [128, KC, NT], BF16)
    Hb = {(kc, tb): H.sub("H%d_%d" % (kc, tb)) for kc in range(KC) for tb in range(NTB)}
    ARENA = P.sb("ARENA", [128, 13824], F32)
    CHAIN = nc.alloc_sbuf_tensor("CHAIN", [128, 3072], F32R)
    WREG = P.sb("WREG", [128, 3584], F32)

    def carve(region, byte_off, dtype, shape, name):
        esz = 2 if dtype == BF16 else 4
        n = 1
        for d_ in shape[1:]:
            n *= d_
        base = region.t.bitcast(dtype) if dtype != F32 else region.t
        ap = base[:, byte_off // esz: byte_off // esz + n]
        if len(shape) == 3:
            ap = ap.rearrange("p (a b) -> p a b", a=shape[1])
        elif len(shape) == 4:
            ap = ap.rearrange("p (a b c) -> p a b c", a=shape[1], b=shape[2])
        return P.view(name, ap)

    CONST = P.sb("CONST", [128, 1536], F32)
    ident = CONST
    ONESB = P.sb("ONESB", [128, 128], BF16)
    IDB = P.sb("IDB", [128, 128], BF16)

    PSALL = nc.alloc_psum_tensor("psall", [128, 4096], F32)
    PS = [P.view("ps%d" % i, PSALL[:, i * 512:(i + 1) * 512], psum=True) for i in range(8)]
    ps_rr = [0]

    def psum():
        b = PS[ps_rr[0] % 8]
        ps_rr[0] += 1
        return b

    P.dma(CONST[:, :], consts[:, :], writes=[CONST])
    P.add("dve", lambda e: e.memset(ONESB[:, :], 1.0), writes=[ONESB])
    P.add("dve", lambda e: e.tensor_copy(IDB[:, :], CONST[:, 0:128]), reads=[CONST], writes=[IDB])

    XT = [carve(ARENA, i * 4096, F32, [128, D], "XT%d" % i) for i in range(2)]
    for blk in range(NT // 128):
        xt = XT[blk % 2]
        P.dma(xt[:, :], x_in[blk * 128:(blk + 1) * 128, :], writes=[xt])
        tb = (blk * 128) // TB
        for half in range(2):
            pb = psum()
            for j in range(4):
                kc = half * 4 + j
                P.add("pe", lambda e, pb=pb, xt=xt, kc=kc, j=j: e.transpose(
                    pb[:, j * 128:(j + 1) * 128], xt[:, kc * 128:(kc + 1) * 128], CONST[:, 0:128]),
                    reads=[xt, CONST], writes=[pb])
            wr = [Xb[(half * 4 + j, tb)] for j in range(4)]
            eng = "act" if half == 0 else "dve"
            if eng == "act":
                P.add("act", lambda e, pb=pb, half=half, blk=blk: e.copy(
                    X[:, half * 4:half * 4 + 4, blk * 128:(blk + 1) * 128],
                    pb[:, :].rearrange("p (j t) -> p j t", j=4)), reads=[pb], writes=wr)
            else:
                P.add("dve", lambda e, pb=pb, half=half, blk=blk: e.tensor_copy(
                    X[:, half * 4:half * 4 + 4, blk * 128:(blk + 1) * 128],
                    pb[:, :].rearrange("p (j t) -> p j t", j=4)), reads=[pb], writes=wr)

    STG = [P.sb("STG%d" % i, [128, 128], F32) for i in range(2)]
    stg_rr = [0]

    def load_fm(dst, dst_flat_ap, src_rows_ap, R):
        r0 = 0
        while r0 < R:
            r = min(128, R - r0)
            stg = STG[stg_rr[0] % 2]
            stg_rr[0] += 1
            P.dma(stg[0:r, :], src_rows_ap[r0:r0 + r, :], writes=[stg])
            pb = psum()
            P.add("pe", lambda e, pb=pb, stg=stg, r=r: e.transpose(pb[:, 0:r], stg[0:r, :], CONST[0:r, 0:r]),
                  reads=[stg, CONST], writes=[pb])
            P.add("dve", lambda e, pb=pb, r=r, r0=r0: e.tensor_copy(dst_flat_ap[:, r0:r0 + r], pb[:, 0:r]),
                  reads=[pb], writes=[dst])
            r0 += r

    G1 = P.sb("G1", [128, DEPTH, KC], F32)
    G2 = P.sb("G2", [128, DEPTH, KC], F32)
    BADA = P.sb("BADA", [128, DEPTH, 48], F32)
    CV = P.sb("CV", [128, 2, KC], F32)
    SCV = P.sb("SCV", [128, 2, KC], BF16)
    load_fm(G1, G1[:, :, :].rearrange("p l k -> p (l k)"), norm1_g.rearrange("l (k f) -> (l k) f", f=128), 32)
    load_fm(G2, G2[:, :, :].rearrange("p l k -> p (l k)"), norm2_g.rearrange("l (k f) -> (l k) f", f=128), 32)
    load_fm(BADA, BADA[:, :, :].rearrange("p l k -> p (l k)"), b_ada.rearrange("l (k f) -> (l k) f", f=128), 192)
    load_fm(CV, CV[:, :, :].rearrange("p v k -> p (v k)"), cvec.rearrange("v (k f) -> (v k) f", f=128), 16)
    P.add("act", lambda e: e.activation(SCV[:, :, :], CV[:, :, :], AF.Silu), reads=[CV], writes=[SCV])

    MOD = [P.sb("MOD%d" % l, [128, 48, 2], F32) for l in range(DEPTH)]
    A1 = [P.sb("A1_%d" % l, [128, KC, 2], F32) for l in range(DEPTH)]
    A2 = [P.sb("A2_%d" % l, [128, KC, 2], F32) for l in range(DEPTH)]
    WA = [P.sb("WA%d" % i, [128, KC, 128], BF16) for i in range(3)]
    wa_rr = [0]

    MODg = [[MOD[l].sub("MOD%d_%d" % (l, g)) for g in range(6)] for l in range(DEPTH)]

    class WStream:
        def __init__(self, slots, n, issue):
            self.slots, self.n, self.issue, self.nxt = slots, n, issue, 0

        def need(self, i):
            k = len(self.slots)
            while self.nxt <= min(i + k - 1, self.n - 1):
                self.issue(self.nxt, self.slots[self.nxt % k])
                self.nxt += 1
            return self.slots[i % k]

    def mod_gen(l):
        ws = WStream(WA, 48, lambda i, wa: P.dma(wa[:, :, :].rearrange("p k n -> p (k n)"), w_ada[l, i, :, :],
                                                 writes=[wa], q="pool"))
        for n in range(48):
            wa = ws.need(n)
            pm = PS[6 + mod_bank[0] % 2]
            mod_bank[0] += 1
            for kc in range(KC):
                P.add("pe", lambda e: e.matmul(pm[:, 0:2], wa[:, kc, :], SCV[:, :, kc],
                                               start=(kc == 0), stop=(kc == KC - 1)), reads=[wa, SCV], writes=[pm])
            P.add("dve", lambda e: e.tensor_scalar(MOD[l][:, n, :], pm[:, 0:2], BADA[:, l, n:n + 1], None, ALU.add),
                  reads=[pm, BADA], writes=[MODg[l][n // 8]])
            if n == 15:
                P.add("dve", lambda e: e.scalar_tensor_tensor(
                    A1[l][:, :, :], MOD[l][:, 8:16, :], 1.0, G1[:, l, :].unsqueeze(2).to_broadcast([128, KC, 2]),
                    ALU.add, ALU.mult), reads=[MODg[l][1], G1], writes=[A1[l]])
            if n == 39:
                P.add("dve", lambda e: e.scalar_tensor_tensor(
                    A2[l][:, :, :], MOD[l][:, 32:40, :], 1.0, G2[:, l, :].unsqueeze(2).to_broadcast([128, KC, 2]),
                    ALU.add, ALU.mult), reads=[MODg[l][4], G2], writes=[A2[l]])
            yield n

    mod_state = {"gen": None}
    mod_bank = [0]

    def pump(k):
        g = mod_state["gen"]
        if g is None:
            return
        for _ in range(k):
            try:
                next(g)
            except StopIteration:
                mod_state["gen"] = None
                return

    SQ = [P.sb("SQ%d" % i, [128, TB], BF16) for i in range(2)]
    RSTD = [P.sb("RSTD%d" % i, [128, TB], F32) for i in range(2)]
    TMPN = [P.sb("TMPN%d" % i, [128, TB], F32) for i in range(2)]
    EPSC = P.sb("EPSC", [128, 1], F32)
    P.add("dve", lambda e: e.memset(EPSC[:, :], EPS), writes=[EPSC])
    nrr = [0]

    def emit_norm_mod(l, which):
        A = A1[l] if which == 1 else A2[l]
        shoff = 0 if which == 1 else 24
        for tb in range(NTB):
            v = 0 if tb == 0 else 1
            rstd = RSTD[nrr[0] % 2]
            nrr[0] += 1
            sl = slice(tb * TB, (tb + 1) * TB)
            pb = psum()
            for kc in range(KC):
                sq = SQ[kc % 2]
                P.add("act", lambda e, sq=sq, sl=sl, kc=kc: e.activation(
                    sq[:, :], X[:, kc, sl], AF.Square), reads=[Xb[(kc, tb)]], writes=[sq])
                P.add("pe", lambda e, pb=pb, sq=sq, kc=kc: e.matmul(
                    pb[:, :], ONESB[:, :], sq[:, :], start=(kc == 0), stop=(kc == KC - 1)),
                    reads=[sq, ONESB], writes=[pb])
            P.add("act", lambda e, pb=pb, rstd=rstd: e.activation(
                rstd[:, :], pb[:, :], AF.Ln, bias=EPSC[:, 0:1], scale=1.0 / D),
                reads=[pb, EPSC], writes=[rstd])
            P.add("act", lambda e, rstd=rstd: e.activation(rstd[:, :], rstd[:, :], AF.Exp, scale=-0.5), reads=[rstd], writes=[rstd])
            for kc in range(KC):
                tmp = TMPN[kc % 2]
                P.add("dve", lambda e, tmp=tmp, kc=kc, sl=sl, rstd=rstd: e.tensor_tensor(
                    tmp[:, :], X[:, kc, sl], rstd[:, :], ALU.mult),
                    reads=[Xb[(kc, tb)], rstd], writes=[tmp])
                P.add("act", lambda e, tmp=tmp, kc=kc, sl=sl, v=v, A=A, l=l, shoff=shoff: e.activation(
                    H[:, kc, sl], tmp[:, :], AF.Identity,
                    bias=MOD[l][:, shoff + kc, v:v + 1], scale=A[:, kc, v:v + 1]),
                    reads=[tmp, A, MODg[l][shoff // 8]], writes=[Hb[(kc, tb)]])

    def resid_from_psum(pb, l, gate_off, n, tb, ncols=TB, col0=0):
        v = 0 if tb == 0 else 1
        sl = slice(tb * TB + col0, tb * TB + col0 + ncols)
        P.add("dve", lambda e: e.scalar_tensor_tensor(
            X[:, n, sl], pb[:, 0:ncols], MOD[l][:, gate_off + n, v:v + 1], X[:, n, sl], ALU.mult, ALU.add),
            reads=[pb, MODg[l][gate_off // 8], Xb[(n, tb)]], writes=[Xb[(n, tb)]])

    WUP = [carve(WREG, i * 4096, BF16, [128, KC, 2, 128], "WUP%d" % i) for i in range(2)]
    wup_rr = [0]
    CT = [P.sb("CT%d" % i, [128, NT], F32) for i in range(4)]
    ct_rr = [0]
    NFH = NFC // 2
    GB = carve(ARENA, 0, BF16, [128, NFH, NT], "GB")
    GBb = {(i, tb): GB.sub("GB%d_%d" % (i, tb)) for i in range(NFH) for tb in range(NTB)}
    FCW = P.sb("FCW", [128, DEPTH, 3, 44], F32)
    FCB = P.sb("FCB", [128, DEPTH, 44], F32)
    load_fm(FCW, FCW[:, :, :, :].rearrange("p l t k -> p (l t k)"),
            ffn_conv_w.rearrange("l t (k f) -> (l t k) f", f=128), DEPTH * 3 * 44)
    load_fm(FCB, FCB[:, :, :].rearrange("p l k -> p (l k)"), ffn_conv_b.rearrange("l (k f) -> (l k) f", f=128),
            DEPTH * 44)
    WDN = [carve(WREG, 8192 + i * 2816, BF16, [128, NFH, 128], "WDN%d" % i) for i in range(2)]
    wdn_rr = [0]
    grp_rr = [0]
    dn_rr = [0]

    def bank_group():
        g = grp_rr[0] % 2
        grp_rr[0] += 1
        return g * 3

    def conv_from_psum(b0, w0, w1, w2, bias, ct, ctb):
        pbs = [PS[b0], PS[b0 + 1], PS[b0 + 2]]
        samp = PSALL[:, (b0 + 1) * 512:(b0 + 3) * 512]
        P.add("act", lambda e: e.activation(ct[:, 0:512], PS[b0][:, :], AF.Identity, bias=bias, scale=w1),
              reads=[pbs[0], FCWb], writes=[ctb])
        P.add("act", lambda e: e.activation(ct[:, 512:NT], samp, AF.Identity, bias=bias, scale=w1),
              reads=[pbs[1], pbs[2], FCWb], writes=[ctb])
        for (t0, ln) in SEQS:
            if t0 < 512:
                src = lambda a, b_: PS[b0][:, a:b_]
                rd = [pbs[0]]
            else:
                src = lambda a, b_: PSALL[:, (b0 + 1) * 512 + a - 512:(b0 + 1) * 512 + b_ - 512]
                rd = [pbs[1], pbs[2]]
            P.add("dve", lambda e, src=src, t0=t0, ln=ln: e.scalar_tensor_tensor(
                ct[:, t0 + 1:t0 + ln], src(t0, t0 + ln - 1), w0, ct[:, t0 + 1:t0 + ln], ALU.mult, ALU.add),
                reads=rd + [FCWb, ctb], writes=[ctb])
            P.add("dve", lambda e, src=src, t0=t0, ln=ln: e.scalar_tensor_tensor(
                ct[:, t0:t0 + ln - 1], src(t0 + 1, t0 + ln), w2, ct[:, t0:t0 + ln - 1], ALU.mult, ALU.add),
                reads=rd + [FCWb, ctb], writes=[ctb])

    FCWb = FCW

    def emit_ffn(l):
        wus = WStream(WUP, NFC, lambda i, wu: P.dma(wu[:, :, :, :].rearrange("p k g n -> p (k g n)"),
                                                    ffn_w_up[l, i, :, :], writes=[wu], q="pool"))
        wds = WStream(WDN, 2 * KC, lambda i, wd: P.dma(wd[:, :, :].rearrange("p i n -> p (i n)"),
                                                       ffn_w_down[l, i // KC, i % KC, :, :], writes=[wd], q="pool"))
        for hf in range(2):
            for piece in range(hf * NFH, (hf + 1) * NFH):
                wu = wus.need(piece)
                if piece == (hf + 1) * NFH - 1:
                    wds.need(hf * KC)
                i = piece
                cts = []
                for g in range(2):
                    b0 = bank_group()
                    for tb in range(NTB):
                        pb = PS[b0 + tb]
                        for kc in range(KC):
                            P.add("pe", lambda e: e.matmul(
                                pb[:, :], wu[:, kc, g, :], H[:, kc, tb * TB:(tb + 1) * TB],
                                start=(kc == 0), stop=(kc == KC - 1)),
                                reads=[wu, Hb[(kc, tb)]], writes=[pb])
                    ct = CT[ct_rr[0] % 4]
                    ct_rr[0] += 1
                    ch = g * NFC + i
                    conv_from_psum(b0, FCW[:, l, 0, ch:ch + 1], FCW[:, l, 1, ch:ch + 1], FCW[:, l, 2, ch:ch + 1],
                                   FCB[:, l, ch:ch + 1], ct, ct)
                    cts.append(ct)
                ctv, ctg = cts
                pump(2)
                P.add("act", lambda e: e.activation(ctg[:, :], ctg[:, :], AF.Silu), reads=[ctg], writes=[ctg])
                P.add("pool", lambda e: e.tensor_tensor(GB[:, i - hf * NFH, :], ctv[:, :], ctg[:, :], ALU.mult),
                      reads=[ctv, ctg], writes=[GBb[(i - hf * NFH, tb)] for tb in range(NTB)])
            for piece in range(KC):
                wd = wds.need(hf * KC + piece)
                n = piece
                for tb in range(NTB):
                    pb = PS[dn_rr[0] % 6]
                    dn_rr[0] += 1
                    for i in range(NFH):
                        P.add("pe", lambda e: e.matmul(
                            pb[:, :], wd[:, i, :], GB[:, i, tb * TB:(tb + 1) * TB],
                            start=(i == 0), stop=(i == NFH - 1)),
                            reads=[wd, GBb[(i, tb)]], writes=[pb])
                    resid_from_psum(pb, l, 40, n, tb)
                pump(2)

    WO = [carve(WREG, i * 2048, BF16, [128, KC, 128], "WO%d" % i) for i in range(2)]
    wo_rr = [0]

    def emit_wout(l, w_dram, OT, OTb):
        wos = WStream(WO, KC, lambda i, wo: P.dma(wo[:, :, :].rearrange("p k n -> p (k n)"), w_dram[i, :, :],
                                                  writes=[wo], q="pool"))
        for n in range(KC):
            wo = wos.need(n)
            pump(1)
            for tb in range(NTB):
                pb = psum()
                for kc in range(KC):
                    P.add("pe", lambda e, pb=pb, wo=wo, kc=kc, tb=tb: e.matmul(
                        pb[:, :], wo[:, kc, :], OT[:, kc, tb * TB:(tb + 1) * TB],
                        start=(kc == 0), stop=(kc == KC - 1)), reads=[wo, OTb[kc]], writes=[pb])
                resid_from_psum(pb, l, 16, n, tb)

    rb_t = nc.dram_tensor("rbpad", [480, 127], F32)
    RBPAD = P.view("rbpad", rb_t.ap())
    J2B = P.sb("J2B", [128, 64], BF16)
    P.add("dve", lambda e: e.tensor_copy(J2B[:, :], CONST[:, 128:192]), reads=[CONST], writes=[J2B])

    def emit_rbpad():
        RBP = carve(ARENA, 16384, F32, [128, 4, 127], "RBP")
        P.add("pool", lambda e: e.memset(RBP[:, :, :], 0.0), writes=[RBP])
        P.dma(RBP[0:120, :, 48:79], na_rel_bias.rearrange("j (p a) f -> p (j a) f", a=2)[:, :, :]
              if False else na_rel_bias.rearrange("j r f -> (j r) f").rearrange("(p a) f -> p a f", a=4),
              reads=[], writes=[RBP], allow_slow_non_contiguous=True)
        P.dma(rb_t.ap().rearrange("(p a) f -> p a f", a=4), RBP[0:120, :, :], reads=[RBP], writes=[RBPAD])

    def psum_of(lst, st):
        b = PS[lst[st[0] % len(lst)]]
        st[0] += 1
        return b

    def emit_na(l):
        j = l // 2
        OT = carve(ARENA, 0, BF16, [128, KC, NT], "OT")
        OTb = [OT.sub("OT%d" % c) for c in range(KC)]
        QZ = [carve(ARENA, 24576 + i * 3072, BF16, [128, NT], "QZ%d" % i) for i in range(2)]
        KT = carve(ARENA, 30720, BF16, [128, NT], "KT")
        VT = carve(ARENA, 33792, BF16, [128, 12, 128], "VT")
        VTS = carve(ARENA, 36864, BF16, [128, 7, 128], "VTS")
        KCT = carve(ARENA, 38656, BF16, [128, 256], "KCT")
        VC = carve(ARENA, 39168, BF16, [128, 2, 128], "VC")
        PTC = carve(ARENA, 39680, BF16, [128, 2, 1024], "PTC")
        PTL = [carve(ARENA, 43776 + i * 512, BF16, [128, 4, 64], "PTL%d" % i) for i in range(2)]
        PTP = [carve(ARENA, 44800 + i * 1024, BF16, [128, 2, 256], "PTP%d" % i) for i in range(2)]
        HKR = carve(ARENA, 46848, BF16, [128, 14, 2, 64], "HKR")
        HKZ = carve(ARENA, 50432, BF16, [128, 14, 2, 64], "HKZ")
        CKS = TMPN[0].alias(TMPN[0].t[:, :].rearrange("p (a b) -> p a b", a=4))
        CVS = TMPN[1].alias(TMPN[1].t[:, :].rearrange("p (a b) -> p a b", a=4))
        RD = RSTD[0].alias(RSTD[0].t[:, :])
        KCS = RSTD[1].alias(RSTD[1].t[:, 0:256].rearrange("p (a b) -> p a b", a=2))
        WQ = [carve(WREG, i * 6144, BF16, [128, KC, 3, 128], "WQ%d" % i) for i in range(2)]
        lo = [0]
        hi = [0]
        LO = [0, 1, 2, 3]
        HI = [4, 5, 6, 7]
        P.add("pool", lambda e: e.memset(QZ[0][64:128, :], 0.0), writes=[QZ[0]])
        P.add("pool", lambda e: e.memset(QZ[1][0:64, :], 0.0), writes=[QZ[1]])
        P.add("pool", lambda e: e.memset(HKZ[:, :, :, :], 0.0), writes=[HKZ])
        ptl_rr = [0]
        ptp_rr = [0]
        wqs = WStream(WQ, KC, lambda i, wq: P.dma(wq[:, :, :, :].rearrange("p k g n -> p (k g n)"),
                                                  na_w_qkv[j, i, :, :], writes=[wq], q="pool"))
        for c in range(KC):
            wq = wqs.need(c)
            pump(3)
            for tb in range(NTB):
                sl = slice(tb * TB, (tb + 1) * TB)
                pb = psum_of(LO, lo)
                for kc in range(KC):
                    P.add("pe", lambda e, pb=pb, kc=kc, sl=sl: e.matmul(
                        pb[:, :], wq[:, kc, 0, :], H[:, kc, sl], start=(kc == 0), stop=(kc == KC - 1)),
                        reads=[wq, Hb[(kc, tb)]], writes=[pb])
                P.add("act", lambda e, pb=pb, sl=sl: e.activation(QZ[0][0:64, sl], pb[0:64, :], AF.Copy, scale=0.125),
                      reads=[pb], writes=[QZ[0]])
                P.add("act", lambda e, pb=pb, sl=sl: e.activation(QZ[1][64:128, sl], pb[64:128, :], AF.Copy, scale=0.125),
                      reads=[pb], writes=[QZ[1]])
                pb = psum_of(LO, lo)
                for kc in range(KC):
                    P.add("pe", lambda e, pb=pb, kc=kc, sl=sl: e.matmul(
                        pb[:, :], wq[:, kc, 1, :], H[:, kc, sl], start=(kc == 0), stop=(kc == KC - 1)),
                        reads=[wq, Hb[(kc, tb)]], writes=[pb])
                P.add("dve", lambda e, pb=pb, sl=sl: e.tensor_copy(KT[:, sl], pb[:, :]), reads=[pb], writes=[KT])
            for g in range(3):
                pb = psum_of(LO, lo)
                for b in range(4):
                    blk = g * 4 + b
                    for kc in range(KC):
                        P.add("pe", lambda e, pb=pb, kc=kc, b=b, blk=blk: e.matmul(
                            pb[:, b * 128:(b + 1) * 128], H[:, kc, blk * 128:(blk + 1) * 128], wq[:, kc, 2, :],
                            start=(kc == 0), stop=(kc == KC - 1)),
                            reads=[wq, Hb[(kc, blk // 4)]], writes=[pb])
                P.add("dve", lambda e, pb=pb, g=g: e.tensor_copy(
                    VT[:, g * 4:(g + 1) * 4, :], pb[:, :].rearrange("p (b f) -> p b f", b=4)),
                    reads=[pb], writes=[VT])
                if g == 0:
                    P.add("act", lambda e, pb=pb: e.copy(CVS[:, :, :], pb[:, :].rearrange("p (b f) -> p b f", b=4)),
                          reads=[pb], writes=[CVS])
                    for sq in range(2):
                        P.dma(cv_out[sq, j, :, c * 128:(c + 1) * 128].rearrange("(b p) f -> p b f", p=128),
                              CVS[:, sq * 2:sq * 2 + 2, :], reads=[CVS])
            pb = psum_of(LO, lo)
            for b in range(4):
                for kc in range(KC):
                    P.add("pe", lambda e, pb=pb, kc=kc, b=b: e.matmul(
                        pb[:, b * 128:(b + 1) * 128], H[:, kc, b * 128:(b + 1) * 128], wq[:, kc, 1, :],
                        start=(kc == 0), stop=(kc == KC - 1)), reads=[wq, Hb[(kc, 0)]], writes=[pb])
            P.add("act", lambda e, pb=pb: e.copy(CKS[:, :, :], pb[:, :].rearrange("p (b f) -> p b f", b=4)),
                  reads=[pb], writes=[CKS])
            for sq in range(2):
                P.dma(ck_out[sq, j, :, c * 128:(c + 1) * 128].rearrange("(b p) f -> p b f", p=128),
                      CKS[:, sq * 2:sq * 2 + 2, :], reads=[CKS])
            for g in range(2):
                pb = psum_of(LO, lo)
                nb = 4 if g == 0 else 3
                for b in range(nb):
                    m = g * 4 + b
                    t0 = 512 + 64 + 128 * m
                    for kc in range(KC):
                        P.add("pe", lambda e, pb=pb, kc=kc, b=b, t0=t0: e.matmul(
                            pb[:, b * 128:(b + 1) * 128], H[:, kc, t0:t0 + 128], wq[:, kc, 2, :],
                            start=(kc == 0), stop=(kc == KC - 1)),
                            reads=[wq, Hb[(kc, 1)], Hb[(kc, 2)]], writes=[pb])
                P.add("dve", lambda e, pb=pb, g=g, nb=nb: e.tensor_copy(
                    VTS[:, g * 4:g * 4 + nb, :], pb[:, 0:nb * 128].rearrange("p (b f) -> p b f", b=nb)),
                    reads=[pb], writes=[VTS])
            P.dma(KCS[:, :, :], cache_k[j, :, c * 128:(c + 1) * 128].rearrange("(b p) f -> p b f", p=128),
                  writes=[KCS])
            pb = psum_of(LO, lo)
            for b in range(2):
                P.add("pe", lambda e, pb=pb, b=b: e.transpose(pb[:, b * 128:(b + 1) * 128], KCS[:, b, :], CONST[:, 0:128]),
                      reads=[KCS, CONST], writes=[pb])
            P.add("act", lambda e, pb=pb: e.copy(KCT[:, :], pb[:, 0:256]), reads=[pb], writes=[KCT])
            P.dma(VC[:, :, :], cache_v[j, :, c * 128:(c + 1) * 128].rearrange("(b p) f -> p b f", p=128),
                  writes=[VC], q="pool")
            for sq in range(2):
                pbo = psum_of(HI, hi)
                for hh in range(2):
                    hs = slice(hh * 64, hh * 64 + 64)
                    ptp = PTP[ptp_rr[0] % 2]
                    ptp_rr[0] += 1
                    pbs = psum_of(LO, lo)
                    for kb in range(2):
                        P.add("pe", lambda e, pbs=pbs, kb=kb, hh=hh, sq=sq: e.matmul(
                            pbs[:, kb * 256:(kb + 1) * 256], KT[:, sq * 256 + kb * 128:sq * 256 + (kb + 1) * 128],
                            QZ[hh][:, sq * 256:(sq + 1) * 256], start=True, stop=True),
                            reads=[KT, QZ[hh]], writes=[pbs])
                    P.add("act", lambda e, pbs=pbs, ptp=ptp: e.activation(
                        ptp[:, :, :], pbs[:, :].rearrange("p (b q) -> p b q", b=2), AF.Exp),
                        reads=[pbs], writes=[ptp])
                    for kb in range(2):
                        P.add("pe", lambda e, kb=kb, hs=hs, ptp=ptp, sq=sq: e.matmul(
                            pbo[hs, 0:256], VT[:, sq * 2 + kb, hs], ptp[:, kb, :], start=(kb == 0), stop=(kb == 1)),
                            reads=[VT, ptp], writes=[pbo])
                    for kb in range(2):
                        P.add("pe", lambda e, kb=kb, hs=hs, ptp=ptp: e.matmul(
                            pbo[hs, 256:512], ONESB[:, 0:64], ptp[:, kb, :], start=(kb == 0), stop=(kb == 1)),
                            reads=[ONESB, ptp], writes=[pbo])
                P.add("act", lambda e, pbo=pbo: e.activation(RD[:, 0:256], pbo[:, 256:512], AF.Ln), reads=[pbo], writes=[RD])
                P.add("act", lambda e: e.activation(RD[:, 0:256], RD[:, 0:256], AF.Exp, scale=-1.0), reads=[RD], writes=[RD])
                P.add("dve", lambda e, pbo=pbo, sq=sq: e.tensor_tensor(
                    OT[:, c, sq * 256:(sq + 1) * 256], pbo[:, 0:256], RD[:, 0:256], ALU.mult),
                    reads=[pbo, RD], writes=[OTb[c]])
            pbo_s = [PS[4], PS[5]]
            pbd_s = [PS[6], PS[7]]
            for hh in range(2):
                h = 2 * c + hh
                hs = slice(hh * 64, hh * 64 + 64)
                for u in range(2):
                    src = bass.AP(rb_t, (j * 240 + h * 15 + u) * 127, [[1, 64], [127, 14], [1, 64]])
                    P.dma(HKR[0:64, :, u, :], src, reads=[RBPAD], writes=[HKR], q="pool")
                P.add("pool", lambda e: e.tensor_tensor(
                    HKZ[0:64, :, :, :].rearrange("p a u k -> p (a u) k"),
                    HKR[0:64, :, :, :].rearrange("p a u k -> p (a u) k"),
                    CONST[0:64, 192:256].unsqueeze(1).to_broadcast([64, 28, 64]), ALU.add),
                    reads=[HKR, CONST], writes=[HKZ])
                for kb in range(2):
                    for qb in range(2):
                        pbs = psum_of(LO, lo)
                        P.add("pe", lambda e, pbs=pbs, kb=kb, qb=qb, hh=hh: e.matmul(
                            pbs[:, :], KCT[:, kb * 128:(kb + 1) * 128], QZ[hh][:, 512 + qb * 512:512 + (qb + 1) * 512],
                            start=True, stop=True), reads=[KCT, QZ[hh]], writes=[pbs])
                        P.add("act", lambda e, pbs=pbs, kb=kb, qb=qb: e.activation(
                            PTC[:, kb, qb * 512:(qb + 1) * 512], pbs[:, :], AF.Exp), reads=[pbs], writes=[PTC])
                def row_scores(r):
                    rs = min(max(r - 4, 0), 8)
                    ptl = PTL[ptl_rr[0] % 2]
                    ptl_rr[0] += 1
                    pbs = psum_of(LO, lo)
                    qsl = slice(512 + 64 * r, 512 + 64 * r + 64)
                    for i in range(4):
                        kr = rs + 2 * i
                        ri = kr - r + 7
                        k0 = 512 + 64 * kr
                        P.add("pe", lambda e: e.matmul(
                            pbs[:, i * 64:(i + 1) * 64], KT[:, k0:k0 + 128], QZ[hh][:, qsl], start=True, stop=False),
                            reads=[KT, QZ[hh]], writes=[pbs])
                        P.add("pe", lambda e: e.matmul(
                            pbs[:, i * 64:(i + 1) * 64], HKZ[:, ri, :, :].rearrange("p u k -> p (u k)"), J2B[:, :],
                            start=False, stop=True), reads=[HKZ, J2B], writes=[pbs])
                    P.add("act", lambda e: e.activation(
                        ptl[:, :, :], pbs[:, 0:256].rearrange("p (i q) -> p i q", i=4), AF.Exp),
                        reads=[pbs], writes=[ptl])
                    return ptl

                def row_pv(r, ptl):
                    rs = min(max(r - 4, 0), 8)
                    pbo = pbo_s[r // 8]
                    pbd = pbd_s[r // 8]
                    osl = slice((r % 8) * 64, (r % 8) * 64 + 64)
                    for pbx, isden in ((pbo, False), (pbd, True)):
                        for i in range(4):
                            kr = rs + 2 * i
                            if kr % 2 == 0:
                                vv = VT[:, 4 + kr // 2, hs]
                                vb = VT
                            else:
                                vv = VTS[:, (kr - 1) // 2, hs]
                                vb = VTS
                            lhs = ONESB[:, 0:64] if isden else vv
                            P.add("pe", lambda e: e.matmul(
                                pbx[hs, osl], lhs, ptl[:, i, :], start=(i == 0), stop=False),
                                reads=[vb, ONESB, ptl], writes=[pbx])
                        for kb in range(2):
                            lhs = ONESB[:, 0:64] if isden else VC[:, kb, hs]
                            P.add("pe", lambda e: e.matmul(
                                pbx[hs, osl], lhs, PTC[:, kb, 64 * r:64 * r + 64], start=False, stop=(kb == 1)),
                                reads=[VC, ONESB, PTC], writes=[pbx])

                ptl_prev = row_scores(0)
                for r in range(1, 16):
                    ptl_cur = row_scores(r)
                    row_pv(r - 1, ptl_prev)
                    ptl_prev = ptl_cur
                row_pv(15, ptl_prev)
            for half in range(2):
                P.add("act", lambda e, half=half: e.activation(RD[:, :], pbd_s[half][:, :], AF.Ln),
                      reads=[pbd_s[half]], writes=[RD])
                P.add("act", lambda e: e.activation(RD[:, :], RD[:, :], AF.Exp, scale=-1.0), reads=[RD], writes=[RD])
                P.add("dve", lambda e, half=half: e.tensor_tensor(
                    OT[:, c, 512 + half * 512:512 + (half + 1) * 512], pbo_s[half][:, :], RD[:, :], ALU.mult),
                    reads=[pbo_s[half], RD], writes=[OTb[c]])
        P.fence()
        emit_wout(l, na_w_out[j], OT, OTb)


    C_I = CONST[:, 0:128]
    C_TRIF = CONST[:, 256:384]
    C_TRIB = CONST[:, 384:512]
    C_EVEN = CONST[:, 512:640]
    C_ODD = CONST[:, 640:768]
    C_MINC = [CONST[:, 768:896], CONST[:, 1024:1152]]
    C_MSTR = [CONST[:, 896:1024], CONST[:, 1152:1280]]
    C_ONES = CONST[:, 1280:1408]
    C_NEG1 = CONST[:, 1408:1536]
    GCW = P.sb("GCW", [128, 2, 3, 24], F32)
    load_fm(GCW, GCW[:, :, :, :].rearrange("p j t k -> p (j t k)"),
            gdn_conv_w.rearrange("j t (k f) -> (j t k) f", f=128), 144)
    GNG = P.sb("GNG", [128, 2], F32)
    load_fm(GNG, GNG[:, :], gdn_norm_g, 2)
    ZEROC = P.sb("ZEROC", [128, 1], F32)
    P.add("dve", lambda e: e.memset(ZEROC[:, :], 0.0), writes=[ZEROC])
    ev_rr = [0]

    def evac(dst_ap, dst_bufs, src_ap, pbs, scale=None):
        use_act = True
        ev_rr[0] += 1
        if use_act:
            if scale is None:
                P.add("act", lambda e: e.copy(dst_ap, src_ap), reads=pbs, writes=dst_bufs)
            else:
                P.add("act", lambda e: e.activation(dst_ap, src_ap, AF.Copy, scale=scale), reads=pbs, writes=dst_bufs)
        else:
            if scale is None:
                P.add("dve", lambda e: e.tensor_copy(dst_ap, src_ap), reads=pbs, writes=dst_bufs)
            else:
                P.add("dve", lambda e: e.tensor_scalar(dst_ap, src_ap, scale, None, ALU.mult), reads=pbs, writes=dst_bufs)

    def emit_gdn(l):
        j = l // 2
        off = [0]

        def AR(dtype, shape, name):
            esz = 2 if dtype == BF16 else 4
            n = esz
            for d_ in shape[1:]:
                n *= d_
            v = carve(ARENA, off[0], dtype, shape, name)
            off[0] += (n + 63) // 64 * 64
            assert off[0] <= 55296, off[0]
            return v

        TK = [AR(F32, [128, 12, 16], "TK%d" % i) for i in range(12)]
        GTOK, GC, BETA, GCB, EGC, BEG, GLE, GLO, EDE, EDO, TMPA, TMPB = TK
        AB = P.view("AB", ARENA.t[:, (10 * 768) // 4:(12 * 768) // 4].rearrange("p (b c) -> p b c", b=12))
        QNb = AR(BF16, [128, NT], "QNb")
        KNb = AR(BF16, [128, NT], "KNb")
        OTh = AR(BF16, [128, NT], "OTh")
        BS = []
        BSR = []
        for d_ in range(2):
            row, rowr = [], []
            for i in range(3):
                k_ = d_ * 3 + i
                apr = CHAIN[:, k_ * 512:(k_ + 1) * 512].rearrange("p (a b) -> p a b", a=4)
                apf = CHAIN.bitcast(F32)[:, k_ * 512:(k_ + 1) * 512].rearrange("p (a b) -> p a b", a=4)
                row.append(P.view("BS%d_%d" % (d_, i), apf))
                rowr.append(apr)
            BS.append(row)
            BSR.append(rowr)
        TTbs = [AR(BF16, [128, 4, 128], "TTb%d" % d_) for d_ in range(2)]
        VBs = [AR(BF16, [128, 4, 128], "VB%d" % d_) for d_ in range(2)]
        KBGs = [AR(BF16, [128, 4, 128], "KBG%d" % d_) for d_ in range(2)]
        F = [b_.alias(b_.t[:, :].rearrange("p (a b) -> p a b", a=4)) for b_ in (RSTD[0], RSTD[1], TMPN[0], TMPN[1])]
        TC = F
        SETS = []
        for i in range(4):
            SETS.append(dict(U=AR(BF16, [128, 4, 128], "U%d" % i), NWT=AR(BF16, [128, 4, 128], "NWT%d" % i),
                             PT=AR(BF16, [128, 4, 128], "PT%d" % i), KD=[AR(BF16, [128, 4, 128], "KDe%d" % i),
                                                                        AR(BF16, [128, 4, 128], "KDo%d" % i)],
                             QG=AR(BF16, [128, 4, 128], "QG%d" % i)))
        CH = []
        for i in range(4):
            CH.append(dict(S=AR(F32, [128, 128], "S%d" % i), Sb=AR(BF16, [128, 128], "Sb%d" % i),
                           VN=AR(BF16, [128, 128], "VN%d" % i)))
        WAB = AR(BF16, [128, KC, 32], "WAB")
        DTB16 = AR(F32, [128, 16], "DTB16")
        NEGA = AR(F32, [128, 16], "NEGA")
        WI = [carve(WREG, 4096 + i * 4096, BF16, [128, KC, 2, 128], "WI%d" % i) for i in range(2)]
        WOH = [carve(WREG, i * 2048, BF16, [128, D], "WOH%d" % i) for i in range(2)]
        CTQ, CTK, CTV, CTZ = CT
        OF = CTQ
        OFb = [OF.sub("OF%d" % b) for b in range(12)]
        for ch in CH:
            P.add("pool", lambda e, ch=ch: e.memset(ch["VN"][:, :], 0.0), writes=[ch["VN"]])

        P.dma(WAB[:, :, :].rearrange("p k n -> p (k n)"), gdn_w_ab[j, :, :], writes=[WAB], q="pool")
        P.dma(DTB16[:, :], gdn_dt_bias[j:j + 1, :].to_broadcast([128, 16]), writes=[DTB16])
        P.dma(NEGA[:, :], gdn_a_log[j:j + 1, :].to_broadcast([128, 16]), writes=[NEGA])
        P.add("act", lambda e: e.activation(NEGA[:, :], NEGA[:, :], AF.Exp), reads=[NEGA], writes=[NEGA])
        pb = psum()
        for blk in range(12):
            for kc in range(KC):
                P.add("pe", lambda e, pb=pb, blk=blk, kc=kc: e.matmul(
                    pb[:, blk * 32:(blk + 1) * 32], H[:, kc, blk * 128:(blk + 1) * 128], WAB[:, kc, :],
                    start=(kc == 0), stop=(kc == KC - 1)), reads=[WAB, Hb[(kc, blk // 4)]], writes=[pb])
        P.add("dve", lambda e, pb=pb: e.tensor_copy(AB[:, :, :], pb[:, 0:384].rearrange("p (b c) -> p b c", b=12)),
              reads=[pb], writes=[TMPA, TMPB])
        bc16 = lambda t: t[:, :].unsqueeze(1).to_broadcast([128, 12, 16])
        P.add("dve", lambda e: e.tensor_tensor(GTOK[:, :, :], AB[:, :, 0:16], bc16(DTB16), ALU.add),
              reads=[TMPA, TMPB, DTB16], writes=[GTOK])
        P.add("act", lambda e: e.activation(GTOK[:, :, :], GTOK[:, :, :], AF.Exp), reads=[GTOK], writes=[GTOK])
        P.add("dve", lambda e: e.tensor_scalar(GTOK[:, :, :], GTOK[:, :, :], 1.0, None, ALU.add), reads=[GTOK], writes=[GTOK])
        P.add("act", lambda e: e.activation(GTOK[:, :, :], GTOK[:, :, :], AF.Ln), reads=[GTOK], writes=[GTOK])
        P.add("dve", lambda e: e.scalar_tensor_tensor(GTOK[:, :, :], GTOK[:, :, :], -1.0, bc16(NEGA), ALU.mult, ALU.mult),
              reads=[GTOK, NEGA], writes=[GTOK])
        P.add("act", lambda e: e.activation(BETA[:, :, :], AB[:, :, 16:32], AF.Exp, scale=-1.0),
              reads=[TMPA, TMPB], writes=[BETA])
        P.add("dve", lambda e: e.tensor_scalar(BETA[:, :, :], BETA[:, :, :], 1.0, None, ALU.add), reads=[BETA], writes=[BETA])
        P.add("act", lambda e: e.activation(GCB[:, :, :], BETA[:, :, :], AF.Ln), reads=[BETA], writes=[GCB])
        P.add("dve", lambda e: e.reciprocal(BETA[:, :, :], BETA[:, :, :]), reads=[BETA], writes=[BETA])
        pb = psum()
        for blk in range(12):
            for d in range(2):
                tri = C_TRIF if d == 0 else C_TRIB
                P.add("pe", lambda e, pb=pb, blk=blk, d=d, tri=tri: e.matmul(
                    pb[:, blk * 16 + d * 8:blk * 16 + d * 8 + 8], tri, GTOK[:, blk, d * 8:d * 8 + 8],
                    start=True, stop=True), reads=[CONST, GTOK], writes=[pb])
        P.add("dve", lambda e, pb=pb: e.tensor_copy(GC[:, :, :], pb[:, 0:192].rearrange("p (b c) -> p b c", b=12)),
              reads=[pb], writes=[GC])
        for (dst, cm) in ((GLE, C_EVEN), (GLO, C_ODD)):
            pb = psum()
            for blk in range(12):
                P.add("pe", lambda e, pb=pb, blk=blk, cm=cm: e.matmul(
                    pb[:, blk * 16:(blk + 1) * 16], cm, GTOK[:, blk, :], start=True, stop=True),
                    reads=[CONST, GTOK], writes=[pb])
            P.add("dve", lambda e, pb=pb, dst=dst: e.tensor_copy(
                dst[:, :, :], pb[:, 0:192].rearrange("p (b c) -> p b c", b=12)), reads=[pb], writes=[dst])
        P.add("dve", lambda e: e.tensor_tensor(GCB[:, :, :], GC[:, :, :], GCB[:, :, :], ALU.subtract),
              reads=[GC, GCB], writes=[GCB])
        P.add("act", lambda e: e.activation(EGC[:, :, :], GC[:, :, :], AF.Exp), reads=[GC], writes=[EGC])
        P.add("dve", lambda e: e.tensor_tensor(BEG[:, :, :], BETA[:, :, :], EGC[:, :, :], ALU.mult),
              reads=[BETA, EGC], writes=[BEG])
        for (dst, gl, mcol) in ((EDE, GLE, C_EVEN[:, 0:1]), (EDO, GLO, C_ODD[:, 0:1])):
            P.add("dve", lambda e, dst=dst, gl=gl: e.tensor_tensor(dst[:, :, :], gl[:, :, :], GC[:, :, :], ALU.subtract),
                  reads=[gl, GC], writes=[dst])
            P.add("dve", lambda e, dst=dst: e.tensor_scalar(dst[:, :, :], dst[:, :, :], 0.0, None, ALU.min),
                  reads=[dst], writes=[dst])
            P.add("act", lambda e, dst=dst: e.activation(dst[:, :, :], dst[:, :, :], AF.Exp), reads=[dst], writes=[dst])
            P.add("dve", lambda e, dst=dst, mcol=mcol: e.tensor_scalar(dst[:, :, :], dst[:, :, :], mcol, None, ALU.mult),
                  reads=[dst, CONST], writes=[dst])
        P.add("act", lambda e: e.activation(GLE[:, :, :], GLE[:, :, :], AF.Exp), reads=[GLE], writes=[GLE])
        P.add("act", lambda e: e.activation(GLO[:, :, :], GLO[:, :, :], AF.Exp), reads=[GLO], writes=[GLO])
        EGL = [GLE, GLO]
        ED = [EDE, EDO]
        NGC = TMPA
        P.add("dve", lambda e: e.tensor_scalar(NGC[:, :, :], GC[:, :, :], -1.0, None, ALU.mult), reads=[GC], writes=[NGC])

        wis_ = WStream(WI, 16, lambda i, wi: P.dma(wi[:, :, :, :].rearrange("p k g n -> p (k g n)"),
                                                   gdn_w_in[j, i // 2, i % 2, :, :], writes=[wi], q="pool"))
        whs_ = WStream(WOH, 8, lambda i, woh: P.dma(woh[:, :], gdn_w_out[j, i * 128:(i + 1) * 128, :],
                                                    writes=[woh], q="pool"))
        PRE_BANKS, SCAN_BANKS = [0, 1, 2, 3, 4], [5, 6, 7]
        pre_rr, scan_rr = [0], [0]
        for h in range(8):
            pump(3)
            for pi in range(2):
                wi = wis_.need(h * 2 + pi)
                for g in range(2):
                    t = pi * 2 + g
                    b0 = bank_group()
                    for tb in range(NTB):
                        pbk = PS[b0 + tb]
                        for kc in range(KC):
                            P.add("pe", lambda e, pbk=pbk, wi=wi, kc=kc, g=g, tb=tb: e.matmul(
                                pbk[:, :], wi[:, kc, g, :], H[:, kc, tb * TB:(tb + 1) * TB],
                                start=(kc == 0), stop=(kc == KC - 1)), reads=[wi, Hb[(kc, tb)]], writes=[pbk])
                    ct = CT[t]
                    if t < 3:
                        ch = t * 8 + h
                        conv_from_psum(b0, GCW[:, j, 0, ch:ch + 1], GCW[:, j, 1, ch:ch + 1], GCW[:, j, 2, ch:ch + 1],
                                       ZEROC[:, 0:1], ct, ct)
                        P.add("act", lambda e, ct=ct: e.activation(ct[:, :], ct[:, :], AF.Silu), reads=[ct], writes=[ct])
                    else:
                        P.add("act", lambda e, ct=ct, b0=b0: e.activation(ct[:, 0:512], PS[b0][:, :], AF.Silu),
                              reads=[PS[b0]], writes=[ct])
                        P.add("act", lambda e, ct=ct, b0=b0: e.activation(
                            ct[:, 512:NT], PSALL[:, (b0 + 1) * 512:(b0 + 3) * 512], AF.Silu),
                            reads=[PS[b0 + 1], PS[b0 + 2]], writes=[ct])
            for (ct, dstb, sc, keep) in ((CTQ, QNb, 128.0 ** -0.5, False), (CTK, KNb, 1.0, True)):
                for tb in range(NTB):
                    sl = slice(tb * TB, (tb + 1) * TB)
                    sqf = TC[tb % 2]
                    rs = TC[2 + tb % 2]
                    sqv = sqf[:, :, :].rearrange("p a b -> p (a b)")
                    rsv = rs[:, :, :].rearrange("p a b -> p (a b)")
                    P.add("act", lambda e, ct=ct, sl=sl, sqv=sqv: e.activation(sqv, ct[:, sl], AF.Square),
                          reads=[ct], writes=[sqf])
                    pbk = psum()
                    P.add("pe", lambda e, pbk=pbk, sqv=sqv: e.matmul(pbk[:, :], C_ONES, sqv, start=True, stop=True),
                          reads=[CONST, sqf], writes=[pbk])
                    P.add("act", lambda e, pbk=pbk, rsv=rsv: e.activation(rsv, pbk[:, :], AF.Ln, bias=EPSC[:, 0:1], scale=1.0),
                          reads=[pbk, EPSC], writes=[rs])
                    P.add("act", lambda e, rsv=rsv: e.activation(rsv, rsv, AF.Exp, scale=-0.5), reads=[rs], writes=[rs])
                    if keep:
                        P.add("pool", lambda e, ct=ct, sl=sl, rsv=rsv: e.tensor_tensor(ct[:, sl], ct[:, sl], rsv, ALU.mult),
                              reads=[ct, rs], writes=[ct])
                        P.add("act", lambda e, ct=ct, sl=sl, dstb=dstb: e.copy(dstb[:, sl], ct[:, sl]),
                              reads=[ct], writes=[dstb])
                    else:
                        P.add("pool", lambda e, ct=ct, sl=sl, rsv=rsv, dstb=dstb: e.tensor_tensor(
                            dstb[:, sl], ct[:, sl], rsv, ALU.mult), reads=[ct, rs], writes=[dstb])
            P.add("pool", lambda e: e.memset(OF[:, :], 0.0), writes=[OF] + OFb)

            f4 = lambda T_: T_[:, :, :].rearrange("p a b -> p (a b)")
            v4 = lambda pb_: pb_[:, :].rearrange("p (b f) -> p b f", b=4)
            Ibc = C_I.unsqueeze(1).to_broadcast([128, 4, 128])

            def pre_dir(g, d, st, pk, pv, pg, pq):
                cd = d * 8 + h
                blks = slice(g * 4, g * 4 + 4)
                tsl = slice(g * 512, (g + 1) * 512)
                col = lambda T_: T_[:, blks, cd:cd + 1].to_broadcast([128, 4, 128])
                B = BS[d]
                Fa, Fb = F[2 * d], F[2 * d + 1]
                VB, KBG = VBs[d], KBGs[d]
                P.add("dve", lambda e: e.tensor_tensor(VB[:, :, :], v4(pv), col(BETA), ALU.mult),
                      reads=[pv, BETA], writes=[VB])
                P.add("dve", lambda e: e.tensor_tensor(KBG[:, :, :], v4(pk), col(BEG), ALU.mult),
                      reads=[pk, BEG], writes=[KBG])
                for u in range(2):
                    P.add("dve", lambda e: e.tensor_tensor(st["KD"][u][:, :, :], v4(pk), col(ED[u]), ALU.mult),
                          reads=[pk, ED[u]], writes=[st["KD"][u]])
                yield
                P.add("pool", lambda e: e.tensor_tensor(Fa[:, :, :], Ibc, col(EGC), ALU.mult),
                      reads=[CONST, EGC], writes=[Fa])
                pr = psum_of(PRE_BANKS, pre_rr)
                for b in range(4):
                    P.add("pe", lambda e: e.matmul(pr[:, b * 128:(b + 1) * 128], C_ONES, Fa[:, b, :],
                                                   start=True, stop=True), reads=[CONST, Fa], writes=[pr])
                P.add("dve", lambda e: e.scalar_tensor_tensor(f4(st["QG"]), pr[:, :], 128.0 ** -0.5, QNb[:, tsl],
                                                              ALU.mult, ALU.mult), reads=[pr, QNb], writes=[st["QG"]])
                yield
                P.add("pool", lambda e: e.tensor_tensor(Fa[:, :, :], Ibc, col(GC), ALU.mult),
                      reads=[CONST, GC], writes=[Fa])
                P.add("pool", lambda e: e.tensor_tensor(Fb[:, :, :], Ibc, col(GCB), ALU.mult),
                      reads=[CONST, GCB], writes=[Fb])
                pa = psum_of(PRE_BANKS, pre_rr)
                pbb = psum_of(PRE_BANKS, pre_rr)
                for (pz, dg, msk) in ((pa, Fa, C_MINC[d]), (pbb, Fb, C_MSTR[d])):
                    for b in range(4):
                        o_ = pz[:, b * 128:(b + 1) * 128]
                        P.add("pe", lambda e: e.matmul(o_, C_ONES, dg[:, b, :], start=True, stop=False),
                              reads=[CONST, dg], writes=[pz])
                        P.add("pe", lambda e: e.matmul(o_, C_I, msk, start=False, stop=True),
                              reads=[CONST], writes=[pz])
                BT, Bs_, M_ = B
                BTr, Bsr, Mr = [x_.t for x_ in B]
                r4 = lambda ap_: ap_[:, :, :].rearrange("p a b -> p (a b)")
                for b in range(4):
                    ngc = NGC[:, g * 4 + b, cd:cd + 1]
                    P.add("act", lambda e: e.activation(Mr[:, b, :], pa[:, b * 128:(b + 1) * 128], AF.Exp, bias=ngc),
                          reads=[pa, NGC], writes=[M_])
                    P.add("act", lambda e: e.activation(BTr[:, b, :], pbb[:, b * 128:(b + 1) * 128], AF.Exp, bias=ngc),
                          reads=[pbb, NGC], writes=[BT])
                yield
                pg = psum_of(PRE_BANKS, pre_rr)
                pq = psum_of(PRE_BANKS, pre_rr)
                for b in range(4):
                    c0 = g * 512 + b * 128
                    P.add("pe", lambda e: e.matmul(pg[:, b * 128:(b + 1) * 128], KNb[:, c0:c0 + 128], KNb[:, c0:c0 + 128],
                                                   start=True, stop=True), reads=[KNb], writes=[pg])
                for b in range(4):
                    c0 = g * 512 + b * 128
                    P.add("pe", lambda e: e.matmul(pq[:, b * 128:(b + 1) * 128], KNb[:, c0:c0 + 128], QNb[:, c0:c0 + 128],
                                                   start=True, stop=True), reads=[KNb, QNb], writes=[pq])
                P.add("dve", lambda e: e.scalar_tensor_tensor(f4(st["PT"]), pq[:, :], 128.0 ** -0.5, f4(M_),
                                                              ALU.mult, ALU.mult), reads=[pq, M_], writes=[st["PT"]])
                P.add("dve", lambda e: e.scalar_tensor_tensor(r4(BTr), pg[:, :], -1.0, f4(BT), ALU.mult, ALU.mult),
                      reads=[pg, BT], writes=[BT])
                yield
                pt_ = psum_of(PRE_BANKS, pre_rr)
                for b in range(4):
                    P.add("pe", lambda e: e.transpose(pt_[:, b * 128:(b + 1) * 128], BT[:, b, :], C_I),
                          reads=[BT, CONST], writes=[pt_])
                evac(r4(Bsr), [Bs_], pt_[:, :], [pt_])
                P.add("pool", lambda e: e.tensor_tensor(Mr[:, :, :], BT[:, :, :], Ibc, ALU.add),
                      reads=[BT, CONST], writes=[M_])
                yield
                TTb = TTbs[d]
                for k in range(5):
                    if k < 4:
                        px = psum_of(PRE_BANKS, pre_rr)
                        for b in range(4):
                            P.add("pe", lambda e: e.matmul(px[:, b * 128:(b + 1) * 128], Bsr[:, b, :], BTr[:, b, :],
                                                           start=True, stop=True), reads=[Bs_, BT], writes=[px])
                    py = psum_of(PRE_BANKS, pre_rr)
                    for b in range(4):
                        P.add("pe", lambda e: e.matmul(py[:, b * 128:(b + 1) * 128], BTr[:, b, :], Bsr[:, b, :],
                                                       start=True, stop=True), reads=[Bs_, BT], writes=[py])
                    if k < 4:
                        evac(r4(BTr), [BT], px[:, :], [px])
                    evac(r4(Bsr), [Bs_], py[:, :], [py])
                    yield
                    pm_ = psum_of(PRE_BANKS, pre_rr)
                    for b in range(4):
                        o_ = pm_[:, b * 128:(b + 1) * 128]
                        P.add("pe", lambda e: e.matmul(o_, Bsr[:, b, :], Mr[:, b, :], start=True, stop=True),
                              reads=[Bs_, M_], writes=[pm_])
                    if k < 4:
                        P.add("dve", lambda e: e.tensor_tensor(r4(Mr), pm_[:, :], f4(M_), ALU.add),
                              reads=[pm_, M_], writes=[M_])
                    else:
                        P.add("dve", lambda e: e.tensor_tensor(f4(TTb), pm_[:, :], f4(M_), ALU.add),
                              reads=[pm_, M_], writes=[TTb])
                    yield
                pu = psum_of(PRE_BANKS, pre_rr)
                pw = psum_of(PRE_BANKS, pre_rr)
                for b in range(4):
                    P.add("pe", lambda e: e.matmul(pu[:, b * 128:(b + 1) * 128], TTb[:, b, :], VB[:, b, :],
                                                   start=True, stop=True), reads=[TTb, VB], writes=[pu])
                for b in range(4):
                    P.add("pe", lambda e: e.matmul(pw[:, b * 128:(b + 1) * 128], KBG[:, b, :], TTb[:, b, :],
                                                   start=True, stop=True), reads=[TTb, KBG], writes=[pw])
                evac(f4(st["U"]), [st["U"]], pu[:, :], [pu])
                evac(f4(st["NWT"]), [st["NWT"]], pw[:, :], [pw], scale=-1.0)
                yield

            def precompute2_gen(g, sts):
                pk = psum_of(PRE_BANKS, pre_rr)
                pv = psum_of(PRE_BANKS, pre_rr)
                for b in range(4):
                    c0 = g * 512 + b * 128
                    P.add("pe", lambda e: e.transpose(pk[:, b * 128:(b + 1) * 128], CTK[:, c0:c0 + 128], C_I),
                          reads=[CTK, CONST], writes=[pk])
                for b in range(4):
                    c0 = g * 512 + b * 128
                    P.add("pe", lambda e: e.transpose(pv[:, b * 128:(b + 1) * 128], CTV[:, c0:c0 + 128], C_I),
                          reads=[CTV, CONST], writes=[pv])
                gens = [pre_dir(g, d_, sts[d_], pk, pv, None, None) for d_ in range(2)]
                while gens:
                    for gi in list(gens):
                        try:
                            next(gi)
                        except StopIteration:
                            gens.remove(gi)
                    yield

            def interleave(gen, rounds, every):
                i = 0
                r = 0
                for _ in gen:
                    i += 1
                    if i % every == 0 and r < len(rounds):
                        rounds[r]()
                        r += 1
                while r < len(rounds):
                    rounds[r]()
                    r += 1

            def scan_step(ch, st, d, blk, u):
                cd = d * 8 + h
                bi = blk % 4
                rsl = slice(u * 64, u * 64 + 64)
                S, Sb, VN = ch["S"], ch["Sb"], ch["VN"]
                p1 = psum_of(SCAN_BANKS, scan_rr)
                P.add("pe", lambda e: e.matmul(p1[rsl, 0:128], st["NWT"][:, bi, rsl], Sb[:, :], start=True, stop=True),
                      reads=[st["NWT"], Sb], writes=[p1])
                P.add("dve", lambda e: e.tensor_tensor(VN[rsl, :], st["U"][rsl, bi, :], p1[rsl, 0:128], ALU.add),
                      reads=[st["U"], p1], writes=[VN])
                p2 = psum_of(SCAN_BANKS, scan_rr)
                P.add("pe", lambda e: e.matmul(p2[:, 0:64], Sb[:, :], st["QG"][:, bi, rsl], start=True, stop=False),
                      reads=[Sb, st["QG"]], writes=[p2])
                P.add("pe", lambda e: e.matmul(p2[:, 0:64], VN[:, :], st["PT"][:, bi, rsl], start=False, stop=True),
                      reads=[VN, st["PT"]], writes=[p2])
                c0 = blk * 128 + u * 64
                P.add("dve", lambda e: e.tensor_tensor(OF[:, c0:c0 + 64], OF[:, c0:c0 + 64], p2[:, 0:64], ALU.add),
                      reads=[OFb[blk], p2], writes=[OFb[blk]])
                p3 = psum_of(SCAN_BANKS, scan_rr)
                P.add("pe", lambda e: e.matmul(p3[:, 0:128], st["KD"][u][:, bi, :], VN[:, :], start=True, stop=True),
                      reads=[st["KD"][u], VN], writes=[p3])
                P.add("dve", lambda e: e.scalar_tensor_tensor(Sb[:, :], S[:, :], EGL[u][:, blk, cd:cd + 1], p3[:, 0:128],
                                                              ALU.mult, ALU.add), reads=[S, EGL[u], p3], writes=[Sb])
                P.add("dve", lambda e: e.scalar_tensor_tensor(S[:, :], S[:, :], EGL[u][:, blk, cd:cd + 1], p3[:, 0:128],
                                                              ALU.mult, ALU.add), reads=[S, EGL[u], p3], writes=[S])

            def chain_steps(d, blocks):
                steps = [(b, u) for b in blocks for u in range(2)]
                return steps if d == 0 else steps[::-1]

            def init_chain(ch, d, seq):
                if seq < 2:
                    P.add("pool", lambda e: e.memset(ch["S"][:, :], 0.0), writes=[ch["S"]])
                else:
                    P.dma(ch["S"][:, :], state_gdn[j, d, h, :, :], writes=[ch["S"]])
                P.add("act", lambda e: e.copy(ch["Sb"][:, :], ch["S"][:, :]), reads=[ch["S"]], writes=[ch["Sb"]])

            for _ in precompute2_gen(0, [SETS[2], SETS[3]]):
                pass
            chains = []
            for d in range(2):
                for seq in range(2):
                    ch = CH[d * 2 + seq]
                    init_chain(ch, d, seq)
                    chains.append((ch, SETS[2 + d], d, seq, chain_steps(d, [2 * seq, 2 * seq + 1])))

            def roundA(si):
                for (ch, st, d, seq, steps) in chains:
                    scan_step(ch, st, d, steps[si][0], steps[si][1])

            interleave(precompute2_gen(1, [SETS[0], SETS[1]]), [lambda si=si: roundA(si) for si in range(4)], 3)
            for (ch, st, d, seq, steps) in chains:
                P.dma(st_out[seq, j, d, h, :, :], ch["S"][:, :], reads=[ch["S"]])
            chB = [CH[0], CH[1]]
            stepsB = [chain_steps(d, list(range(4, 12))) for d in range(2)]
            for d in range(2):
                init_chain(chB[d], d, 2)

            def stepB(d, si):
                blk, u = stepsB[d][si]
                scan_step(chB[d], SETS[(blk // 4 - 1) * 2 + d], d, blk, u)

            interleave(precompute2_gen(2, [SETS[2], SETS[3]]), [lambda si=si: stepB(0, si) for si in range(8)], 2)
            for si in range(16):
                stepB(1, si)
                if si < 8:
                    stepB(0, 8 + si)

            for tb in range(NTB):
                sl = slice(tb * TB, (tb + 1) * TB)
                sqf = TC[tb % 2]
                rs = TC[2 + tb % 2]
                sqv = sqf[:, :, :].rearrange("p a b -> p (a b)")
                rsv = rs[:, :, :].rearrange("p a b -> p (a b)")
                P.add("act", lambda e, sl=sl, sqv=sqv: e.activation(sqv, OF[:, sl], AF.Square),
                      reads=OFb[tb * 4:tb * 4 + 4] + [OF], writes=[sqf])
                pbk = psum()
                P.add("pe", lambda e, pbk=pbk, sqv=sqv: e.matmul(pbk[:, :], C_ONES, sqv, start=True, stop=True),
                      reads=[CONST, sqf], writes=[pbk])
                P.add("act", lambda e, pbk=pbk, rsv=rsv: e.activation(rsv, pbk[:, :], AF.Ln, bias=EPSC[:, 0:1], scale=1.0 / 128),
                      reads=[pbk, EPSC], writes=[rs])
                P.add("act", lambda e, rsv=rsv: e.activation(rsv, rsv, AF.Exp, scale=-0.5), reads=[rs], writes=[rs])
                P.add("pool", lambda e, sl=sl, rsv=rsv: e.tensor_tensor(rsv, OF[:, sl], rsv, ALU.mult),
                      reads=OFb[tb * 4:tb * 4 + 4] + [rs, OF], writes=[rs])
                P.add("dve", lambda e, sl=sl, rsv=rsv: e.scalar_tensor_tensor(
                    OTh[:, sl], rsv, GNG[:, j:j + 1], CTZ[:, sl], ALU.mult, ALU.mult), reads=[rs, GNG, CTZ], writes=[OTh])
            woh = whs_.need(h)
            for n in range(KC):
                for tb in range(NTB):
                    pbk = psum()
                    P.add("pe", lambda e, pbk=pbk, n=n, tb=tb, woh=woh: e.matmul(
                        pbk[:, :], woh[:, n * 128:(n + 1) * 128], OTh[:, tb * TB:(tb + 1) * TB], start=True, stop=True),
                        reads=[woh, OTh], writes=[pbk])
                    resid_from_psum(pbk, l, 16, n, tb)

    P.fence()
    emit_rbpad()
    mod_state["gen"] = mod_gen(0)
    pump(24)
    for l in range(cfg.n_layers):
        emit_norm_mod(l, 1)
        P.fence()
        if l % 2 == 1 and cfg.do_na:
            emit_na(l)
        if l % 2 == 0 and cfg.do_gdn:
            emit_gdn(l)
        pump(48)
        emit_norm_mod(l, 2)
        P.fence()
        if l + 1 < cfg.n_layers:
            mod_state["gen"] = mod_gen(l + 1)
        emit_ffn(l)
        pump(48)
    P.fence()

    FG = carve(ARENA, 8192, F32, [128, D], "FG")
    P.dma(FG[:, :], final_g.rearrange("(o d) -> o d", o=1).to_broadcast([128, D]), writes=[FG])
    YT = [carve(ARENA, i * 4096, F32, [128, D], "YT%d" % i) for i in range(2)]
    YSQ = carve(ARENA, 12288, F32, [128, D], "YSQ")
    SS = [P.sb("SS%d" % i, [128, 1], F32) for i in range(2)]
    for blk in range(NT // 128):
        yt = YT[blk % 2]
        ss = SS[blk % 2]
        tb = (blk * 128) // TB
        for half in range(2):
            pb = psum()
            for j in range(4):
                kc = half * 4 + j
                P.add("pe", lambda e, pb=pb, kc=kc, j=j, blk=blk: e.transpose(
                    pb[:, j * 128:(j + 1) * 128], X[:, kc, blk * 128:(blk + 1) * 128], CONST[:, 0:128]),
                    reads=[Xb[(kc, tb)], CONST], writes=[pb])
            P.add("act", lambda e, pb=pb, yt=yt, half=half: e.copy(yt[:, half * 512:(half + 1) * 512], pb[:, :]),
                  reads=[pb], writes=[yt])
        P.add("act", lambda e, yt=yt, ss=ss: e.activation(YSQ[:, :], yt[:, :], AF.Square, accum_out=ss[:, 0:1]),
              reads=[yt], writes=[YSQ, ss])
        P.add("act", lambda e, ss=ss: e.activation(ss[:, :], ss[:, :], AF.Ln, bias=EPSC[:, 0:1], scale=1.0 / D),
              reads=[ss, EPSC], writes=[ss])
        P.add("act", lambda e, ss=ss: e.activation(ss[:, :], ss[:, :], AF.Exp, scale=-0.5), reads=[ss], writes=[ss])
        P.add("dve", lambda e, yt=yt, ss=ss: e.scalar_tensor_tensor(
            yt[:, :], yt[:, :], ss[:, 0:1], FG[:, :], ALU.mult, ALU.mult), reads=[yt, ss, FG], writes=[yt])
        P.dma(y_out[blk * 128:(blk + 1) * 128, :], yt[:, :], reads=[yt])

    P.emit()
    return nc


def make_consts():
    c = np.zeros((128, 1536), np.float32)
    c[:, 0:128] = np.eye(128, dtype=np.float32)
    J = np.zeros((64, 64), np.float32)
    J[np.arange(64), 63 - np.arange(64)] = 1.0
    c[0:64, 128:192] = J
    c[64:128, 128:192] = J
    qc = 63 - np.arange(64)[:, None]
    kc = np.arange(64)[None, :]
    cs = np.clip(qc - 8, 0, 48)
    inside = (kc >= cs) & (kc < cs + 16)
    c[0:64, 192:256] = np.where(inside, 0.0, -30000.0)
    jj = np.arange(128)[:, None]
    ii = np.arange(128)[None, :]
    same = (jj // 64) == (ii // 64)
    c[:, 256:384] = (same & (jj <= ii)).astype(np.float32)
    c[:, 384:512] = (same & (jj >= ii)).astype(np.float32)
    c[:, 512:640] = (jj < 64).astype(np.float32) * np.ones((1, 128), np.float32)
    c[:, 640:768] = (jj >= 64).astype(np.float32) * np.ones((1, 128), np.float32)
    NEG = -30000.0
    c[:, 768:896] = np.where(same & (jj <= ii), 0.0, NEG)
    c[:, 896:1024] = np.where(same & (jj < ii), 0.0, NEG)
    c[:, 1024:1152] = np.where(same & (jj >= ii), 0.0, NEG)
    c[:, 1152:1280] = np.where(same & (jj > ii), 0.0, NEG)
    c[:, 1280:1408] = 1.0
    c[:, 1408:1536] = -1.0
    return c


_NC_CACHE = {}


def kernel(x_prompt, x_sample, state_gdn, cache_k, cache_v, c, c_ctx, w_ada, b_ada, norm1_g, norm2_g,
           gdn_w_in, gdn_conv_w, gdn_a_log, gdn_dt_bias, gdn_norm_g, gdn_w_out,
           na_w_qkv, na_rel_bias, na_w_out, ffn_w_up, ffn_conv_w, ffn_conv_b, ffn_w_down, final_g, _cfg=Cfg, _cores=8):
    f = lambda a: np.ascontiguousarray(np.asarray(a, dtype=np.float32))
    nc = build(_cfg)
    consts = make_consts()
    c_ = np.ascontiguousarray
    w_ada_t = c_(f(w_ada).reshape(4, 8, 128, 48, 128).transpose(0, 3, 2, 1, 4)).reshape(4, 48, 128, 1024)
    w_up_t = c_(f(ffn_w_up).reshape(4, 8, 128, 2, 22, 128).transpose(0, 4, 2, 1, 3, 5)).reshape(4, 22, 128, 2048)
    w_dn_t = c_(f(ffn_w_down).reshape(4, 2, 11, 128, 8, 128).transpose(0, 1, 4, 3, 2, 5)).reshape(4, 2, 8, 128, 1408)
    qkv_t = c_(f(na_w_qkv).reshape(2, 8, 128, 3, 8, 128).transpose(0, 4, 2, 1, 3, 5)).reshape(2, 8, 128, 3072)
    nwo_t = c_(f(na_w_out).reshape(2, 8, 128, 8, 128).transpose(0, 3, 2, 1, 4)).reshape(2, 8, 128, 1024)
    gin = f(gdn_w_in)
    gin_t = c_(gin[:, :, :4096].reshape(2, 8, 128, 2, 2, 8, 128).transpose(0, 5, 3, 2, 1, 4, 6)).reshape(2, 8, 2, 128, 2048)
    gab_t = c_(gin[:, :, 4096:4128].reshape(2, 8, 128, 32).transpose(0, 2, 1, 3)).reshape(2, 128, 256)
    shared = {
        "w_ada": w_ada_t, "b_ada": f(b_ada), "norm1_g": f(norm1_g), "norm2_g": f(norm2_g),
        "gdn_w_in": gin_t, "gdn_w_ab": gab_t, "gdn_conv_w": f(gdn_conv_w),
        "gdn_a_log": f(gdn_a_log).reshape(2, 16), "gdn_dt_bias": f(gdn_dt_bias).reshape(2, 16),
        "gdn_norm_g": f(gdn_norm_g), "gdn_w_out": f(gdn_w_out),
        "na_w_qkv": qkv_t, "na_rel_bias": f(na_rel_bias).reshape(2, 240, 31), "na_w_out": nwo_t,
        "ffn_w_up": w_up_t, "ffn_conv_w": f(ffn_conv_w), "ffn_conv_b": f(ffn_conv_b),
        "ffn_w_down": w_dn_t, "final_g": f(final_g), "consts": consts,
    }
    xp = f(x_prompt)
    xs = f(x_sample)
    in_maps = []
    for i in range(_cores):
        m = dict(shared)
        m["x_in"] = np.concatenate([xp[2 * i].reshape(256, D), xp[2 * i + 1].reshape(256, D), xs[i]], axis=0)
        m["state_gdn"] = f(state_gdn[i])
        m["cache_k"] = f(cache_k[i]).reshape(2, 256, 1024)
        m["cache_v"] = f(cache_v[i]).reshape(2, 256, 1024)
        m["cvec"] = np.stack([f(c_ctx), f(c[i])], axis=0)
        in_maps.append(m)
    res = run_bass_kernel_spmd(nc, in_maps, core_ids=list(range(_cores)))
    R = res.results
    y_prompt = np.zeros((16, 256, D), np.float32)
    y_sample = np.zeros((8, 1024, D), np.float32)
    new_state = np.zeros((16, 2, 2, 8, 128, 128), np.float32)
    new_k = np.zeros((16, 2, 256, 16, 64), np.float32)
    new_v = np.zeros((16, 2, 256, 16, 64), np.float32)
    for i in range(_cores):
        y = R[i]["y_out"]
        y_prompt[2 * i] = y[0:256]
        y_prompt[2 * i + 1] = y[256:512]
        y_sample[i] = y[512:]
        new_state[2 * i:2 * i + 2] = R[i]["st_out"]
        new_k[2 * i:2 * i + 2] = R[i]["ck_out"].reshape(2, 2, 256, 16, 64)
        new_v[2 * i:2 * i + 2] = R[i]["cv_out"].reshape(2, 2, 256, 16, 64)
    return (y_prompt, y_sample, new_state, new_k, new_v)
```

```python
import numpy as np
import concourse.bass as bass
import concourse.mybir as mybir
from concourse.bass_utils import run_bass_kernel_spmd

F32 = mybir.dt.float32
F32R = mybir.dt.float32r
BF16 = mybir.dt.bfloat16
AF = mybir.ActivationFunctionType
ALU = mybir.AluOpType
AX = mybir.AxisListType

D = 1024
KC = 8
DEPTH = 4
NP_TOK = 512
NS_TOK = 1024
NT = NP_TOK + NS_TOK
TB = 512
NTB = NT // TB
D_FF = 2816
NFC = D_FF // 128
GDN_DIN = 4128
EPS = 1e-6
SEQS = [(0, 256), (256, 256), (512, 1024)]


class Buf:
    __slots__ = ("name", "t", "psum", "lastw", "readers", "dma_readers", "root")

    def __init__(self, name, t, psum=False):
        self.name = name
        self.t = t
        self.psum = psum
        self.lastw = None
        self.readers = {}
        self.dma_readers = []
        self.root = self

    def alias(self, ap, name=None):
        b = Buf(name or self.name, ap, self.psum)
        b.root = self.root
        return b

    def __getitem__(self, idx):
        return self.t[idx]

    def sub(self, name=None):
        return Buf(name or self.name, self.t, self.psum)


class Op:
    __slots__ = ("eng", "fn", "reads", "writes", "dma", "waits", "sig", "sem", "sigval", "deps", "n")


class _Rec:
    def __init__(self):
        self.call = None

    def __getattr__(self, name):
        def f(*args, **kwargs):
            assert self.call is None
            self.call = (name, args, kwargs)
            return None
        return f


def _bind(fn):
    r = _Rec()
    fn(r)
    name, args, kwargs = r.call
    return lambda e: getattr(e, name)(*args, **kwargs)


class Prog:
    ENGS = ("pe", "act", "dve", "pool", "sp")

    def __init__(self, nc, n_dma_ring=8):
        self.nc = nc
        self.ops = []
        self.nring = n_dma_ring
        self.sbuf_bytes = 0

    def sb(self, name, shape, dtype):
        t = self.nc.alloc_sbuf_tensor(name, list(shape), dtype)
        return Buf(name, t)

    def ps(self, name, shape, dtype=F32):
        t = self.nc.alloc_psum_tensor(name, list(shape), dtype)
        return Buf(name, t, psum=True)

    def add(self, eng, fn, reads=(), writes=(), dma=False):
        op = Op()
        op.eng = eng
        op.fn = _bind(fn)
        op.reads = [b.root for b in reads if b is not None]
        op.writes = [b.root for b in writes if b is not None]
        op.dma = dma
        op.waits = []
        op.sig = dma
        op.sem = None
        op.sigval = 0
        op.n = len(self.ops)
        self.ops.append(op)
        return op

    def fence(self):
        op = Op()
        op.eng = None
        op.n = len(self.ops)
        op.dma = False
        op.sig = False
        self.ops.append(op)

    def view(self, name, ap, psum=False):
        return Buf(name, ap, psum)

    def dma(self, out_ap, in_ap, reads=(), writes=(), q="sp", **kw):
        return self.add(q, lambda e: e.dma_start(out=out_ap, in_=in_ap, **kw), reads, writes, dma=True)

    def resolve(self):
        last_on = {}
        dma_since = []
        pending = {}
        real_ops = []
        for op in self.ops:
            if op.eng is None:
                snap = (dict(last_on), list(dma_since))
                dma_since = []
                for e in self.ENGS:
                    pending[e] = snap
                continue
            real_ops.append(op)
            wdeps = []
            rdeps = []
            for b in op.reads:
                if b.lastw is not None:
                    wdeps.append(b.lastw)
                if b.psum:
                    for e, r in b.readers.items():
                        if e != op.eng:
                            rdeps.append(r)
            for b in op.writes:
                if b.lastw is not None:
                    wdeps.append(b.lastw)
                rdeps.extend(b.readers.values())
                rdeps.extend(b.dma_readers)
            for b in op.writes:
                b.lastw = op
                b.readers = {}
                b.dma_readers = []
            for b in op.reads:
                if op.dma:
                    b.dma_readers.append(op)
                else:
                    b.readers[op.eng] = op
            deps = {}
            for d in wdeps:
                if d is op:
                    continue
                if d.dma or op.dma:
                    deps[d.n] = d
                elif d.eng == op.eng:
                    if op.eng != "pe":
                        deps[d.n] = d
                else:
                    deps[d.n] = d
            for d in rdeps:
                if d is op:
                    continue
                if d.dma or op.dma:
                    deps[d.n] = d
                elif d.eng != op.eng:
                    deps[d.n] = d
            if pending.get(op.eng) is not None:
                lo, dl = pending[op.eng]
                pending[op.eng] = None
                for e2, d in lo.items():
                    if e2 == op.eng and e2 == "pe" and not op.dma:
                        continue
                    deps[d.n] = d
                for d in dl:
                    deps[d.n] = d
            op.deps = list(deps.values())
            for d in op.deps:
                d.sig = True
            if op.dma:
                dma_since.append(op)
            else:
                last_on[op.eng] = op
        self.ops = real_ops

    def emit(self):
        nc = self.nc
        self.resolve()
        streams = {e: [] for e in self.ENGS}
        for op in self.ops:
            streams[op.eng].append(op)
        import contextlib
        with contextlib.ExitStack() as es:
            esem = {e: es.enter_context(nc.semaphore("s_" + e)) for e in self.ENGS}
            rings = {e: [es.enter_context(nc.semaphore("d_%s%d" % (e, i))) for i in range(self.nring)]
                     for e in ("sp", "pool", "act")}
            fin = es.enter_context(nc.semaphore("fin"))
            cnt = {e: 0 for e in self.ENGS}
            ringcnt = {e: [0] * self.nring for e in rings}
            ringpos = {e: 0 for e in rings}
            ring_prev = {}
            for op in self.ops:
                if op.dma:
                    r = ringpos[op.eng] % self.nring
                    ringpos[op.eng] += 1
                    ringcnt[op.eng][r] += 1
                    op.sem = rings[op.eng][r]
                    op.sigval = 16 * ringcnt[op.eng][r]
                    if ringcnt[op.eng][r] > 1:
                        ring_prev[op.n] = (op.sem, op.sigval - 16)
                elif op.sig:
                    cnt[op.eng] += 1
                    op.sem = esem[op.eng]
                    op.sigval = cnt[op.eng]
            waited = {e: {} for e in self.ENGS}
            for e in self.ENGS:
                for op in streams[e]:
                    need = {}
                    for d in op.deps:
                        k = id(d.sem)
                        if k not in need or need[k][1] < d.sigval:
                            need[k] = (d.sem, d.sigval)
                    if op.n in ring_prev:
                        s, v = ring_prev[op.n]
                        k = id(s)
                        if k not in need or need[k][1] < v:
                            need[k] = (s, v)
                    for k, (s, v) in need.items():
                        if waited[e].get(k, 0) >= v:
                            continue
                        waited[e][k] = v
                        op.waits.append((s, v))
            final_waits = []
            for e in rings:
                for r in range(self.nring):
                    if ringcnt[e][r] > 0:
                        final_waits.append((rings[e][r], 16 * ringcnt[e][r]))

            def run_stream(ename, eng):
                for op in streams[ename]:
                    for s, v in op.waits:
                        eng.wait_ge(s, v)
                    ins = op.fn(eng)
                    if op.sig:
                        ins.then_inc(op.sem, 16 if op.dma else 1)
                if ename == "sp":
                    for s, v in final_waits:
                        eng.wait_ge(s, v)

            with nc.Block() as block:
                @block.tensor
                def _(eng):
                    run_stream("pe", eng)

                @block.scalar
                def _(eng):
                    run_stream("act", eng)

                @block.vector
                def _(eng):
                    run_stream("dve", eng)

                @block.gpsimd
                def _(eng):
                    run_stream("pool", eng)

                @block.sync
                def _(eng):
                    run_stream("sp", eng)


class Cfg:
    n_layers = DEPTH
    do_gdn = True
    do_na = True


def build(cfg=Cfg):
    nc = bass.Bass("TRN2", target_bir_lowering=False)
    P = Prog(nc)

    def din(name, shape):
        return nc.dram_tensor(name, list(shape), F32, kind="ExternalInput").ap()

    def dout(name, shape):
        return nc.dram_tensor(name, list(shape), F32, kind="ExternalOutput").ap()

    x_in = din("x_in", [NT, D])
    state_gdn = din("state_gdn", [2, 2, 8, 128, 128])
    cache_k = din("cache_k", [2, 256, 1024])
    cache_v = din("cache_v", [2, 256, 1024])
    cvec = din("cvec", [2, D])
    w_ada = din("w_ada", [DEPTH, 48, 128, 1024])
    b_ada = din("b_ada", [DEPTH, 6 * D])
    norm1_g = din("norm1_g", [DEPTH, D])
    norm2_g = din("norm2_g", [DEPTH, D])
    gdn_w_in = din("gdn_w_in", [2, 8, 2, 128, 2048])
    gdn_w_ab = din("gdn_w_ab", [2, 128, 256])
    gdn_conv_w = din("gdn_conv_w", [2, 3, 3072])
    gdn_a_log = din("gdn_a_log", [2, 16])
    gdn_dt_bias = din("gdn_dt_bias", [2, 16])
    gdn_norm_g = din("gdn_norm_g", [2, 128])
    gdn_w_out = din("gdn_w_out", [2, D, D])
    na_w_qkv = din("na_w_qkv", [2, 8, 128, 3072])
    na_rel_bias = din("na_rel_bias", [2, 16 * 15, 31])
    na_w_out = din("na_w_out", [2, 8, 128, 1024])
    ffn_w_up = din("ffn_w_up", [DEPTH, 22, 128, 2048])
    ffn_conv_w = din("ffn_conv_w", [DEPTH, 3, 2 * D_FF])
    ffn_conv_b = din("ffn_conv_b", [DEPTH, 2 * D_FF])
    ffn_w_down = din("ffn_w_down", [DEPTH, 2, 8, 128, 1408])
    final_g = din("final_g", [D])
    consts = din("consts", [128, 1536])

    y_out = dout("y_out", [NT, D])
    st_out = dout("st_out", [2, 2, 2, 8, 128, 128])
    ck_out = dout("ck_out", [2, 2, 256, 1024])
    cv_out = dout("cv_out", [2, 2, 256, 1024])

    X = P.sb("X", [128, KC, NT], F32)
    Xb = {(kc, tb): X.sub("X%d_%d" % (kc, tb)) for kc in range(KC) for tb in range(NTB)}
    H = P.sb("H", [128, KC, NT], BF16)
    Hb = {(kc, tb): H.sub("H%d_%d" % (kc, tb)) for kc in range(KC) for tb in range(NTB)}
    ARENA = P.sb("ARENA", [128, 13824], F32)
    CHAIN = nc.alloc_sbuf_tensor("CHAIN", [128, 3072], F32R)
    WREG = P.sb("WREG", [128, 3584], F32)

    def carve(region, byte_off, dtype, shape, name):
        esz = 2 if dtype == BF16 else 4
        n = 1
        for d_ in shape[1:]:
            n *= d_
        base = region.t.bitcast(dtype) if dtype != F32 else region.t
        ap = base[:, byte_off // esz: byte_off // esz + n]
        if len(shape) == 3:
            ap = ap.rearrange("p (a b) -> p a b", a=shape[1])
        elif len(shape) == 4:
            ap = ap.rearrange("p (a b c) -> p a b c", a=shape[1], b=shape[2])
        return P.view(name, ap)

    CONST = P.sb("CONST", [128, 1536], F32)
    ident = CONST
    ONESB = P.sb("ONESB", [128, 128], BF16)
    IDB = P.sb("IDB", [128, 128], BF16)

    PSALL = nc.alloc_psum_tensor("psall", [128, 4096], F32)
    PS = [P.view("ps%d" % i, PSALL[:, i * 512:(i + 1) * 512], psum=True) for i in range(8)]
    ps_rr = [0]

    def psum():
        b = PS[ps_rr[0] % 8]
        ps_rr[0] += 1
        return b

    P.dma(CONST[:, :], consts[:, :], writes=[CONST])
    P.add("dve", lambda e: e.memset(ONESB[:, :], 1.0), writes=[ONESB])
    P.add("dve", lambda e: e.tensor_copy(IDB[:, :], CONST[:, 0:128]), reads=[CONST], writes=[IDB])

    XT = [carve(ARENA, i * 4096, F32, [128, D], "XT%d" % i) for i in range(2)]
    for blk in range(NT // 128):
        xt = XT[blk % 2]
        P.dma(xt[:, :], x_in[blk * 128:(blk + 1) * 128, :], writes=[xt])
        tb = (blk * 128) // TB
        for half in range(2):
            pb = psum()
            for j in range(4):
                kc = half * 4 + j
                P.add("pe", lambda e, pb=pb, xt=xt, kc=kc, j=j: e.transpose(
                    pb[:, j * 128:(j + 1) * 128], xt[:, kc * 128:(kc + 1) * 128], CONST[:, 0:128]),
                    reads=[xt, CONST], writes=[pb])
            wr = [Xb[(half * 4 + j, tb)] for j in range(4)]
            eng = "act" if half == 0 else "dve"
            if eng == "act":
                P.add("act", lambda e, pb=pb, half=half, blk=blk: e.copy(
                    X[:, half * 4:half * 4 + 4, blk * 128:(blk + 1) * 128],
                    pb[:, :].rearrange("p (j t) -> p j t", j=4)), reads=[pb], writes=wr)
            else:
                P.add("dve", lambda e, pb=pb, half=half, blk=blk: e.tensor_copy(
                    X[:, half * 4:half * 4 + 4, blk * 128:(blk + 1) * 128],
                    pb[:, :].rearrange("p (j t) -> p j t", j=4)), reads=[pb], writes=wr)

    STG = [P.sb("STG%d" % i, [128, 128], F32) for i in range(2)]
    stg_rr = [0]

    def load_fm(dst, dst_flat_ap, src_rows_ap, R):
        r0 = 0
        while r0 < R:
            r = min(128, R - r0)
            stg = STG[stg_rr[0] % 2]
            stg_rr[0] += 1
            P.dma(stg[0:r, :], src_rows_ap[r0:r0 + r, :], writes=[stg])
            pb = psum()
            P.add("pe", lambda e, pb=pb, stg=stg, r=r: e.transpose(pb[:, 0:r], stg[0:r, :], CONST[0:r, 0:r]),
                  reads=[stg, CONST], writes=[pb])
            P.add("dve", lambda e, pb=pb, r=r, r0=r0: e.tensor_copy(dst_flat_ap[:, r0:r0 + r], pb[:, 0:r]),
                  reads=[pb], writes=[dst])
            r0 += r

    G1 = P.sb("G1", [128, DEPTH, KC], F32)
    G2 = P.sb("G2", [128, DEPTH, KC], F32)
    BADA = P.sb("BADA", [128, DEPTH, 48], F32)
    CV = P.sb("CV", [128, 2, KC], F32)
    SCV = P.sb("SCV", [128, 2, KC], BF16)
    load_fm(G1, G1[:, :, :].rearrange("p l k -> p (l k)"), norm1_g.rearrange("l (k f) -> (l k) f", f=128), 32)
    load_fm(G2, G2[:, :, :].rearrange("p l k -> p (l k)"), norm2_g.rearrange("l (k f) -> (l k) f", f=128), 32)
    load_fm(BADA, BADA[:, :, :].rearrange("p l k -> p (l k)"), b_ada.rearrange("l (k f) -> (l k) f", f=128), 192)
    load_fm(CV, CV[:, :, :].rearrange("p v k -> p (v k)"), cvec.rearrange("v (k f) -> (v k) f", f=128), 16)
    P.add("act", lambda e: e.activation(SCV[:, :, :], CV[:, :, :], AF.Silu), reads=[CV], writes=[SCV])

    MOD = [P.sb("MOD%d" % l, [128, 48, 2], F32) for l in range(DEPTH)]
    A1 = [P.sb("A1_%d" % l, [128, KC, 2], F32) for l in range(DEPTH)]
    A2 = [P.sb("A2_%d" % l, [128, KC, 2], F32) for l in range(DEPTH)]
    WA = [P.sb("WA%d" % i, [128, KC, 128], BF16) for i in range(3)]
    wa_rr = [0]

    MODg = [[MOD[l].sub("MOD%d_%d" % (l, g)) for g in range(6)] for l in range(DEPTH)]

    class WStream:
        def __init__(self, slots, n, issue):
            self.slots, self.n, self.issue, self.nxt = slots, n, issue, 0

        def need(self, i):
            k = len(self.slots)
            while self.nxt <= min(i + k - 1, self.n - 1):
                self.issue(self.nxt, self.slots[self.nxt % k])
                self.nxt += 1
            return self.slots[i % k]

    def mod_gen(l):
        ws = WStream(WA, 48, lambda i, wa: P.dma(wa[:, :, :].rearrange("p k n -> p (k n)"), w_ada[l, i, :, :],
                                                 writes=[wa], q="pool"))
        for n in range(48):
            wa = ws.need(n)
            pm = PS[6 + mod_bank[0] % 2]
            mod_bank[0] += 1
            for kc in range(KC):
                P.add("pe", lambda e: e.matmul(pm[:, 0:2], wa[:, kc, :], SCV[:, :, kc],
                                               start=(kc == 0), stop=(kc == KC - 1)), reads=[wa, SCV], writes=[pm])
            P.add("dve", lambda e: e.tensor_scalar(MOD[l][:, n, :], pm[:, 0:2], BADA[:, l, n:n + 1], None, ALU.add),
                  reads=[pm, BADA], writes=[MODg[l][n // 8]])
            if n == 15:
                P.add("dve", lambda e: e.scalar_tensor_tensor(
                    A1[l][:, :, :], MOD[l][:, 8:16, :], 1.0, G1[:, l, :].unsqueeze(2).to_broadcast([128, KC, 2]),
                    ALU.add, ALU.mult), reads=[MODg[l][1], G1], writes=[A1[l]])
            if n == 39:
                P.add("dve", lambda e: e.scalar_tensor_tensor(
                    A2[l][:, :, :], MOD[l][:, 32:40, :], 1.0, G2[:, l, :].unsqueeze(2).to_broadcast([128, KC, 2]),
                    ALU.add, ALU.mult), reads=[MODg[l][4], G2], writes=[A2[l]])
            yield n

    mod_state = {"gen": None}
    mod_bank = [0]

    def pump(k):
        g = mod_state["gen"]
        if g is None:
            return
        for _ in range(k):
            try:
                next(g)
            except StopIteration:
                mod_state["gen"] = None
                return

    SQ = [P.sb("SQ%d" % i, [128, TB], BF16) for i in range(2)]
    RSTD = [P.sb("RSTD%d" % i, [128, TB], F32) for i in range(2)]
    TMPN = [P.sb("TMPN%d" % i, [128, TB], F32) for i in range(2)]
    EPSC = P.sb("EPSC", [128, 1], F32)
    P.add("dve", lambda e: e.memset(EPSC[:, :], EPS), writes=[EPSC])
    nrr = [0]

    def emit_norm_mod(l, which):
        A = A1[l] if which == 1 else A2[l]
        shoff = 0 if which == 1 else 24
        for tb in range(NTB):
            v = 0 if tb == 0 else 1
            rstd = RSTD[nrr[0] % 2]
            nrr[0] += 1
            sl = slice(tb * TB, (tb + 1) * TB)
            pb = psum()
            for kc in range(KC):
                sq = SQ[kc % 2]
                P.add("act", lambda e, sq=sq, sl=sl, kc=kc: e.activation(
                    sq[:, :], X[:, kc, sl], AF.Square), reads=[Xb[(kc, tb)]], writes=[sq])
                P.add("pe", lambda e, pb=pb, sq=sq, kc=kc: e.matmul(
                    pb[:, :], ONESB[:, :], sq[:, :], start=(kc == 0), stop=(kc == KC - 1)),
                    reads=[sq, ONESB], writes=[pb])
            P.add("act", lambda e, pb=pb, rstd=rstd: e.activation(
                rstd[:, :], pb[:, :], AF.Ln, bias=EPSC[:, 0:1], scale=1.0 / D),
                reads=[pb, EPSC], writes=[rstd])
            P.add("act", lambda e, rstd=rstd: e.activation(rstd[:, :], rstd[:, :], AF.Exp, scale=-0.5), reads=[rstd], writes=[rstd])
            for kc in range(KC):
                tmp = TMPN[kc % 2]
                P.add("dve", lambda e, tmp=tmp, kc=kc, sl=sl, rstd=rstd: e.tensor_tensor(
                    tmp[:, :], X[:, kc, sl], rstd[:, :], ALU.mult),
                    reads=[Xb[(kc, tb)], rstd], writes=[tmp])
                P.add("act", lambda e, tmp=tmp, kc=kc, sl=sl, v=v, A=A, l=l, shoff=shoff: e.activation(
                    H[:, kc, sl], tmp[:, :], AF.Identity,
                    bias=MOD[l][:, shoff + kc, v:v + 1], scale=A[:, kc, v:v + 1]),
                    reads=[tmp, A, MODg[l][shoff // 8]], writes=[Hb[(kc, tb)]])

    def resid_from_psum(pb, l, gate_off, n, tb, ncols=TB, col0=0):
        v = 0 if tb == 0 else 1
        sl = slice(tb * TB + col0, tb * TB + col0 + ncols)
        P.add("dve", lambda e: e.scalar_tensor_tensor(
            X[:, n, sl], pb[:, 0:ncols], MOD[l][:, gate_off + n, v:v + 1], X[:, n, sl], ALU.mult, ALU.add),
            reads=[pb, MODg[l][gate_off // 8], Xb[(n, tb)]], writes=[Xb[(n, tb)]])

    WUP = [carve(WREG, i * 4096, BF16, [128, KC, 2, 128], "WUP%d" % i) for i in range(2)]
    wup_rr = [0]
    CT = [P.sb("CT%d" % i, [128, NT], F32) for i in range(4)]
    ct_rr = [0]
    NFH = NFC // 2
    GB = carve(ARENA, 0, BF16, [128, NFH, NT], "GB")
    GBb = {(i, tb): GB.sub("GB%d_%d" % (i, tb)) for i in range(NFH) for tb in range(NTB)}
    FCW = P.sb("FCW", [128, DEPTH, 3, 44], F32)
    FCB = P.sb("FCB", [128, DEPTH, 44], F32)
    load_fm(FCW, FCW[:, :, :, :].rearrange("p l t k -> p (l t k)"),
            ffn_conv_w.rearrange("l t (k f) -> (l t k) f", f=128), DEPTH * 3 * 44)
    load_fm(FCB, FCB[:, :, :].rearrange("p l k -> p (l k)"), ffn_conv_b.rearrange("l (k f) -> (l k) f", f=128),
            DEPTH * 44)
    WDN = [carve(WREG, 8192 + i * 2816, BF16, [128, NFH, 128], "WDN%d" % i) for i in range(2)]
    wdn_rr = [0]
    grp_rr = [0]
    dn_rr = [0]

    def bank_group():
        g = grp_rr[0] % 2
        grp_rr[0] += 1
        return g * 3

    def conv_from_psum(b0, w0, w1, w2, bias, ct, ctb):
        pbs = [PS[b0], PS[b0 + 1], PS[b0 + 2]]
        samp = PSALL[:, (b0 + 1) * 512:(b0 + 3) * 512]
        P.add("act", lambda e: e.activation(ct[:, 0:512], PS[b0][:, :], AF.Identity, bias=bias, scale=w1),
              reads=[pbs[0], FCWb], writes=[ctb])
        P.add("act", lambda e: e.activation(ct[:, 512:NT], samp, AF.Identity, bias=bias, scale=w1),
              reads=[pbs[1], pbs[2], FCWb], writes=[ctb])
        for (t0, ln) in SEQS:
            if t0 < 512:
                src = lambda a, b_: PS[b0][:, a:b_]
                rd = [pbs[0]]
            else:
                src = lambda a, b_: PSALL[:, (b0 + 1) * 512 + a - 512:(b0 + 1) * 512 + b_ - 512]
                rd = [pbs[1], pbs[2]]
            P.add("dve", lambda e, src=src, t0=t0, ln=ln: e.scalar_tensor_tensor(
                ct[:, t0 + 1:t0 + ln], src(t0, t0 + ln - 1), w0, ct[:, t0 + 1:t0 + ln], ALU.mult, ALU.add),
                reads=rd + [FCWb, ctb], writes=[ctb])
            P.add("dve", lambda e, src=src, t0=t0, ln=ln: e.scalar_tensor_tensor(
                ct[:, t0:t0 + ln - 1], src(t0 + 1, t0 + ln), w2, ct[:, t0:t0 + ln - 1], ALU.mult, ALU.add),
                reads=rd + [FCWb, ctb], writes=[ctb])

    FCWb = FCW

    def emit_ffn(l):
        wus = WStream(WUP, NFC, lambda i, wu: P.dma(wu[:, :, :, :].rearrange("p k g n -> p (k g n)"),
                                                    ffn_w_up[l, i, :, :], writes=[wu], q="pool"))
        wds = WStream(WDN, 2 * KC, lambda i, wd: P.dma(wd[:, :, :].rearrange("p i n -> p (i n)"),
                                                       ffn_w_down[l, i // KC, i % KC, :, :], writes=[wd], q="pool"))
        for hf in range(2):
            for piece in range(hf * NFH, (hf + 1) * NFH):
                wu = wus.need(piece)
                if piece == (hf + 1) * NFH - 1:
                    wds.need(hf * KC)
                i = piece
                cts = []
                for g in range(2):
                    b0 = bank_group()
                    for tb in range(NTB):
                        pb = PS[b0 + tb]
                        for kc in range(KC):
                            P.add("pe", lambda e: e.matmul(
                                pb[:, :], wu[:, kc, g, :], H[:, kc, tb * TB:(tb + 1) * TB],
                                start=(kc == 0), stop=(kc == KC - 1)),
                                reads=[wu, Hb[(kc, tb)]], writes=[pb])
                    ct = CT[ct_rr[0] % 4]
                    ct_rr[0] += 1
                    ch = g * NFC + i
                    conv_from_psum(b0, FCW[:, l, 0, ch:ch + 1], FCW[:, l, 1, ch:ch + 1], FCW[:, l, 2, ch:ch + 1],
                                   FCB[:, l, ch:ch + 1], ct, ct)
                    cts.append(ct)
                ctv, ctg = cts
                pump(2)
                P.add("act", lambda e: e.activation(ctg[:, :], ctg[:, :], AF.Silu), reads=[ctg], writes=[ctg])
                P.add("pool", lambda e: e.tensor_tensor(GB[:, i - hf * NFH, :], ctv[:, :], ctg[:, :], ALU.mult),
                      reads=[ctv, ctg], writes=[GBb[(i - hf * NFH, tb)] for tb in range(NTB)])
            for piece in range(KC):
                wd = wds.need(hf * KC + piece)
                n = piece
                for tb in range(NTB):
                    pb = PS[dn_rr[0] % 6]
                    dn_rr[0] += 1
                    for i in range(NFH):
                        P.add("pe", lambda e: e.matmul(
                            pb[:, :], wd[:, i, :], GB[:, i, tb * TB:(tb + 1) * TB],
                            start=(i == 0), stop=(i == NFH - 1)),
                            reads=[wd, GBb[(i, tb)]], writes=[pb])
                    resid_from_psum(pb, l, 40, n, tb)
                pump(2)

    WO = [carve(WREG, i * 2048, BF16, [128, KC, 128], "WO%d" % i) for i in range(2)]
    wo_rr = [0]

    def emit_wout(l, w_dram, OT, OTb):
        wos = WStream(WO, KC, lambda i, wo: P.dma(wo[:, :, :].rearrange("p k n -> p (k n)"), w_dram[i, :, :],
                                                  writes=[wo], q="pool"))
        for n in range(KC):
            wo = wos.need(n)
            pump(1)
            for tb in range(NTB):
                pb = psum()
                for kc in range(KC):
                    P.add("pe", lambda e, pb=pb, wo=wo, kc=kc, tb=tb: e.matmul(
                        pb[:, :], wo[:, kc, :], OT[:, kc, tb * TB:(tb + 1) * TB],
                        start=(kc == 0), stop=(kc == KC - 1)), reads=[wo, OTb[kc]], writes=[pb])
                resid_from_psum(pb, l, 16, n, tb)

    rb_t = nc.dram_tensor("rbpad", [480, 127], F32)
    RBPAD = P.view("rbpad", rb_t.ap())
    J2B = P.sb("J2B", [128, 64], BF16)
    P.add("dve", lambda e: e.tensor_copy(J2B[:, :], CONST[:, 128:192]), reads=[CONST], writes=[J2B])

    def emit_rbpad():
        RBP = carve(ARENA, 16384, F32, [128, 4, 127], "RBP")
        P.add("pool", lambda e: e.memset(RBP[:, :, :], 0.0), writes=[RBP])
        P.dma(RBP[0:120, :, 48:79], na_rel_bias.rearrange("j (p a) f -> p (j a) f", a=2)[:, :, :]
              if False else na_rel_bias.rearrange("j r f -> (j r) f").rearrange("(p a) f -> p a f", a=4),
              reads=[], writes=[RBP], allow_slow_non_contiguous=True)
        P.dma(rb_t.ap().rearrange("(p a) f -> p a f", a=4), RBP[0:120, :, :], reads=[RBP], writes=[RBPAD])

    def psum_of(lst, st):
        b = PS[lst[st[0] % len(lst)]]
        st[0] += 1
        return b

    def emit_na(l):
        j = l // 2
        OT = carve(ARENA, 0, BF16, [128, KC, NT], "OT")
        OTb = [OT.sub("OT%d" % c) for c in range(KC)]
        QZ = [carve(ARENA, 24576 + i * 3072, BF16, [128, NT], "QZ%d" % i) for i in range(2)]
        KT = carve(ARENA, 30720, BF16, [128, NT], "KT")
        VT = carve(ARENA, 33792, BF16, [128, 12, 128], "VT")
        VTS = carve(ARENA, 36864, BF16, [128, 7, 128], "VTS")
        KCT = carve(ARENA, 38656, BF16, [128, 256], "KCT")
        VC = carve(ARENA, 39168, BF16, [128, 2, 128], "VC")
        PTC = carve(ARENA, 39680, BF16, [128, 2, 1024], "PTC")
        PTL = [carve(ARENA, 43776 + i * 512, BF16, [128, 4, 64], "PTL%d" % i) for i in range(2)]
        PTP = [carve(ARENA, 44800 + i * 1024, BF16, [128, 2, 256], "PTP%d" % i) for i in range(2)]
        HKR = carve(ARENA, 46848, BF16, [128, 14, 2, 64], "HKR")
        HKZ = carve(ARENA, 50432, BF16, [128, 14, 2, 64], "HKZ")
        CKS = TMPN[0].alias(TMPN[0].t[:, :].rearrange("p (a b) -> p a b", a=4))
        CVS = TMPN[1].alias(TMPN[1].t[:, :].rearrange("p (a b) -> p a b", a=4))
        RD = RSTD[0].alias(RSTD[0].t[:, :])
        KCS = RSTD[1].alias(RSTD[1].t[:, 0:256].rearrange("p (a b) -> p a b", a=2))
        WQ = [carve(WREG, i * 6144, BF16, [128, KC, 3, 128], "WQ%d" % i) for i in range(2)]
        lo = [0]
        hi = [0]
        LO = [0, 1, 2, 3]
        HI = [4, 5, 6, 7]
        P.add("pool", lambda e: e.memset(QZ[0][64:128, :], 0.0), writes=[QZ[0]])
        P.add("pool", lambda e: e.memset(QZ[1][0:64, :], 0.0), writes=[QZ[1]])
        P.add("pool", lambda e: e.memset(HKZ[:, :, :, :], 0.0), writes=[HKZ])
        ptl_rr = [0]
        ptp_rr = [0]
        wqs = WStream(WQ, KC, lambda i, wq: P.dma(wq[:, :, :, :].rearrange("p k g n -> p (k g n)"),
                                                  na_w_qkv[j, i, :, :], writes=[wq], q="pool"))
        for c in range(KC):
            wq = wqs.need(c)
            pump(3)
            for tb in range(NTB):
                sl = slice(tb * TB, (tb + 1) * TB)
                pb = psum_of(LO, lo)
                for kc in range(KC):
                    P.add("pe", lambda e, pb=pb, kc=kc, sl=sl: e.matmul(
                        pb[:, :], wq[:, kc, 0, :], H[:, kc, sl], start=(kc == 0), stop=(kc == KC - 1)),
                        reads=[wq, Hb[(kc, tb)]], writes=[pb])
                P.add("act", lambda e, pb=pb, sl=sl: e.activation(QZ[0][0:64, sl], pb[0:64, :], AF.Copy, scale=0.125),
                      reads=[pb], writes=[QZ[0]])
                P.add("act", lambda e, pb=pb, sl=sl: e.activation(QZ[1][64:128, sl], pb[64:128, :], AF.Copy, scale=0.125),
                      reads=[pb], writes=[QZ[1]])
                pb = psum_of(LO, lo)
                for kc in range(KC):
                    P.add("pe", lambda e, pb=pb, kc=kc, sl=sl: e.matmul(
                        pb[:, :], wq[:, kc, 1, :], H[:, kc, sl], start=(kc == 0), stop=(kc == KC - 1)),
                        reads=[wq, Hb[(kc, tb)]], writes=[pb])
                P.add("dve", lambda e, pb=pb, sl=sl: e.tensor_copy(KT[:, sl], pb[:, :]), reads=[pb], writes=[KT])
            for g in range(3):
                pb = psum_of(LO, lo)
                for b in range(4):
                    blk = g * 4 + b
                    for kc in range(KC):
                        P.add("pe", lambda e, pb=pb, kc=kc, b=b, blk=blk: e.matmul(
                            pb[:, b * 128:(b + 1) * 128], H[:, kc, blk * 128:(blk + 1) * 128], wq[:, kc, 2, :],
                            start=(kc == 0), stop=(kc == KC - 1)),
                            reads=[wq, Hb[(kc, blk // 4)]], writes=[pb])
                P.add("dve", lambda e, pb=pb, g=g: e.tensor_copy(
                    VT[:, g * 4:(g + 1) * 4, :], pb[:, :].rearrange("p (b f) -> p b f", b=4)),
                    reads=[pb], writes=[VT])
                if g == 0:
                    P.add("act", lambda e, pb=pb: e.copy(CVS[:, :, :], pb[:, :].rearrange("p (b f) -> p b f", b=4)),
                          reads=[pb], writes=[CVS])
                    for sq in range(2):
                        P.dma(cv_out[sq, j, :, c * 128:(c + 1) * 128].rearrange("(b p) f -> p b f", p=128),
                              CVS[:, sq * 2:sq * 2 + 2, :], reads=[CVS])
            pb = psum_of(LO, lo)
            for b in range(4):
                for kc in range(KC):
                    P.add("pe", lambda e, pb=pb, kc=kc, b=b: e.matmul(
                        pb[:, b * 128:(b + 1) * 128], H[:, kc, b * 128:(b + 1) * 128], wq[:, kc, 1, :],
                        start=(kc == 0), stop=(kc == KC - 1)), reads=[wq, Hb[(kc, 0)]], writes=[pb])
            P.add("act", lambda e, pb=pb: e.copy(CKS[:, :, :], pb[:, :].rearrange("p (b f) -> p b f", b=4)),
                  reads=[pb], writes=[CKS])
            for sq in range(2):
                P.dma(ck_out[sq, j, :, c * 128:(c + 1) * 128].rearrange("(b p) f -> p b f", p=128),
                      CKS[:, sq * 2:sq * 2 + 2, :], reads=[CKS])
            for g in range(2):
                pb = psum_of(LO, lo)
                nb = 4 if g == 0 else 3
                for b in range(nb):
                    m = g * 4 + b
                    t0 = 512 + 64 + 128 * m
                    for kc in range(KC):
                        P.add("pe", lambda e, pb=pb, kc=kc, b=b, t0=t0: e.matmul(
                            pb[:, b * 128:(b + 1) * 128], H[:, kc, t0:t0 + 128], wq[:, kc, 2, :],
                            start=(kc == 0), stop=(kc == KC - 1)),
                            reads=[wq, Hb[(kc, 1)], Hb[(kc, 2)]], writes=[pb])
                P.add("dve", lambda e, pb=pb, g=g, nb=nb: e.tensor_copy(
                    VTS[:, g * 4:g * 4 + nb, :], pb[:, 0:nb * 128].rearrange("p (b f) -> p b f", b=nb)),
                    reads=[pb], writes=[VTS])
            P.dma(KCS[:, :, :], cache_k[j, :, c * 128:(c + 1) * 128].rearrange("(b p) f -> p b f", p=128),
                  writes=[KCS])
            pb = psum_of(LO, lo)
            for b in range(2):
                P.add("pe", lambda e, pb=pb, b=b: e.transpose(pb[:, b * 128:(b + 1) * 128], KCS[:, b, :], CONST[:, 0:128]),
                      reads=[KCS, CONST], writes=[pb])
            P.add("act", lambda e, pb=pb: e.copy(KCT[:, :], pb[:, 0:256]), reads=[pb], writes=[KCT])
            P.dma(VC[:, :, :], cache_v[j, :, c * 128:(c + 1) * 128].rearrange("(b p) f -> p b f", p=128),
                  writes=[VC], q="pool")
            for sq in range(2):
                pbo = psum_of(HI, hi)
                for hh in range(2):
                    hs = slice(hh * 64, hh * 64 + 64)
                    ptp = PTP[ptp_rr[0] % 2]
                    ptp_rr[0] += 1
                    pbs = psum_of(LO, lo)
                    for kb in range(2):
                        P.add("pe", lambda e, pbs=pbs, kb=kb, hh=hh, sq=sq: e.matmul(
                            pbs[:, kb * 256:(kb + 1) * 256], KT[:, sq * 256 + kb * 128:sq * 256 + (kb + 1) * 128],
                            QZ[hh][:, sq * 256:(sq + 1) * 256], start=True, stop=True),
                            reads=[KT, QZ[hh]], writes=[pbs])
                    P.add("act", lambda e, pbs=pbs, ptp=ptp: e.activation(
                        ptp[:, :, :], pbs[:, :].rearrange("p (b q) -> p b q", b=2), AF.Exp),
                        reads=[pbs], writes=[ptp])
                    for kb in range(2):
                        P.add("pe", lambda e, kb=kb, hs=hs, ptp=ptp, sq=sq: e.matmul(
                            pbo[hs, 0:256], VT[:, sq * 2 + kb, hs], ptp[:, kb, :], start=(kb == 0), stop=(kb == 1)),
                            reads=[VT, ptp], writes=[pbo])
                    for kb in range(2):
                        P.add("pe", lambda e, kb=kb, hs=hs, ptp=ptp: e.matmul(
                            pbo[hs, 256:512], ONESB[:, 0:64], ptp[:, kb, :], start=(kb == 0), stop=(kb == 1)),
                            reads=[ONESB, ptp], writes=[pbo])
                P.add("act", lambda e, pbo=pbo: e.activation(RD[:, 0:256], pbo[:, 256:512], AF.Ln), reads=[pbo], writes=[RD])
                P.add("act", lambda e: e.activation(RD[:, 0:256], RD[:, 0:256], AF.Exp, scale=-1.0), reads=[RD], writes=[RD])
                P.add("dve", lambda e, pbo=pbo, sq=sq: e.tensor_tensor(
                    OT[:, c, sq * 256:(sq + 1) * 256], pbo[:, 0:256], RD[:, 0:256], ALU.mult),
                    reads=[pbo, RD], writes=[OTb[c]])
            pbo_s = [PS[4], PS[5]]
            pbd_s = [PS[6], PS[7]]
            for hh in range(2):
                h = 2 * c + hh
                hs = slice(hh * 64, hh * 64 + 64)
                for u in range(2):
                    src = bass.AP(rb_t, (j * 240 + h * 15 + u) * 127, [[1, 64], [127, 14], [1, 64]])
                    P.dma(HKR[0:64, :, u, :], src, reads=[RBPAD], writes=[HKR], q="pool")
                P.add("pool", lambda e: e.tensor_tensor(
                    HKZ[0:64, :, :, :].rearrange("p a u k -> p (a u) k"),
                    HKR[0:64, :, :, :].rearrange("p a u k -> p (a u) k"),
                    CONST[0:64, 192:256].unsqueeze(1).to_broadcast([64, 28, 64]), ALU.add),
                    reads=[HKR, CONST], writes=[HKZ])
                for kb in range(2):
                    for qb in range(2):
                        pbs = psum_of(LO, lo)
                        P.add("pe", lambda e, pbs=pbs, kb=kb, qb=qb, hh=hh: e.matmul(
                            pbs[:, :], KCT[:, kb * 128:(kb + 1) * 128], QZ[hh][:, 512 + qb * 512:512 + (qb + 1) * 512],
                            start=True, stop=True), reads=[KCT, QZ[hh]], writes=[pbs])
                        P.add("act", lambda e, pbs=pbs, kb=kb, qb=qb: e.activation(
                            PTC[:, kb, qb * 512:(qb + 1) * 512], pbs[:, :], AF.Exp), reads=[pbs], writes=[PTC])
                def row_scores(r):
                    rs = min(max(r - 4, 0), 8)
                    ptl = PTL[ptl_rr[0] % 2]
                    ptl_rr[0] += 1
                    pbs = psum_of(LO, lo)
                    qsl = slice(512 + 64 * r, 512 + 64 * r + 64)
                    for i in range(4):
                        kr = rs + 2 * i
                        ri = kr - r + 7
                        k0 = 512 + 64 * kr
                        P.add("pe", lambda e: e.matmul(
                            pbs[:, i * 64:(i + 1) * 64], KT[:, k0:k0 + 128], QZ[hh][:, qsl], start=True, stop=False),
                            reads=[KT, QZ[hh]], writes=[pbs])
                        P.add("pe", lambda e: e.matmul(
                            pbs[:, i * 64:(i + 1) * 64], HKZ[:, ri, :, :].rearrange("p u k -> p (u k)"), J2B[:, :],
                            start=False, stop=True), reads=[HKZ, J2B], writes=[pbs])
                    P.add("act", lambda e: e.activation(
                        ptl[:, :, :], pbs[:, 0:256].rearrange("p (i q) -> p i q", i=4), AF.Exp),
                        reads=[pbs], writes=[ptl])
                    return ptl

                def row_pv(r, ptl):
                    rs = min(max(r - 4, 0), 8)
                    pbo = pbo_s[r // 8]
                    pbd = pbd_s[r // 8]
                    osl = slice((r % 8) * 64, (r % 8) * 64 + 64)
                    for pbx, isden in ((pbo, False), (pbd, True)):
                        for i in range(4):
                            kr = rs + 2 * i
                            if kr % 2 == 0:
                                vv = VT[:, 4 + kr // 2, hs]
                                vb = VT
                            else:
                                vv = VTS[:, (kr - 1) // 2, hs]
                                vb = VTS
                            lhs = ONESB[:, 0:64] if isden else vv
                            P.add("pe", lambda e: e.matmul(
                                pbx[hs, osl], lhs, ptl[:, i, :], start=(i == 0), stop=False),
                                reads=[vb, ONESB, ptl], writes=[pbx])
                        for kb in range(2):
                            lhs = ONESB[:, 0:64] if isden else VC[:, kb, hs]
                            P.add("pe", lambda e: e.matmul(
                                pbx[hs, osl], lhs, PTC[:, kb, 64 * r:64 * r + 64], start=False, stop=(kb == 1)),
                                reads=[VC, ONESB, PTC], writes=[pbx])

                ptl_prev = row_scores(0)
                for r in range(1, 16):
                    ptl_cur = row_scores(r)
                    row_pv(r - 1, ptl_prev)
                    ptl_prev = ptl_cur
                row_pv(15, ptl_prev)
            for half in range(2):
                P.add("act", lambda e, half=half: e.activation(RD[:, :], pbd_s[half][:, :], AF.Ln),
                      reads=[pbd_s[half]], writes=[RD])
                P.add("act", lambda e: e.activation(RD[:, :], RD[:, :], AF.Exp, scale=-1.0), reads=[RD], writes=[RD])
                P.add("dve", lambda e, half=half: e.tensor_tensor(
                    OT[:, c, 512 + half * 512:512 + (half + 1) * 512], pbo_s[half][:, :], RD[:, :], ALU.mult),
                    reads=[pbo_s[half], RD], writes=[OTb[c]])
        P.fence()
        emit_wout(l, na_w_out[j], OT, OTb)


    C_I = CONST[:, 0:128]
    C_TRIF = CONST[:, 256:384]
    C_TRIB = CONST[:, 384:512]
    C_EVEN = CONST[:, 512:640]
    C_ODD = CONST[:, 640:768]
    C_MINC = [CONST[:, 768:896], CONST[:, 1024:1152]]
    C_MSTR = [CONST[:, 896:1024], CONST[:, 1152:1280]]
    C_ONES = CONST[:, 1280:1408]
    C_NEG1 = CONST[:, 1408:1536]
    GCW = P.sb("GCW", [128, 2, 3, 24], F32)
    load_fm(GCW, GCW[:, :, :, :].rearrange("p j t k -> p (j t k)"),
            gdn_conv_w.rearrange("j t (k f) -> (j t k) f", f=128), 144)
    GNG = P.sb("GNG", [128, 2], F32)
    load_fm(GNG, GNG[:, :], gdn_norm_g, 2)
    ZEROC = P.sb("ZEROC", [128, 1], F32)
    P.add("dve", lambda e: e.memset(ZEROC[:, :], 0.0), writes=[ZEROC])
    ev_rr = [0]

    def evac(dst_ap, dst_bufs, src_ap, pbs, scale=None):
        use_act = True
        ev_rr[0] += 1
        if use_act:
            if scale is None:
                P.add("act", lambda e: e.copy(dst_ap, src_ap), reads=pbs, writes=dst_bufs)
            else:
                P.add("act", lambda e: e.activation(dst_ap, src_ap, AF.Copy, scale=scale), reads=pbs, writes=dst_bufs)
        else:
            if scale is None:
                P.add("dve", lambda e: e.tensor_copy(dst_ap, src_ap), reads=pbs, writes=dst_bufs)
            else:
                P.add("dve", lambda e: e.tensor_scalar(dst_ap, src_ap, scale, None, ALU.mult), reads=pbs, writes=dst_bufs)

    def emit_gdn(l):
        j = l // 2
        off = [0]

        def AR(dtype, shape, name):
            esz = 2 if dtype == BF16 else 4
            n = esz
            for d_ in shape[1:]:
                n *= d_
            v = carve(ARENA, off[0], dtype, shape, name)
            off[0] += (n + 63) // 64 * 64
            assert off[0] <= 55296, off[0]
            return v

        TK = [AR(F32, [128, 12, 16], "TK%d" % i) for i in range(12)]
        GTOK, GC, BETA, GCB, EGC, BEG, GLE, GLO, EDE, EDO, TMPA, TMPB = TK
        AB = P.view("AB", ARENA.t[:, (10 * 768) // 4:(12 * 768) // 4].rearrange("p (b c) -> p b c", b=12))
        QNb = AR(BF16, [128, NT], "QNb")
        KNb = AR(BF16, [128, NT], "KNb")
        OTh = AR(BF16, [128, NT], "OTh")
        BS = []
        BSR = []
        for d_ in range(2):
            row, rowr = [], []
            for i in range(3):
                k_ = d_ * 3 + i
                apr = CHAIN[:, k_ * 512:(k_ + 1) * 512].rearrange("p (a b) -> p a b", a=4)
                apf = CHAIN.bitcast(F32)[:, k_ * 512:(k_ + 1) * 512].rearrange("p (a b) -> p a b", a=4)
                row.append(P.view("BS%d_%d" % (d_, i), apf))
                rowr.append(apr)
            BS.append(row)
            BSR.append(rowr)
        TTbs = [AR(BF16, [128, 4, 128], "TTb%d" % d_) for d_ in range(2)]
        VBs = [AR(BF16, [128, 4, 128], "VB%d" % d_) for d_ in range(2)]
        KBGs = [AR(BF16, [128, 4, 128], "KBG%d" % d_) for d_ in range(2)]
        F = [b_.alias(b_.t[:, :].rearrange("p (a b) -> p a b", a=4)) for b_ in (RSTD[0], RSTD[1], TMPN[0], TMPN[1])]
        TC = F
        SETS = []
        for i in range(4):
            SETS.append(dict(U=AR(BF16, [128, 4, 128], "U%d" % i), NWT=AR(BF16, [128, 4, 128], "NWT%d" % i),
                             PT=AR(BF16, [128, 4, 128], "PT%d" % i), KD=[AR(BF16, [128, 4, 128], "KDe%d" % i),
                                                                        AR(BF16, [128, 4, 128], "KDo%d" % i)],
                             QG=AR(BF16, [128, 4, 128], "QG%d" % i)))
        CH = []
        for i in range(4):
            CH.append(dict(S=AR(F32, [128, 128], "S%d" % i), Sb=AR(BF16, [128, 128], "Sb%d" % i),
                           VN=AR(BF16, [128, 128], "VN%d" % i)))
        WAB = AR(BF16, [128, KC, 32], "WAB")
        DTB16 = AR(F32, [128, 16], "DTB16")
        NEGA = AR(F32, [128, 16], "NEGA")
        WI = [carve(WREG, 4096 + i * 4096, BF16, [128, KC, 2, 128], "WI%d" % i) for i in range(2)]
        WOH = [carve(WREG, i * 2048, BF16, [128, D], "WOH%d" % i) for i in range(2)]
        CTQ, CTK, CTV, CTZ = CT
        OF = CTQ
        OFb = [OF.sub("OF%d" % b) for b in range(12)]
        for ch in CH:
            P.add("pool", lambda e, ch=ch: e.memset(ch["VN"][:, :], 0.0), writes=[ch["VN"]])

        P.dma(WAB[:, :, :].rearrange("p k n -> p (k n)"), gdn_w_ab[j, :, :], writes=[WAB], q="pool")
        P.dma(DTB16[:, :], gdn_dt_bias[j:j + 1, :].to_broadcast([128, 16]), writes=[DTB16])
        P.dma(NEGA[:, :], gdn_a_log[j:j + 1, :].to_broadcast([128, 16]), writes=[NEGA])
        P.add("act", lambda e: e.activation(NEGA[:, :], NEGA[:, :], AF.Exp), reads=[NEGA], writes=[NEGA])
        pb = psum()
        for blk in range(12):
            for kc in range(KC):
                P.add("pe", lambda e, pb=pb, blk=blk, kc=kc: e.matmul(
                    pb[:, blk * 32:(blk + 1) * 32], H[:, kc, blk * 128:(blk + 1) * 128], WAB[:, kc, :],
                    start=(kc == 0), stop=(kc == KC - 1)), reads=[WAB, Hb[(kc, blk // 4)]], writes=[pb])
        P.add("dve", lambda e, pb=pb: e.tensor_copy(AB[:, :, :], pb[:, 0:384].rearrange("p (b c) -> p b c", b=12)),
              reads=[pb], writes=[TMPA, TMPB])
        bc16 = lambda t: t[:, :].unsqueeze(1).to_broadcast([128, 12, 16])
        P.add("dve", lambda e: e.tensor_tensor(GTOK[:, :, :], AB[:, :, 0:16], bc16(DTB16), ALU.add),
              reads=[TMPA, TMPB, DTB16], writes=[GTOK])
        P.add("act", lambda e: e.activation(GTOK[:, :, :], GTOK[:, :, :], AF.Exp), reads=[GTOK], writes=[GTOK])
        P.add("dve", lambda e: e.tensor_scalar(GTOK[:, :, :], GTOK[:, :, :], 1.0, None, ALU.add), reads=[GTOK], writes=[GTOK])
        P.add("act", lambda e: e.activation(GTOK[:, :, :], GTOK[:, :, :], AF.Ln), reads=[GTOK], writes=[GTOK])
        P.add("dve", lambda e: e.scalar_tensor_tensor(GTOK[:, :, :], GTOK[:, :, :], -1.0, bc16(NEGA), ALU.mult, ALU.mult),
              reads=[GTOK, NEGA], writes=[GTOK])
        P.add("act", lambda e: e.activation(BETA[:, :, :], AB[:, :, 16:32], AF.Exp, scale=-1.0),
              reads=[TMPA, TMPB], writes=[BETA])
        P.add("dve", lambda e: e.tensor_scalar(BETA[:, :, :], BETA[:, :, :], 1.0, None, ALU.add), reads=[BETA], writes=[BETA])
        P.add("act", lambda e: e.activation(GCB[:, :, :], BETA[:, :, :], AF.Ln), reads=[BETA], writes=[GCB])
        P.add("dve", lambda e: e.reciprocal(BETA[:, :, :], BETA[:, :, :]), reads=[BETA], writes=[BETA])
        pb = psum()
        for blk in range(12):
            for d in range(2):
                tri = C_TRIF if d == 0 else C_TRIB
                P.add("pe", lambda e, pb=pb, blk=blk, d=d, tri=tri: e.matmul(
                    pb[:, blk * 16 + d * 8:blk * 16 + d * 8 + 8], tri, GTOK[:, blk, d * 8:d * 8 + 8],
                    start=True, stop=True), reads=[CONST, GTOK], writes=[pb])
        P.add("dve", lambda e, pb=pb: e.tensor_copy(GC[:, :, :], pb[:, 0:192].rearrange("p (b c) -> p b c", b=12)),
              reads=[pb], writes=[GC])
        for (dst, cm) in ((GLE, C_EVEN), (GLO, C_ODD)):
            pb = psum()
            for blk in range(12):
                P.add("pe", lambda e, pb=pb, blk=blk, cm=cm: e.matmul(
                    pb[:, blk * 16:(blk + 1) * 16], cm, GTOK[:, blk, :], start=True, stop=True),
                    reads=[CONST, GTOK], writes=[pb])
            P.add("dve", lambda e, pb=pb, dst=dst: e.tensor_copy(
                dst[:, :, :], pb[:, 0:192].rearrange("p (b c) -> p b c", b=12)), reads=[pb], writes=[dst])
        P.add("dve", lambda e: e.tensor_tensor(GCB[:, :, :], GC[:, :, :], GCB[:, :, :], ALU.subtract),
              reads=[GC, GCB], writes=[GCB])
        P.add("act", lambda e: e.activation(EGC[:, :, :], GC[:, :, :], AF.Exp), reads=[GC], writes=[EGC])
        P.add("dve", lambda e: e.tensor_tensor(BEG[:, :, :], BETA[:, :, :], EGC[:, :, :], ALU.mult),
              reads=[BETA, EGC], writes=[BEG])
        for (dst, gl, mcol) in ((EDE, GLE, C_EVEN[:, 0:1]), (EDO, GLO, C_ODD[:, 0:1])):
            P.add("dve", lambda e, dst=dst, gl=gl: e.tensor_tensor(dst[:, :, :], gl[:, :, :], GC[:, :, :], ALU.subtract),
                  reads=[gl, GC], writes=[dst])
            P.add("dve", lambda e, dst=dst: e.tensor_scalar(dst[:, :, :], dst[:, :, :], 0.0, None, ALU.min),
                  reads=[dst], writes=[dst])
            P.add("act", lambda e, dst=dst: e.activation(dst[:, :, :], dst[:, :, :], AF.Exp), reads=[dst], writes=[dst])
            P.add("dve", lambda e, dst=dst, mcol=mcol: e.tensor_scalar(dst[:, :, :], dst[:, :, :], mcol, None, ALU.mult),
                  reads=[dst, CONST], writes=[dst])
        P.add("act", lambda e: e.activation(GLE[:, :, :], GLE[:, :, :], AF.Exp), reads=[GLE], writes=[GLE])
        P.add("act", lambda e: e.activation(GLO[:, :, :], GLO[:, :, :], AF.Exp), reads=[GLO], writes=[GLO])
        EGL = [GLE, GLO]
        ED = [EDE, EDO]
        NGC = TMPA
        P.add("dve", lambda e: e.tensor_scalar(NGC[:, :, :], GC[:, :, :], -1.0, None, ALU.mult), reads=[GC], writes=[NGC])

        wis_ = WStream(WI, 16, lambda i, wi: P.dma(wi[:, :, :, :].rearrange("p k g n -> p (k g n)"),
                                                   gdn_w_in[j, i // 2, i % 2, :, :], writes=[wi], q="pool"))
        whs_ = WStream(WOH, 8, lambda i, woh: P.dma(woh[:, :], gdn_w_out[j, i * 128:(i + 1) * 128, :],
                                                    writes=[woh], q="pool"))
        PRE_BANKS, SCAN_BANKS = [0, 1, 2, 3, 4], [5, 6, 7]
        pre_rr, scan_rr = [0], [0]
        for h in range(8):
            pump(3)
            for pi in range(2):
                wi = wis_.need(h * 2 + pi)
                for g in range(2):
                    t = pi * 2 + g
                    b0 = bank_group()
                    for tb in range(NTB):
                        pbk = PS[b0 + tb]
                        for kc in range(KC):
                            P.add("pe", lambda e, pbk=pbk, wi=wi, kc=kc, g=g, tb=tb: e.matmul(
                                pbk[:, :], wi[:, kc, g, :], H[:, kc, tb * TB:(tb + 1) * TB],
                                start=(kc == 0), stop=(kc == KC - 1)), reads=[wi, Hb[(kc, tb)]], writes=[pbk])
                    ct = CT[t]
                    if t < 3:
                        ch = t * 8 + h
                        conv_from_psum(b0, GCW[:, j, 0, ch:ch + 1], GCW[:, j, 1, ch:ch + 1], GCW[:, j, 2, ch:ch + 1],
                                       ZEROC[:, 0:1], ct, ct)
                        P.add("act", lambda e, ct=ct: e.activation(ct[:, :], ct[:, :], AF.Silu), reads=[ct], writes=[ct])
                    else:
                        P.add("act", lambda e, ct=ct, b0=b0: e.activation(ct[:, 0:512], PS[b0][:, :], AF.Silu),
                              reads=[PS[b0]], writes=[ct])
                        P.add("act", lambda e, ct=ct, b0=b0: e.activation(
                            ct[:, 512:NT], PSALL[:, (b0 + 1) * 512:(b0 + 3) * 512], AF.Silu),
                            reads=[PS[b0 + 1], PS[b0 + 2]], writes=[ct])
            f2 = lambda T_: T_[:, :, :].rearrange("p a b -> p (a b)")
            for (ct, dstb, keep) in ((CTQ, QNb, False), (CTK, KNb, True)):
                sls = [slice(tb * TB, (tb + 1) * TB) for tb in range(NTB)]
                for tb in range(NTB):
                    P.add("act", lambda e: e.activation(f2(TC[tb]), ct[:, sls[tb]], AF.Square), reads=[ct], writes=[TC[tb]])
                pbks = []
                for tb in range(NTB):
                    pbk = psum()
                    pbks.append(pbk)
                    P.add("pe", lambda e: e.matmul(pbk[:, :], C_ONES, f2(TC[tb]), start=True, stop=True),
                          reads=[CONST, TC[tb]], writes=[pbk])
                for tb in range(NTB):
                    P.add("act", lambda e: e.activation(f2(TC[tb]), pbks[tb][:, :], AF.Ln, bias=EPSC[:, 0:1], scale=1.0),
                          reads=[pbks[tb], EPSC], writes=[TC[tb]])
                for tb in range(NTB):
                    P.add("act", lambda e: e.activation(f2(TC[tb]), f2(TC[tb]), AF.Exp, scale=-0.5),
                          reads=[TC[tb]], writes=[TC[tb]])
                for tb in range(NTB):
                    sl = sls[tb]
                    if keep:
                        P.add("pool", lambda e: e.tensor_tensor(ct[:, sl], ct[:, sl], f2(TC[tb]), ALU.mult),
                              reads=[ct, TC[tb]], writes=[ct])
                        P.add("act", lambda e: e.copy(dstb[:, sl], ct[:, sl]), reads=[ct], writes=[dstb])
                    else:
                        P.add("pool", lambda e: e.tensor_tensor(dstb[:, sl], ct[:, sl], f2(TC[tb]), ALU.mult),
                              reads=[ct, TC[tb]], writes=[dstb])
            P.add("pool", lambda e: e.memset(OF[:, :], 0.0), writes=[OF] + OFb)

            f4 = lambda T_: T_[:, :, :].rearrange("p a b -> p (a b)")
            v4 = lambda pb_: pb_[:, :].rearrange("p (b f) -> p b f", b=4)
            Ibc = C_I.unsqueeze(1).to_broadcast([128, 4, 128])

            def pre_dir(g, d, st, pk, pv, pg, pq):
                cd = d * 8 + h
                blks = slice(g * 4, g * 4 + 4)
                tsl = slice(g * 512, (g + 1) * 512)
                col = lambda T_: T_[:, blks, cd:cd + 1].to_broadcast([128, 4, 128])
                B = BS[d]
                Fa, Fb = F[2 * d], F[2 * d + 1]
                VB, KBG = VBs[d], KBGs[d]
                P.add("dve", lambda e: e.tensor_tensor(VB[:, :, :], v4(pv), col(BETA), ALU.mult),
                      reads=[pv, BETA], writes=[VB])
                P.add("dve", lambda e: e.tensor_tensor(KBG[:, :, :], v4(pk), col(BEG), ALU.mult),
                      reads=[pk, BEG], writes=[KBG])
                for u in range(2):
                    P.add("dve", lambda e: e.tensor_tensor(st["KD"][u][:, :, :], v4(pk), col(ED[u]), ALU.mult),
                          reads=[pk, ED[u]], writes=[st["KD"][u]])
                yield
                P.add("pool", lambda e: e.tensor_tensor(Fa[:, :, :], Ibc, col(EGC), ALU.mult),
                      reads=[CONST, EGC], writes=[Fa])
                pr = psum_of(PRE_BANKS, pre_rr)
                for b in range(4):
                    P.add("pe", lambda e: e.matmul(pr[:, b * 128:(b + 1) * 128], C_ONES, Fa[:, b, :],
                                                   start=True, stop=True), reads=[CONST, Fa], writes=[pr])
                P.add("dve", lambda e: e.scalar_tensor_tensor(f4(st["QG"]), pr[:, :], 128.0 ** -0.5, QNb[:, tsl],
                                                              ALU.mult, ALU.mult), reads=[pr, QNb], writes=[st["QG"]])
                yield
                P.add("pool", lambda e: e.tensor_tensor(Fa[:, :, :], Ibc, col(GC), ALU.mult),
                      reads=[CONST, GC], writes=[Fa])
                P.add("pool", lambda e: e.tensor_tensor(Fb[:, :, :], Ibc, col(GCB), ALU.mult),
                      reads=[CONST, GCB], writes=[Fb])
                pa = psum_of(PRE_BANKS, pre_rr)
                pbb = psum_of(PRE_BANKS, pre_rr)
                for (pz, dg, msk) in ((pa, Fa, C_MINC[d]), (pbb, Fb, C_MSTR[d])):
                    for b in range(4):
                        o_ = pz[:, b * 128:(b + 1) * 128]
                        P.add("pe", lambda e: e.matmul(o_, C_ONES, dg[:, b, :], start=True, stop=False),
                              reads=[CONST, dg], writes=[pz])
                        P.add("pe", lambda e: e.matmul(o_, C_I, msk, start=False, stop=True),
                              reads=[CONST], writes=[pz])
                BT, Bs_, M_ = B
                BTr, Bsr, Mr = [x_.t for x_ in B]
                r4 = lambda ap_: ap_[:, :, :].rearrange("p a b -> p (a b)")
                for b in range(4):
                    ngc = NGC[:, g * 4 + b, cd:cd + 1]
                    P.add("act", lambda e: e.activation(Mr[:, b, :], pa[:, b * 128:(b + 1) * 128], AF.Exp, bias=ngc),
                          reads=[pa, NGC], writes=[M_])
                    P.add("act", lambda e: e.activation(BTr[:, b, :], pbb[:, b * 128:(b + 1) * 128], AF.Exp, bias=ngc),
                          reads=[pbb, NGC], writes=[BT])
                yield
                pg = psum_of(PRE_BANKS, pre_rr)
                pq = psum_of(PRE_BANKS, pre_rr)
                for b in range(4):
                    c0 = g * 512 + b * 128
                    P.add("pe", lambda e: e.matmul(pg[:, b * 128:(b + 1) * 128], KNb[:, c0:c0 + 128], KNb[:, c0:c0 + 128],
                                                   start=True, stop=True), reads=[KNb], writes=[pg])
                for b in range(4):
                    c0 = g * 512 + b * 128
                    P.add("pe", lambda e: e.matmul(pq[:, b * 128:(b + 1) * 128], KNb[:, c0:c0 + 128], QNb[:, c0:c0 + 128],
                                                   start=True, stop=True), reads=[KNb, QNb], writes=[pq])
                P.add("dve", lambda e: e.scalar_tensor_tensor(f4(st["PT"]), pq[:, :], 128.0 ** -0.5, f4(M_),
                                                              ALU.mult, ALU.mult), reads=[pq, M_], writes=[st["PT"]])
                P.add("dve", lambda e: e.scalar_tensor_tensor(r4(BTr), pg[:, :], -1.0, f4(BT), ALU.mult, ALU.mult),
                      reads=[pg, BT], writes=[BT])
                yield
                pt_ = psum_of(PRE_BANKS, pre_rr)
                for b in range(4):
                    P.add("pe", lambda e: e.transpose(pt_[:, b * 128:(b + 1) * 128], BT[:, b, :], C_I),
                          reads=[BT, CONST], writes=[pt_])
                evac(r4(Bsr), [Bs_], pt_[:, :], [pt_])
                P.add("pool", lambda e: e.tensor_tensor(Mr[:, :, :], BT[:, :, :], Ibc, ALU.add),
                      reads=[BT, CONST], writes=[M_])
                yield
                TTb = TTbs[d]
                for k in range(5):
                    if k < 4:
                        px = psum_of(PRE_BANKS, pre_rr)
                        for b in range(4):
                            P.add("pe", lambda e: e.matmul(px[:, b * 128:(b + 1) * 128], Bsr[:, b, :], BTr[:, b, :],
                                                           start=True, stop=True), reads=[Bs_, BT], writes=[px])
                    py = psum_of(PRE_BANKS, pre_rr)
                    for b in range(4):
                        P.add("pe", lambda e: e.matmul(py[:, b * 128:(b + 1) * 128], BTr[:, b, :], Bsr[:, b, :],
                                                       start=True, stop=True), reads=[Bs_, BT], writes=[py])
                    if k < 4:
                        evac(r4(BTr), [BT], px[:, :], [px])
                    evac(r4(Bsr), [Bs_], py[:, :], [py])
                    yield
                    pm_ = psum_of(PRE_BANKS, pre_rr)
                    for b in range(4):
                        o_ = pm_[:, b * 128:(b + 1) * 128]
                        P.add("pe", lambda e: e.matmul(o_, Bsr[:, b, :], Mr[:, b, :], start=True, stop=True),
                              reads=[Bs_, M_], writes=[pm_])
                    if k < 4:
                        P.add("dve", lambda e: e.tensor_tensor(r4(Mr), pm_[:, :], f4(M_), ALU.add),
                              reads=[pm_, M_], writes=[M_])
                    else:
                        P.add("dve", lambda e: e.tensor_tensor(f4(TTb), pm_[:, :], f4(M_), ALU.add),
                              reads=[pm_, M_], writes=[TTb])
                    yield
                pu = psum_of(PRE_BANKS, pre_rr)
                pw = psum_of(PRE_BANKS, pre_rr)
                for b in range(4):
                    P.add("pe", lambda e: e.matmul(pu[:, b * 128:(b + 1) * 128], TTb[:, b, :], VB[:, b, :],
                                                   start=True, stop=True), reads=[TTb, VB], writes=[pu])
                for b in range(4):
                    P.add("pe", lambda e: e.matmul(pw[:, b * 128:(b + 1) * 128], KBG[:, b, :], TTb[:, b, :],
                                                   start=True, stop=True), reads=[TTb, KBG], writes=[pw])
                evac(f4(st["U"]), [st["U"]], pu[:, :], [pu])
                evac(f4(st["NWT"]), [st["NWT"]], pw[:, :], [pw], scale=-1.0)
                yield

            def precompute2_gen(g, sts):
                pk = psum_of(PRE_BANKS, pre_rr)
                pv = psum_of(PRE_BANKS, pre_rr)
                for b in range(4):
                    c0 = g * 512 + b * 128
                    P.add("pe", lambda e: e.transpose(pk[:, b * 128:(b + 1) * 128], CTK[:, c0:c0 + 128], C_I),
                          reads=[CTK, CONST], writes=[pk])
                for b in range(4):
                    c0 = g * 512 + b * 128
                    P.add("pe", lambda e: e.transpose(pv[:, b * 128:(b + 1) * 128], CTV[:, c0:c0 + 128], C_I),
                          reads=[CTV, CONST], writes=[pv])
                gens = [pre_dir(g, d_, sts[d_], pk, pv, None, None) for d_ in range(2)]
                while gens:
                    for gi in list(gens):
                        try:
                            next(gi)
                        except StopIteration:
                            gens.remove(gi)
                    yield

            def interleave(gen, rounds, every):
                i = 0
                r = 0
                for _ in gen:
                    i += 1
                    if i % every == 0 and r < len(rounds):
                        rounds[r]()
                        r += 1
                while r < len(rounds):
                    rounds[r]()
                    r += 1

            def scan_step(ch, st, d, blk, u):
                cd = d * 8 + h
                bi = blk % 4
                rsl = slice(u * 64, u * 64 + 64)
                S, Sb, VN = ch["S"], ch["Sb"], ch["VN"]
                p1 = psum_of(SCAN_BANKS, scan_rr)
                P.add("pe", lambda e: e.matmul(p1[rsl, 0:128], st["NWT"][:, bi, rsl], Sb[:, :], start=True, stop=True),
                      reads=[st["NWT"], Sb], writes=[p1])
                P.add("dve", lambda e: e.tensor_tensor(VN[rsl, :], st["U"][rsl, bi, :], p1[rsl, 0:128], ALU.add),
                      reads=[st["U"], p1], writes=[VN])
                p2 = psum_of(SCAN_BANKS, scan_rr)
                P.add("pe", lambda e: e.matmul(p2[:, 0:64], Sb[:, :], st["QG"][:, bi, rsl], start=True, stop=False),
                      reads=[Sb, st["QG"]], writes=[p2])
                P.add("pe", lambda e: e.matmul(p2[:, 0:64], VN[:, :], st["PT"][:, bi, rsl], start=False, stop=True),
                      reads=[VN, st["PT"]], writes=[p2])
                c0 = blk * 128 + u * 64
                P.add("dve", lambda e: e.tensor_tensor(OF[:, c0:c0 + 64], OF[:, c0:c0 + 64], p2[:, 0:64], ALU.add),
                      reads=[OFb[blk], p2], writes=[OFb[blk]])
                p3 = psum_of(SCAN_BANKS, scan_rr)
                P.add("pe", lambda e: e.matmul(p3[:, 0:128], st["KD"][u][:, bi, :], VN[:, :], start=True, stop=True),
                      reads=[st["KD"][u], VN], writes=[p3])
                P.add("dve", lambda e: e.scalar_tensor_tensor(Sb[:, :], S[:, :], EGL[u][:, blk, cd:cd + 1], p3[:, 0:128],
                                                              ALU.mult, ALU.add), reads=[S, EGL[u], p3], writes=[Sb])
                P.add("dve", lambda e: e.scalar_tensor_tensor(S[:, :], S[:, :], EGL[u][:, blk, cd:cd + 1], p3[:, 0:128],
                                                              ALU.mult, ALU.add), reads=[S, EGL[u], p3], writes=[S])

            def chain_steps(d, blocks):
                steps = [(b, u) for b in blocks for u in range(2)]
                return steps if d == 0 else steps[::-1]

            def init_chain(ch, d, seq):
                if seq < 2:
                    P.add("pool", lambda e: e.memset(ch["S"][:, :], 0.0), writes=[ch["S"]])
                else:
                    P.dma(ch["S"][:, :], state_gdn[j, d, h, :, :], writes=[ch["S"]])
                P.add("act", lambda e: e.copy(ch["Sb"][:, :], ch["S"][:, :]), reads=[ch["S"]], writes=[ch["Sb"]])

            for _ in precompute2_gen(0, [SETS[2], SETS[3]]):
                pass
            chains = []
            for d in range(2):
                for seq in range(2):
                    ch = CH[d * 2 + seq]
                    init_chain(ch, d, seq)
                    chains.append((ch, SETS[2 + d], d, seq, chain_steps(d, [2 * seq, 2 * seq + 1])))

            def roundA(si):
                for (ch, st, d, seq, steps) in chains:
                    scan_step(ch, st, d, steps[si][0], steps[si][1])

            interleave(precompute2_gen(1, [SETS[0], SETS[1]]), [lambda si=si: roundA(si) for si in range(4)], 3)
            for (ch, st, d, seq, steps) in chains:
                P.dma(st_out[seq, j, d, h, :, :], ch["S"][:, :], reads=[ch["S"]])
            chB = [CH[0], CH[1]]
            stepsB = [chain_steps(d, list(range(4, 12))) for d in range(2)]
            for d in range(2):
                init_chain(chB[d], d, 2)

            def stepB(d, si):
                blk, u = stepsB[d][si]
                scan_step(chB[d], SETS[(blk // 4 - 1) * 2 + d], d, blk, u)

            interleave(precompute2_gen(2, [SETS[2], SETS[3]]), [lambda si=si: stepB(0, si) for si in range(8)], 2)
            for si in range(16):
                stepB(1, si)
                if si < 8:
                    stepB(0, 8 + si)

            sls = [slice(tb * TB, (tb + 1) * TB) for tb in range(NTB)]
            for tb in range(NTB):
                P.add("act", lambda e: e.activation(f2(TC[tb]), OF[:, sls[tb]], AF.Square),
                      reads=OFb[tb * 4:tb * 4 + 4] + [OF], writes=[TC[tb]])
            pbks = []
            for tb in range(NTB):
                pbk = psum()
                pbks.append(pbk)
                P.add("pe", lambda e: e.matmul(pbk[:, :], C_ONES, f2(TC[tb]), start=True, stop=True),
                      reads=[CONST, TC[tb]], writes=[pbk])
            for tb in range(NTB):
                P.add("act", lambda e: e.activation(f2(TC[tb]), pbks[tb][:, :], AF.Ln, bias=EPSC[:, 0:1], scale=1.0 / 128),
                      reads=[pbks[tb], EPSC], writes=[TC[tb]])
            for tb in range(NTB):
                P.add("act", lambda e: e.activation(f2(TC[tb]), f2(TC[tb]), AF.Exp, scale=-0.5), reads=[TC[tb]], writes=[TC[tb]])
            for tb in range(NTB):
                sl = sls[tb]
                P.add("pool", lambda e: e.tensor_tensor(f2(TC[tb]), OF[:, sl], f2(TC[tb]), ALU.mult),
                      reads=OFb[tb * 4:tb * 4 + 4] + [TC[tb], OF], writes=[TC[tb]])
                P.add("dve", lambda e: e.scalar_tensor_tensor(
                    OTh[:, sl], f2(TC[tb]), GNG[:, j:j + 1], CTZ[:, sl], ALU.mult, ALU.mult),
                    reads=[TC[tb], GNG, CTZ], writes=[OTh])
            woh = whs_.need(h)
            for n in range(KC):
                for tb in range(NTB):
                    pbk = psum()
                    P.add("pe", lambda e, pbk=pbk, n=n, tb=tb, woh=woh: e.matmul(
                        pbk[:, :], woh[:, n * 128:(n + 1) * 128], OTh[:, tb * TB:(tb + 1) * TB], start=True, stop=True),
                        reads=[woh, OTh], writes=[pbk])
                    resid_from_psum(pbk, l, 16, n, tb)

    P.fence()
    emit_rbpad()
    mod_state["gen"] = mod_gen(0)
    pump(24)
    for l in range(cfg.n_layers):
        emit_norm_mod(l, 1)
        P.fence()
        if l % 2 == 1 and cfg.do_na:
            emit_na(l)
        if l % 2 == 0 and cfg.do_gdn:
            emit_gdn(l)
        pump(48)
        emit_norm_mod(l, 2)
        P.fence()
        if l + 1 < cfg.n_layers:
            mod_state["gen"] = mod_gen(l + 1)
        emit_ffn(l)
        pump(48)
    P.fence()

    FG = carve(ARENA, 8192, F32, [128, D], "FG")
    P.dma(FG[:, :], final_g.rearrange("(o d) -> o d", o=1).to_broadcast([128, D]), writes=[FG])
    YT = [carve(ARENA, i * 4096, F32, [128, D], "YT%d" % i) for i in range(2)]
    YSQ = carve(ARENA, 12288, F32, [128, D], "YSQ")
    SS = [P.sb("SS%d" % i, [128, 1], F32) for i in range(2)]
    for blk in range(NT // 128):
        yt = YT[blk % 2]
        ss = SS[blk % 2]
        tb = (blk * 128) // TB
        for half in range(2):
            pb = psum()
            for j in range(4):
                kc = half * 4 + j
                P.add("pe", lambda e, pb=pb, kc=kc, j=j, blk=blk: e.transpose(
                    pb[:, j * 128:(j + 1) * 128], X[:, kc, blk * 128:(blk + 1) * 128], CONST[:, 0:128]),
                    reads=[Xb[(kc, tb)], CONST], writes=[pb])
            P.add("act", lambda e, pb=pb, yt=yt, half=half: e.copy(yt[:, half * 512:(half + 1) * 512], pb[:, :]),
                  reads=[pb], writes=[yt])
        P.add("act", lambda e, yt=yt, ss=ss: e.activation(YSQ[:, :], yt[:, :], AF.Square, accum_out=ss[:, 0:1]),
              reads=[yt], writes=[YSQ, ss])
        P.add("act", lambda e, ss=ss: e.activation(ss[:, :], ss[:, :], AF.Ln, bias=EPSC[:, 0:1], scale=1.0 / D),
              reads=[ss, EPSC], writes=[ss])
        P.add("act", lambda e, ss=ss: e.activation(ss[:, :], ss[:, :], AF.Exp, scale=-0.5), reads=[ss], writes=[ss])
        P.add("dve", lambda e, yt=yt, ss=ss: e.scalar_tensor_tensor(
            yt[:, :], yt[:, :], ss[:, 0:1], FG[:, :], ALU.mult, ALU.mult), reads=[yt, ss, FG], writes=[yt])
        P.dma(y_out[blk * 128:(blk + 1) * 128, :], yt[:, :], reads=[yt])

    P.emit()
    return nc


def make_consts():
    c = np.zeros((128, 1536), np.float32)
    c[:, 0:128] = np.eye(128, dtype=np.float32)
    J = np.zeros((64, 64), np.float32)
    J[np.arange(64), 63 - np.arange(64)] = 1.0
    c[0:64, 128:192] = J
    c[64:128, 128:192] = J
    qc = 63 - np.arange(64)[:, None]
    kc = np.arange(64)[None, :]
    cs = np.clip(qc - 8, 0, 48)
    inside = (kc >= cs) & (kc < cs + 16)
    c[0:64, 192:256] = np.where(inside, 0.0, -30000.0)
    jj = np.arange(128)[:, None]
    ii = np.arange(128)[None, :]
    same = (jj // 64) == (ii // 64)
    c[:, 256:384] = (same & (jj <= ii)).astype(np.float32)
    c[:, 384:512] = (same & (jj >= ii)).astype(np.float32)
    c[:, 512:640] = (jj < 64).astype(np.float32) * np.ones((1, 128), np.float32)
    c[:, 640:768] = (jj >= 64).astype(np.float32) * np.ones((1, 128), np.float32)
    NEG = -30000.0
    c[:, 768:896] = np.where(same & (jj <= ii), 0.0, NEG)
    c[:, 896:1024] = np.where(same & (jj < ii), 0.0, NEG)
    c[:, 1024:1152] = np.where(same & (jj >= ii), 0.0, NEG)
    c[:, 1152:1280] = np.where(same & (jj > ii), 0.0, NEG)
    c[:, 1280:1408] = 1.0
    c[:, 1408:1536] = -1.0
    return c


_NC_CACHE = {}


def kernel(x_prompt, x_sample, state_gdn, cache_k, cache_v, c, c_ctx, w_ada, b_ada, norm1_g, norm2_g,
           gdn_w_in, gdn_conv_w, gdn_a_log, gdn_dt_bias, gdn_norm_g, gdn_w_out,
           na_w_qkv, na_rel_bias, na_w_out, ffn_w_up, ffn_conv_w, ffn_conv_b, ffn_w_down, final_g, _cfg=Cfg, _cores=8):
    f = lambda a: np.ascontiguousarray(np.asarray(a, dtype=np.float32))
    nc = build(_cfg)
    consts = make_consts()
    c_ = np.ascontiguousarray
    w_ada_t = c_(f(w_ada).reshape(4, 8, 128, 48, 128).transpose(0, 3, 2, 1, 4)).reshape(4, 48, 128, 1024)
    w_up_t = c_(f(ffn_w_up).reshape(4, 8, 128, 2, 22, 128).transpose(0, 4, 2, 1, 3, 5)).reshape(4, 22, 128, 2048)
    w_dn_t = c_(f(ffn_w_down).reshape(4, 2, 11, 128, 8, 128).transpose(0, 1, 4, 3, 2, 5)).reshape(4, 2, 8, 128, 1408)
    qkv_t = c_(f(na_w_qkv).reshape(2, 8, 128, 3, 8, 128).transpose(0, 4, 2, 1, 3, 5)).reshape(2, 8, 128, 3072)
    nwo_t = c_(f(na_w_out).reshape(2, 8, 128, 8, 128).transpose(0, 3, 2, 1, 4)).reshape(2, 8, 128, 1024)
    gin = f(gdn_w_in)
    gin_t = c_(gin[:, :, :4096].reshape(2, 8, 128, 2, 2, 8, 128).transpose(0, 5, 3, 2, 1, 4, 6)).reshape(2, 8, 2, 128, 2048)
    gab_t = c_(gin[:, :, 4096:4128].reshape(2, 8, 128, 32).transpose(0, 2, 1, 3)).reshape(2, 128, 256)
    shared = {
        "w_ada": w_ada_t, "b_ada": f(b_ada), "norm1_g": f(norm1_g), "norm2_g": f(norm2_g),
        "gdn_w_in": gin_t, "gdn_w_ab": gab_t, "gdn_conv_w": f(gdn_conv_w),
        "gdn_a_log": f(gdn_a_log).reshape(2, 16), "gdn_dt_bias": f(gdn_dt_bias).reshape(2, 16),
        "gdn_norm_g": f(gdn_norm_g), "gdn_w_out": f(gdn_w_out),
        "na_w_qkv": qkv_t, "na_rel_bias": f(na_rel_bias).reshape(2, 240, 31), "na_w_out": nwo_t,
        "ffn_w_up": w_up_t, "ffn_conv_w": f(ffn_conv_w), "ffn_conv_b": f(ffn_conv_b),
        "ffn_w_down": w_dn_t, "final_g": f(final_g), "consts": consts,
    }
    xp = f(x_prompt)
    xs = f(x_sample)
    in_maps = []
    for i in range(_cores):
        m = dict(shared)
        m["x_in"] = np.concatenate([xp[2 * i].reshape(256, D), xp[2 * i + 1].reshape(256, D), xs[i]], axis=0)
        m["state_gdn"] = f(state_gdn[i])
        m["cache_k"] = f(cache_k[i]).reshape(2, 256, 1024)
        m["cache_v"] = f(cache_v[i]).reshape(2, 256, 1024)
        m["cvec"] = np.stack([f(c_ctx), f(c[i])], axis=0)
        in_maps.append(m)
    res = run_bass_kernel_spmd(nc, in_maps, core_ids=list(range(_cores)))
    R = res.results
    y_prompt = np.zeros((16, 256, D), np.float32)
    y_sample = np.zeros((8, 1024, D), np.float32)
    new_state = np.zeros((16, 2, 2, 8, 128, 128), np.float32)
    new_k = np.zeros((16, 2, 256, 16, 64), np.float32)
    new_v = np.zeros((16, 2, 256, 16, 64), np.float32)
    for i in range(_cores):
        y = R[i]["y_out"]
        y_prompt[2 * i] = y[0:256]
        y_prompt[2 * i + 1] = y[256:512]
        y_sample[i] = y[512:]
        new_state[2 * i:2 * i + 2] = R[i]["st_out"]
        new_k[2 * i:2 * i + 2] = R[i]["ck_out"].reshape(2, 2, 256, 16, 64)
        new_v[2 * i:2 * i + 2] = R[i]["cv_out"].reshape(2, 2, 256, 16, 64)
    return (y_prompt, y_sample, new_state, new_k, new_v)
```

```python
import numpy as np
import concourse.bass as bass
import concourse.mybir as mybir
from concourse.bass_utils import run_bass_kernel_spmd

F32 = mybir.dt.float32
F32R = mybir.dt.float32r
BF16 = mybir.dt.bfloat16
AF = mybir.ActivationFunctionType
ALU = mybir.AluOpType
AX = mybir.AxisListType

D = 1024
KC = 8
DEPTH = 4
NP_TOK = 512
NS_TOK = 1024
NT = NP_TOK + NS_TOK
TB = 512
NTB = NT // TB
D_FF = 2816
NFC = D_FF // 128
GDN_DIN = 4128
EPS = 1e-6
SEQS = [(0, 256), (256, 256), (512, 1024)]


class Buf:
    __slots__ = ("name", "t", "psum", "lastw", "readers", "dma_readers", "root")

    def __init__(self, name, t, psum=False):
        self.name = name
        self.t = t
        self.psum = psum
        self.lastw = None
        self.readers = {}
        self.dma_readers = []
        self.root = self

    def alias(self, ap, name=None):
        b = Buf(name or self.name, ap, self.psum)
        b.root = self.root
        return b

    def __getitem__(self, idx):
        return self.t[idx]

    def sub(self, name=None):
        return Buf(name or self.name, self.t, self.psum)


class Op:
    __slots__ = ("eng", "fn", "reads", "writes", "dma", "waits", "sig", "sem", "sigval", "deps", "n")


class _Rec:
    def __init__(self):
        self.call = None

    def __getattr__(self, name):
        def f(*args, **kwargs):
            assert self.call is None
            self.call = (name, args, kwargs)
            return None
        return f


def _bind(fn):
    r = _Rec()
    fn(r)
    name, args, kwargs = r.call
    return lambda e: getattr(e, name)(*args, **kwargs)


class Prog:
    ENGS = ("pe", "act", "dve", "pool", "sp")

    def __init__(self, nc, n_dma_ring=8):
        self.nc = nc
        self.ops = []
        self.nring = n_dma_ring
        self.sbuf_bytes = 0

    def sb(self, name, shape, dtype):
        t = self.nc.alloc_sbuf_tensor(name, list(shape), dtype)
        return Buf(name, t)

    def ps(self, name, shape, dtype=F32):
        t = self.nc.alloc_psum_tensor(name, list(shape), dtype)
        return Buf(name, t, psum=True)

    def add(self, eng, fn, reads=(), writes=(), dma=False):
        op = Op()
        op.eng = eng
        op.fn = _bind(fn)
        op.reads = [b.root for b in reads if b is not None]
        op.writes = [b.root for b in writes if b is not None]
        op.dma = dma
        op.waits = []
        op.sig = dma
        op.sem = None
        op.sigval = 0
        op.n = len(self.ops)
        self.ops.append(op)
        return op

    def fence(self):
        op = Op()
        op.eng = None
        op.n = len(self.ops)
        op.dma = False
        op.sig = False
        self.ops.append(op)

    def view(self, name, ap, psum=False):
        return Buf(name, ap, psum)

    def dma(self, out_ap, in_ap, reads=(), writes=(), q="sp", **kw):
        return self.add(q, lambda e: e.dma_start(out=out_ap, in_=in_ap, **kw), reads, writes, dma=True)

    def resolve(self):
        last_on = {}
        dma_since = []
        pending = {}
        real_ops = []
        for op in self.ops:
            if op.eng is None:
                snap = (dict(last_on), list(dma_since))
                dma_since = []
                for e in self.ENGS:
                    pending[e] = snap
                continue
            real_ops.append(op)
            wdeps = []
            rdeps = []
            for b in op.reads:
                if b.lastw is not None:
                    wdeps.append(b.lastw)
                if b.psum:
                    for e, r in b.readers.items():
                        if e != op.eng:
                            rdeps.append(r)
            for b in op.writes:
                if b.lastw is not None:
                    wdeps.append(b.lastw)
                rdeps.extend(b.readers.values())
                rdeps.extend(b.dma_readers)
            for b in op.writes:
                b.lastw = op
                b.readers = {}
                b.dma_readers = []
            for b in op.reads:
                if op.dma:
                    b.dma_readers.append(op)
                else:
                    b.readers[op.eng] = op
            deps = {}
            for d in wdeps:
                if d is op:
                    continue
                if d.dma or op.dma:
                    deps[d.n] = d
                elif d.eng == op.eng:
                    if op.eng != "pe":
                        deps[d.n] = d
                else:
                    deps[d.n] = d
            for d in rdeps:
                if d is op:
                    continue
                if d.dma or op.dma:
                    deps[d.n] = d
                elif d.eng != op.eng:
                    deps[d.n] = d
            if pending.get(op.eng) is not None:
                lo, dl = pending[op.eng]
                pending[op.eng] = None
                for e2, d in lo.items():
                    if e2 == op.eng and e2 == "pe" and not op.dma:
                        continue
                    deps[d.n] = d
                for d in dl:
                    deps[d.n] = d
            op.deps = list(deps.values())
            for d in op.deps:
                d.sig = True
            if op.dma:
                dma_since.append(op)
            else:
                last_on[op.eng] = op
        self.ops = real_ops

    def emit(self):
        nc = self.nc
        self.resolve()
        streams = {e: [] for e in self.ENGS}
        for op in self.ops:
            streams[op.eng].append(op)
        import contextlib
        with contextlib.ExitStack() as es:
            esem = {e: es.enter_context(nc.semaphore("s_" + e)) for e in self.ENGS}
            rings = {e: [es.enter_context(nc.semaphore("d_%s%d" % (e, i))) for i in range(self.nring)]
                     for e in ("sp", "pool", "act")}
            fin = es.enter_context(nc.semaphore("fin"))
            cnt = {e: 0 for e in self.ENGS}
            ringcnt = {e: [0] * self.nring for e in rings}
            ringpos = {e: 0 for e in rings}
            ring_prev = {}
            for op in self.ops:
                if op.dma:
                    r = ringpos[op.eng] % self.nring
                    ringpos[op.eng] += 1
                    ringcnt[op.eng][r] += 1
                    op.sem = rings[op.eng][r]
                    op.sigval = 16 * ringcnt[op.eng][r]
                    if ringcnt[op.eng][r] > 1:
                        ring_prev[op.n] = (op.sem, op.sigval - 16)
                elif op.sig:
                    cnt[op.eng] += 1
                    op.sem = esem[op.eng]
                    op.sigval = cnt[op.eng]
            waited = {e: {} for e in self.ENGS}
            for e in self.ENGS:
                for op in streams[e]:
                    need = {}
                    for d in op.deps:
                        k = id(d.sem)
                        if k not in need or need[k][1] < d.sigval:
                            need[k] = (d.sem, d.sigval)
                    if op.n in ring_prev:
                        s, v = ring_prev[op.n]
                        k = id(s)
                        if k not in need or need[k][1] < v:
                            need[k] = (s, v)
                    for k, (s, v) in need.items():
                        if waited[e].get(k, 0) >= v:
                            continue
                        waited[e][k] = v
                        op.waits.append((s, v))
            final_waits = []
            for e in rings:
                for r in range(self.nring):
                    if ringcnt[e][r] > 0:
                        final_waits.append((rings[e][r], 16 * ringcnt[e][r]))

            def run_stream(ename, eng):
                for op in streams[ename]:
                    for s, v in op.waits:
                        eng.wait_ge(s, v)
                    ins = op.fn(eng)
                    if op.sig:
                        ins.then_inc(op.sem, 16 if op.dma else 1)
                if ename == "sp":
                    for s, v in final_waits:
                        eng.wait_ge(s, v)

            with nc.Block() as block:
                @block.tensor
                def _(eng):
                    run_stream("pe", eng)

                @block.scalar
                def _(eng):
                    run_stream("act", eng)

                @block.vector
                def _(eng):
                    run_stream("dve", eng)

                @block.gpsimd
                def _(eng):
                    run_stream("pool", eng)

                @block.sync
                def _(eng):
                    run_stream("sp", eng)


class Cfg:
    n_layers = DEPTH
    do_gdn = True
    do_na = True


def build(cfg=Cfg):
    nc = bass.Bass("TRN2", target_bir_lowering=False)
    P = Prog(nc)

    def din(name, shape):
        return nc.dram_tensor(name, list(shape), F32, kind="ExternalInput").ap()

    def dout(name, shape):
        return nc.dram_tensor(name, list(shape), F32, kind="ExternalOutput").ap()

    x_in = din("x_in", [NT, D])
    state_gdn = din("state_gdn", [2, 2, 8, 128, 128])
    cache_k = din("cache_k", [2, 256, 1024])
    cache_v = din("cache_v", [2, 256, 1024])
    cvec = din("cvec", [2, D])
    w_ada = din("w_ada", [DEPTH, 48, 128, 1024])
    b_ada = din("b_ada", [DEPTH, 6 * D])
    norm1_g = din("norm1_g", [DEPTH, D])
    norm2_g = din("norm2_g", [DEPTH, D])
    gdn_w_in = din("gdn_w_in", [2, 8, 2, 128, 2048])
    gdn_w_ab = din("gdn_w_ab", [2, 128, 256])
    gdn_conv_w = din("gdn_conv_w", [2, 3, 3072])
    gdn_a_log = din("gdn_a_log", [2, 16])
    gdn_dt_bias = din("gdn_dt_bias", [2, 16])
    gdn_norm_g = din("gdn_norm_g", [2, 128])
    gdn_w_out = din("gdn_w_out", [2, D, D])
    na_w_qkv = din("na_w_qkv", [2, 8, 128, 3072])
    na_rel_bias = din("na_rel_bias", [2, 16 * 15, 31])
    na_w_out = din("na_w_out", [2, 8, 128, 1024])
    ffn_w_up = din("ffn_w_up", [DEPTH, 22, 128, 2048])
    ffn_conv_w = din("ffn_conv_w", [DEPTH, 3, 2 * D_FF])
    ffn_conv_b = din("ffn_conv_b", [DEPTH, 2 * D_FF])
    ffn_w_down = din("ffn_w_down", [DEPTH, 2, 8, 128, 1408])
    final_g = din("final_g", [D])
    consts = din("consts", [128, 1536])

    y_out = dout("y_out", [NT, D])
    st_out = dout("st_out", [2, 2, 2, 8, 128, 128])
    ck_out = dout("ck_out", [2, 2, 256, 1024])
    cv_out = dout("cv_out", [2, 2, 256, 1024])

    X = P.sb("X", [128, KC, NT], F32)
    Xb = {(kc, tb): X.sub("X%d_%d" % (kc, tb)) for kc in range(KC) for tb in range(NTB)}
    H = P.sb("H", [128, KC, NT], BF16)
    Hb = {(kc, tb): H.sub("H%d_%d" % (kc, tb)) for kc in range(KC) for tb in range(NTB)}
    ARENA = P.sb("ARENA", [128, 13824], F32)
    CHAIN = nc.alloc_sbuf_tensor("CHAIN", [128, 3072], F32R)
    WREG = P.sb("WREG", [128, 3584], F32)

    def carve(region, byte_off, dtype, shape, name):
        esz = 2 if dtype == BF16 else 4
        n = 1
        for d_ in shape[1:]:
            n *= d_
        base = region.t.bitcast(dtype) if dtype != F32 else region.t
        ap = base[:, byte_off // esz: byte_off // esz + n]
        if len(shape) == 3:
            ap = ap.rearrange("p (a b) -> p a b", a=shape[1])
        elif len(shape) == 4:
            ap = ap.rearrange("p (a b c) -> p a b c", a=shape[1], b=shape[2])
        return P.view(name, ap)

    CONST = P.sb("CONST", [128, 1536], F32)
    ident = CONST
    ONESB = P.sb("ONESB", [128, 128], BF16)
    IDB = P.sb("IDB", [128, 128], BF16)

    PSALL = nc.alloc_psum_tensor("psall", [128, 4096], F32)
    PS = [P.view("ps%d" % i, PSALL[:, i * 512:(i + 1) * 512], psum=True) for i in range(8)]
    ps_rr = [0]

    def psum():
        b = PS[ps_rr[0] % 8]
        ps_rr[0] += 1
        return b

    P.dma(CONST[:, :], consts[:, :], writes=[CONST])
    P.add("dve", lambda e: e.memset(ONESB[:, :], 1.0), writes=[ONESB])
    P.add("dve", lambda e: e.tensor_copy(IDB[:, :], CONST[:, 0:128]), reads=[CONST], writes=[IDB])

    XT = [carve(ARENA, i * 4096, F32, [128, D], "XT%d" % i) for i in range(2)]
    for blk in range(NT // 128):
        xt = XT[blk % 2]
        P.dma(xt[:, :], x_in[blk * 128:(blk + 1) * 128, :], writes=[xt])
        tb = (blk * 128) // TB
        for half in range(2):
            pb = psum()
            for j in range(4):
                kc = half * 4 + j
                P.add("pe", lambda e, pb=pb, xt=xt, kc=kc, j=j: e.transpose(
                    pb[:, j * 128:(j + 1) * 128], xt[:, kc * 128:(kc + 1) * 128], CONST[:, 0:128]),
                    reads=[xt, CONST], writes=[pb])
            wr = [Xb[(half * 4 + j, tb)] for j in range(4)]
            eng = "act" if half == 0 else "dve"
            if eng == "act":
                P.add("act", lambda e, pb=pb, half=half, blk=blk: e.copy(
                    X[:, half * 4:half * 4 + 4, blk * 128:(blk + 1) * 128],
                    pb[:, :].rearrange("p (j t) -> p j t", j=4)), reads=[pb], writes=wr)
            else:
                P.add("dve", lambda e, pb=pb, half=half, blk=blk: e.tensor_copy(
                    X[:, half * 4:half * 4 + 4, blk * 128:(blk + 1) * 128],
                    pb[:, :].rearrange("p (j t) -> p j t", j=4)), reads=[pb], writes=wr)

    STG = [P.sb("STG%d" % i, [128, 128], F32) for i in range(2)]
    stg_rr = [0]

    def load_fm(dst, dst_flat_ap, src_rows_ap, R):
        r0 = 0
        while r0 < R:
            r = min(128, R - r0)
            stg = STG[stg_rr[0] % 2]
            stg_rr[0] += 1
            P.dma(stg[0:r, :], src_rows_ap[r0:r0 + r, :], writes=[stg])
            pb = psum()
            P.add("pe", lambda e, pb=pb, stg=stg, r=r: e.transpose(pb[:, 0:r], stg[0:r, :], CONST[0:r, 0:r]),
                  reads=[stg, CONST], writes=[pb])
            P.add("dve", lambda e, pb=pb, r=r, r0=r0: e.tensor_copy(dst_flat_ap[:, r0:r0 + r], pb[:, 0:r]),
                  reads=[pb], writes=[dst])
            r0 += r

    G1 = P.sb("G1", [128, DEPTH, KC], F32)
    G2 = P.sb("G2", [128, DEPTH, KC], F32)
    BADA = P.sb("BADA", [128, DEPTH, 48], F32)
    CV = P.sb("CV", [128, 2, KC], F32)
    SCV = P.sb("SCV", [128, 2, KC], BF16)
    load_fm(G1, G1[:, :, :].rearrange("p l k -> p (l k)"), norm1_g.rearrange("l (k f) -> (l k) f", f=128), 32)
    load_fm(G2, G2[:, :, :].rearrange("p l k -> p (l k)"), norm2_g.rearrange("l (k f) -> (l k) f", f=128), 32)
    load_fm(BADA, BADA[:, :, :].rearrange("p l k -> p (l k)"), b_ada.rearrange("l (k f) -> (l k) f", f=128), 192)
    load_fm(CV, CV[:, :, :].rearrange("p v k -> p (v k)"), cvec.rearrange("v (k f) -> (v k) f", f=128), 16)
    P.add("act", lambda e: e.activation(SCV[:, :, :], CV[:, :, :], AF.Silu), reads=[CV], writes=[SCV])

    MOD = [P.sb("MOD%d" % l, [128, 48, 2], F32) for l in range(DEPTH)]
    A1 = [P.sb("A1_%d" % l, [128, KC, 2], F32) for l in range(DEPTH)]
    A2 = [P.sb("A2_%d" % l, [128, KC, 2], F32) for l in range(DEPTH)]
    WA = [P.sb("WA%d" % i, [128, KC, 128], BF16) for i in range(3)]
    wa_rr = [0]

    MODg = [[MOD[l].sub("MOD%d_%d" % (l, g)) for g in range(6)] for l in range(DEPTH)]

    class WStream:
        def __init__(self, slots, n, issue):
            self.slots, self.n, self.issue, self.nxt = slots, n, issue, 0

        def need(self, i):
            k = len(self.slots)
            while self.nxt <= min(i + k - 1, self.n - 1):
                self.issue(self.nxt, self.slots[self.nxt % k])
                self.nxt += 1
            return self.slots[i % k]

    def mod_gen(l):
        ws = WStream(WA, 48, lambda i, wa: P.dma(wa[:, :, :].rearrange("p k n -> p (k n)"), w_ada[l, i, :, :],
                                                 writes=[wa], q="pool"))
        for n in range(48):
            wa = ws.need(n)
            pm = PS[6 + mod_bank[0] % 2]
            mod_bank[0] += 1
            for kc in range(KC):
                P.add("pe", lambda e: e.matmul(pm[:, 0:2], wa[:, kc, :], SCV[:, :, kc],
                                               start=(kc == 0), stop=(kc == KC - 1)), reads=[wa, SCV], writes=[pm])
            P.add("dve", lambda e: e.tensor_scalar(MOD[l][:, n, :], pm[:, 0:2], BADA[:, l, n:n + 1], None, ALU.add),
                  reads=[pm, BADA], writes=[MODg[l][n // 8]])
            if n == 15:
                P.add("dve", lambda e: e.scalar_tensor_tensor(
                    A1[l][:, :, :], MOD[l][:, 8:16, :], 1.0, G1[:, l, :].unsqueeze(2).to_broadcast([128, KC, 2]),
                    ALU.add, ALU.mult), reads=[MODg[l][1], G1], writes=[A1[l]])
            if n == 39:
                P.add("dve", lambda e: e.scalar_tensor_tensor(
                    A2[l][:, :, :], MOD[l][:, 32:40, :], 1.0, G2[:, l, :].unsqueeze(2).to_broadcast([128, KC, 2]),
                    ALU.add, ALU.mult), reads=[MODg[l][4], G2], writes=[A2[l]])
            yield n

    mod_state = {"gen": None}
    mod_bank = [0]

    def pump(k):
        g = mod_state["gen"]
        if g is None:
            return
        for _ in range(k):
            try:
                next(g)
            except StopIteration:
                mod_state["gen"] = None
                return

    SQ = [P.sb("SQ%d" % i, [128, TB], BF16) for i in range(2)]
    RSTD = [P.sb("RSTD%d" % i, [128, TB], F32) for i in range(2)]
    TMPN = [P.sb("TMPN%d" % i, [128, TB], F32) for i in range(2)]
    EPSC = P.sb("EPSC", [128, 1], F32)
    P.add("dve", lambda e: e.memset(EPSC[:, :], EPS), writes=[EPSC])
    nrr = [0]

    def emit_norm_mod(l, which):
        A = A1[l] if which == 1 else A2[l]
        shoff = 0 if which == 1 else 24
        for tb in range(NTB):
            v = 0 if tb == 0 else 1
            rstd = RSTD[nrr[0] % 2]
            nrr[0] += 1
            sl = slice(tb * TB, (tb + 1) * TB)
            pb = psum()
            for kc in range(KC):
                sq = SQ[kc % 2]
                P.add("act", lambda e, sq=sq, sl=sl, kc=kc: e.activation(
                    sq[:, :], X[:, kc, sl], AF.Square), reads=[Xb[(kc, tb)]], writes=[sq])
                P.add("pe", lambda e, pb=pb, sq=sq, kc=kc: e.matmul(
                    pb[:, :], ONESB[:, :], sq[:, :], start=(kc == 0), stop=(kc == KC - 1)),
                    reads=[sq, ONESB], writes=[pb])
            P.add("act", lambda e, pb=pb, rstd=rstd: e.activation(
                rstd[:, :], pb[:, :], AF.Ln, bias=EPSC[:, 0:1], scale=1.0 / D),
                reads=[pb, EPSC], writes=[rstd])
            P.add("act", lambda e, rstd=rstd: e.activation(rstd[:, :], rstd[:, :], AF.Exp, scale=-0.5), reads=[rstd], writes=[rstd])
            for kc in range(KC):
                tmp = TMPN[kc % 2]
                P.add("dve", lambda e, tmp=tmp, kc=kc, sl=sl, rstd=rstd: e.tensor_tensor(
                    tmp[:, :], X[:, kc, sl], rstd[:, :], ALU.mult),
                    reads=[Xb[(kc, tb)], rstd], writes=[tmp])
                P.add("act", lambda e, tmp=tmp, kc=kc, sl=sl, v=v, A=A, l=l, shoff=shoff: e.activation(
                    H[:, kc, sl], tmp[:, :], AF.Identity,
                    bias=MOD[l][:, shoff + kc, v:v + 1], scale=A[:, kc, v:v + 1]),
                    reads=[tmp, A, MODg[l][shoff // 8]], writes=[Hb[(kc, tb)]])

    def resid_from_psum(pb, l, gate_off, n, tb, ncols=TB, col0=0):
        v = 0 if tb == 0 else 1
        sl = slice(tb * TB + col0, tb * TB + col0 + ncols)
        P.add("dve", lambda e: e.scalar_tensor_tensor(
            X[:, n, sl], pb[:, 0:ncols], MOD[l][:, gate_off + n, v:v + 1], X[:, n, sl], ALU.mult, ALU.add),
            reads=[pb, MODg[l][gate_off // 8], Xb[(n, tb)]], writes=[Xb[(n, tb)]])

    WUP = [carve(WREG, i * 4096, BF16, [128, KC, 2, 128], "WUP%d" % i) for i in range(2)]
    wup_rr = [0]
    CT = [P.sb("CT%d" % i, [128, NT], F32) for i in range(4)]
    ct_rr = [0]
    NFH = NFC // 2
    GB = carve(ARENA, 0, BF16, [128, NFH, NT], "GB")
    GBb = {(i, tb): GB.sub("GB%d_%d" % (i, tb)) for i in range(NFH) for tb in range(NTB)}
    FCW = P.sb("FCW", [128, DEPTH, 3, 44], F32)
    FCB = P.sb("FCB", [128, DEPTH, 44], F32)
    load_fm(FCW, FCW[:, :, :, :].rearrange("p l t k -> p (l t k)"),
            ffn_conv_w.rearrange("l t (k f) -> (l t k) f", f=128), DEPTH * 3 * 44)
    load_fm(FCB, FCB[:, :, :].rearrange("p l k -> p (l k)"), ffn_conv_b.rearrange("l (k f) -> (l k) f", f=128),
            DEPTH * 44)
    WDN = [carve(WREG, 8192 + i * 2816, BF16, [128, NFH, 128], "WDN%d" % i) for i in range(2)]
    wdn_rr = [0]
    grp_rr = [0]
    dn_rr = [0]

    def bank_group():
        g = grp_rr[0] % 2
        grp_rr[0] += 1
        return g * 3

    def conv_from_psum(b0, w0, w1, w2, bias, ct, ctb):
        pbs = [PS[b0], PS[b0 + 1], PS[b0 + 2]]
        samp = PSALL[:, (b0 + 1) * 512:(b0 + 3) * 512]
        P.add("act", lambda e: e.activation(ct[:, 0:512], PS[b0][:, :], AF.Identity, bias=bias, scale=w1),
              reads=[pbs[0], FCWb], writes=[ctb])
        P.add("act", lambda e: e.activation(ct[:, 512:NT], samp, AF.Identity, bias=bias, scale=w1),
              reads=[pbs[1], pbs[2], FCWb], writes=[ctb])
        for (t0, ln) in SEQS:
            if t0 < 512:
                src = lambda a, b_: PS[b0][:, a:b_]
                rd = [pbs[0]]
            else:
                src = lambda a, b_: PSALL[:, (b0 + 1) * 512 + a - 512:(b0 + 1) * 512 + b_ - 512]
                rd = [pbs[1], pbs[2]]
            P.add("dve", lambda e, src=src, t0=t0, ln=ln: e.scalar_tensor_tensor(
                ct[:, t0 + 1:t0 + ln], src(t0, t0 + ln - 1), w0, ct[:, t0 + 1:t0 + ln], ALU.mult, ALU.add),
                reads=rd + [FCWb, ctb], writes=[ctb])
            P.add("dve", lambda e, src=src, t0=t0, ln=ln: e.scalar_tensor_tensor(
                ct[:, t0:t0 + ln - 1], src(t0 + 1, t0 + ln), w2, ct[:, t0:t0 + ln - 1], ALU.mult, ALU.add),
                reads=rd + [FCWb, ctb], writes=[ctb])

    FCWb = FCW

    def emit_ffn(l):
        wus = WStream(WUP, NFC, lambda i, wu: P.dma(wu[:, :, :, :].rearrange("p k g n -> p (k g n)"),
                                                    ffn_w_up[l, i, :, :], writes=[wu], q="pool"))
        wds = WStream(WDN, 2 * KC, lambda i, wd: P.dma(wd[:, :, :].rearrange("p i n -> p (i n)"),
                                                       ffn_w_down[l, i // KC, i % KC, :, :], writes=[wd], q="pool"))
        for hf in range(2):
            for piece in range(hf * NFH, (hf + 1) * NFH):
                wu = wus.need(piece)
                if piece == (hf + 1) * NFH - 1:
                    wds.need(hf * KC)
                i = piece
                cts = []
                for g in range(2):
                    b0 = bank_group()
                    for tb in range(NTB):
                        pb = PS[b0 + tb]
                        for kc in range(KC):
                            P.add("pe", lambda e: e.matmul(
                                pb[:, :], wu[:, kc, g, :], H[:, kc, tb * TB:(tb + 1) * TB],
                                start=(kc == 0), stop=(kc == KC - 1)),
                                reads=[wu, Hb[(kc, tb)]], writes=[pb])
                    ct = CT[ct_rr[0] % 4]
                    ct_rr[0] += 1
                    ch = g * NFC + i
                    conv_from_psum(b0, FCW[:, l, 0, ch:ch + 1], FCW[:, l, 1, ch:ch + 1], FCW[:, l, 2, ch:ch + 1],
                                   FCB[:, l, ch:ch + 1], ct, ct)
                    cts.append(ct)
                ctv, ctg = cts
                pump(2)
                P.add("act", lambda e: e.activation(ctg[:, :], ctg[:, :], AF.Silu), reads=[ctg], writes=[ctg])
                P.add("pool", lambda e: e.tensor_tensor(GB[:, i - hf * NFH, :], ctv[:, :], ctg[:, :], ALU.mult),
                      reads=[ctv, ctg], writes=[GBb[(i - hf * NFH, tb)] for tb in range(NTB)])
            for piece in range(KC):
                wd = wds.need(hf * KC + piece)
                n = piece
                for tb in range(NTB):
                    pb = PS[dn_rr[0] % 6]
                    dn_rr[0] += 1
                    for i in range(NFH):
                        P.add("pe", lambda e: e.matmul(
                            pb[:, :], wd[:, i, :], GB[:, i, tb * TB:(tb + 1) * TB],
                            start=(i == 0), stop=(i == NFH - 1)),
                            reads=[wd, GBb[(i, tb)]], writes=[pb])
                    resid_from_psum(pb, l, 40, n, tb)
                pump(2)

    WO = [carve(WREG, i * 2048, BF16, [128, KC, 128], "WO%d" % i) for i in range(2)]
    wo_rr = [0]

    def emit_wout(l, w_dram, OT, OTb):
        wos = WStream(WO, KC, lambda i, wo: P.dma(wo[:, :, :].rearrange("p k n -> p (k n)"), w_dram[i, :, :],
                                                  writes=[wo], q="pool"))
        for n in range(KC):
            wo = wos.need(n)
            pump(1)
            for tb in range(NTB):
                pb = psum()
                for kc in range(KC):
                    P.add("pe", lambda e, pb=pb, wo=wo, kc=kc, tb=tb: e.matmul(
                        pb[:, :], wo[:, kc, :], OT[:, kc, tb * TB:(tb + 1) * TB],
                        start=(kc == 0), stop=(kc == KC - 1)), reads=[wo, OTb[kc]], writes=[pb])
                resid_from_psum(pb, l, 16, n, tb)

    rb_t = nc.dram_tensor("rbpad", [480, 127], F32)
    RBPAD = P.view("rbpad", rb_t.ap())
    J2B = P.sb("J2B", [128, 64], BF16)
    P.add("dve", lambda e: e.tensor_copy(J2B[:, :], CONST[:, 128:192]), reads=[CONST], writes=[J2B])

    def emit_rbpad():
        RBP = carve(ARENA, 16384, F32, [128, 4, 127], "RBP")
        P.add("pool", lambda e: e.memset(RBP[:, :, :], 0.0), writes=[RBP])
        P.dma(RBP[0:120, :, 48:79], na_rel_bias.rearrange("j (p a) f -> p (j a) f", a=2)[:, :, :]
              if False else na_rel_bias.rearrange("j r f -> (j r) f").rearrange("(p a) f -> p a f", a=4),
              reads=[], writes=[RBP], allow_slow_non_contiguous=True)
        P.dma(rb_t.ap().rearrange("(p a) f -> p a f", a=4), RBP[0:120, :, :], reads=[RBP], writes=[RBPAD])

    def psum_of(lst, st):
        b = PS[lst[st[0] % len(lst)]]
        st[0] += 1
        return b

    def emit_na(l):
        j = l // 2
        OT = carve(ARENA, 0, BF16, [128, KC, NT], "OT")
        OTb = [OT.sub("OT%d" % c) for c in range(KC)]
        QZ = [carve(ARENA, 24576 + i * 3072, BF16, [128, NT], "QZ%d" % i) for i in range(2)]
        KT = carve(ARENA, 30720, BF16, [128, NT], "KT")
        VT = carve(ARENA, 33792, BF16, [128, 12, 128], "VT")
        VTS = carve(ARENA, 36864, BF16, [128, 7, 128], "VTS")
        KCT = carve(ARENA, 38656, BF16, [128, 256], "KCT")
        VC = carve(ARENA, 39168, BF16, [128, 2, 128], "VC")
        PTC = carve(ARENA, 39680, BF16, [128, 2, 1024], "PTC")
        PTL = [carve(ARENA, 43776 + i * 512, BF16, [128, 4, 64], "PTL%d" % i) for i in range(2)]
        PTP = [carve(ARENA, 44800 + i * 1024, BF16, [128, 2, 256], "PTP%d" % i) for i in range(2)]
        HKR = carve(ARENA, 46848, BF16, [128, 14, 2, 64], "HKR")
        HKZ = carve(ARENA, 50432, BF16, [128, 14, 2, 64], "HKZ")
        CKS = TMPN[0].alias(TMPN[0].t[:, :].rearrange("p (a b) -> p a b", a=4))
        CVS = TMPN[1].alias(TMPN[1].t[:, :].rearrange("p (a b) -> p a b", a=4))
        RD = RSTD[0].alias(RSTD[0].t[:, :])
        KCS = RSTD[1].alias(RSTD[1].t[:, 0:256].rearrange("p (a b) -> p a b", a=2))
        WQ = [carve(WREG, i * 6144, BF16, [128, KC, 3, 128], "WQ%d" % i) for i in range(2)]
        lo = [0]
        hi = [0]
        LO = [0, 1, 2, 3]
        HI = [4, 5, 6, 7]
        P.add("pool", lambda e: e.memset(QZ[0][64:128, :], 0.0), writes=[QZ[0]])
        P.add("pool", lambda e: e.memset(QZ[1][0:64, :], 0.0), writes=[QZ[1]])
        P.add("pool", lambda e: e.memset(HKZ[:, :, :, :], 0.0), writes=[HKZ])
        ptl_rr = [0]
        ptp_rr = [0]
        wqs = WStream(WQ, KC, lambda i, wq: P.dma(wq[:, :, :, :].rearrange("p k g n -> p (k g n)"),
                                                  na_w_qkv[j, i, :, :], writes=[wq], q="pool"))
        for c in range(KC):
            wq = wqs.need(c)
            pump(3)
            for tb in range(NTB):
                sl = slice(tb * TB, (tb + 1) * TB)
                pb = psum_of(LO, lo)
                for kc in range(KC):
                    P.add("pe", lambda e, pb=pb, kc=kc, sl=sl: e.matmul(
                        pb[:, :], wq[:, kc, 0, :], H[:, kc, sl], start=(kc == 0), stop=(kc == KC - 1)),
                        reads=[wq, Hb[(kc, tb)]], writes=[pb])
                P.add("act", lambda e, pb=pb, sl=sl: e.activation(QZ[0][0:64, sl], pb[0:64, :], AF.Copy, scale=0.125),
                      reads=[pb], writes=[QZ[0]])
                P.add("act", lambda e, pb=pb, sl=sl: e.activation(QZ[1][64:128, sl], pb[64:128, :], AF.Copy, scale=0.125),
                      reads=[pb], writes=[QZ[1]])
                pb = psum_of(LO, lo)
                for kc in range(KC):
                    P.add("pe", lambda e, pb=pb, kc=kc, sl=sl: e.matmul(
                        pb[:, :], wq[:, kc, 1, :], H[:, kc, sl], start=(kc == 0), stop=(kc == KC - 1)),
                        reads=[wq, Hb[(kc, tb)]], writes=[pb])
                P.add("dve", lambda e, pb=pb, sl=sl: e.tensor_copy(KT[:, sl], pb[:, :]), reads=[pb], writes=[KT])
            for g in range(3):
                pb = psum_of(LO, lo)
                for b in range(4):
                    blk = g * 4 + b
                    for kc in range(KC):
                        P.add("pe", lambda e, pb=pb, kc=kc, b=b, blk=blk: e.matmul(
                            pb[:, b * 128:(b + 1) * 128], H[:, kc, blk * 128:(blk + 1) * 128], wq[:, kc, 2, :],
                            start=(kc == 0), stop=(kc == KC - 1)),
                            reads=[wq, Hb[(kc, blk // 4)]], writes=[pb])
                P.add("dve", lambda e, pb=pb, g=g: e.tensor_copy(
                    VT[:, g * 4:(g + 1) * 4, :], pb[:, :].rearrange("p (b f) -> p b f", b=4)),
                    reads=[pb], writes=[VT])
                if g == 0:
                    P.add("act", lambda e, pb=pb: e.copy(CVS[:, :, :], pb[:, :].rearrange("p (b f) -> p b f", b=4)),
                          reads=[pb], writes=[CVS])
                    for sq in range(2):
                        P.dma(cv_out[sq, j, :, c * 128:(c + 1) * 128].rearrange("(b p) f -> p b f", p=128),
                              CVS[:, sq * 2:sq * 2 + 2, :], reads=[CVS])
            pb = psum_of(LO, lo)
            for b in range(4):
                for kc in range(KC):
                    P.add("pe", lambda e, pb=pb, kc=kc, b=b: e.matmul(
                        pb[:, b * 128:(b + 1) * 128], H[:, kc, b * 128:(b + 1) * 128], wq[:, kc, 1, :],
                        start=(kc == 0), stop=(kc == KC - 1)), reads=[wq, Hb[(kc, 0)]], writes=[pb])
            P.add("act", lambda e, pb=pb: e.copy(CKS[:, :, :], pb[:, :].rearrange("p (b f) -> p b f", b=4)),
                  reads=[pb], writes=[CKS])
            for sq in range(2):
                P.dma(ck_out[sq, j, :, c * 128:(c + 1) * 128].rearrange("(b p) f -> p b f", p=128),
                      CKS[:, sq * 2:sq * 2 + 2, :], reads=[CKS])
            for g in range(2):
                pb = psum_of(LO, lo)
                nb = 4 if g == 0 else 3
                for b in range(nb):
                    m = g * 4 + b
                    t0 = 512 + 64 + 128 * m
                    for kc in range(KC):
                        P.add("pe", lambda e, pb=pb, kc=kc, b=b, t0=t0: e.matmul(
                            pb[:, b * 128:(b + 1) * 128], H[:, kc, t0:t0 + 128], wq[:, kc, 2, :],
                            start=(kc == 0), stop=(kc == KC - 1)),
                            reads=[wq, Hb[(kc, 1)], Hb[(kc, 2)]], writes=[pb])
                P.add("dve", lambda e, pb=pb, g=g, nb=nb: e.tensor_copy(
                    VTS[:, g * 4:g * 4 + nb, :], pb[:, 0:nb * 128].rearrange("p (b f) -> p b f", b=nb)),
                    reads=[pb], writes=[VTS])
            P.dma(KCS[:, :, :], cache_k[j, :, c * 128:(c + 1) * 128].rearrange("(b p) f -> p b f", p=128),
                  writes=[KCS])
            pb = psum_of(LO, lo)
            for b in range(2):
                P.add("pe", lambda e, pb=pb, b=b: e.transpose(pb[:, b * 128:(b + 1) * 128], KCS[:, b, :], CONST[:, 0:128]),
                      reads=[KCS, CONST], writes=[pb])
            P.add("act", lambda e, pb=pb: e.copy(KCT[:, :], pb[:, 0:256]), reads=[pb], writes=[KCT])
            P.dma(VC[:, :, :], cache_v[j, :, c * 128:(c + 1) * 128].rearrange("(b p) f -> p b f", p=128),
                  writes=[VC], q="pool")
            for sq in range(2):
                pbo = psum_of(HI, hi)
                for hh in range(2):
                    hs = slice(hh * 64, hh * 64 + 64)
                    ptp = PTP[ptp_rr[0] % 2]
                    ptp_rr[0] += 1
                    pbs = psum_of(LO, lo)
                    for kb in range(2):
                        P.add("pe", lambda e, pbs=pbs, kb=kb, hh=hh, sq=sq: e.matmul(
                            pbs[:, kb * 256:(kb + 1) * 256], KT[:, sq * 256 + kb * 128:sq * 256 + (kb + 1) * 128],
                            QZ[hh][:, sq * 256:(sq + 1) * 256], start=True, stop=True),
                            reads=[KT, QZ[hh]], writes=[pbs])
                    P.add("act", lambda e, pbs=pbs, ptp=ptp: e.activation(
                        ptp[:, :, :], pbs[:, :].rearrange("p (b q) -> p b q", b=2), AF.Exp),
                        reads=[pbs], writes=[ptp])
                    for kb in range(2):
                        P.add("pe", lambda e, kb=kb, hs=hs, ptp=ptp, sq=sq: e.matmul(
                            pbo[hs, 0:256], VT[:, sq * 2 + kb, hs], ptp[:, kb, :], start=(kb == 0), stop=(kb == 1)),
                            reads=[VT, ptp], writes=[pbo])
                    for kb in range(2):
                        P.add("pe", lambda e, kb=kb, hs=hs, ptp=ptp: e.matmul(
                            pbo[hs, 256:512], ONESB[:, 0:64], ptp[:, kb, :], start=(kb == 0), stop=(kb == 1)),
                            reads=[ONESB, ptp], writes=[pbo])
                P.add("act", lambda e, pbo=pbo: e.activation(RD[:, 0:256], pbo[:, 256:512], AF.Ln), reads=[pbo], writes=[RD])
                P.add("act", lambda e: e.activation(RD[:, 0:256], RD[:, 0:256], AF.Exp, scale=-1.0), reads=[RD], writes=[RD])
                P.add("dve", lambda e, pbo=pbo, sq=sq: e.tensor_tensor(
                    OT[:, c, sq * 256:(sq + 1) * 256], pbo[:, 0:256], RD[:, 0:256], ALU.mult),
                    reads=[pbo, RD], writes=[OTb[c]])
            pbo_s = [PS[4], PS[5]]
            pbd_s = [PS[6], PS[7]]
            for hh in range(2):
                h = 2 * c + hh
                hs = slice(hh * 64, hh * 64 + 64)
                for u in range(2):
                    src = bass.AP(rb_t, (j * 240 + h * 15 + u) * 127, [[1, 64], [127, 14], [1, 64]])
                    P.dma(HKR[0:64, :, u, :], src, reads=[RBPAD], writes=[HKR], q="pool")
                P.add("pool", lambda e: e.tensor_tensor(
                    HKZ[0:64, :, :, :].rearrange("p a u k -> p (a u) k"),
                    HKR[0:64, :, :, :].rearrange("p a u k -> p (a u) k"),
                    CONST[0:64, 192:256].unsqueeze(1).to_broadcast([64, 28, 64]), ALU.add),
                    reads=[HKR, CONST], writes=[HKZ])
                for kb in range(2):
                    for qb in range(2):
                        pbs = psum_of(LO, lo)
                        P.add("pe", lambda e, pbs=pbs, kb=kb, qb=qb, hh=hh: e.matmul(
                            pbs[:, :], KCT[:, kb * 128:(kb + 1) * 128], QZ[hh][:, 512 + qb * 512:512 + (qb + 1) * 512],
                            start=True, stop=True), reads=[KCT, QZ[hh]], writes=[pbs])
                        P.add("act", lambda e, pbs=pbs, kb=kb, qb=qb: e.activation(
                            PTC[:, kb, qb * 512:(qb + 1) * 512], pbs[:, :], AF.Exp), reads=[pbs], writes=[PTC])
                def row_scores(r):
                    rs = min(max(r - 4, 0), 8)
                    ptl = PTL[ptl_rr[0] % 2]
                    ptl_rr[0] += 1
                    pbs = psum_of(LO, lo)
                    qsl = slice(512 + 64 * r, 512 + 64 * r + 64)
                    for i in range(4):
                        kr = rs + 2 * i
                        ri = kr - r + 7
                        k0 = 512 + 64 * kr
                        P.add("pe", lambda e: e.matmul(
                            pbs[:, i * 64:(i + 1) * 64], KT[:, k0:k0 + 128], QZ[hh][:, qsl], start=True, stop=False),
                            reads=[KT, QZ[hh]], writes=[pbs])
                        P.add("pe", lambda e: e.matmul(
                            pbs[:, i * 64:(i + 1) * 64], HKZ[:, ri, :, :].rearrange("p u k -> p (u k)"), J2B[:, :],
                            start=False, stop=True), reads=[HKZ, J2B], writes=[pbs])
                    P.add("act", lambda e: e.activation(
                        ptl[:, :, :], pbs[:, 0:256].rearrange("p (i q) -> p i q", i=4), AF.Exp),
                        reads=[pbs], writes=[ptl])
                    return ptl

                def row_pv(r, ptl):
                    rs = min(max(r - 4, 0), 8)
                    pbo = pbo_s[r // 8]
                    pbd = pbd_s[r // 8]
                    osl = slice((r % 8) * 64, (r % 8) * 64 + 64)
                    for pbx, isden in ((pbo, False), (pbd, True)):
                        for i in range(4):
                            kr = rs + 2 * i
                            if kr % 2 == 0:
                                vv = VT[:, 4 + kr // 2, hs]
                                vb = VT
                            else:
                                vv = VTS[:, (kr - 1) // 2, hs]
                                vb = VTS
                            lhs = ONESB[:, 0:64] if isden else vv
                            P.add("pe", lambda e: e.matmul(
                                pbx[hs, osl], lhs, ptl[:, i, :], start=(i == 0), stop=False),
                                reads=[vb, ONESB, ptl], writes=[pbx])
                        for kb in range(2):
                            lhs = ONESB[:, 0:64] if isden else VC[:, kb, hs]
                            P.add("pe", lambda e: e.matmul(
                                pbx[hs, osl], lhs, PTC[:, kb, 64 * r:64 * r + 64], start=False, stop=(kb == 1)),
                                reads=[VC, ONESB, PTC], writes=[pbx])

                ptl_prev = row_scores(0)
                for r in range(1, 16):
                    ptl_cur = row_scores(r)
                    row_pv(r - 1, ptl_prev)
                    ptl_prev = ptl_cur
                row_pv(15, ptl_prev)
            for half in range(2):
                P.add("act", lambda e, half=half: e.activation(RD[:, :], pbd_s[half][:, :], AF.Ln),
                      reads=[pbd_s[half]], writes=[RD])
                P.add("act", lambda e: e.activation(RD[:, :], RD[:, :], AF.Exp, scale=-1.0), reads=[RD], writes=[RD])
                P.add("dve", lambda e, half=half: e.tensor_tensor(
                    OT[:, c, 512 + half * 512:512 + (half + 1) * 512], pbo_s[half][:, :], RD[:, :], ALU.mult),
                    reads=[pbo_s[half], RD], writes=[OTb[c]])
        P.fence()
        emit_wout(l, na_w_out[j], OT, OTb)


    C_I = CONST[:, 0:128]
    C_TRIF = CONST[:, 256:384]
    C_TRIB = CONST[:, 384:512]
    C_EVEN = CONST[:, 512:640]
    C_ODD = CONST[:, 640:768]
    C_MINC = [CONST[:, 768:896], CONST[:, 1024:1152]]
    C_MSTR = [CONST[:, 896:1024], CONST[:, 1152:1280]]
    C_ONES = CONST[:, 1280:1408]
    C_NEG1 = CONST[:, 1408:1536]
    GCW = P.sb("GCW", [128, 2, 3, 24], F32)
    load_fm(GCW, GCW[:, :, :, :].rearrange("p j t k -> p (j t k)"),
            gdn_conv_w.rearrange("j t (k f) -> (j t k) f", f=128), 144)
    GNG = P.sb("GNG", [128, 2], F32)
    load_fm(GNG, GNG[:, :], gdn_norm_g, 2)
    ZEROC = P.sb("ZEROC", [128, 1], F32)
    P.add("dve", lambda e: e.memset(ZEROC[:, :], 0.0), writes=[ZEROC])
    ev_rr = [0]

    def evac(dst_ap, dst_bufs, src_ap, pbs, scale=None):
        use_act = True
        ev_rr[0] += 1
        if use_act:
            if scale is None:
                P.add("act", lambda e: e.copy(dst_ap, src_ap), reads=pbs, writes=dst_bufs)
            else:
                P.add("act", lambda e: e.activation(dst_ap, src_ap, AF.Copy, scale=scale), reads=pbs, writes=dst_bufs)
        else:
            if scale is None:
                P.add("dve", lambda e: e.tensor_copy(dst_ap, src_ap), reads=pbs, writes=dst_bufs)
            else:
                P.add("dve", lambda e: e.tensor_scalar(dst_ap, src_ap, scale, None, ALU.mult), reads=pbs, writes=dst_bufs)

    def emit_gdn(l):
        j = l // 2
        off = [0]

        def AR(dtype, shape, name):
            esz = 2 if dtype == BF16 else 4
            n = esz
            for d_ in shape[1:]:
                n *= d_
            v = carve(ARENA, off[0], dtype, shape, name)
            off[0] += (n + 63) // 64 * 64
            assert off[0] <= 55296, off[0]
            return v

        TK = [AR(F32, [128, 12, 16], "TK%d" % i) for i in range(12)]
        GTOK, GC, BETA, GCB, EGC, BEG, GLE, GLO, EDE, EDO, TMPA, TMPB = TK
        AB = P.view("AB", ARENA.t[:, (10 * 768) // 4:(12 * 768) // 4].rearrange("p (b c) -> p b c", b=12))
        QNb = AR(BF16, [128, NT], "QNb")
        KNb = AR(BF16, [128, NT], "KNb")
        OTh = AR(BF16, [128, NT], "OTh")
        BS = []
        BSR = []
        for d_ in range(2):
            row, rowr = [], []
            for i in range(3):
                k_ = d_ * 3 + i
                apr = CHAIN[:, k_ * 512:(k_ + 1) * 512].rearrange("p (a b) -> p a b", a=4)
                apf = CHAIN.bitcast(F32)[:, k_ * 512:(k_ + 1) * 512].rearrange("p (a b) -> p a b", a=4)
                row.append(P.view("BS%d_%d" % (d_, i), apf))
                rowr.append(apr)
            BS.append(row)
            BSR.append(rowr)
        TTbs = [AR(BF16, [128, 4, 128], "TTb%d" % d_) for d_ in range(2)]
        VBs = [AR(BF16, [128, 4, 128], "VB%d" % d_) for d_ in range(2)]
        KBGs = [AR(BF16, [128, 4, 128], "KBG%d" % d_) for d_ in range(2)]
        F = [b_.alias(b_.t[:, :].rearrange("p (a b) -> p a b", a=4)) for b_ in (RSTD[0], RSTD[1], TMPN[0], TMPN[1])]
        TC = F
        SETS = []
        for i in range(4):
            SETS.append(dict(U=AR(BF16, [128, 4, 128], "U%d" % i), NWT=AR(BF16, [128, 4, 128], "NWT%d" % i),
                             PT=AR(BF16, [128, 4, 128], "PT%d" % i), KD=[AR(BF16, [128, 4, 128], "KDe%d" % i),
                                                                        AR(BF16, [128, 4, 128], "KDo%d" % i)],
                             QG=AR(BF16, [128, 4, 128], "QG%d" % i)))
        CH = []
        for i in range(4):
            CH.append(dict(S=AR(F32, [128, 128], "S%d" % i), Sb=AR(BF16, [128, 128], "Sb%d" % i),
                           VN=AR(BF16, [128, 128], "VN%d" % i)))
        WAB = AR(BF16, [128, KC, 32], "WAB")
        DTB16 = AR(F32, [128, 16], "DTB16")
        NEGA = AR(F32, [128, 16], "NEGA")
        WI = [carve(WREG, 4096 + i * 4096, BF16, [128, KC, 2, 128], "WI%d" % i) for i in range(2)]
        WOH = [carve(WREG, i * 2048, BF16, [128, D], "WOH%d" % i) for i in range(2)]
        CTQ, CTK, CTV, CTZ = CT
        OF = CTQ
        OFb = [OF.sub("OF%d" % b) for b in range(12)]
        for ch in CH:
            P.add("pool", lambda e, ch=ch: e.memset(ch["VN"][:, :], 0.0), writes=[ch["VN"]])

        P.dma(WAB[:, :, :].rearrange("p k n -> p (k n)"), gdn_w_ab[j, :, :], writes=[WAB], q="pool")
        P.dma(DTB16[:, :], gdn_dt_bias[j:j + 1, :].to_broadcast([128, 16]), writes=[DTB16])
        P.dma(NEGA[:, :], gdn_a_log[j:j + 1, :].to_broadcast([128, 16]), writes=[NEGA])
        P.add("act", lambda e: e.activation(NEGA[:, :], NEGA[:, :], AF.Exp), reads=[NEGA], writes=[NEGA])
        pb = psum()
        for blk in range(12):
            for kc in range(KC):
                P.add("pe", lambda e, pb=pb, blk=blk, kc=kc: e.matmul(
                    pb[:, blk * 32:(blk + 1) * 32], H[:, kc, blk * 128:(blk + 1) * 128], WAB[:, kc, :],
                    start=(kc == 0), stop=(kc == KC - 1)), reads=[WAB, Hb[(kc, blk // 4)]], writes=[pb])
        P.add("dve", lambda e, pb=pb: e.tensor_copy(AB[:, :, :], pb[:, 0:384].rearrange("p (b c) -> p b c", b=12)),
              reads=[pb], writes=[TMPA, TMPB])
        bc16 = lambda t: t[:, :].unsqueeze(1).to_broadcast([128, 12, 16])
        P.add("dve", lambda e: e.tensor_tensor(GTOK[:, :, :], AB[:, :, 0:16], bc16(DTB16), ALU.add),
              reads=[TMPA, TMPB, DTB16], writes=[GTOK])
        P.add("act", lambda e: e.activation(GTOK[:, :, :], GTOK[:, :, :], AF.Exp), reads=[GTOK], writes=[GTOK])
        P.add("dve", lambda e: e.tensor_scalar(GTOK[:, :, :], GTOK[:, :, :], 1.0, None, ALU.add), reads=[GTOK], writes=[GTOK])
        P.add("act", lambda e: e.activation(GTOK[:, :, :], GTOK[:, :, :], AF.Ln), reads=[GTOK], writes=[GTOK])
        P.add("dve", lambda e: e.scalar_tensor_tensor(GTOK[:, :, :], GTOK[:, :, :], -1.0, bc16(NEGA), ALU.mult, ALU.mult),
              reads=[GTOK, NEGA], writes=[GTOK])
        P.add("act", lambda e: e.activation(BETA[:, :, :], AB[:, :, 16:32], AF.Exp, scale=-1.0),
              reads=[TMPA, TMPB], writes=[BETA])
        P.add("dve", lambda e: e.tensor_scalar(BETA[:, :, :], BETA[:, :, :], 1.0, None, ALU.add), reads=[BETA], writes=[BETA])
        P.add("act", lambda e: e.activation(GCB[:, :, :], BETA[:, :, :], AF.Ln), reads=[BETA], writes=[GCB])
        P.add("dve", lambda e: e.reciprocal(BETA[:, :, :], BETA[:, :, :]), reads=[BETA], writes=[BETA])
        pb = psum()
        for blk in range(12):
            for d in range(2):
                tri = C_TRIF if d == 0 else C_TRIB
                P.add("pe", lambda e, pb=pb, blk=blk, d=d, tri=tri: e.matmul(
                    pb[:, blk * 16 + d * 8:blk * 16 + d * 8 + 8], tri, GTOK[:, blk, d * 8:d * 8 + 8],
                    start=True, stop=True), reads=[CONST, GTOK], writes=[pb])
        P.add("dve", lambda e, pb=pb: e.tensor_copy(GC[:, :, :], pb[:, 0:192].rearrange("p (b c) -> p b c", b=12)),
              reads=[pb], writes=[GC])
        for (dst, cm) in ((GLE, C_EVEN), (GLO, C_ODD)):
            pb = psum()
            for blk in range(12):
                P.add("pe", lambda e, pb=pb, blk=blk, cm=cm: e.matmul(
                    pb[:, blk * 16:(blk + 1) * 16], cm, GTOK[:, blk, :], start=True, stop=True),
                    reads=[CONST, GTOK], writes=[pb])
            P.add("dve", lambda e, pb=pb, dst=dst: e.tensor_copy(
                dst[:, :, :], pb[:, 0:192].rearrange("p (b c) -> p b c", b=12)), reads=[pb], writes=[dst])
        P.add("dve", lambda e: e.tensor_tensor(GCB[:, :, :], GC[:, :, :], GCB[:, :, :], ALU.subtract),
              reads=[GC, GCB], writes=[GCB])
        P.add("act", lambda e: e.activation(EGC[:, :, :], GC[:, :, :], AF.Exp), reads=[GC], writes=[EGC])
        P.add("dve", lambda e: e.tensor_tensor(BEG[:, :, :], BETA[:, :, :], EGC[:, :, :], ALU.mult),
              reads=[BETA, EGC], writes=[BEG])
        for (dst, gl, mcol) in ((EDE, GLE, C_EVEN[:, 0:1]), (EDO, GLO, C_ODD[:, 0:1])):
            P.add("dve", lambda e, dst=dst, gl=gl: e.tensor_tensor(dst[:, :, :], gl[:, :, :], GC[:, :, :], ALU.subtract),
                  reads=[gl, GC], writes=[dst])
            P.add("dve", lambda e, dst=dst: e.tensor_scalar(dst[:, :, :], dst[:, :, :], 0.0, None, ALU.min),
                  reads=[dst], writes=[dst])
            P.add("act", lambda e, dst=dst: e.activation(dst[:, :, :], dst[:, :, :], AF.Exp), reads=[dst], writes=[dst])
            P.add("dve", lambda e, dst=dst, mcol=mcol: e.tensor_scalar(dst[:, :, :], dst[:, :, :], mcol, None, ALU.mult),
                  reads=[dst, CONST], writes=[dst])
        P.add("act", lambda e: e.activation(GLE[:, :, :], GLE[:, :, :], AF.Exp), reads=[GLE], writes=[GLE])
        P.add("act", lambda e: e.activation(GLO[:, :, :], GLO[:, :, :], AF.Exp), reads=[GLO], writes=[GLO])
        EGL = [GLE, GLO]
        ED = [EDE, EDO]
        NGC = TMPA
        P.add("dve", lambda e: e.tensor_scalar(NGC[:, :, :], GC[:, :, :], -1.0, None, ALU.mult), reads=[GC], writes=[NGC])

        wis_ = WStream(WI, 16, lambda i, wi: P.dma(wi[:, :, :, :].rearrange("p k g n -> p (k g n)"),
                                                   gdn_w_in[j, i // 2, i % 2, :, :], writes=[wi], q="pool"))
        whs_ = WStream(WOH, 8, lambda i, woh: P.dma(woh[:, :], gdn_w_out[j, i * 128:(i + 1) * 128, :],
                                                    writes=[woh], q="pool"))
        PRE_BANKS, SCAN_BANKS = [0, 1, 2, 3, 4], [5, 6, 7]
        pre_rr, scan_rr = [0], [0]
        for h in range(8):
            pump(3)
            for pi in range(2):
                wi = wis_.need(h * 2 + pi)
                for g in range(2):
                    t = pi * 2 + g
                    b0 = bank_group()
                    for tb in range(NTB):
                        pbk = PS[b0 + tb]
                        for kc in range(KC):
                            P.add("pe", lambda e, pbk=pbk, wi=wi, kc=kc, g=g, tb=tb: e.matmul(
                                pbk[:, :], wi[:, kc, g, :], H[:, kc, tb * TB:(tb + 1) * TB],
                                start=(kc == 0), stop=(kc == KC - 1)), reads=[wi, Hb[(kc, tb)]], writes=[pbk])
                    ct = CT[t]
                    if t < 3:
                        ch = t * 8 + h
                        conv_from_psum(b0, GCW[:, j, 0, ch:ch + 1], GCW[:, j, 1, ch:ch + 1], GCW[:, j, 2, ch:ch + 1],
                                       ZEROC[:, 0:1], ct, ct)
                        P.add("act", lambda e, ct=ct: e.activation(ct[:, :], ct[:, :], AF.Silu), reads=[ct], writes=[ct])
                    else:
                        P.add("act", lambda e, ct=ct, b0=b0: e.activation(ct[:, 0:512], PS[b0][:, :], AF.Silu),
                              reads=[PS[b0]], writes=[ct])
                        P.add("act", lambda e, ct=ct, b0=b0: e.activation(
                            ct[:, 512:NT], PSALL[:, (b0 + 1) * 512:(b0 + 3) * 512], AF.Silu),
                            reads=[PS[b0 + 1], PS[b0 + 2]], writes=[ct])
            f2 = lambda T_: T_[:, :, :].rearrange("p a b -> p (a b)")
            for (ct, dstb, keep) in ((CTQ, QNb, False), (CTK, KNb, True)):
                sls = [slice(tb * TB, (tb + 1) * TB) for tb in range(NTB)]
                for tb in range(NTB):
                    P.add("act", lambda e: e.activation(f2(TC[tb]), ct[:, sls[tb]], AF.Square), reads=[ct], writes=[TC[tb]])
                pbks = []
                for tb in range(NTB):
                    pbk = psum()
                    pbks.append(pbk)
                    P.add("pe", lambda e: e.matmul(pbk[:, :], C_ONES, f2(TC[tb]), start=True, stop=True),
                          reads=[CONST, TC[tb]], writes=[pbk])
                for tb in range(NTB):
                    P.add("act", lambda e: e.activation(f2(TC[tb]), pbks[tb][:, :], AF.Ln, bias=EPSC[:, 0:1], scale=1.0),
                          reads=[pbks[tb], EPSC], writes=[TC[tb]])
                for tb in range(NTB):
                    P.add("act", lambda e: e.activation(f2(TC[tb]), f2(TC[tb]), AF.Exp, scale=-0.5),
                          reads=[TC[tb]], writes=[TC[tb]])
                for tb in range(NTB):
                    sl = sls[tb]
                    if keep:
                        P.add("pool", lambda e: e.tensor_tensor(ct[:, sl], ct[:, sl], f2(TC[tb]), ALU.mult),
                              reads=[ct, TC[tb]], writes=[ct])
                        P.add("act", lambda e: e.copy(dstb[:, sl], ct[:, sl]), reads=[ct], writes=[dstb])
                    else:
                        P.add("pool", lambda e: e.tensor_tensor(dstb[:, sl], ct[:, sl], f2(TC[tb]), ALU.mult),
                              reads=[ct, TC[tb]], writes=[dstb])
            P.add("pool", lambda e: e.memset(OF[:, :], 0.0), writes=[OF] + OFb)

            f4 = lambda T_: T_[:, :, :].rearrange("p a b -> p (a b)")
            v4 = lambda pb_: pb_[:, :].rearrange("p (b f) -> p b f", b=4)
            Ibc = C_I.unsqueeze(1).to_broadcast([128, 4, 128])

            def pre_dir(g, d, st, pk, pv, pg, pq):
                cd = d * 8 + h
                blks = slice(g * 4, g * 4 + 4)
                tsl = slice(g * 512, (g + 1) * 512)
                col = lambda T_: T_[:, blks, cd:cd + 1].to_broadcast([128, 4, 128])
                B = BS[d]
                Fa, Fb = F[2 * d], F[2 * d + 1]
                VB, KBG = VBs[d], KBGs[d]
                P.add("dve", lambda e: e.tensor_tensor(VB[:, :, :], v4(pv), col(BETA), ALU.mult),
                      reads=[pv, BETA], writes=[VB])
                P.add("dve", lambda e: e.tensor_tensor(KBG[:, :, :], v4(pk), col(BEG), ALU.mult),
                      reads=[pk, BEG], writes=[KBG])
                for u in range(2):
                    P.add("dve", lambda e: e.tensor_tensor(st["KD"][u][:, :, :], v4(pk), col(ED[u]), ALU.mult),
                          reads=[pk, ED[u]], writes=[st["KD"][u]])
                yield
                P.add("pool", lambda e: e.tensor_tensor(Fa[:, :, :], Ibc, col(EGC), ALU.mult),
                      reads=[CONST, EGC], writes=[Fa])
                pr = psum_of(PRE_BANKS, pre_rr)
                for b in range(4):
                    P.add("pe", lambda e: e.matmul(pr[:, b * 128:(b + 1) * 128], C_ONES, Fa[:, b, :],
                                                   start=True, stop=True), reads=[CONST, Fa], writes=[pr])
                P.add("dve", lambda e: e.scalar_tensor_tensor(f4(st["QG"]), pr[:, :], 128.0 ** -0.5, QNb[:, tsl],
                                                              ALU.mult, ALU.mult), reads=[pr, QNb], writes=[st["QG"]])
                yield
                P.add("pool", lambda e: e.tensor_tensor(Fa[:, :, :], Ibc, col(GC), ALU.mult),
                      reads=[CONST, GC], writes=[Fa])
                P.add("pool", lambda e: e.tensor_tensor(Fb[:, :, :], Ibc, col(GCB), ALU.mult),
                      reads=[CONST, GCB], writes=[Fb])
                pa = psum_of(PRE_BANKS, pre_rr)
                pbb = psum_of(PRE_BANKS, pre_rr)
                for (pz, dg, msk) in ((pa, Fa, C_MINC[d]), (pbb, Fb, C_MSTR[d])):
                    for b in range(4):
                        o_ = pz[:, b * 128:(b + 1) * 128]
                        P.add("pe", lambda e: e.matmul(o_, C_ONES, dg[:, b, :], start=True, stop=False),
                              reads=[CONST, dg], writes=[pz])
                        P.add("pe", lambda e: e.matmul(o_, C_I, msk, start=False, stop=True),
                              reads=[CONST], writes=[pz])
                BT, Bs_, M_ = B
                BTr, Bsr, Mr = [x_.t for x_ in B]
                r4 = lambda ap_: ap_[:, :, :].rearrange("p a b -> p (a b)")
                for b in range(4):
                    ngc = NGC[:, g * 4 + b, cd:cd + 1]
                    P.add("act", lambda e: e.activation(Mr[:, b, :], pa[:, b * 128:(b + 1) * 128], AF.Exp, bias=ngc),
                          reads=[pa, NGC], writes=[M_])
                    P.add("act", lambda e: e.activation(BTr[:, b, :], pbb[:, b * 128:(b + 1) * 128], AF.Exp, bias=ngc),
                          reads=[pbb, NGC], writes=[BT])
                yield
                pg = psum_of(PRE_BANKS, pre_rr)
                pq = psum_of(PRE_BANKS, pre_rr)
                for b in range(4):
                    c0 = g * 512 + b * 128
                    P.add("pe", lambda e: e.matmul(pg[:, b * 128:(b + 1) * 128], KNb[:, c0:c0 + 128], KNb[:, c0:c0 + 128],
                                                   start=True, stop=True), reads=[KNb], writes=[pg])
                for b in range(4):
                    c0 = g * 512 + b * 128
                    P.add("pe", lambda e: e.matmul(pq[:, b * 128:(b + 1) * 128], KNb[:, c0:c0 + 128], QNb[:, c0:c0 + 128],
                                                   start=True, stop=True), reads=[KNb, QNb], writes=[pq])
                P.add("dve", lambda e: e.scalar_tensor_tensor(f4(st["PT"]), pq[:, :], 128.0 ** -0.5, f4(M_),
                                                              ALU.mult, ALU.mult), reads=[pq, M_], writes=[st["PT"]])
                P.add("dve", lambda e: e.scalar_tensor_tensor(r4(BTr), pg[:, :], -1.0, f4(BT), ALU.mult, ALU.mult),
                      reads=[pg, BT], writes=[BT])
                yield
                pt_ = psum_of(PRE_BANKS, pre_rr)
                for b in range(4):
                    P.add("pe", lambda e: e.transpose(pt_[:, b * 128:(b + 1) * 128], BT[:, b, :], C_I),
                          reads=[BT, CONST], writes=[pt_])
                evac(r4(Bsr), [Bs_], pt_[:, :], [pt_])
                P.add("pool", lambda e: e.tensor_tensor(Mr[:, :, :], BT[:, :, :], Ibc, ALU.add),
                      reads=[BT, CONST], writes=[M_])
                yield
                TTb = TTbs[d]
                for k in range(5):
                    if k < 4:
                        px = psum_of(PRE_BANKS, pre_rr)
                        for b in range(4):
                            P.add("pe", lambda e: e.matmul(px[:, b * 128:(b + 1) * 128], Bsr[:, b, :], BTr[:, b, :],
                                                           start=True, stop=True), reads=[Bs_, BT], writes=[px])
                    py = psum_of(PRE_BANKS, pre_rr)
                    for b in range(4):
                        P.add("pe", lambda e: e.matmul(py[:, b * 128:(b + 1) * 128], BTr[:, b, :], Bsr[:, b, :],
                                                       start=True, stop=True), reads=[Bs_, BT], writes=[py])
                    if k < 4:
                        evac(r4(BTr), [BT], px[:, :], [px])
                    evac(r4(Bsr), [Bs_], py[:, :], [py])
                    yield
                    pm_ = psum_of(PRE_BANKS, pre_rr)
                    for b in range(4):
                        o_ = pm_[:, b * 128:(b + 1) * 128]
                        P.add("pe", lambda e: e.matmul(o_, Bsr[:, b, :], Mr[:, b, :], start=True, stop=True),
                              reads=[Bs_, M_], writes=[pm_])
                    if k < 4:
                        P.add("dve", lambda e: e.tensor_tensor(r4(Mr), pm_[:, :], f4(M_), ALU.add),
                              reads=[pm_, M_], writes=[M_])
                    else:
                        P.add("dve", lambda e: e.tensor_tensor(f4(TTb), pm_[:, :], f4(M_), ALU.add),
                              reads=[pm_, M_], writes=[TTb])
                    yield
                pu = psum_of(PRE_BANKS, pre_rr)
                pw = psum_of(PRE_BANKS, pre_rr)
                for b in range(4):
                    P.add("pe", lambda e: e.matmul(pu[:, b * 128:(b + 1) * 128], TTb[:, b, :], VB[:, b, :],
                                                   start=True, stop=True), reads=[TTb, VB], writes=[pu])
                for b in range(4):
                    P.add("pe", lambda e: e.matmul(pw[:, b * 128:(b + 1) * 128], KBG[:, b, :], TTb[:, b, :],
                                                   start=True, stop=True), reads=[TTb, KBG], writes=[pw])
                evac(f4(st["U"]), [st["U"]], pu[:, :], [pu])
                evac(f4(st["NWT"]), [st["NWT"]], pw[:, :], [pw], scale=-1.0)
                yield

            def precompute2_gen(g, sts):
                pk = psum_of(PRE_BANKS, pre_rr)
                pv = psum_of(PRE_BANKS, pre_rr)
                for b in range(4):
                    c0 = g * 512 + b * 128
                    P.add("pe", lambda e: e.transpose(pk[:, b * 128:(b + 1) * 128], CTK[:, c0:c0 + 128], C_I),
                          reads=[CTK, CONST], writes=[pk])
                for b in range(4):
                    c0 = g * 512 + b * 128
                    P.add("pe", lambda e: e.transpose(pv[:, b * 128:(b + 1) * 128], CTV[:, c0:c0 + 128], C_I),
                          reads=[CTV, CONST], writes=[pv])
                gens = [pre_dir(g, d_, sts[d_], pk, pv, None, None) for d_ in range(2)]
                while gens:
                    for gi in list(gens):
                        try:
                            next(gi)
                        except StopIteration:
                            gens.remove(gi)
                    yield

            def interleave(gen, rounds, every):
                i = 0
                r = 0
                for _ in gen:
                    i += 1
                    if i % every == 0 and r < len(rounds):
                        rounds[r]()
                        r += 1
                while r < len(rounds):
                    rounds[r]()
                    r += 1

            def scan_round(items):
                b1 = psum_of(SCAN_BANKS, scan_rr)
                b2 = psum_of(SCAN_BANKS, scan_rr)
                b3 = psum_of(SCAN_BANKS, scan_rr)
                for ci, (ch, st, d, blk, u) in enumerate(items):
                    bi = blk % 4
                    rsl = slice(u * 64, u * 64 + 64)
                    P.add("pe", lambda e: e.matmul(b1[rsl, ci * 128:(ci + 1) * 128], st["NWT"][:, bi, rsl], ch["Sb"][:, :],
                                                   start=True, stop=True), reads=[st["NWT"], ch["Sb"]], writes=[b1])
                for ci, (ch, st, d, blk, u) in enumerate(items):
                    bi = blk % 4
                    rsl = slice(u * 64, u * 64 + 64)
                    P.add("dve", lambda e: e.tensor_tensor(ch["VN"][rsl, :], st["U"][rsl, bi, :],
                                                           b1[rsl, ci * 128:(ci + 1) * 128], ALU.add),
                          reads=[st["U"], b1], writes=[ch["VN"]])
                for ci, (ch, st, d, blk, u) in enumerate(items):
                    bi = blk % 4
                    rsl = slice(u * 64, u * 64 + 64)
                    o2 = b2[:, ci * 64:(ci + 1) * 64]
                    P.add("pe", lambda e: e.matmul(o2, ch["Sb"][:, :], st["QG"][:, bi, rsl], start=True, stop=False),
                          reads=[ch["Sb"], st["QG"]], writes=[b2])
                    P.add("pe", lambda e: e.matmul(o2, ch["VN"][:, :], st["PT"][:, bi, rsl], start=False, stop=True),
                          reads=[ch["VN"], st["PT"]], writes=[b2])
                    P.add("pe", lambda e: e.matmul(b3[:, ci * 128:(ci + 1) * 128], st["KD"][u][:, bi, :], ch["VN"][:, :],
                                                   start=True, stop=True), reads=[st["KD"][u], ch["VN"]], writes=[b3])
                for ci, (ch, st, d, blk, u) in enumerate(items):
                    cd = d * 8 + h
                    S, Sb = ch["S"], ch["Sb"]
                    o3 = b3[:, ci * 128:(ci + 1) * 128]
                    P.add("dve", lambda e: e.scalar_tensor_tensor(Sb[:, :], S[:, :], EGL[u][:, blk, cd:cd + 1], o3,
                                                                  ALU.mult, ALU.add), reads=[S, EGL[u], b3], writes=[Sb])
                    P.add("dve", lambda e: e.scalar_tensor_tensor(S[:, :], S[:, :], EGL[u][:, blk, cd:cd + 1], o3,
                                                                  ALU.mult, ALU.add), reads=[S, EGL[u], b3], writes=[S])
                for ci, (ch, st, d, blk, u) in enumerate(items):
                    c0 = blk * 128 + u * 64
                    P.add("dve", lambda e: e.tensor_tensor(OF[:, c0:c0 + 64], OF[:, c0:c0 + 64],
                                                           b2[:, ci * 64:(ci + 1) * 64], ALU.add),
                          reads=[OFb[blk], b2], writes=[OFb[blk]])

            def scan_step(ch, st, d, blk, u):
                scan_round([(ch, st, d, blk, u)])

            def chain_steps(d, blocks):
                steps = [(b, u) for b in blocks for u in range(2)]
                return steps if d == 0 else steps[::-1]

            def init_chain(ch, d, seq):
                if seq < 2:
                    P.add("pool", lambda e: e.memset(ch["S"][:, :], 0.0), writes=[ch["S"]])
                else:
                    P.dma(ch["S"][:, :], state_gdn[j, d, h, :, :], writes=[ch["S"]])
                P.add("act", lambda e: e.copy(ch["Sb"][:, :], ch["S"][:, :]), reads=[ch["S"]], writes=[ch["Sb"]])

            for _ in precompute2_gen(0, [SETS[2], SETS[3]]):
                pass
            chains = []
            for d in range(2):
                for seq in range(2):
                    ch = CH[d * 2 + seq]
                    init_chain(ch, d, seq)
                    chains.append((ch, SETS[2 + d], d, seq, chain_steps(d, [2 * seq, 2 * seq + 1])))

            def roundA(si):
                scan_round([(ch, st, d, steps[si][0], steps[si][1]) for (ch, st, d, seq, steps) in chains])

            interleave(precompute2_gen(1, [SETS[0], SETS[1]]), [lambda si=si: roundA(si) for si in range(4)], 3)
            for (ch, st, d, seq, steps) in chains:
                P.dma(st_out[seq, j, d, h, :, :], ch["S"][:, :], reads=[ch["S"]])
            chB = [CH[0], CH[1]]
            stepsB = [chain_steps(d, list(range(4, 12))) for d in range(2)]
            for d in range(2):
                init_chain(chB[d], d, 2)

            def stepB(d, si):
                blk, u = stepsB[d][si]
                scan_step(chB[d], SETS[(blk // 4 - 1) * 2 + d], d, blk, u)

            interleave(precompute2_gen(2, [SETS[2], SETS[3]]), [lambda si=si: stepB(0, si) for si in range(8)], 2)
            def itemB(d, si):
                blk, u = stepsB[d][si]
                return (chB[d], SETS[(blk // 4 - 1) * 2 + d], d, blk, u)

            for si in range(16):
                scan_round([itemB(1, si)] + ([itemB(0, 8 + si)] if si < 8 else []))

            sls = [slice(tb * TB, (tb + 1) * TB) for tb in range(NTB)]
            for tb in range(NTB):
                P.add("act", lambda e: e.activation(f2(TC[tb]), OF[:, sls[tb]], AF.Square),
                      reads=OFb[tb * 4:tb * 4 + 4] + [OF], writes=[TC[tb]])
            pbks = []
            for tb in range(NTB):
                pbk = psum()
                pbks.append(pbk)
                P.add("pe", lambda e: e.matmul(pbk[:, :], C_ONES, f2(TC[tb]), start=True, stop=True),
                      reads=[CONST, TC[tb]], writes=[pbk])
            for tb in range(NTB):
                P.add("act", lambda e: e.activation(f2(TC[tb]), pbks[tb][:, :], AF.Ln, bias=EPSC[:, 0:1], scale=1.0 / 128),
                      reads=[pbks[tb], EPSC], writes=[TC[tb]])
            for tb in range(NTB):
                P.add("act", lambda e: e.activation(f2(TC[tb]), f2(TC[tb]), AF.Exp, scale=-0.5), reads=[TC[tb]], writes=[TC[tb]])
            for tb in range(NTB):
                sl = sls[tb]
                P.add("pool", lambda e: e.tensor_tensor(f2(TC[tb]), OF[:, sl], f2(TC[tb]), ALU.mult),
                      reads=OFb[tb * 4:tb * 4 + 4] + [TC[tb], OF], writes=[TC[tb]])
                P.add("dve", lambda e: e.scalar_tensor_tensor(
                    OTh[:, sl], f2(TC[tb]), GNG[:, j:j + 1], CTZ[:, sl], ALU.mult, ALU.mult),
                    reads=[TC[tb], GNG, CTZ], writes=[OTh])
            woh = whs_.need(h)
            for n in range(KC):
                for tb in range(NTB):
                    pbk = psum()
                    P.add("pe", lambda e, pbk=pbk, n=n, tb=tb, woh=woh: e.matmul(
                        pbk[:, :], woh[:, n * 128:(n + 1) * 128], OTh[:, tb * TB:(tb + 1) * TB], start=True, stop=True),
                        reads=[woh, OTh], writes=[pbk])
                    resid_from_psum(pbk, l, 16, n, tb)

    P.fence()
    emit_rbpad()
    mod_state["gen"] = mod_gen(0)
    pump(24)
    for l in range(cfg.n_layers):
        emit_norm_mod(l, 1)
        P.fence()
        if l % 2 == 1 and cfg.do_na:
            emit_na(l)
        if l % 2 == 0 and cfg.do_gdn:
            emit_gdn(l)
        pump(48)
        emit_norm_mod(l, 2)
        P.fence()
        if l + 1 < cfg.n_layers:
            mod_state["gen"] = mod_gen(l + 1)
        emit_ffn(l)
        pump(48)
    P.fence()

    FG = carve(ARENA, 8192, F32, [128, D], "FG")
    P.dma(FG[:, :], final_g.rearrange("(o d) -> o d", o=1).to_broadcast([128, D]), writes=[FG])
    YT = [carve(ARENA, i * 4096, F32, [128, D], "YT%d" % i) for i in range(2)]
    YSQ = carve(ARENA, 12288, F32, [128, D], "YSQ")
    SS = [P.sb("SS%d" % i, [128, 1], F32) for i in range(2)]
    for blk in range(NT // 128):
        yt = YT[blk % 2]
        ss = SS[blk % 2]
        tb = (blk * 128) // TB
        for half in range(2):
            pb = psum()
            for j in range(4):
                kc = half * 4 + j
                P.add("pe", lambda e, pb=pb, kc=kc, j=j, blk=blk: e.transpose(
                    pb[:, j * 128:(j + 1) * 128], X[:, kc, blk * 128:(blk + 1) * 128], CONST[:, 0:128]),
                    reads=[Xb[(kc, tb)], CONST], writes=[pb])
            P.add("act", lambda e, pb=pb, yt=yt, half=half: e.copy(yt[:, half * 512:(half + 1) * 512], pb[:, :]),
                  reads=[pb], writes=[yt])
        P.add("act", lambda e, yt=yt, ss=ss: e.activation(YSQ[:, :], yt[:, :], AF.Square, accum_out=ss[:, 0:1]),
              reads=[yt], writes=[YSQ, ss])
        P.add("act", lambda e, ss=ss: e.activation(ss[:, :], ss[:, :], AF.Ln, bias=EPSC[:, 0:1], scale=1.0 / D),
              reads=[ss, EPSC], writes=[ss])
        P.add("act", lambda e, ss=ss: e.activation(ss[:, :], ss[:, :], AF.Exp, scale=-0.5), reads=[ss], writes=[ss])
        P.add("dve", lambda e, yt=yt, ss=ss: e.scalar_tensor_tensor(
            yt[:, :], yt[:, :], ss[:, 0:1], FG[:, :], ALU.mult, ALU.mult), reads=[yt, ss, FG], writes=[yt])
        P.dma(y_out[blk * 128:(blk + 1) * 128, :], yt[:, :], reads=[yt])

    P.emit()
    return nc


def make_consts():
    c = np.zeros((128, 1536), np.float32)
    c[:, 0:128] = np.eye(128, dtype=np.float32)
    J = np.zeros((64, 64), np.float32)
    J[np.arange(64), 63 - np.arange(64)] = 1.0
    c[0:64, 128:192] = J
    c[64:128, 128:192] = J
    qc = 63 - np.arange(64)[:, None]
    kc = np.arange(64)[None, :]
    cs = np.clip(qc - 8, 0, 48)
    inside = (kc >= cs) & (kc < cs + 16)
    c[0:64, 192:256] = np.where(inside, 0.0, -30000.0)
    jj = np.arange(128)[:, None]
    ii = np.arange(128)[None, :]
    same = (jj // 64) == (ii // 64)
    c[:, 256:384] = (same & (jj <= ii)).astype(np.float32)
    c[:, 384:512] = (same & (jj >= ii)).astype(np.float32)
    c[:, 512:640] = (jj < 64).astype(np.float32) * np.ones((1, 128), np.float32)
    c[:, 640:768] = (jj >= 64).astype(np.float32) * np.ones((1, 128), np.float32)
    NEG = -30000.0
    c[:, 768:896] = np.where(same & (jj <= ii), 0.0, NEG)
    c[:, 896:1024] = np.where(same & (jj < ii), 0.0, NEG)
    c[:, 1024:1152] = np.where(same & (jj >= ii), 0.0, NEG)
    c[:, 1152:1280] = np.where(same & (jj > ii), 0.0, NEG)
    c[:, 1280:1408] = 1.0
    c[:, 1408:1536] = -1.0
    return c


_NC_CACHE = {}


def kernel(x_prompt, x_sample, state_gdn, cache_k, cache_v, c, c_ctx, w_ada, b_ada, norm1_g, norm2_g,
           gdn_w_in, gdn_conv_w, gdn_a_log, gdn_dt_bias, gdn_norm_g, gdn_w_out,
           na_w_qkv, na_rel_bias, na_w_out, ffn_w_up, ffn_conv_w, ffn_conv_b, ffn_w_down, final_g, _cfg=Cfg, _cores=8):
    f = lambda a: np.ascontiguousarray(np.asarray(a, dtype=np.float32))
    nc = build(_cfg)
    consts = make_consts()
    c_ = np.ascontiguousarray
    w_ada_t = c_(f(w_ada).reshape(4, 8, 128, 48, 128).transpose(0, 3, 2, 1, 4)).reshape(4, 48, 128, 1024)
    w_up_t = c_(f(ffn_w_up).reshape(4, 8, 128, 2, 22, 128).transpose(0, 4, 2, 1, 3, 5)).reshape(4, 22, 128, 2048)
    w_dn_t = c_(f(ffn_w_down).reshape(4, 2, 11, 128, 8, 128).transpose(0, 1, 4, 3, 2, 5)).reshape(4, 2, 8, 128, 1408)
    qkv_t = c_(f(na_w_qkv).reshape(2, 8, 128, 3, 8, 128).transpose(0, 4, 2, 1, 3, 5)).reshape(2, 8, 128, 3072)
    nwo_t = c_(f(na_w_out).reshape(2, 8, 128, 8, 128).transpose(0, 3, 2, 1, 4)).reshape(2, 8, 128, 1024)
    gin = f(gdn_w_in)
    gin_t = c_(gin[:, :, :4096].reshape(2, 8, 128, 2, 2, 8, 128).transpose(0, 5, 3, 2, 1, 4, 6)).reshape(2, 8, 2, 128, 2048)
    gab_t = c_(gin[:, :, 4096:4128].reshape(2, 8, 128, 32).transpose(0, 2, 1, 3)).reshape(2, 128, 256)
    shared = {
        "w_ada": w_ada_t, "b_ada": f(b_ada), "norm1_g": f(norm1_g), "norm2_g": f(norm2_g),
        "gdn_w_in": gin_t, "gdn_w_ab": gab_t, "gdn_conv_w": f(gdn_conv_w),
        "gdn_a_log": f(gdn_a_log).reshape(2, 16), "gdn_dt_bias": f(gdn_dt_bias).reshape(2, 16),
        "gdn_norm_g": f(gdn_norm_g), "gdn_w_out": f(gdn_w_out),
        "na_w_qkv": qkv_t, "na_rel_bias": f(na_rel_bias).reshape(2, 240, 31), "na_w_out": nwo_t,
        "ffn_w_up": w_up_t, "ffn_conv_w": f(ffn_conv_w), "ffn_conv_b": f(ffn_conv_b),
        "ffn_w_down": w_dn_t, "final_g": f(final_g), "consts": consts,
    }
    xp = f(x_prompt)
    xs = f(x_sample)
    in_maps = []
    for i in range(_cores):
        m = dict(shared)
        m["x_in"] = np.concatenate([xp[2 * i].reshape(256, D), xp[2 * i + 1].reshape(256, D), xs[i]], axis=0)
        m["state_gdn"] = f(state_gdn[i])
        m["cache_k"] = f(cache_k[i]).reshape(2, 256, 1024)
        m["cache_v"] = f(cache_v[i]).reshape(2, 256, 1024)
        m["cvec"] = np.stack([f(c_ctx), f(c[i])], axis=0)
        in_maps.append(m)
    res = run_bass_kernel_spmd(nc, in_maps, core_ids=list(range(_cores)))
    R = res.results
    y_prompt = np.zeros((16, 256, D), np.float32)
    y_sample = np.zeros((8, 1024, D), np.float32)
    new_state = np.zeros((16, 2, 2, 8, 128, 128), np.float32)
    new_k = np.zeros((16, 2, 256, 16, 64), np.float32)
    new_v = np.zeros((16, 2, 256, 16, 64), np.float32)
    for i in range(_cores):
        y = R[i]["y_out"]
        y_prompt[2 * i] = y[0:256]
        y_prompt[2 * i + 1] = y[256:512]
        y_sample[i] = y[512:]
        new_state[2 * i:2 * i + 2] = R[i]["st_out"]
        new_k[2 * i:2 * i + 2] = R[i]["ck_out"].reshape(2, 2, 256, 16, 64)
        new_v[2 * i:2 * i + 2] = R[i]["cv_out"].reshape(2, 2, 256, 16, 64)
    return (y_prompt, y_sample, new_state, new_k, new_v)
```

```python
import numpy as np
import concourse.bass as bass
import concourse.mybir as mybir
from concourse.bass_utils import run_bass_kernel_spmd

F32 = mybir.dt.float32
F32R = mybir.dt.float32r
BF16 = mybir.dt.bfloat16
AF = mybir.ActivationFunctionType
ALU = mybir.AluOpType
AX = mybir.AxisListType

D = 1024
KC = 8
DEPTH = 4
NP_TOK = 512
NS_TOK = 1024
NT = NP_TOK + NS_TOK
TB = 512
NTB = NT // TB
D_FF = 2816
NFC = D_FF // 128
GDN_DIN = 4128
EPS = 1e-6
SEQS = [(0, 256), (256, 256), (512, 1024)]


class Buf:
    __slots__ = ("name", "t", "psum", "lastw", "readers", "dma_readers", "root")

    def __init__(self, name, t, psum=False):
        self.name = name
        self.t = t
        self.psum = psum
        self.lastw = None
        self.readers = {}
        self.dma_readers = []
        self.root = self

    def alias(self, ap, name=None):
        b = Buf(name or self.name, ap, self.psum)
        b.root = self.root
        return b

    def __getitem__(self, idx):
        return self.t[idx]

    def sub(self, name=None):
        return Buf(name or self.name, self.t, self.psum)


class Op:
    __slots__ = ("eng", "fn", "reads", "writes", "dma", "waits", "sig", "sem", "sigval", "deps", "n")


class _Rec:
    def __init__(self):
        self.call = None

    def __getattr__(self, name):
        def f(*args, **kwargs):
            assert self.call is None
            self.call = (name, args, kwargs)
            return None
        return f


def _bind(fn):
    r = _Rec()
    fn(r)
    name, args, kwargs = r.call
    return lambda e: getattr(e, name)(*args, **kwargs)


class Prog:
    ENGS = ("pe", "act", "dve", "pool", "sp")

    def __init__(self, nc, n_dma_ring=8):
        self.nc = nc
        self.ops = []
        self.nring = n_dma_ring
        self.sbuf_bytes = 0

    def sb(self, name, shape, dtype):
        t = self.nc.alloc_sbuf_tensor(name, list(shape), dtype)
        return Buf(name, t)

    def ps(self, name, shape, dtype=F32):
        t = self.nc.alloc_psum_tensor(name, list(shape), dtype)
        return Buf(name, t, psum=True)

    def add(self, eng, fn, reads=(), writes=(), dma=False):
        op = Op()
        op.eng = eng
        op.fn = _bind(fn)
        op.reads = [b.root for b in reads if b is not None]
        op.writes = [b.root for b in writes if b is not None]
        op.dma = dma
        op.waits = []
        op.sig = dma
        op.sem = None
        op.sigval = 0
        op.n = len(self.ops)
        self.ops.append(op)
        return op

    def fence(self):
        op = Op()
        op.eng = None
        op.n = len(self.ops)
        op.dma = False
        op.sig = False
        self.ops.append(op)

    def view(self, name, ap, psum=False):
        return Buf(name, ap, psum)

    def dma(self, out_ap, in_ap, reads=(), writes=(), q="sp", **kw):
        return self.add(q, lambda e: e.dma_start(out=out_ap, in_=in_ap, **kw), reads, writes, dma=True)

    def resolve(self):
        last_on = {}
        dma_since = []
        pending = {}
        real_ops = []
        for op in self.ops:
            if op.eng is None:
                snap = (dict(last_on), list(dma_since))
                dma_since = []
                for e in self.ENGS:
                    pending[e] = snap
                continue
            real_ops.append(op)
            wdeps = []
            rdeps = []
            for b in op.reads:
                if b.lastw is not None:
                    wdeps.append(b.lastw)
                if b.psum:
                    for e, r in b.readers.items():
                        if e != op.eng:
                            rdeps.append(r)
            for b in op.writes:
                if b.lastw is not None:
                    wdeps.append(b.lastw)
                rdeps.extend(b.readers.values())
                rdeps.extend(b.dma_readers)
            for b in op.writes:
                b.lastw = op
                b.readers = {}
                b.dma_readers = []
            for b in op.reads:
                if op.dma:
                    b.dma_readers.append(op)
                else:
                    b.readers[op.eng] = op
            deps = {}
            for d in wdeps:
                if d is op:
                    continue
                if d.dma or op.dma:
                    deps[d.n] = d
                elif d.eng == op.eng:
                    if op.eng != "pe":
                        deps[d.n] = d
                else:
                    deps[d.n] = d
            for d in rdeps:
                if d is op:
                    continue
                if d.dma or op.dma:
                    deps[d.n] = d
                elif d.eng != op.eng:
                    deps[d.n] = d
            if pending.get(op.eng) is not None:
                lo, dl = pending[op.eng]
                pending[op.eng] = None
                for e2, d in lo.items():
                    if e2 == op.eng and e2 == "pe" and not op.dma:
                        continue
                    deps[d.n] = d
                for d in dl:
                    deps[d.n] = d
            op.deps = list(deps.values())
            for d in op.deps:
                d.sig = True
            if op.dma:
                dma_since.append(op)
            else:
                last_on[op.eng] = op
        self.ops = real_ops

    def emit(self):
        nc = self.nc
        self.resolve()
        streams = {e: [] for e in self.ENGS}
        for op in self.ops:
            streams[op.eng].append(op)
        import contextlib
        with contextlib.ExitStack() as es:
            esem = {e: es.enter_context(nc.semaphore("s_" + e)) for e in self.ENGS}
            rings = {e: [es.enter_context(nc.semaphore("d_%s%d" % (e, i))) for i in range(self.nring)]
                     for e in ("sp", "pool", "act")}
            fin = es.enter_context(nc.semaphore("fin"))
            cnt = {e: 0 for e in self.ENGS}
            ringcnt = {e: [0] * self.nring for e in rings}
            ringpos = {e: 0 for e in rings}
            ring_prev = {}
            for op in self.ops:
                if op.dma:
                    r = ringpos[op.eng] % self.nring
                    ringpos[op.eng] += 1
                    ringcnt[op.eng][r] += 1
                    op.sem = rings[op.eng][r]
                    op.sigval = 16 * ringcnt[op.eng][r]
                    if ringcnt[op.eng][r] > 1:
                        ring_prev[op.n] = (op.sem, op.sigval - 16)
                elif op.sig:
                    cnt[op.eng] += 1
                    op.sem = esem[op.eng]
                    op.sigval = cnt[op.eng]
            waited = {e: {} for e in self.ENGS}
            for e in self.ENGS:
                for op in streams[e]:
                    need = {}
                    for d in op.deps:
                        k = id(d.sem)
                        if k not in need or need[k][1] < d.sigval:
                            need[k] = (d.sem, d.sigval)
                    if op.n in ring_prev:
                        s, v = ring_prev[op.n]
                        k = id(s)
                        if k not in need or need[k][1] < v:
                            need[k] = (s, v)
                    for k, (s, v) in need.items():
                        if waited[e].get(k, 0) >= v:
                            continue
                        waited[e][k] = v
                        op.waits.append((s, v))
            final_waits = []
            for e in rings:
                for r in range(self.nring):
                    if ringcnt[e][r] > 0:
                        final_waits.append((rings[e][r], 16 * ringcnt[e][r]))

            def run_stream(ename, eng):
                for op in streams[ename]:
                    for s, v in op.waits:
                        eng.wait_ge(s, v)
                    ins = op.fn(eng)
                    if op.sig:
                        ins.then_inc(op.sem, 16 if op.dma else 1)
                if ename == "sp":
                    for s, v in final_waits:
                        eng.wait_ge(s, v)

            with nc.Block() as block:
                @block.tensor
                def _(eng):
                    run_stream("pe", eng)

                @block.scalar
                def _(eng):
                    run_stream("act", eng)

                @block.vector
                def _(eng):
                    run_stream("dve", eng)

                @block.gpsimd
                def _(eng):
                    run_stream("pool", eng)

                @block.sync
                def _(eng):
                    run_stream("sp", eng)


class Cfg:
    n_layers = DEPTH
    do_gdn = True
    do_na = True


def build(cfg=Cfg):
    nc = bass.Bass("TRN2", target_bir_lowering=False)
    P = Prog(nc)

    def din(name, shape):
        return nc.dram_tensor(name, list(shape), F32, kind="ExternalInput").ap()

    def dout(name, shape):
        return nc.dram_tensor(name, list(shape), F32, kind="ExternalOutput").ap()

    x_in = din("x_in", [NT, D])
    state_gdn = din("state_gdn", [2, 2, 8, 128, 128])
    cache_k = din("cache_k", [2, 256, 1024])
    cache_v = din("cache_v", [2, 256, 1024])
    cvec = din("cvec", [2, D])
    w_ada = din("w_ada", [DEPTH, 48, 128, 1024])
    b_ada = din("b_ada", [DEPTH, 6 * D])
    norm1_g = din("norm1_g", [DEPTH, D])
    norm2_g = din("norm2_g", [DEPTH, D])
    gdn_w_in = din("gdn_w_in", [2, 8, 2, 128, 2048])
    gdn_w_ab = din("gdn_w_ab", [2, 128, 256])
    gdn_conv_w = din("gdn_conv_w", [2, 3, 3072])
    gdn_a_log = din("gdn_a_log", [2, 16])
    gdn_dt_bias = din("gdn_dt_bias", [2, 16])
    gdn_norm_g = din("gdn_norm_g", [2, 128])
    gdn_w_out = din("gdn_w_out", [2, D, D])
    na_w_qkv = din("na_w_qkv", [2, 8, 128, 3072])
    na_rel_bias = din("na_rel_bias", [2, 16 * 15, 31])
    na_w_out = din("na_w_out", [2, 8, 128, 1024])
    ffn_w_up = din("ffn_w_up", [DEPTH, 22, 128, 2048])
    ffn_conv_w = din("ffn_conv_w", [DEPTH, 3, 2 * D_FF])
    ffn_conv_b = din("ffn_conv_b", [DEPTH, 2 * D_FF])
    ffn_w_down = din("ffn_w_down", [DEPTH, 2, 8, 128, 1408])
    final_g = din("final_g", [D])
    consts = din("consts", [128, 1536])

    y_out = dout("y_out", [NT, D])
    st_out = dout("st_out", [2, 2, 2, 8, 128, 128])
    ck_out = dout("ck_out", [2, 2, 256, 1024])
    cv_out = dout("cv_out", [2, 2, 256, 1024])

    X = P.sb("X", [128, KC, NT], F32)
    Xb = {(kc, tb): X.sub("X%d_%d" % (kc, tb)) for kc in range(KC) for tb in range(NTB)}
    H = P.sb("H", [128, KC, NT], BF16)
    Hb = {(kc, tb): H.sub("H%d_%d" % (kc, tb)) for kc in range(KC) for tb in range(NTB)}
    ARENA = P.sb("ARENA", [128, 13824], F32)
    CHAIN = nc.alloc_sbuf_tensor("CHAIN", [128, 3072], F32R)
    WREG = P.sb("WREG", [128, 3584], F32)

    def carve(region, byte_off, dtype, shape, name):
        esz = 2 if dtype == BF16 else 4
        n = 1
        for d_ in shape[1:]:
            n *= d_
        base = region.t.bitcast(dtype) if dtype != F32 else region.t
        ap = base[:, byte_off // esz: byte_off // esz + n]
        if len(shape) == 3:
            ap = ap.rearrange("p (a b) -> p a b", a=shape[1])
        elif len(shape) == 4:
            ap = ap.rearrange("p (a b c) -> p a b c", a=shape[1], b=shape[2])
        return P.view(name, ap)

    CONST = P.sb("CONST", [128, 1536], F32)
    ident = CONST
    ONESB = P.sb("ONESB", [128, 128], BF16)
    IDB = P.sb("IDB", [128, 128], BF16)

    PSALL = nc.alloc_psum_tensor("psall", [128, 4096], F32)
    PS = [P.view("ps%d" % i, PSALL[:, i * 512:(i + 1) * 512], psum=True) for i in range(8)]
    ps_rr = [0]

    def psum():
        b = PS[ps_rr[0] % 8]
        ps_rr[0] += 1
        return b

    P.dma(CONST[:, :], consts[:, :], writes=[CONST])
    P.add("dve", lambda e: e.memset(ONESB[:, :], 1.0), writes=[ONESB])
    P.add("dve", lambda e: e.tensor_copy(IDB[:, :], CONST[:, 0:128]), reads=[CONST], writes=[IDB])

    XT = [carve(ARENA, i * 4096, F32, [128, D], "XT%d" % i) for i in range(2)]
    for blk in range(NT // 128):
        xt = XT[blk % 2]
        P.dma(xt[:, :], x_in[blk * 128:(blk + 1) * 128, :], writes=[xt])
        tb = (blk * 128) // TB
        for half in range(2):
            pb = psum()
            for j in range(4):
                kc = half * 4 + j
                P.add("pe", lambda e, pb=pb, xt=xt, kc=kc, j=j: e.transpose(
                    pb[:, j * 128:(j + 1) * 128], xt[:, kc * 128:(kc + 1) * 128], CONST[:, 0:128]),
                    reads=[xt, CONST], writes=[pb])
            wr = [Xb[(half * 4 + j, tb)] for j in range(4)]
            eng = "act" if half == 0 else "dve"
            if eng == "act":
                P.add("act", lambda e, pb=pb, half=half, blk=blk: e.copy(
                    X[:, half * 4:half * 4 + 4, blk * 128:(blk + 1) * 128],
                    pb[:, :].rearrange("p (j t) -> p j t", j=4)), reads=[pb], writes=wr)
            else:
                P.add("dve", lambda e, pb=pb, half=half, blk=blk: e.tensor_copy(
                    X[:, half * 4:half * 4 + 4, blk * 128:(blk + 1) * 128],
                    pb[:, :].rearrange("p (j t) -> p j t", j=4)), reads=[pb], writes=wr)

    STG = [P.sb("STG%d" % i, [128, 128], F32) for i in range(2)]
    stg_rr = [0]

    def load_fm(dst, dst_flat_ap, src_rows_ap, R):
        r0 = 0
        while r0 < R:
            r = min(128, R - r0)
            stg = STG[stg_rr[0] % 2]
            stg_rr[0] += 1
            P.dma(stg[0:r, :], src_rows_ap[r0:r0 + r, :], writes=[stg])
            pb = psum()
            P.add("pe", lambda e, pb=pb, stg=stg, r=r: e.transpose(pb[:, 0:r], stg[0:r, :], CONST[0:r, 0:r]),
                  reads=[stg, CONST], writes=[pb])
            P.add("dve", lambda e, pb=pb, r=r, r0=r0: e.tensor_copy(dst_flat_ap[:, r0:r0 + r], pb[:, 0:r]),
                  reads=[pb], writes=[dst])
            r0 += r

    G1 = P.sb("G1", [128, DEPTH, KC], F32)
    G2 = P.sb("G2", [128, DEPTH, KC], F32)
    BADA = P.sb("BADA", [128, DEPTH, 48], F32)
    CV = P.sb("CV", [128, 2, KC], F32)
    SCV = P.sb("SCV", [128, 2, KC], BF16)
    load_fm(G1, G1[:, :, :].rearrange("p l k -> p (l k)"), norm1_g.rearrange("l (k f) -> (l k) f", f=128), 32)
    load_fm(G2, G2[:, :, :].rearrange("p l k -> p (l k)"), norm2_g.rearrange("l (k f) -> (l k) f", f=128), 32)
    load_fm(BADA, BADA[:, :, :].rearrange("p l k -> p (l k)"), b_ada.rearrange("l (k f) -> (l k) f", f=128), 192)
    load_fm(CV, CV[:, :, :].rearrange("p v k -> p (v k)"), cvec.rearrange("v (k f) -> (v k) f", f=128), 16)
    P.add("act", lambda e: e.activation(SCV[:, :, :], CV[:, :, :], AF.Silu), reads=[CV], writes=[SCV])

    MOD = [P.sb("MOD%d" % l, [128, 48, 2], F32) for l in range(DEPTH)]
    A1 = [P.sb("A1_%d" % l, [128, KC, 2], F32) for l in range(DEPTH)]
    A2 = [P.sb("A2_%d" % l, [128, KC, 2], F32) for l in range(DEPTH)]
    WA = [P.sb("WA%d" % i, [128, KC, 128], BF16) for i in range(3)]
    wa_rr = [0]

    MODg = [[MOD[l].sub("MOD%d_%d" % (l, g)) for g in range(6)] for l in range(DEPTH)]

    class WStream:
        def __init__(self, slots, n, issue):
            self.slots, self.n, self.issue, self.nxt = slots, n, issue, 0

        def need(self, i):
            k = len(self.slots)
            while self.nxt <= min(i + k - 1, self.n - 1):
                self.issue(self.nxt, self.slots[self.nxt % k])
                self.nxt += 1
            return self.slots[i % k]

    def mod_gen(l):
        ws = WStream(WA, 48, lambda i, wa: P.dma(wa[:, :, :].rearrange("p k n -> p (k n)"), w_ada[l, i, :, :],
                                                 writes=[wa], q="pool"))
        for n in range(48):
            wa = ws.need(n)
            pm = PS[6 + mod_bank[0] % 2]
            mod_bank[0] += 1
            for kc in range(KC):
                P.add("pe", lambda e: e.matmul(pm[:, 0:2], wa[:, kc, :], SCV[:, :, kc],
                                               start=(kc == 0), stop=(kc == KC - 1)), reads=[wa, SCV], writes=[pm])
            P.add("dve", lambda e: e.tensor_scalar(MOD[l][:, n, :], pm[:, 0:2], BADA[:, l, n:n + 1], None, ALU.add),
                  reads=[pm, BADA], writes=[MODg[l][n // 8]])
            if n == 15:
                P.add("dve", lambda e: e.scalar_tensor_tensor(
                    A1[l][:, :, :], MOD[l][:, 8:16, :], 1.0, G1[:, l, :].unsqueeze(2).to_broadcast([128, KC, 2]),
                    ALU.add, ALU.mult), reads=[MODg[l][1], G1], writes=[A1[l]])
            if n == 39:
                P.add("dve", lambda e: e.scalar_tensor_tensor(
                    A2[l][:, :, :], MOD[l][:, 32:40, :], 1.0, G2[:, l, :].unsqueeze(2).to_broadcast([128, KC, 2]),
                    ALU.add, ALU.mult), reads=[MODg[l][4], G2], writes=[A2[l]])
            yield n

    mod_state = {"gen": None}
    mod_bank = [0]

    def pump(k):
        g = mod_state["gen"]
        if g is None:
            return
        for _ in range(k):
            try:
                next(g)
            except StopIteration:
                mod_state["gen"] = None
                return

    SQ = [P.sb("SQ%d" % i, [128, TB], BF16) for i in range(2)]
    RSTD = [P.sb("RSTD%d" % i, [128, TB], F32) for i in range(2)]
    TMPN = [P.sb("TMPN%d" % i, [128, TB], F32) for i in range(2)]
    EPSC = P.sb("EPSC", [128, 1], F32)
    P.add("dve", lambda e: e.memset(EPSC[:, :], EPS), writes=[EPSC])
    nrr = [0]

    def emit_norm_mod(l, which):
        A = A1[l] if which == 1 else A2[l]
        shoff = 0 if which == 1 else 24
        for tb in range(NTB):
            v = 0 if tb == 0 else 1
            rstd = RSTD[nrr[0] % 2]
            nrr[0] += 1
            sl = slice(tb * TB, (tb + 1) * TB)
            pb = psum()
            for kc in range(KC):
                sq = SQ[kc % 2]
                P.add("act", lambda e, sq=sq, sl=sl, kc=kc: e.activation(
                    sq[:, :], X[:, kc, sl], AF.Square), reads=[Xb[(kc, tb)]], writes=[sq])
                P.add("pe", lambda e, pb=pb, sq=sq, kc=kc: e.matmul(
                    pb[:, :], ONESB[:, :], sq[:, :], start=(kc == 0), stop=(kc == KC - 1)),
                    reads=[sq, ONESB], writes=[pb])
            P.add("act", lambda e, pb=pb, rstd=rstd: e.activation(
                rstd[:, :], pb[:, :], AF.Ln, bias=EPSC[:, 0:1], scale=1.0 / D),
                reads=[pb, EPSC], writes=[rstd])
            P.add("act", lambda e, rstd=rstd: e.activation(rstd[:, :], rstd[:, :], AF.Exp, scale=-0.5), reads=[rstd], writes=[rstd])
            for kc in range(KC):
                tmp = TMPN[kc % 2]
                P.add("dve", lambda e, tmp=tmp, kc=kc, sl=sl, rstd=rstd: e.tensor_tensor(
                    tmp[:, :], X[:, kc, sl], rstd[:, :], ALU.mult),
                    reads=[Xb[(kc, tb)], rstd], writes=[tmp])
                P.add("act", lambda e, tmp=tmp, kc=kc, sl=sl, v=v, A=A, l=l, shoff=shoff: e.activation(
                    H[:, kc, sl], tmp[:, :], AF.Identity,
                    bias=MOD[l][:, shoff + kc, v:v + 1], scale=A[:, kc, v:v + 1]),
                    reads=[tmp, A, MODg[l][shoff // 8]], writes=[Hb[(kc, tb)]])

    def resid_from_psum(pb, l, gate_off, n, tb, ncols=TB, col0=0):
        v = 0 if tb == 0 else 1
        sl = slice(tb * TB + col0, tb * TB + col0 + ncols)
        P.add("dve", lambda e: e.scalar_tensor_tensor(
            X[:, n, sl], pb[:, 0:ncols], MOD[l][:, gate_off + n, v:v + 1], X[:, n, sl], ALU.mult, ALU.add),
            reads=[pb, MODg[l][gate_off // 8], Xb[(n, tb)]], writes=[Xb[(n, tb)]])

    WUP = [carve(WREG, i * 4096, BF16, [128, KC, 2, 128], "WUP%d" % i) for i in range(2)]
    wup_rr = [0]
    CT = [P.sb("CT%d" % i, [128, NT], F32) for i in range(4)]
    ct_rr = [0]
    NFH = NFC // 2
    GB = carve(ARENA, 0, BF16, [128, NFH, NT], "GB")
    GBb = {(i, tb): GB.sub("GB%d_%d" % (i, tb)) for i in range(NFH) for tb in range(NTB)}
    FCW = P.sb("FCW", [128, DEPTH, 3, 44], F32)
    FCB = P.sb("FCB", [128, DEPTH, 44], F32)
    load_fm(FCW, FCW[:, :, :, :].rearrange("p l t k -> p (l t k)"),
            ffn_conv_w.rearrange("l t (k f) -> (l t k) f", f=128), DEPTH * 3 * 44)
    load_fm(FCB, FCB[:, :, :].rearrange("p l k -> p (l k)"), ffn_conv_b.rearrange("l (k f) -> (l k) f", f=128),
            DEPTH * 44)
    WDN = [carve(WREG, 8192 + i * 2816, BF16, [128, NFH, 128], "WDN%d" % i) for i in range(2)]
    wdn_rr = [0]
    grp_rr = [0]
    dn_rr = [0]

    def bank_group():
        g = grp_rr[0] % 2
        grp_rr[0] += 1
        return g * 3

    def conv_from_psum(b0, w0, w1, w2, bias, ct, ctb):
        pbs = [PS[b0], PS[b0 + 1], PS[b0 + 2]]
        samp = PSALL[:, (b0 + 1) * 512:(b0 + 3) * 512]
        P.add("act", lambda e: e.activation(ct[:, 0:512], PS[b0][:, :], AF.Identity, bias=bias, scale=w1),
              reads=[pbs[0], FCWb], writes=[ctb])
        P.add("act", lambda e: e.activation(ct[:, 512:NT], samp, AF.Identity, bias=bias, scale=w1),
              reads=[pbs[1], pbs[2], FCWb], writes=[ctb])
        for (t0, ln) in SEQS:
            if t0 < 512:
                src = lambda a, b_: PS[b0][:, a:b_]
                rd = [pbs[0]]
            else:
                src = lambda a, b_: PSALL[:, (b0 + 1) * 512 + a - 512:(b0 + 1) * 512 + b_ - 512]
                rd = [pbs[1], pbs[2]]
            P.add("dve", lambda e, src=src, t0=t0, ln=ln: e.scalar_tensor_tensor(
                ct[:, t0 + 1:t0 + ln], src(t0, t0 + ln - 1), w0, ct[:, t0 + 1:t0 + ln], ALU.mult, ALU.add),
                reads=rd + [FCWb, ctb], writes=[ctb])
            P.add("dve", lambda e, src=src, t0=t0, ln=ln: e.scalar_tensor_tensor(
                ct[:, t0:t0 + ln - 1], src(t0 + 1, t0 + ln), w2, ct[:, t0:t0 + ln - 1], ALU.mult, ALU.add),
                reads=rd + [FCWb, ctb], writes=[ctb])

    FCWb = FCW

    def emit_ffn(l):
        wus = WStream(WUP, NFC, lambda i, wu: P.dma(wu[:, :, :, :].rearrange("p k g n -> p (k g n)"),
                                                    ffn_w_up[l, i, :, :], writes=[wu], q="pool"))
        wds = WStream(WDN, 2 * KC, lambda i, wd: P.dma(wd[:, :, :].rearrange("p i n -> p (i n)"),
                                                       ffn_w_down[l, i // KC, i % KC, :, :], writes=[wd], q="pool"))
        for hf in range(2):
            for piece in range(hf * NFH, (hf + 1) * NFH):
                wu = wus.need(piece)
                if piece == (hf + 1) * NFH - 1:
                    wds.need(hf * KC)
                i = piece
                cts = []
                for g in range(2):
                    b0 = bank_group()
                    for tb in range(NTB):
                        pb = PS[b0 + tb]
                        for kc in range(KC):
                            P.add("pe", lambda e: e.matmul(
                                pb[:, :], wu[:, kc, g, :], H[:, kc, tb * TB:(tb + 1) * TB],
                                start=(kc == 0), stop=(kc == KC - 1)),
                                reads=[wu, Hb[(kc, tb)]], writes=[pb])
                    ct = CT[ct_rr[0] % 4]
                    ct_rr[0] += 1
                    ch = g * NFC + i
                    conv_from_psum(b0, FCW[:, l, 0, ch:ch + 1], FCW[:, l, 1, ch:ch + 1], FCW[:, l, 2, ch:ch + 1],
                                   FCB[:, l, ch:ch + 1], ct, ct)
                    cts.append(ct)
                ctv, ctg = cts
                pump(2)
                P.add("act", lambda e: e.activation(ctg[:, :], ctg[:, :], AF.Silu), reads=[ctg], writes=[ctg])
                P.add("pool", lambda e: e.tensor_tensor(GB[:, i - hf * NFH, :], ctv[:, :], ctg[:, :], ALU.mult),
                      reads=[ctv, ctg], writes=[GBb[(i - hf * NFH, tb)] for tb in range(NTB)])
            for piece in range(KC):
                wd = wds.need(hf * KC + piece)
                n = piece
                for tb in range(NTB):
                    pb = PS[dn_rr[0] % 6]
                    dn_rr[0] += 1
                    for i in range(NFH):
                        P.add("pe", lambda e: e.matmul(
                            pb[:, :], wd[:, i, :], GB[:, i, tb * TB:(tb + 1) * TB],
                            start=(i == 0), stop=(i == NFH - 1)),
                            reads=[wd, GBb[(i, tb)]], writes=[pb])
                    resid_from_psum(pb, l, 40, n, tb)
                pump(2)

    WO = [carve(WREG, i * 2048, BF16, [128, KC, 128], "WO%d" % i) for i in range(2)]
    wo_rr = [0]

    def emit_wout(l, w_dram, OT, OTb):
        wos = WStream(WO, KC, lambda i, wo: P.dma(wo[:, :, :].rearrange("p k n -> p (k n)"), w_dram[i, :, :],
                                                  writes=[wo], q="pool"))
        for n in range(KC):
            wo = wos.need(n)
            pump(1)
            for tb in range(NTB):
                pb = psum()
                for kc in range(KC):
                    P.add("pe", lambda e, pb=pb, wo=wo, kc=kc, tb=tb: e.matmul(
                        pb[:, :], wo[:, kc, :], OT[:, kc, tb * TB:(tb + 1) * TB],
                        start=(kc == 0), stop=(kc == KC - 1)), reads=[wo, OTb[kc]], writes=[pb])
                resid_from_psum(pb, l, 16, n, tb)

    rb_t = nc.dram_tensor("rbpad", [480, 127], F32)
    RBPAD = P.view("rbpad", rb_t.ap())
    J2B = P.sb("J2B", [128, 64], BF16)
    P.add("dve", lambda e: e.tensor_copy(J2B[:, :], CONST[:, 128:192]), reads=[CONST], writes=[J2B])

    def emit_rbpad():
        RBP = carve(ARENA, 16384, F32, [128, 4, 127], "RBP")
        P.add("pool", lambda e: e.memset(RBP[:, :, :], 0.0), writes=[RBP])
        P.dma(RBP[0:120, :, 48:79], na_rel_bias.rearrange("j (p a) f -> p (j a) f", a=2)[:, :, :]
              if False else na_rel_bias.rearrange("j r f -> (j r) f").rearrange("(p a) f -> p a f", a=4),
              reads=[], writes=[RBP], allow_slow_non_contiguous=True)
        P.dma(rb_t.ap().rearrange("(p a) f -> p a f", a=4), RBP[0:120, :, :], reads=[RBP], writes=[RBPAD])

    def psum_of(lst, st):
        b = PS[lst[st[0] % len(lst)]]
        st[0] += 1
        return b

    def emit_na(l):
        j = l // 2
        OT = carve(ARENA, 0, BF16, [128, KC, NT], "OT")
        OTb = [OT.sub("OT%d" % c) for c in range(KC)]
        QZ = [carve(ARENA, 24576 + i * 3072, BF16, [128, NT], "QZ%d" % i) for i in range(2)]
        KT = carve(ARENA, 30720, BF16, [128, NT], "KT")
        VT = carve(ARENA, 33792, BF16, [128, 12, 128], "VT")
        VTS = carve(ARENA, 36864, BF16, [128, 7, 128], "VTS")
        KCT = carve(ARENA, 38656, BF16, [128, 256], "KCT")
        VC = carve(ARENA, 39168, BF16, [128, 2, 128], "VC")
        PTC = carve(ARENA, 39680, BF16, [128, 2, 1024], "PTC")
        PTL = [carve(ARENA, 43776 + i * 512, BF16, [128, 4, 64], "PTL%d" % i) for i in range(2)]
        PTP = [carve(ARENA, 44800 + i * 1024, BF16, [128, 2, 256], "PTP%d" % i) for i in range(2)]
        HKR = carve(ARENA, 46848, BF16, [128, 14, 2, 64], "HKR")
        HKZ = carve(ARENA, 50432, BF16, [128, 14, 2, 64], "HKZ")
        CKS = TMPN[0].alias(TMPN[0].t[:, :].rearrange("p (a b) -> p a b", a=4))
        CVS = TMPN[1].alias(TMPN[1].t[:, :].rearrange("p (a b) -> p a b", a=4))
        RD = RSTD[0].alias(RSTD[0].t[:, :])
        KCS = RSTD[1].alias(RSTD[1].t[:, 0:256].rearrange("p (a b) -> p a b", a=2))
        WQ = [carve(WREG, i * 6144, BF16, [128, KC, 3, 128], "WQ%d" % i) for i in range(2)]
        lo = [0]
        hi = [0]
        LO = [0, 1, 2, 3]
        HI = [4, 5, 6, 7]
        P.add("pool", lambda e: e.memset(QZ[0][64:128, :], 0.0), writes=[QZ[0]])
        P.add("pool", lambda e: e.memset(QZ[1][0:64, :], 0.0), writes=[QZ[1]])
        P.add("pool", lambda e: e.memset(HKZ[:, :, :, :], 0.0), writes=[HKZ])
        ptl_rr = [0]
        ptp_rr = [0]
        wqs = WStream(WQ, KC, lambda i, wq: P.dma(wq[:, :, :, :].rearrange("p k g n -> p (k g n)"),
                                                  na_w_qkv[j, i, :, :], writes=[wq], q="pool"))
        for c in range(KC):
            wq = wqs.need(c)
            pump(3)
            for tb in range(NTB):
                sl = slice(tb * TB, (tb + 1) * TB)
                pb = psum_of(LO, lo)
                for kc in range(KC):
                    P.add("pe", lambda e, pb=pb, kc=kc, sl=sl: e.matmul(
                        pb[:, :], wq[:, kc, 0, :], H[:, kc, sl], start=(kc == 0), stop=(kc == KC - 1)),
                        reads=[wq, Hb[(kc, tb)]], writes=[pb])
                P.add("act", lambda e, pb=pb, sl=sl: e.activation(QZ[0][0:64, sl], pb[0:64, :], AF.Copy, scale=0.125),
                      reads=[pb], writes=[QZ[0]])
                P.add("act", lambda e, pb=pb, sl=sl: e.activation(QZ[1][64:128, sl], pb[64:128, :], AF.Copy, scale=0.125),
                      reads=[pb], writes=[QZ[1]])
                pb = psum_of(LO, lo)
                for kc in range(KC):
                    P.add("pe", lambda e, pb=pb, kc=kc, sl=sl: e.matmul(
                        pb[:, :], wq[:, kc, 1, :], H[:, kc, sl], start=(kc == 0), stop=(kc == KC - 1)),
                        reads=[wq, Hb[(kc, tb)]], writes=[pb])
                P.add("dve", lambda e, pb=pb, sl=sl: e.tensor_copy(KT[:, sl], pb[:, :]), reads=[pb], writes=[KT])
            for g in range(3):
                pb = psum_of(LO, lo)
                for b in range(4):
                    blk = g * 4 + b
                    for kc in range(KC):
                        P.add("pe", lambda e, pb=pb, kc=kc, b=b, blk=blk: e.matmul(
                            pb[:, b * 128:(b + 1) * 128], H[:, kc, blk * 128:(blk + 1) * 128], wq[:, kc, 2, :],
                            start=(kc == 0), stop=(kc == KC - 1)),
                            reads=[wq, Hb[(kc, blk // 4)]], writes=[pb])
                P.add("dve", lambda e, pb=pb, g=g: e.tensor_copy(
                    VT[:, g * 4:(g + 1) * 4, :], pb[:, :].rearrange("p (b f) -> p b f", b=4)),
                    reads=[pb], writes=[VT])
                if g == 0:
                    P.add("act", lambda e, pb=pb: e.copy(CVS[:, :, :], pb[:, :].rearrange("p (b f) -> p b f", b=4)),
                          reads=[pb], writes=[CVS])
                    for sq in range(2):
                        P.dma(cv_out[sq, j, :, c * 128:(c + 1) * 128].rearrange("(b p) f -> p b f", p=128),
                              CVS[:, sq * 2:sq * 2 + 2, :], reads=[CVS])
            pb = psum_of(LO, lo)
            for b in range(4):
                for kc in range(KC):
                    P.add("pe", lambda e, pb=pb, kc=kc, b=b: e.matmul(
                        pb[:, b * 128:(b + 1) * 128], H[:, kc, b * 128:(b + 1) * 128], wq[:, kc, 1, :],
                        start=(kc == 0), stop=(kc == KC - 1)), reads=[wq, Hb[(kc, 0)]], writes=[pb])
            P.add("act", lambda e, pb=pb: e.copy(CKS[:, :, :], pb[:, :].rearrange("p (b f) -> p b f", b=4)),
                  reads=[pb], writes=[CKS])
            for sq in range(2):
                P.dma(ck_out[sq, j, :, c * 128:(c + 1) * 128].rearrange("(b p) f -> p b f", p=128),
                      CKS[:, sq * 2:sq * 2 + 2, :], reads=[CKS])
            for g in range(2):
                pb = psum_of(LO, lo)
                nb = 4 if g == 0 else 3
                for b in range(nb):
                    m = g * 4 + b
                    t0 = 512 + 64 + 128 * m
                    for kc in range(KC):
                        P.add("pe", lambda e, pb=pb, kc=kc, b=b, t0=t0: e.matmul(
                            pb[:, b * 128:(b + 1) * 128], H[:, kc, t0:t0 + 128], wq[:, kc, 2, :],
                            start=(kc == 0), stop=(kc == KC - 1)),
                            reads=[wq, Hb[(kc, 1)], Hb[(kc, 2)]], writes=[pb])
                P.add("dve", lambda e, pb=pb, g=g, nb=nb: e.tensor_copy(
                    VTS[:, g * 4:g * 4 + nb, :], pb[:, 0:nb * 128].rearrange("p (b f) -> p b f", b=nb)),
                    reads=[pb], writes=[VTS])
            P.dma(KCS[:, :, :], cache_k[j, :, c * 128:(c + 1) * 128].rearrange("(b p) f -> p b f", p=128),
                  writes=[KCS])
            pb = psum_of(LO, lo)
            for b in range(2):
                P.add("pe", lambda e, pb=pb, b=b: e.transpose(pb[:, b * 128:(b + 1) * 128], KCS[:, b, :], CONST[:, 0:128]),
                      reads=[KCS, CONST], writes=[pb])
            P.add("act", lambda e, pb=pb: e.copy(KCT[:, :], pb[:, 0:256]), reads=[pb], writes=[KCT])
            P.dma(VC[:, :, :], cache_v[j, :, c * 128:(c + 1) * 128].rearrange("(b p) f -> p b f", p=128),
                  writes=[VC], q="pool")
            for sq in range(2):
                pbo = psum_of(HI, hi)
                for hh in range(2):
                    hs = slice(hh * 64, hh * 64 + 64)
                    ptp = PTP[ptp_rr[0] % 2]
                    ptp_rr[0] += 1
                    pbs = psum_of(LO, lo)
                    for kb in range(2):
                        P.add("pe", lambda e, pbs=pbs, kb=kb, hh=hh, sq=sq: e.matmul(
                            pbs[:, kb * 256:(kb + 1) * 256], KT[:, sq * 256 + kb * 128:sq * 256 + (kb + 1) * 128],
                            QZ[hh][:, sq * 256:(sq + 1) * 256], start=True, stop=True),
                            reads=[KT, QZ[hh]], writes=[pbs])
                    P.add("act", lambda e, pbs=pbs, ptp=ptp: e.activation(
                        ptp[:, :, :], pbs[:, :].rearrange("p (b q) -> p b q", b=2), AF.Exp),
                        reads=[pbs], writes=[ptp])
                    for kb in range(2):
                        P.add("pe", lambda e, kb=kb, hs=hs, ptp=ptp, sq=sq: e.matmul(
                            pbo[hs, 0:256], VT[:, sq * 2 + kb, hs], ptp[:, kb, :], start=(kb == 0), stop=(kb == 1)),
                            reads=[VT, ptp], writes=[pbo])
                    for kb in range(2):
                        P.add("pe", lambda e, kb=kb, hs=hs, ptp=ptp: e.matmul(
                            pbo[hs, 256:512], ONESB[:, 0:64], ptp[:, kb, :], start=(kb == 0), stop=(kb == 1)),
                            reads=[ONESB, ptp], writes=[pbo])
                P.add("act", lambda e, pbo=pbo: e.activation(RD[:, 0:256], pbo[:, 256:512], AF.Ln), reads=[pbo], writes=[RD])
                P.add("act", lambda e: e.activation(RD[:, 0:256], RD[:, 0:256], AF.Exp, scale=-1.0), reads=[RD], writes=[RD])
                P.add("dve", lambda e, pbo=pbo, sq=sq: e.tensor_tensor(
                    OT[:, c, sq * 256:(sq + 1) * 256], pbo[:, 0:256], RD[:, 0:256], ALU.mult),
                    reads=[pbo, RD], writes=[OTb[c]])
            pbo_s = [PS[4], PS[5]]
            pbd_s = [PS[6], PS[7]]
            for hh in range(2):
                h = 2 * c + hh
                hs = slice(hh * 64, hh * 64 + 64)
                for u in range(2):
                    src = bass.AP(rb_t, (j * 240 + h * 15 + u) * 127, [[1, 64], [127, 14], [1, 64]])
                    P.dma(HKR[0:64, :, u, :], src, reads=[RBPAD], writes=[HKR], q="pool")
                P.add("pool", lambda e: e.tensor_tensor(
                    HKZ[0:64, :, :, :].rearrange("p a u k -> p (a u) k"),
                    HKR[0:64, :, :, :].rearrange("p a u k -> p (a u) k"),
                    CONST[0:64, 192:256].unsqueeze(1).to_broadcast([64, 28, 64]), ALU.add),
                    reads=[HKR, CONST], writes=[HKZ])
                for kb in range(2):
                    for qb in range(2):
                        pbs = psum_of(LO, lo)
                        P.add("pe", lambda e, pbs=pbs, kb=kb, qb=qb, hh=hh: e.matmul(
                            pbs[:, :], KCT[:, kb * 128:(kb + 1) * 128], QZ[hh][:, 512 + qb * 512:512 + (qb + 1) * 512],
                            start=True, stop=True), reads=[KCT, QZ[hh]], writes=[pbs])
                        P.add("act", lambda e, pbs=pbs, kb=kb, qb=qb: e.activation(
                            PTC[:, kb, qb * 512:(qb + 1) * 512], pbs[:, :], AF.Exp), reads=[pbs], writes=[PTC])
                def row_scores(r):
                    rs = min(max(r - 4, 0), 8)
                    ptl = PTL[ptl_rr[0] % 2]
                    ptl_rr[0] += 1
                    pbs = psum_of(LO, lo)
                    qsl = slice(512 + 64 * r, 512 + 64 * r + 64)
                    for i in range(4):
                        kr = rs + 2 * i
                        ri = kr - r + 7
                        k0 = 512 + 64 * kr
                        P.add("pe", lambda e: e.matmul(
                            pbs[:, i * 64:(i + 1) * 64], KT[:, k0:k0 + 128], QZ[hh][:, qsl], start=True, stop=False),
                            reads=[KT, QZ[hh]], writes=[pbs])
                        P.add("pe", lambda e: e.matmul(
                            pbs[:, i * 64:(i + 1) * 64], HKZ[:, ri, :, :].rearrange("p u k -> p (u k)"), J2B[:, :],
                            start=False, stop=True), reads=[HKZ, J2B], writes=[pbs])
                    P.add("act", lambda e: e.activation(
                        ptl[:, :, :], pbs[:, 0:256].rearrange("p (i q) -> p i q", i=4), AF.Exp),
                        reads=[pbs], writes=[ptl])
                    return ptl

                def row_pv(r, ptl):
                    rs = min(max(r - 4, 0), 8)
                    pbo = pbo_s[r // 8]
                    pbd = pbd_s[r // 8]
                    osl = slice((r % 8) * 64, (r % 8) * 64 + 64)
                    for pbx, isden in ((pbo, False), (pbd, True)):
                        for i in range(4):
                            kr = rs + 2 * i
                            if kr % 2 == 0:
                                vv = VT[:, 4 + kr // 2, hs]
                                vb = VT
                            else:
                                vv = VTS[:, (kr - 1) // 2, hs]
                                vb = VTS
                            lhs = ONESB[:, 0:64] if isden else vv
                            P.add("pe", lambda e: e.matmul(
                                pbx[hs, osl], lhs, ptl[:, i, :], start=(i == 0), stop=False),
                                reads=[vb, ONESB, ptl], writes=[pbx])
                        for kb in range(2):
                            lhs = ONESB[:, 0:64] if isden else VC[:, kb, hs]
                            P.add("pe", lambda e: e.matmul(
                                pbx[hs, osl], lhs, PTC[:, kb, 64 * r:64 * r + 64], start=False, stop=(kb == 1)),
                                reads=[VC, ONESB, PTC], writes=[pbx])

                ptl_prev = row_scores(0)
                for r in range(1, 16):
                    ptl_cur = row_scores(r)
                    row_pv(r - 1, ptl_prev)
                    ptl_prev = ptl_cur
                row_pv(15, ptl_prev)
            for half in range(2):
                P.add("act", lambda e, half=half: e.activation(RD[:, :], pbd_s[half][:, :], AF.Ln),
                      reads=[pbd_s[half]], writes=[RD])
                P.add("act", lambda e: e.activation(RD[:, :], RD[:, :], AF.Exp, scale=-1.0), reads=[RD], writes=[RD])
                P.add("dve", lambda e, half=half: e.tensor_tensor(
                    OT[:, c, 512 + half * 512:512 + (half + 1) * 512], pbo_s[half][:, :], RD[:, :], ALU.mult),
                    reads=[pbo_s[half], RD], writes=[OTb[c]])
        P.fence()
        emit_wout(l, na_w_out[j], OT, OTb)


    C_I = CONST[:, 0:128]
    C_TRIF = CONST[:, 256:384]
    C_TRIB = CONST[:, 384:512]
    C_EVEN = CONST[:, 512:640]
    C_ODD = CONST[:, 640:768]
    C_MINC = [CONST[:, 768:896], CONST[:, 1024:1152]]
    C_MSTR = [CONST[:, 896:1024], CONST[:, 1152:1280]]
    C_ONES = CONST[:, 1280:1408]
    C_NEG1 = CONST[:, 1408:1536]
    GCW = P.sb("GCW", [128, 2, 3, 24], F32)
    load_fm(GCW, GCW[:, :, :, :].rearrange("p j t k -> p (j t k)"),
            gdn_conv_w.rearrange("j t (k f) -> (j t k) f", f=128), 144)
    GNG = P.sb("GNG", [128, 2], F32)
    load_fm(GNG, GNG[:, :], gdn_norm_g, 2)
    ZEROC = P.sb("ZEROC", [128, 1], F32)
    P.add("dve", lambda e: e.memset(ZEROC[:, :], 0.0), writes=[ZEROC])
    ev_rr = [0]

    def evac(dst_ap, dst_bufs, src_ap, pbs, scale=None):
        use_act = True
        ev_rr[0] += 1
        if use_act:
            if scale is None:
                P.add("act", lambda e: e.copy(dst_ap, src_ap), reads=pbs, writes=dst_bufs)
            else:
                P.add("act", lambda e: e.activation(dst_ap, src_ap, AF.Copy, scale=scale), reads=pbs, writes=dst_bufs)
        else:
            if scale is None:
                P.add("dve", lambda e: e.tensor_copy(dst_ap, src_ap), reads=pbs, writes=dst_bufs)
            else:
                P.add("dve", lambda e: e.tensor_scalar(dst_ap, src_ap, scale, None, ALU.mult), reads=pbs, writes=dst_bufs)

    def emit_gdn(l):
        j = l // 2
        off = [0]

        def AR(dtype, shape, name):
            esz = 2 if dtype == BF16 else 4
            n = esz
            for d_ in shape[1:]:
                n *= d_
            v = carve(ARENA, off[0], dtype, shape, name)
            off[0] += (n + 63) // 64 * 64
            assert off[0] <= 55296, off[0]
            return v

        TK = [AR(F32, [128, 12, 16], "TK%d" % i) for i in range(12)]
        GTOK, GC, BETA, GCB, EGC, BEG, GLE, GLO, EDE, EDO, TMPA, TMPB = TK
        AB = P.view("AB", ARENA.t[:, (10 * 768) // 4:(12 * 768) // 4].rearrange("p (b c) -> p b c", b=12))
        QNb = AR(BF16, [128, NT], "QNb")
        KNb = AR(BF16, [128, NT], "KNb")
        OTh = AR(BF16, [128, NT], "OTh")
        BS = []
        BSR = []
        for d_ in range(2):
            row, rowr = [], []
            for i in range(3):
                k_ = d_ * 3 + i
                apr = CHAIN[:, k_ * 512:(k_ + 1) * 512].rearrange("p (a b) -> p a b", a=4)
                apf = CHAIN.bitcast(F32)[:, k_ * 512:(k_ + 1) * 512].rearrange("p (a b) -> p a b", a=4)
                row.append(P.view("BS%d_%d" % (d_, i), apf))
                rowr.append(apr)
            BS.append(row)
            BSR.append(rowr)
        TTbs = [AR(BF16, [128, 4, 128], "TTb%d" % d_) for d_ in range(2)]
        VBs = [AR(BF16, [128, 4, 128], "VB%d" % d_) for d_ in range(2)]
        KBGs = [AR(BF16, [128, 4, 128], "KBG%d" % d_) for d_ in range(2)]
        F = [b_.alias(b_.t[:, :].rearrange("p (a b) -> p a b", a=4)) for b_ in (RSTD[0], RSTD[1], TMPN[0], TMPN[1])]
        TC = F
        SETS = []
        for i in range(4):
            SETS.append(dict(U=AR(BF16, [128, 4, 128], "U%d" % i), NWT=AR(BF16, [128, 4, 128], "NWT%d" % i),
                             PT=AR(BF16, [128, 4, 128], "PT%d" % i), KD=[AR(BF16, [128, 4, 128], "KDe%d" % i),
                                                                        AR(BF16, [128, 4, 128], "KDo%d" % i)],
                             QG=AR(BF16, [128, 4, 128], "QG%d" % i)))
        CH = []
        for i in range(4):
            CH.append(dict(S=AR(F32, [128, 128], "S%d" % i), Sb=AR(BF16, [128, 128], "Sb%d" % i),
                           VN=AR(BF16, [128, 128], "VN%d" % i)))
        WAB = AR(BF16, [128, KC, 32], "WAB")
        DTB16 = AR(F32, [128, 16], "DTB16")
        NEGA = AR(F32, [128, 16], "NEGA")
        WI = [carve(WREG, 4096 + i * 4096, BF16, [128, KC, 2, 128], "WI%d" % i) for i in range(2)]
        WOH = [carve(WREG, i * 2048, BF16, [128, D], "WOH%d" % i) for i in range(2)]
        CTQ, CTK, CTV, CTZ = CT
        OF = CTQ
        OFb = [OF.sub("OF%d" % b) for b in range(12)]
        for ch in CH:
            P.add("pool", lambda e, ch=ch: e.memset(ch["VN"][:, :], 0.0), writes=[ch["VN"]])

        P.dma(WAB[:, :, :].rearrange("p k n -> p (k n)"), gdn_w_ab[j, :, :], writes=[WAB], q="pool")
        P.dma(DTB16[:, :], gdn_dt_bias[j:j + 1, :].to_broadcast([128, 16]), writes=[DTB16])
        P.dma(NEGA[:, :], gdn_a_log[j:j + 1, :].to_broadcast([128, 16]), writes=[NEGA])
        P.add("act", lambda e: e.activation(NEGA[:, :], NEGA[:, :], AF.Exp), reads=[NEGA], writes=[NEGA])
        pb = psum()
        for blk in range(12):
            for kc in range(KC):
                P.add("pe", lambda e, pb=pb, blk=blk, kc=kc: e.matmul(
                    pb[:, blk * 32:(blk + 1) * 32], H[:, kc, blk * 128:(blk + 1) * 128], WAB[:, kc, :],
                    start=(kc == 0), stop=(kc == KC - 1)), reads=[WAB, Hb[(kc, blk // 4)]], writes=[pb])
        P.add("dve", lambda e, pb=pb: e.tensor_copy(AB[:, :, :], pb[:, 0:384].rearrange("p (b c) -> p b c", b=12)),
              reads=[pb], writes=[TMPA, TMPB])
        bc16 = lambda t: t[:, :].unsqueeze(1).to_broadcast([128, 12, 16])
        P.add("dve", lambda e: e.tensor_tensor(GTOK[:, :, :], AB[:, :, 0:16], bc16(DTB16), ALU.add),
              reads=[TMPA, TMPB, DTB16], writes=[GTOK])
        P.add("act", lambda e: e.activation(GTOK[:, :, :], GTOK[:, :, :], AF.Exp), reads=[GTOK], writes=[GTOK])
        P.add("dve", lambda e: e.tensor_scalar(GTOK[:, :, :], GTOK[:, :, :], 1.0, None, ALU.add), reads=[GTOK], writes=[GTOK])
        P.add("act", lambda e: e.activation(GTOK[:, :, :], GTOK[:, :, :], AF.Ln), reads=[GTOK], writes=[GTOK])
        P.add("dve", lambda e: e.scalar_tensor_tensor(GTOK[:, :, :], GTOK[:, :, :], -1.0, bc16(NEGA), ALU.mult, ALU.mult),
              reads=[GTOK, NEGA], writes=[GTOK])
        P.add("act", lambda e: e.activation(BETA[:, :, :], AB[:, :, 16:32], AF.Exp, scale=-1.0),
              reads=[TMPA, TMPB], writes=[BETA])
        P.add("dve", lambda e: e.tensor_scalar(BETA[:, :, :], BETA[:, :, :], 1.0, None, ALU.add), reads=[BETA], writes=[BETA])
        P.add("act", lambda e: e.activation(GCB[:, :, :], BETA[:, :, :], AF.Ln), reads=[BETA], writes=[GCB])
        P.add("dve", lambda e: e.reciprocal(BETA[:, :, :], BETA[:, :, :]), reads=[BETA], writes=[BETA])
        pb = psum()
        for blk in range(12):
            for d in range(2):
                tri = C_TRIF if d == 0 else C_TRIB
                P.add("pe", lambda e, pb=pb, blk=blk, d=d, tri=tri: e.matmul(
                    pb[:, blk * 16 + d * 8:blk * 16 + d * 8 + 8], tri, GTOK[:, blk, d * 8:d * 8 + 8],
                    start=True, stop=True), reads=[CONST, GTOK], writes=[pb])
        P.add("dve", lambda e, pb=pb: e.tensor_copy(GC[:, :, :], pb[:, 0:192].rearrange("p (b c) -> p b c", b=12)),
              reads=[pb], writes=[GC])
        for (dst, cm) in ((GLE, C_EVEN), (GLO, C_ODD)):
            pb = psum()
            for blk in range(12):
                P.add("pe", lambda e, pb=pb, blk=blk, cm=cm: e.matmul(
                    pb[:, blk * 16:(blk + 1) * 16], cm, GTOK[:, blk, :], start=True, stop=True),
                    reads=[CONST, GTOK], writes=[pb])
            P.add("dve", lambda e, pb=pb, dst=dst: e.tensor_copy(
                dst[:, :, :], pb[:, 0:192].rearrange("p (b c) -> p b c", b=12)), reads=[pb], writes=[dst])
        P.add("dve", lambda e: e.tensor_tensor(GCB[:, :, :], GC[:, :, :], GCB[:, :, :], ALU.subtract),
              reads=[GC, GCB], writes=[GCB])
        P.add("act", lambda e: e.activation(EGC[:, :, :], GC[:, :, :], AF.Exp), reads=[GC], writes=[EGC])
        P.add("dve", lambda e: e.tensor_tensor(BEG[:, :, :], BETA[:, :, :], EGC[:, :, :], ALU.mult),
              reads=[BETA, EGC], writes=[BEG])
        for (dst, gl, mcol) in ((EDE, GLE, C_EVEN[:, 0:1]), (EDO, GLO, C_ODD[:, 0:1])):
            P.add("dve", lambda e, dst=dst, gl=gl: e.tensor_tensor(dst[:, :, :], gl[:, :, :], GC[:, :, :], ALU.subtract),
                  reads=[gl, GC], writes=[dst])
            P.add("dve", lambda e, dst=dst: e.tensor_scalar(dst[:, :, :], dst[:, :, :], 0.0, None, ALU.min),
                  reads=[dst], writes=[dst])
            P.add("act", lambda e, dst=dst: e.activation(dst[:, :, :], dst[:, :, :], AF.Exp), reads=[dst], writes=[dst])
            P.add("dve", lambda e, dst=dst, mcol=mcol: e.tensor_scalar(dst[:, :, :], dst[:, :, :], mcol, None, ALU.mult),
                  reads=[dst, CONST], writes=[dst])
        P.add("act", lambda e: e.activation(GLE[:, :, :], GLE[:, :, :], AF.Exp), reads=[GLE], writes=[GLE])
        P.add("act", lambda e: e.activation(GLO[:, :, :], GLO[:, :, :], AF.Exp), reads=[GLO], writes=[GLO])
        EGL = [GLE, GLO]
        ED = [EDE, EDO]
        NGC = TMPA
        P.add("dve", lambda e: e.tensor_scalar(NGC[:, :, :], GC[:, :, :], -1.0, None, ALU.mult), reads=[GC], writes=[NGC])

        wis_ = WStream(WI, 16, lambda i, wi: P.dma(wi[:, :, :, :].rearrange("p k g n -> p (k g n)"),
                                                   gdn_w_in[j, i // 2, i % 2, :, :], writes=[wi], q="pool"))
        whs_ = WStream(WOH, 8, lambda i, woh: P.dma(woh[:, :], gdn_w_out[j, i * 128:(i + 1) * 128, :],
                                                    writes=[woh], q="pool"))
        PRE_BANKS, SCAN_BANKS = [0, 1, 2, 3, 4], [5, 6, 7]
        pre_rr, scan_rr = [0], [0]
        for h in range(8):
            pump(3)
            for pi in range(2):
                wi = wis_.need(h * 2 + pi)
                for g in range(2):
                    t = pi * 2 + g
                    b0 = bank_group()
                    for tb in range(NTB):
                        pbk = PS[b0 + tb]
                        for kc in range(KC):
                            P.add("pe", lambda e, pbk=pbk, wi=wi, kc=kc, g=g, tb=tb: e.matmul(
                                pbk[:, :], wi[:, kc, g, :], H[:, kc, tb * TB:(tb + 1) * TB],
                                start=(kc == 0), stop=(kc == KC - 1)), reads=[wi, Hb[(kc, tb)]], writes=[pbk])
                    ct = CT[t]
                    if t < 3:
                        ch = t * 8 + h
                        conv_from_psum(b0, GCW[:, j, 0, ch:ch + 1], GCW[:, j, 1, ch:ch + 1], GCW[:, j, 2, ch:ch + 1],
                                       ZEROC[:, 0:1], ct, ct)
                        P.add("act", lambda e, ct=ct: e.activation(ct[:, :], ct[:, :], AF.Silu), reads=[ct], writes=[ct])
                    else:
                        P.add("act", lambda e, ct=ct, b0=b0: e.activation(ct[:, 0:512], PS[b0][:, :], AF.Silu),
                              reads=[PS[b0]], writes=[ct])
                        P.add("act", lambda e, ct=ct, b0=b0: e.activation(
                            ct[:, 512:NT], PSALL[:, (b0 + 1) * 512:(b0 + 3) * 512], AF.Silu),
                            reads=[PS[b0 + 1], PS[b0 + 2]], writes=[ct])
            f2 = lambda T_: T_[:, :, :].rearrange("p a b -> p (a b)")
            for (ct, dstb, keep) in ((CTQ, QNb, False), (CTK, KNb, True)):
                sls = [slice(tb * TB, (tb + 1) * TB) for tb in range(NTB)]
                for tb in range(NTB):
                    P.add("act", lambda e: e.activation(f2(TC[tb]), ct[:, sls[tb]], AF.Square), reads=[ct], writes=[TC[tb]])
                pbks = []
                for tb in range(NTB):
                    pbk = psum()
                    pbks.append(pbk)
                    P.add("pe", lambda e: e.matmul(pbk[:, :], C_ONES, f2(TC[tb]), start=True, stop=True),
                          reads=[CONST, TC[tb]], writes=[pbk])
                for tb in range(NTB):
                    P.add("act", lambda e: e.activation(f2(TC[tb]), pbks[tb][:, :], AF.Ln, bias=EPSC[:, 0:1], scale=1.0),
                          reads=[pbks[tb], EPSC], writes=[TC[tb]])
                for tb in range(NTB):
                    P.add("act", lambda e: e.activation(f2(TC[tb]), f2(TC[tb]), AF.Exp, scale=-0.5),
                          reads=[TC[tb]], writes=[TC[tb]])
                for tb in range(NTB):
                    sl = sls[tb]
                    if keep:
                        P.add("pool", lambda e: e.tensor_tensor(ct[:, sl], ct[:, sl], f2(TC[tb]), ALU.mult),
                              reads=[ct, TC[tb]], writes=[ct])
                        P.add("act", lambda e: e.copy(dstb[:, sl], ct[:, sl]), reads=[ct], writes=[dstb])
                    else:
                        P.add("pool", lambda e: e.tensor_tensor(dstb[:, sl], ct[:, sl], f2(TC[tb]), ALU.mult),
                              reads=[ct, TC[tb]], writes=[dstb])
            P.add("pool", lambda e: e.memset(OF[:, :], 0.0), writes=[OF] + OFb)

            f4 = lambda T_: T_[:, :, :].rearrange("p a b -> p (a b)")
            v4 = lambda pb_: pb_[:, :].rearrange("p (b f) -> p b f", b=4)
            Ibc = C_I.unsqueeze(1).to_broadcast([128, 4, 128])

            def pre_dir(g, d, st, pk, pv, pg, pq):
                cd = d * 8 + h
                blks = slice(g * 4, g * 4 + 4)
                tsl = slice(g * 512, (g + 1) * 512)
                col = lambda T_: T_[:, blks, cd:cd + 1].to_broadcast([128, 4, 128])
                B = BS[d]
                Fa, Fb = F[2 * d], F[2 * d + 1]
                VB, KBG = VBs[d], KBGs[d]
                P.add("dve", lambda e: e.tensor_tensor(VB[:, :, :], v4(pv), col(BETA), ALU.mult),
                      reads=[pv, BETA], writes=[VB])
                P.add("dve", lambda e: e.tensor_tensor(KBG[:, :, :], v4(pk), col(BEG), ALU.mult),
                      reads=[pk, BEG], writes=[KBG])
                for u in range(2):
                    P.add("dve", lambda e: e.tensor_tensor(st["KD"][u][:, :, :], v4(pk), col(ED[u]), ALU.mult),
                          reads=[pk, ED[u]], writes=[st["KD"][u]])
                yield
                P.add("pool", lambda e: e.tensor_tensor(Fa[:, :, :], Ibc, col(EGC), ALU.mult),
                      reads=[CONST, EGC], writes=[Fa])
                pr = psum_of(PRE_BANKS, pre_rr)
                for b in range(4):
                    P.add("pe", lambda e: e.matmul(pr[:, b * 128:(b + 1) * 128], C_ONES, Fa[:, b, :],
                                                   start=True, stop=True), reads=[CONST, Fa], writes=[pr])
                P.add("dve", lambda e: e.scalar_tensor_tensor(f4(st["QG"]), pr[:, :], 128.0 ** -0.5, QNb[:, tsl],
                                                              ALU.mult, ALU.mult), reads=[pr, QNb], writes=[st["QG"]])
                yield
                P.add("pool", lambda e: e.tensor_tensor(Fa[:, :, :], Ibc, col(GC), ALU.mult),
                      reads=[CONST, GC], writes=[Fa])
                P.add("pool", lambda e: e.tensor_tensor(Fb[:, :, :], Ibc, col(GCB), ALU.mult),
                      reads=[CONST, GCB], writes=[Fb])
                pa = psum_of(PRE_BANKS, pre_rr)
                pbb = psum_of(PRE_BANKS, pre_rr)
                for (pz, dg, msk) in ((pa, Fa, C_MINC[d]), (pbb, Fb, C_MSTR[d])):
                    for b in range(4):
                        o_ = pz[:, b * 128:(b + 1) * 128]
                        P.add("pe", lambda e: e.matmul(o_, C_ONES, dg[:, b, :], start=True, stop=False),
                              reads=[CONST, dg], writes=[pz])
                        P.add("pe", lambda e: e.matmul(o_, C_I, msk, start=False, stop=True),
                              reads=[CONST], writes=[pz])
                BT, Bs_, M_ = B
                BTr, Bsr, Mr = [x_.t for x_ in B]
                r4 = lambda ap_: ap_[:, :, :].rearrange("p a b -> p (a b)")
                for b in range(4):
                    ngc = NGC[:, g * 4 + b, cd:cd + 1]
                    P.add("act", lambda e: e.activation(Mr[:, b, :], pa[:, b * 128:(b + 1) * 128], AF.Exp, bias=ngc),
                          reads=[pa, NGC], writes=[M_])
                    P.add("act", lambda e: e.activation(BTr[:, b, :], pbb[:, b * 128:(b + 1) * 128], AF.Exp, bias=ngc),
                          reads=[pbb, NGC], writes=[BT])
                yield
                pg = psum_of(PRE_BANKS, pre_rr)
                pq = psum_of(PRE_BANKS, pre_rr)
                for b in range(4):
                    c0 = g * 512 + b * 128
                    P.add("pe", lambda e: e.matmul(pg[:, b * 128:(b + 1) * 128], KNb[:, c0:c0 + 128], KNb[:, c0:c0 + 128],
                                                   start=True, stop=True), reads=[KNb], writes=[pg])
                for b in range(4):
                    c0 = g * 512 + b * 128
                    P.add("pe", lambda e: e.matmul(pq[:, b * 128:(b + 1) * 128], KNb[:, c0:c0 + 128], QNb[:, c0:c0 + 128],
                                                   start=True, stop=True), reads=[KNb, QNb], writes=[pq])
                P.add("dve", lambda e: e.scalar_tensor_tensor(f4(st["PT"]), pq[:, :], 128.0 ** -0.5, f4(M_),
                                                              ALU.mult, ALU.mult), reads=[pq, M_], writes=[st["PT"]])
                P.add("dve", lambda e: e.scalar_tensor_tensor(r4(BTr), pg[:, :], -1.0, f4(BT), ALU.mult, ALU.mult),
                      reads=[pg, BT], writes=[BT])
                yield
                pt_ = psum_of(PRE_BANKS, pre_rr)
                for b in range(4):
                    P.add("pe", lambda e: e.transpose(pt_[:, b * 128:(b + 1) * 128], BT[:, b, :], C_I),
                          reads=[BT, CONST], writes=[pt_])
                evac(r4(Bsr), [Bs_], pt_[:, :], [pt_])
                P.add("pool", lambda e: e.tensor_tensor(Mr[:, :, :], BT[:, :, :], Ibc, ALU.add),
                      reads=[BT, CONST], writes=[M_])
                yield
                TTb = TTbs[d]
                for k in range(5):
                    if k < 4:
                        px = psum_of(PRE_BANKS, pre_rr)
                        for b in range(4):
                            P.add("pe", lambda e: e.matmul(px[:, b * 128:(b + 1) * 128], Bsr[:, b, :], BTr[:, b, :],
                                                           start=True, stop=True), reads=[Bs_, BT], writes=[px])
                    py = psum_of(PRE_BANKS, pre_rr)
                    for b in range(4):
                        P.add("pe", lambda e: e.matmul(py[:, b * 128:(b + 1) * 128], BTr[:, b, :], Bsr[:, b, :],
                                                       start=True, stop=True), reads=[Bs_, BT], writes=[py])
                    if k < 4:
                        evac(r4(BTr), [BT], px[:, :], [px])
                    evac(r4(Bsr), [Bs_], py[:, :], [py])
                    yield
                    pm_ = psum_of(PRE_BANKS, pre_rr)
                    for b in range(4):
                        o_ = pm_[:, b * 128:(b + 1) * 128]
                        P.add("pe", lambda e: e.matmul(o_, Bsr[:, b, :], Mr[:, b, :], start=True, stop=True),
                              reads=[Bs_, M_], writes=[pm_])
                    if k < 4:
                        P.add("dve", lambda e: e.tensor_tensor(r4(Mr), pm_[:, :], f4(M_), ALU.add),
                              reads=[pm_, M_], writes=[M_])
                    else:
                        P.add("dve", lambda e: e.tensor_tensor(f4(TTb), pm_[:, :], f4(M_), ALU.add),
                              reads=[pm_, M_], writes=[TTb])
                    yield
                pu = psum_of(PRE_BANKS, pre_rr)
                pw = psum_of(PRE_BANKS, pre_rr)
                for b in range(4):
                    P.add("pe", lambda e: e.matmul(pu[:, b * 128:(b + 1) * 128], TTb[:, b, :], VB[:, b, :],
                                                   start=True, stop=True), reads=[TTb, VB], writes=[pu])
                for b in range(4):
                    P.add("pe", lambda e: e.matmul(pw[:, b * 128:(b + 1) * 128], KBG[:, b, :], TTb[:, b, :],
                                                   start=True, stop=True), reads=[TTb, KBG], writes=[pw])
                evac(f4(st["U"]), [st["U"]], pu[:, :], [pu])
                evac(f4(st["NWT"]), [st["NWT"]], pw[:, :], [pw], scale=-1.0)
                yield

            def precompute2_gen(gs, sts):
                tiles = {}
                for g in sorted(set(gs)):
                    pk = psum_of(PRE_BANKS, pre_rr)
                    pv = psum_of(PRE_BANKS, pre_rr)
                    for b in range(4):
                        c0 = g * 512 + b * 128
                        P.add("pe", lambda e: e.transpose(pk[:, b * 128:(b + 1) * 128], CTK[:, c0:c0 + 128], C_I),
                              reads=[CTK, CONST], writes=[pk])
                    for b in range(4):
                        c0 = g * 512 + b * 128
                        P.add("pe", lambda e: e.transpose(pv[:, b * 128:(b + 1) * 128], CTV[:, c0:c0 + 128], C_I),
                              reads=[CTV, CONST], writes=[pv])
                    tiles[g] = (pk, pv)
                gens = [pre_dir(gs[d_], d_, sts[d_], tiles[gs[d_]][0], tiles[gs[d_]][1], None, None) for d_ in range(2)]
                while gens:
                    for gi in list(gens):
                        try:
                            next(gi)
                        except StopIteration:
                            gens.remove(gi)
                    yield

            def interleave(gen, rounds, every):
                i = 0
                r = 0
                for _ in gen:
                    i += 1
                    if i % every == 0 and r < len(rounds):
                        rounds[r]()
                        r += 1
                while r < len(rounds):
                    rounds[r]()
                    r += 1

            def scan_round(items):
                b1 = psum_of(SCAN_BANKS, scan_rr)
                b2 = psum_of(SCAN_BANKS, scan_rr)
                b3 = psum_of(SCAN_BANKS, scan_rr)
                for ci, (ch, st, d, blk, u) in enumerate(items):
                    bi = blk % 4
                    rsl = slice(u * 64, u * 64 + 64)
                    P.add("pe", lambda e: e.matmul(b1[rsl, ci * 128:(ci + 1) * 128], st["NWT"][:, bi, rsl], ch["Sb"][:, :],
                                                   start=True, stop=True), reads=[st["NWT"], ch["Sb"]], writes=[b1])
                for ci, (ch, st, d, blk, u) in enumerate(items):
                    bi = blk % 4
                    rsl = slice(u * 64, u * 64 + 64)
                    P.add("dve", lambda e: e.tensor_tensor(ch["VN"][rsl, :], st["U"][rsl, bi, :],
                                                           b1[rsl, ci * 128:(ci + 1) * 128], ALU.add),
                          reads=[st["U"], b1], writes=[ch["VN"]])
                for ci, (ch, st, d, blk, u) in enumerate(items):
                    bi = blk % 4
                    rsl = slice(u * 64, u * 64 + 64)
                    o2 = b2[:, ci * 64:(ci + 1) * 64]
                    P.add("pe", lambda e: e.matmul(o2, ch["Sb"][:, :], st["QG"][:, bi, rsl], start=True, stop=False),
                          reads=[ch["Sb"], st["QG"]], writes=[b2])
                    P.add("pe", lambda e: e.matmul(o2, ch["VN"][:, :], st["PT"][:, bi, rsl], start=False, stop=True),
                          reads=[ch["VN"], st["PT"]], writes=[b2])
                    P.add("pe", lambda e: e.matmul(b3[:, ci * 128:(ci + 1) * 128], st["KD"][u][:, bi, :], ch["VN"][:, :],
                                                   start=True, stop=True), reads=[st["KD"][u], ch["VN"]], writes=[b3])
                for ci, (ch, st, d, blk, u) in enumerate(items):
                    cd = d * 8 + h
                    S, Sb = ch["S"], ch["Sb"]
                    o3 = b3[:, ci * 128:(ci + 1) * 128]
                    P.add("dve", lambda e: e.scalar_tensor_tensor(Sb[:, :], S[:, :], EGL[u][:, blk, cd:cd + 1], o3,
                                                                  ALU.mult, ALU.add), reads=[S, EGL[u], b3], writes=[Sb])
                    P.add("dve", lambda e: e.scalar_tensor_tensor(S[:, :], S[:, :], EGL[u][:, blk, cd:cd + 1], o3,
                                                                  ALU.mult, ALU.add), reads=[S, EGL[u], b3], writes=[S])
                for ci, (ch, st, d, blk, u) in enumerate(items):
                    c0 = blk * 128 + u * 64
                    P.add("dve", lambda e: e.tensor_tensor(OF[:, c0:c0 + 64], OF[:, c0:c0 + 64],
                                                           b2[:, ci * 64:(ci + 1) * 64], ALU.add),
                          reads=[OFb[blk], b2], writes=[OFb[blk]])

            def scan_step(ch, st, d, blk, u):
                scan_round([(ch, st, d, blk, u)])

            def chain_steps(d, blocks):
                steps = [(b, u) for b in blocks for u in range(2)]
                return steps if d == 0 else steps[::-1]

            def init_chain(ch, d, seq):
                if seq < 2:
                    P.add("pool", lambda e: e.memset(ch["S"][:, :], 0.0), writes=[ch["S"]])
                else:
                    P.dma(ch["S"][:, :], state_gdn[j, d, h, :, :], writes=[ch["S"]])
                P.add("act", lambda e: e.copy(ch["Sb"][:, :], ch["S"][:, :]), reads=[ch["S"]], writes=[ch["Sb"]])

            for _ in precompute2_gen([0, 0], [SETS[2], SETS[3]]):
                pass
            chains = []
            for d in range(2):
                for seq in range(2):
                    ch = CH[d * 2 + seq]
                    init_chain(ch, d, seq)
                    chains.append((ch, SETS[2 + d], d, seq, chain_steps(d, [2 * seq, 2 * seq + 1])))

            def roundA(si):
                scan_round([(ch, st, d, steps[si][0], steps[si][1]) for (ch, st, d, seq, steps) in chains])

            interleave(precompute2_gen([1, 2], [SETS[0], SETS[1]]), [lambda si=si: roundA(si) for si in range(4)], 3)
            for (ch, st, d, seq, steps) in chains:
                P.dma(st_out[seq, j, d, h, :, :], ch["S"][:, :], reads=[ch["S"]])
            chB = [CH[0], CH[1]]
            stepsB = [chain_steps(d, list(range(4, 12))) for d in range(2)]
            for d in range(2):
                init_chain(chB[d], d, 2)
            setB = {(0, 1): SETS[0], (1, 2): SETS[1], (0, 2): SETS[2], (1, 1): SETS[3]}

            def roundB(si):
                items = []
                for d in range(2):
                    blk, u = stepsB[d][si]
                    items.append((chB[d], setB[(d, blk // 4)], d, blk, u))
                scan_round(items)

            interleave(precompute2_gen([2, 1], [SETS[2], SETS[3]]), [lambda si=si: roundB(si) for si in range(8)], 2)
            for si in range(8, 16):
                roundB(si)

            sls = [slice(tb * TB, (tb + 1) * TB) for tb in range(NTB)]
            for tb in range(NTB):
                P.add("act", lambda e: e.activation(f2(TC[tb]), OF[:, sls[tb]], AF.Square),
                      reads=OFb[tb * 4:tb * 4 + 4] + [OF], writes=[TC[tb]])
            pbks = []
            for tb in range(NTB):
                pbk = psum()
                pbks.append(pbk)
                P.add("pe", lambda e: e.matmul(pbk[:, :], C_ONES, f2(TC[tb]), start=True, stop=True),
                      reads=[CONST, TC[tb]], writes=[pbk])
            for tb in range(NTB):
                P.add("act", lambda e: e.activation(f2(TC[tb]), pbks[tb][:, :], AF.Ln, bias=EPSC[:, 0:1], scale=1.0 / 128),
                      reads=[pbks[tb], EPSC], writes=[TC[tb]])
            for tb in range(NTB):
                P.add("act", lambda e: e.activation(f2(TC[tb]), f2(TC[tb]), AF.Exp, scale=-0.5), reads=[TC[tb]], writes=[TC[tb]])
            for tb in range(NTB):
                sl = sls[tb]
                P.add("pool", lambda e: e.tensor_tensor(f2(TC[tb]), OF[:, sl], f2(TC[tb]), ALU.mult),
                      reads=OFb[tb * 4:tb * 4 + 4] + [TC[tb], OF], writes=[TC[tb]])
                P.add("dve", lambda e: e.scalar_tensor_tensor(
                    OTh[:, sl], f2(TC[tb]), GNG[:, j:j + 1], CTZ[:, sl], ALU.mult, ALU.mult),
                    reads=[TC[tb], GNG, CTZ], writes=[OTh])
            woh = whs_.need(h)
            for n in range(KC):
                for tb in range(NTB):
                    pbk = psum()
                    P.add("pe", lambda e, pbk=pbk, n=n, tb=tb, woh=woh: e.matmul(
                        pbk[:, :], woh[:, n * 128:(n + 1) * 128], OTh[:, tb * TB:(tb + 1) * TB], start=True, stop=True),
                        reads=[woh, OTh], writes=[pbk])
                    resid_from_psum(pbk, l, 16, n, tb)

    P.fence()
    emit_rbpad()
    mod_state["gen"] = mod_gen(0)
    pump(24)
    for l in range(cfg.n_layers):
        emit_norm_mod(l, 1)
        P.fence()
        if l % 2 == 1 and cfg.do_na:
            emit_na(l)
        if l % 2 == 0 and cfg.do_gdn:
            emit_gdn(l)
        pump(48)
        emit_norm_mod(l, 2)
        P.fence()
        if l + 1 < cfg.n_layers:
            mod_state["gen"] = mod_gen(l + 1)
        emit_ffn(l)
        pump(48)
    P.fence()

    FG = carve(ARENA, 8192, F32, [128, D], "FG")
    P.dma(FG[:, :], final_g.rearrange("(o d) -> o d", o=1).to_broadcast([128, D]), writes=[FG])
    YT = [carve(ARENA, i * 4096, F32, [128, D], "YT%d" % i) for i in range(2)]
    YSQ = carve(ARENA, 12288, F32, [128, D], "YSQ")
    SS = [P.sb("SS%d" % i, [128, 1], F32) for i in range(2)]
    for blk in range(NT // 128):
        yt = YT[blk % 2]
        ss = SS[blk % 2]
        tb = (blk * 128) // TB
        for half in range(2):
            pb = psum()
            for j in range(4):
                kc = half * 4 + j
                P.add("pe", lambda e, pb=pb, kc=kc, j=j, blk=blk: e.transpose(
                    pb[:, j * 128:(j + 1) * 128], X[:, kc, blk * 128:(blk + 1) * 128], CONST[:, 0:128]),
                    reads=[Xb[(kc, tb)], CONST], writes=[pb])
            P.add("act", lambda e, pb=pb, yt=yt, half=half: e.copy(yt[:, half * 512:(half + 1) * 512], pb[:, :]),
                  reads=[pb], writes=[yt])
        P.add("act", lambda e, yt=yt, ss=ss: e.activation(YSQ[:, :], yt[:, :], AF.Square, accum_out=ss[:, 0:1]),
              reads=[yt], writes=[YSQ, ss])
        P.add("act", lambda e, ss=ss: e.activation(ss[:, :], ss[:, :], AF.Ln, bias=EPSC[:, 0:1], scale=1.0 / D),
              reads=[ss, EPSC], writes=[ss])
        P.add("act", lambda e, ss=ss: e.activation(ss[:, :], ss[:, :], AF.Exp, scale=-0.5), reads=[ss], writes=[ss])
        P.add("dve", lambda e, yt=yt, ss=ss: e.scalar_tensor_tensor(
            yt[:, :], yt[:, :], ss[:, 0:1], FG[:, :], ALU.mult, ALU.mult), reads=[yt, ss, FG], writes=[yt])
        P.dma(y_out[blk * 128:(blk + 1) * 128, :], yt[:, :], reads=[yt])

    P.emit()
    return nc


def make_consts():
    c = np.zeros((128, 1536), np.float32)
    c[:, 0:128] = np.eye(128, dtype=np.float32)
    J = np.zeros((64, 64), np.float32)
    J[np.arange(64), 63 - np.arange(64)] = 1.0
    c[0:64, 128:192] = J
    c[64:128, 128:192] = J
    qc = 63 - np.arange(64)[:, None]
    kc = np.arange(64)[None, :]
    cs = np.clip(qc - 8, 0, 48)
    inside = (kc >= cs) & (kc < cs + 16)
    c[0:64, 192:256] = np.where(inside, 0.0, -30000.0)
    jj = np.arange(128)[:, None]
    ii = np.arange(128)[None, :]
    same = (jj // 64) == (ii // 64)
    c[:, 256:384] = (same & (jj <= ii)).astype(np.float32)
    c[:, 384:512] = (same & (jj >= ii)).astype(np.float32)
    c[:, 512:640] = (jj < 64).astype(np.float32) * np.ones((1, 128), np.float32)
    c[:, 640:768] = (jj >= 64).astype(np.float32) * np.ones((1, 128), np.float32)
    NEG = -30000.0
    c[:, 768:896] = np.where(same & (jj <= ii), 0.0, NEG)
    c[:, 896:1024] = np.where(same & (jj < ii), 0.0, NEG)
    c[:, 1024:1152] = np.where(same & (jj >= ii), 0.0, NEG)
    c[:, 1152:1280] = np.where(same & (jj > ii), 0.0, NEG)
    c[:, 1280:1408] = 1.0
    c[:, 1408:1536] = -1.0
    return c


_NC_CACHE = {}


def kernel(x_prompt, x_sample, state_gdn, cache_k, cache_v, c, c_ctx, w_ada, b_ada, norm1_g, norm2_g,
           gdn_w_in, gdn_conv_w, gdn_a_log, gdn_dt_bias, gdn_norm_g, gdn_w_out,
           na_w_qkv, na_rel_bias, na_w_out, ffn_w_up, ffn_conv_w, ffn_conv_b, ffn_w_down, final_g, _cfg=Cfg, _cores=8):
    f = lambda a: np.ascontiguousarray(np.asarray(a, dtype=np.float32))
    nc = build(_cfg)
    consts = make_consts()
    c_ = np.ascontiguousarray
    w_ada_t = c_(f(w_ada).reshape(4, 8, 128, 48, 128).transpose(0, 3, 2, 1, 4)).reshape(4, 48, 128, 1024)
    w_up_t = c_(f(ffn_w_up).reshape(4, 8, 128, 2, 22, 128).transpose(0, 4, 2, 1, 3, 5)).reshape(4, 22, 128, 2048)
    w_dn_t = c_(f(ffn_w_down).reshape(4, 2, 11, 128, 8, 128).transpose(0, 1, 4, 3, 2, 5)).reshape(4, 2, 8, 128, 1408)
    qkv_t = c_(f(na_w_qkv).reshape(2, 8, 128, 3, 8, 128).transpose(0, 4, 2, 1, 3, 5)).reshape(2, 8, 128, 3072)
    nwo_t = c_(f(na_w_out).reshape(2, 8, 128, 8, 128).transpose(0, 3, 2, 1, 4)).reshape(2, 8, 128, 1024)
    gin = f(gdn_w_in)
    gin_t = c_(gin[:, :, :4096].reshape(2, 8, 128, 2, 2, 8, 128).transpose(0, 5, 3, 2, 1, 4, 6)).reshape(2, 8, 2, 128, 2048)
    gab_t = c_(gin[:, :, 4096:4128].reshape(2, 8, 128, 32).transpose(0, 2, 1, 3)).reshape(2, 128, 256)
    shared = {
        "w_ada": w_ada_t, "b_ada": f(b_ada), "norm1_g": f(norm1_g), "norm2_g": f(norm2_g),
        "gdn_w_in": gin_t, "gdn_w_ab": gab_t, "gdn_conv_w": f(gdn_conv_w),
        "gdn_a_log": f(gdn_a_log).reshape(2, 16), "gdn_dt_bias": f(gdn_dt_bias).reshape(2, 16),
        "gdn_norm_g": f(gdn_norm_g), "gdn_w_out": f(gdn_w_out),
        "na_w_qkv": qkv_t, "na_rel_bias": f(na_rel_bias).reshape(2, 240, 31), "na_w_out": nwo_t,
        "ffn_w_up": w_up_t, "ffn_conv_w": f(ffn_conv_w), "ffn_conv_b": f(ffn_conv_b),
        "ffn_w_down": w_dn_t, "final_g": f(final_g), "consts": consts,
    }
    xp = f(x_prompt)
    xs = f(x_sample)
    in_maps = []
    for i in range(_cores):
        m = dict(shared)
        m["x_in"] = np.concatenate([xp[2 * i].reshape(256, D), xp[2 * i + 1].reshape(256, D), xs[i]], axis=0)
        m["state_gdn"] = f(state_gdn[i])
        m["cache_k"] = f(cache_k[i]).reshape(2, 256, 1024)
        m["cache_v"] = f(cache_v[i]).reshape(2, 256, 1024)
        m["cvec"] = np.stack([f(c_ctx), f(c[i])], axis=0)
        in_maps.append(m)
    res = run_bass_kernel_spmd(nc, in_maps, core_ids=list(range(_cores)))
    R = res.results
    y_prompt = np.zeros((16, 256, D), np.float32)
    y_sample = np.zeros((8, 1024, D), np.float32)
    new_state = np.zeros((16, 2, 2, 8, 128, 128), np.float32)
    new_k = np.zeros((16, 2, 256, 16, 64), np.float32)
    new_v = np.zeros((16, 2, 256, 16, 64), np.float32)
    for i in range(_cores):
        y = R[i]["y_out"]
        y_prompt[2 * i] = y[0:256]
        y_prompt[2 * i + 1] = y[256:512]
        y_sample[i] = y[512:]
        new_state[2 * i:2 * i + 2] = R[i]["st_out"]
        new_k[2 * i:2 * i + 2] = R[i]["ck_out"].reshape(2, 2, 256, 16, 64)
        new_v[2 * i:2 * i + 2] = R[i]["cv_out"].reshape(2, 2, 256, 16, 64)
    return (y_prompt, y_sample, new_state, new_k, new_v)
```

```python
import numpy as np
import concourse.bass as bass
import concourse.mybir as mybir
from concourse.bass_utils import run_bass_kernel_spmd

F32 = mybir.dt.float32
F32R = mybir.dt.float32r
BF16 = mybir.dt.bfloat16
AF = mybir.ActivationFunctionType
ALU = mybir.AluOpType
AX = mybir.AxisListType

D = 1024
KC = 8
DEPTH = 4
NP_TOK = 512
NS_TOK = 1024
NT = NP_TOK + NS_TOK
TB = 512
NTB = NT // TB
D_FF = 2816
NFC = D_FF // 128
GDN_DIN = 4128
EPS = 1e-6
SEQS = [(0, 256), (256, 256), (512, 1024)]


class Buf:
    __slots__ = ("name", "t", "psum", "lastw", "readers", "dma_readers", "root")

    def __init__(self, name, t, psum=False):
        self.name = name
        self.t = t
        self.psum = psum
        self.lastw = None
        self.readers = {}
        self.dma_readers = []
        self.root = self

    def alias(self, ap, name=None):
        b = Buf(name or self.name, ap, self.psum)
        b.root = self.root
        return b

    def __getitem__(self, idx):
        return self.t[idx]

    def sub(self, name=None):
        return Buf(name or self.name, self.t, self.psum)


class Op:
    __slots__ = ("eng", "fn", "reads", "writes", "dma", "waits", "sig", "sem", "sigval", "deps", "n")


class _Rec:
    def __init__(self):
        self.call = None

    def __getattr__(self, name):
        def f(*args, **kwargs):
            assert self.call is None
            self.call = (name, args, kwargs)
            return None
        return f


def _bind(fn):
    r = _Rec()
    fn(r)
    name, args, kwargs = r.call
    return lambda e: getattr(e, name)(*args, **kwargs)


class Prog:
    ENGS = ("pe", "act", "dve", "pool", "sp")

    def __init__(self, nc, n_dma_ring=8):
        self.nc = nc
        self.ops = []
        self.nring = n_dma_ring
        self.sbuf_bytes = 0

    def sb(self, name, shape, dtype):
        t = self.nc.alloc_sbuf_tensor(name, list(shape), dtype)
        return Buf(name, t)

    def ps(self, name, shape, dtype=F32):
        t = self.nc.alloc_psum_tensor(name, list(shape), dtype)
        return Buf(name, t, psum=True)

    def add(self, eng, fn, reads=(), writes=(), dma=False):
        op = Op()
        op.eng = eng
        op.fn = _bind(fn)
        op.reads = [b.root for b in reads if b is not None]
        op.writes = [b.root for b in writes if b is not None]
        op.dma = dma
        op.waits = []
        op.sig = dma
        op.sem = None
        op.sigval = 0
        op.n = len(self.ops)
        self.ops.append(op)
        return op

    def fence(self):
        op = Op()
        op.eng = None
        op.n = len(self.ops)
        op.dma = False
        op.sig = False
        self.ops.append(op)

    def view(self, name, ap, psum=False):
        return Buf(name, ap, psum)

    def dma(self, out_ap, in_ap, reads=(), writes=(), q="sp", **kw):
        return self.add(q, lambda e: e.dma_start(out=out_ap, in_=in_ap, **kw), reads, writes, dma=True)

    def resolve(self):
        last_on = {}
        dma_since = []
        pending = {}
        real_ops = []
        for op in self.ops:
            if op.eng is None:
                snap = (dict(last_on), list(dma_since))
                dma_since = []
                for e in self.ENGS:
                    pending[e] = snap
                continue
            real_ops.append(op)
            wdeps = []
            rdeps = []
            for b in op.reads:
                if b.lastw is not None:
                    wdeps.append(b.lastw)
                if b.psum:
                    for e, r in b.readers.items():
                        if e != op.eng:
                            rdeps.append(r)
            for b in op.writes:
                if b.lastw is not None:
                    wdeps.append(b.lastw)
                rdeps.extend(b.readers.values())
                rdeps.extend(b.dma_readers)
            for b in op.writes:
                b.lastw = op
                b.readers = {}
                b.dma_readers = []
            for b in op.reads:
                if op.dma:
                    b.dma_readers.append(op)
                else:
                    b.readers[op.eng] = op
            deps = {}
            for d in wdeps:
                if d is op:
                    continue
                if d.dma or op.dma:
                    deps[d.n] = d
                elif d.eng == op.eng:
                    if op.eng != "pe":
                        deps[d.n] = d
                else:
                    deps[d.n] = d
            for d in rdeps:
                if d is op:
                    continue
                if d.dma or op.dma:
                    deps[d.n] = d
                elif d.eng != op.eng:
                    deps[d.n] = d
            if pending.get(op.eng) is not None:
                lo, dl = pending[op.eng]
                pending[op.eng] = None
                for e2, d in lo.items():
                    if e2 == op.eng and e2 == "pe" and not op.dma:
                        continue
                    deps[d.n] = d
                for d in dl:
                    deps[d.n] = d
            op.deps = list(deps.values())
            for d in op.deps:
                d.sig = True
            if op.dma:
                dma_since.append(op)
            else:
                last_on[op.eng] = op
        self.ops = real_ops

    def emit(self):
        nc = self.nc
        self.resolve()
        streams = {e: [] for e in self.ENGS}
        for op in self.ops:
            streams[op.eng].append(op)
        import contextlib
        with contextlib.ExitStack() as es:
            esem = {e: es.enter_context(nc.semaphore("s_" + e)) for e in self.ENGS}
            rings = {e: [es.enter_context(nc.semaphore("d_%s%d" % (e, i))) for i in range(self.nring)]
                     for e in ("sp", "pool", "act")}
            fin = es.enter_context(nc.semaphore("fin"))
            cnt = {e: 0 for e in self.ENGS}
            ringcnt = {e: [0] * self.nring for e in rings}
            ringpos = {e: 0 for e in rings}
            ring_prev = {}
            for op in self.ops:
                if op.dma:
                    r = ringpos[op.eng] % self.nring
                    ringpos[op.eng] += 1
                    ringcnt[op.eng][r] += 1
                    op.sem = rings[op.eng][r]
                    op.sigval = 16 * ringcnt[op.eng][r]
                    if ringcnt[op.eng][r] > 1:
                        ring_prev[op.n] = (op.sem, op.sigval - 16)
                elif op.sig:
                    cnt[op.eng] += 1
                    op.sem = esem[op.eng]
                    op.sigval = cnt[op.eng]
            waited = {e: {} for e in self.ENGS}
            for e in self.ENGS:
                for op in streams[e]:
                    need = {}
                    for d in op.deps:
                        k = id(d.sem)
                        if k not in need or need[k][1] < d.sigval:
                            need[k] = (d.sem, d.sigval)
                    if op.n in ring_prev:
                        s, v = ring_prev[op.n]
                        k = id(s)
                        if k not in need or need[k][1] < v:
                            need[k] = (s, v)
                    for k, (s, v) in need.items():
                        if waited[e].get(k, 0) >= v:
                            continue
                        waited[e][k] = v
                        op.waits.append((s, v))
            final_waits = []
            for e in rings:
                for r in range(self.nring):
                    if ringcnt[e][r] > 0:
                        final_waits.append((rings[e][r], 16 * ringcnt[e][r]))

            def run_stream(ename, eng):
                for op in streams[ename]:
                    for s, v in op.waits:
                        eng.wait_ge(s, v)
                    ins = op.fn(eng)
                    if op.sig:
                        ins.then_inc(op.sem, 16 if op.dma else 1)
                if ename == "sp":
                    for s, v in final_waits:
                        eng.wait_ge(s, v)

            with nc.Block() as block:
                @block.tensor
                def _(eng):
                    run_stream("pe", eng)

                @block.scalar
                def _(eng):
                    run_stream("act", eng)

                @block.vector
                def _(eng):
                    run_stream("dve", eng)

                @block.gpsimd
                def _(eng):
                    run_stream("pool", eng)

                @block.sync
                def _(eng):
                    run_stream("sp", eng)


class Cfg:
    n_layers = DEPTH
    do_gdn = True
    do_na = True


def build(cfg=Cfg):
    nc = bass.Bass("TRN2", target_bir_lowering=False)
    P = Prog(nc)

    def din(name, shape):
        return nc.dram_tensor(name, list(shape), F32, kind="ExternalInput").ap()

    def dout(name, shape):
        return nc.dram_tensor(name, list(shape), F32, kind="ExternalOutput").ap()

    x_in = din("x_in", [NT, D])
    state_gdn = din("state_gdn", [2, 2, 8, 128, 128])
    cache_k = din("cache_k", [2, 256, 1024])
    cache_v = din("cache_v", [2, 256, 1024])
    cvec = din("cvec", [2, D])
    w_ada = din("w_ada", [DEPTH, 48, 128, 1024])
    b_ada = din("b_ada", [DEPTH, 6 * D])
    norm1_g = din("norm1_g", [DEPTH, D])
    norm2_g = din("norm2_g", [DEPTH, D])
    gdn_w_in = din("gdn_w_in", [2, 8, 2, 128, 2048])
    gdn_w_ab = din("gdn_w_ab", [2, 128, 256])
    gdn_conv_w = din("gdn_conv_w", [2, 3, 3072])
    gdn_a_log = din("gdn_a_log", [2, 16])
    gdn_dt_bias = din("gdn_dt_bias", [2, 16])
    gdn_norm_g = din("gdn_norm_g", [2, 128])
    gdn_w_out = din("gdn_w_out", [2, D, D])
    na_w_qkv = din("na_w_qkv", [2, 8, 128, 3072])
    na_rel_bias = din("na_rel_bias", [2, 16 * 15, 31])
    na_w_out = din("na_w_out", [2, 8, 128, 1024])
    ffn_w_up = din("ffn_w_up", [DEPTH, 22, 128, 2048])
    ffn_conv_w = din("ffn_conv_w", [DEPTH, 3, 2 * D_FF])
    ffn_conv_b = din("ffn_conv_b", [DEPTH, 2 * D_FF])
    ffn_w_down = din("ffn_w_down", [DEPTH, 2, 8, 128, 1408])
    final_g = din("final_g", [D])
    consts = din("consts", [128, 1536])

    y_out = dout("y_out", [NT, D])
    st_out = dout("st_out", [2, 2, 2, 8, 128, 128])
    ck_out = dout("ck_out", [2, 2, 256, 1024])
    cv_out = dout("cv_out", [2, 2, 256, 1024])

    X = P.sb("X", [128, KC, NT], F32)
    Xb = {(kc, tb): X.sub("X%d_%d" % (kc, tb)) for kc in range(KC) for tb in range(NTB)}
    H = P.sb("H", [128, KC, NT], BF16)
    Hb = {(kc, tb): H.sub("H%d_%d" % (kc, tb)) for kc in range(KC) for tb in range(NTB)}
    ARENA = P.sb("ARENA", [128, 13824], F32)
    CHAIN = nc.alloc_sbuf_tensor("CHAIN", [128, 3072], F32R)
    WREG = P.sb("WREG", [128, 3584], F32)

    def carve(region, byte_off, dtype, shape, name):
        esz = 2 if dtype == BF16 else 4
        n = 1
        for d_ in shape[1:]:
            n *= d_
        base = region.t.bitcast(dtype) if dtype != F32 else region.t
        ap = base[:, byte_off // esz: byte_off // esz + n]
        if len(shape) == 3:
            ap = ap.rearrange("p (a b) -> p a b", a=shape[1])
        elif len(shape) == 4:
            ap = ap.rearrange("p (a b c) -> p a b c", a=shape[1], b=shape[2])
        return P.view(name, ap)

    CONST = P.sb("CONST", [128, 1536], F32)
    ident = CONST
    ONESB = P.sb("ONESB", [128, 128], BF16)
    IDB = P.sb("IDB", [128, 128], BF16)

    PSALL = nc.alloc_psum_tensor("psall", [128, 4096], F32)
    PS = [P.view("ps%d" % i, PSALL[:, i * 512:(i + 1) * 512], psum=True) for i in range(8)]
    ps_rr = [0]

    def psum():
        b = PS[ps_rr[0] % 8]
        ps_rr[0] += 1
        return b

    P.dma(CONST[:, :], consts[:, :], writes=[CONST])
    P.add("dve", lambda e: e.memset(ONESB[:, :], 1.0), writes=[ONESB])
    P.add("dve", lambda e: e.tensor_copy(IDB[:, :], CONST[:, 0:128]), reads=[CONST], writes=[IDB])

    XT = [carve(ARENA, i * 4096, F32, [128, D], "XT%d" % i) for i in range(2)]
    for blk in range(NT // 128):
        xt = XT[blk % 2]
        P.dma(xt[:, :], x_in[blk * 128:(blk + 1) * 128, :], writes=[xt])
        tb = (blk * 128) // TB
        for half in range(2):
            pb = psum()
            for j in range(4):
                kc = half * 4 + j
                P.add("pe", lambda e, pb=pb, xt=xt, kc=kc, j=j: e.transpose(
                    pb[:, j * 128:(j + 1) * 128], xt[:, kc * 128:(kc + 1) * 128], CONST[:, 0:128]),
                    reads=[xt, CONST], writes=[pb])
            wr = [Xb[(half * 4 + j, tb)] for j in range(4)]
            eng = "act" if half == 0 else "dve"
            if eng == "act":
                P.add("act", lambda e, pb=pb, half=half, blk=blk: e.copy(
                    X[:, half * 4:half * 4 + 4, blk * 128:(blk + 1) * 128],
                    pb[:, :].rearrange("p (j t) -> p j t", j=4)), reads=[pb], writes=wr)
            else:
                P.add("dve", lambda e, pb=pb, half=half, blk=blk: e.tensor_copy(
                    X[:, half * 4:half * 4 + 4, blk * 128:(blk + 1) * 128],
                    pb[:, :].rearrange("p (j t) -> p j t", j=4)), reads=[pb], writes=wr)

    STG = [P.sb("STG%d" % i, [128, 128], F32) for i in range(2)]
    stg_rr = [0]

    def load_fm(dst, dst_flat_ap, src_rows_ap, R):
        r0 = 0
        while r0 < R:
            r = min(128, R - r0)
            stg = STG[stg_rr[0] % 2]
            stg_rr[0] += 1
            P.dma(stg[0:r, :], src_rows_ap[r0:r0 + r, :], writes=[stg])
            pb = psum()
            P.add("pe", lambda e, pb=pb, stg=stg, r=r: e.transpose(pb[:, 0:r], stg[0:r, :], CONST[0:r, 0:r]),
                  reads=[stg, CONST], writes=[pb])
            P.add("dve", lambda e, pb=pb, r=r, r0=r0: e.tensor_copy(dst_flat_ap[:, r0:r0 + r], pb[:, 0:r]),
                  reads=[pb], writes=[dst])
            r0 += r

    G1 = P.sb("G1", [128, DEPTH, KC], F32)
    G2 = P.sb("G2", [128, DEPTH, KC], F32)
    BADA = P.sb("BADA", [128, DEPTH, 48], F32)
    CV = P.sb("CV", [128, 2, KC], F32)
    SCV = P.sb("SCV", [128, 2, KC], BF16)
    load_fm(G1, G1[:, :, :].rearrange("p l k -> p (l k)"), norm1_g.rearrange("l (k f) -> (l k) f", f=128), 32)
    load_fm(G2, G2[:, :, :].rearrange("p l k -> p (l k)"), norm2_g.rearrange("l (k f) -> (l k) f", f=128), 32)
    load_fm(BADA, BADA[:, :, :].rearrange("p l k -> p (l k)"), b_ada.rearrange("l (k f) -> (l k) f", f=128), 192)
    load_fm(CV, CV[:, :, :].rearrange("p v k -> p (v k)"), cvec.rearrange("v (k f) -> (v k) f", f=128), 16)
    P.add("act", lambda e: e.activation(SCV[:, :, :], CV[:, :, :], AF.Silu), reads=[CV], writes=[SCV])

    MOD = [P.sb("MOD%d" % l, [128, 48, 2], F32) for l in range(DEPTH)]
    A1 = [P.sb("A1_%d" % l, [128, KC, 2], F32) for l in range(DEPTH)]
    A2 = [P.sb("A2_%d" % l, [128, KC, 2], F32) for l in range(DEPTH)]
    WA = [P.sb("WA%d" % i, [128, KC, 128], BF16) for i in range(3)]
    wa_rr = [0]

    MODg = [[MOD[l].sub("MOD%d_%d" % (l, g)) for g in range(6)] for l in range(DEPTH)]

    class WStream:
        def __init__(self, slots, n, issue):
            self.slots, self.n, self.issue, self.nxt = slots, n, issue, 0

        def need(self, i):
            k = len(self.slots)
            while self.nxt <= min(i + k - 1, self.n - 1):
                self.issue(self.nxt, self.slots[self.nxt % k])
                self.nxt += 1
            return self.slots[i % k]

    def mod_gen(l):
        ws = WStream(WA, 48, lambda i, wa: P.dma(wa[:, :, :].rearrange("p k n -> p (k n)"), w_ada[l, i, :, :],
                                                 writes=[wa], q="pool"))
        for n in range(48):
            wa = ws.need(n)
            pm = PS[6 + mod_bank[0] % 2]
            mod_bank[0] += 1
            for kc in range(KC):
                P.add("pe", lambda e: e.matmul(pm[:, 0:2], wa[:, kc, :], SCV[:, :, kc],
                                               start=(kc == 0), stop=(kc == KC - 1)), reads=[wa, SCV], writes=[pm])
            P.add("dve", lambda e: e.tensor_scalar(MOD[l][:, n, :], pm[:, 0:2], BADA[:, l, n:n + 1], None, ALU.add),
                  reads=[pm, BADA], writes=[MODg[l][n // 8]])
            if n == 15:
                P.add("dve", lambda e: e.scalar_tensor_tensor(
                    A1[l][:, :, :], MOD[l][:, 8:16, :], 1.0, G1[:, l, :].unsqueeze(2).to_broadcast([128, KC, 2]),
                    ALU.add, ALU.mult), reads=[MODg[l][1], G1], writes=[A1[l]])
            if n == 39:
                P.add("dve", lambda e: e.scalar_tensor_tensor(
                    A2[l][:, :, :], MOD[l][:, 32:40, :], 1.0, G2[:, l, :].unsqueeze(2).to_broadcast([128, KC, 2]),
                    ALU.add, ALU.mult), reads=[MODg[l][4], G2], writes=[A2[l]])
            yield n

    mod_state = {"gen": None}
    mod_bank = [0]

    def pump(k):
        g = mod_state["gen"]
        if g is None:
            return
        for _ in range(k):
            try:
                next(g)
            except StopIteration:
                mod_state["gen"] = None
                return

    SQ = [P.sb("SQ%d" % i, [128, TB], BF16) for i in range(2)]
    RSTD = [P.sb("RSTD%d" % i, [128, TB], F32) for i in range(2)]
    TMPN = [P.sb("TMPN%d" % i, [128, TB], F32) for i in range(2)]
    EPSC = P.sb("EPSC", [128, 1], F32)
    P.add("dve", lambda e: e.memset(EPSC[:, :], EPS), writes=[EPSC])
    nrr = [0]

    def emit_norm_mod(l, which):
        A = A1[l] if which == 1 else A2[l]
        shoff = 0 if which == 1 else 24
        for tb in range(NTB):
            v = 0 if tb == 0 else 1
            rstd = RSTD[nrr[0] % 2]
            nrr[0] += 1
            sl = slice(tb * TB, (tb + 1) * TB)
            pb = psum()
            for kc in range(KC):
                sq = SQ[kc % 2]
                P.add("act", lambda e, sq=sq, sl=sl, kc=kc: e.activation(
                    sq[:, :], X[:, kc, sl], AF.Square), reads=[Xb[(kc, tb)]], writes=[sq])
                P.add("pe", lambda e, pb=pb, sq=sq, kc=kc: e.matmul(
                    pb[:, :], ONESB[:, :], sq[:, :], start=(kc == 0), stop=(kc == KC - 1)),
                    reads=[sq, ONESB], writes=[pb])
            P.add("act", lambda e, pb=pb, rstd=rstd: e.activation(
                rstd[:, :], pb[:, :], AF.Ln, bias=EPSC[:, 0:1], scale=1.0 / D),
                reads=[pb, EPSC], writes=[rstd])
            P.add("act", lambda e, rstd=rstd: e.activation(rstd[:, :], rstd[:, :], AF.Exp, scale=-0.5), reads=[rstd], writes=[rstd])
            for kc in range(KC):
                tmp = TMPN[kc % 2]
                P.add("dve", lambda e, tmp=tmp, kc=kc, sl=sl, rstd=rstd: e.tensor_tensor(
                    tmp[:, :], X[:, kc, sl], rstd[:, :], ALU.mult),
                    reads=[Xb[(kc, tb)], rstd], writes=[tmp])
                P.add("act", lambda e, tmp=tmp, kc=kc, sl=sl, v=v, A=A, l=l, shoff=shoff: e.activation(
                    H[:, kc, sl], tmp[:, :], AF.Identity,
                    bias=MOD[l][:, shoff + kc, v:v + 1], scale=A[:, kc, v:v + 1]),
                    reads=[tmp, A, MODg[l][shoff // 8]], writes=[Hb[(kc, tb)]])

    def resid_from_psum(pb, l, gate_off, n, tb, ncols=TB, col0=0):
        v = 0 if tb == 0 else 1
        sl = slice(tb * TB + col0, tb * TB + col0 + ncols)
        P.add("dve", lambda e: e.scalar_tensor_tensor(
            X[:, n, sl], pb[:, 0:ncols], MOD[l][:, gate_off + n, v:v + 1], X[:, n, sl], ALU.mult, ALU.add),
            reads=[pb, MODg[l][gate_off // 8], Xb[(n, tb)]], writes=[Xb[(n, tb)]])

    WUP = [carve(WREG, i * 4096, BF16, [128, KC, 2, 128], "WUP%d" % i) for i in range(2)]
    wup_rr = [0]
    CT = [P.sb("CT%d" % i, [128, NT], F32) for i in range(4)]
    ct_rr = [0]
    NFH = NFC // 2
    GB = carve(ARENA, 0, BF16, [128, NFH, NT], "GB")
    GBb = {(i, tb): GB.sub("GB%d_%d" % (i, tb)) for i in range(NFH) for tb in range(NTB)}
    FCW = P.sb("FCW", [128, DEPTH, 3, 44], F32)
    FCB = P.sb("FCB", [128, DEPTH, 44], F32)
    load_fm(FCW, FCW[:, :, :, :].rearrange("p l t k -> p (l t k)"),
            ffn_conv_w.rearrange("l t (k f) -> (l t k) f", f=128), DEPTH * 3 * 44)
    load_fm(FCB, FCB[:, :, :].rearrange("p l k -> p (l k)"), ffn_conv_b.rearrange("l (k f) -> (l k) f", f=128),
            DEPTH * 44)
    WDN = [carve(WREG, 8192 + i * 2816, BF16, [128, NFH, 128], "WDN%d" % i) for i in range(2)]
    wdn_rr = [0]
    grp_rr = [0]
    dn_rr = [0]

    def bank_group():
        g = grp_rr[0] % 2
        grp_rr[0] += 1
        return g * 3

    def conv_from_psum(b0, w0, w1, w2, bias, ct, ctb):
        pbs = [PS[b0], PS[b0 + 1], PS[b0 + 2]]
        samp = PSALL[:, (b0 + 1) * 512:(b0 + 3) * 512]
        P.add("act", lambda e: e.activation(ct[:, 0:512], PS[b0][:, :], AF.Identity, bias=bias, scale=w1),
              reads=[pbs[0], FCWb], writes=[ctb])
        P.add("act", lambda e: e.activation(ct[:, 512:NT], samp, AF.Identity, bias=bias, scale=w1),
              reads=[pbs[1], pbs[2], FCWb], writes=[ctb])
        for (t0, ln) in SEQS:
            if t0 < 512:
                src = lambda a, b_: PS[b0][:, a:b_]
                rd = [pbs[0]]
            else:
                src = lambda a, b_: PSALL[:, (b0 + 1) * 512 + a - 512:(b0 + 1) * 512 + b_ - 512]
                rd = [pbs[1], pbs[2]]
            P.add("dve", lambda e, src=src, t0=t0, ln=ln: e.scalar_tensor_tensor(
                ct[:, t0 + 1:t0 + ln], src(t0, t0 + ln - 1), w0, ct[:, t0 + 1:t0 + ln], ALU.mult, ALU.add),
                reads=rd + [FCWb, ctb], writes=[ctb])
            P.add("dve", lambda e, src=src, t0=t0, ln=ln: e.scalar_tensor_tensor(
                ct[:, t0:t0 + ln - 1], src(t0 + 1, t0 + ln), w2, ct[:, t0:t0 + ln - 1], ALU.mult, ALU.add),
                reads=rd + [FCWb, ctb], writes=[ctb])

    FCWb = FCW

    def emit_ffn(l):
        wus = WStream(WUP, NFC, lambda i, wu: P.dma(wu[:, :, :, :].rearrange("p k g n -> p (k g n)"),
                                                    ffn_w_up[l, i, :, :], writes=[wu], q="pool"))
        wds = WStream(WDN, 2 * KC, lambda i, wd: P.dma(wd[:, :, :].rearrange("p i n -> p (i n)"),
                                                       ffn_w_down[l, i // KC, i % KC, :, :], writes=[wd], q="pool"))
        for hf in range(2):
            for piece in range(hf * NFH, (hf + 1) * NFH):
                wu = wus.need(piece)
                if piece == (hf + 1) * NFH - 1:
                    wds.need(hf * KC)
                i = piece
                cts = []
                for g in range(2):
                    b0 = bank_group()
                    for tb in range(NTB):
                        pb = PS[b0 + tb]
                        for kc in range(KC):
                            P.add("pe", lambda e: e.matmul(
                                pb[:, :], wu[:, kc, g, :], H[:, kc, tb * TB:(tb + 1) * TB],
                                start=(kc == 0), stop=(kc == KC - 1)),
                                reads=[wu, Hb[(kc, tb)]], writes=[pb])
                    ct = CT[ct_rr[0] % 4]
                    ct_rr[0] += 1
                    ch = g * NFC + i
                    conv_from_psum(b0, FCW[:, l, 0, ch:ch + 1], FCW[:, l, 1, ch:ch + 1], FCW[:, l, 2, ch:ch + 1],
                                   FCB[:, l, ch:ch + 1], ct, ct)
                    cts.append(ct)
                ctv, ctg = cts
                pump(2)
                P.add("act", lambda e: e.activation(ctg[:, :], ctg[:, :], AF.Silu), reads=[ctg], writes=[ctg])
                P.add("pool", lambda e: e.tensor_tensor(GB[:, i - hf * NFH, :], ctv[:, :], ctg[:, :], ALU.mult),
                      reads=[ctv, ctg], writes=[GBb[(i - hf * NFH, tb)] for tb in range(NTB)])
            for piece in range(KC):
                wd = wds.need(hf * KC + piece)
                n = piece
                for tb in range(NTB):
                    pb = PS[dn_rr[0] % 6]
                    dn_rr[0] += 1
                    for i in range(NFH):
                        P.add("pe", lambda e: e.matmul(
                            pb[:, :], wd[:, i, :], GB[:, i, tb * TB:(tb + 1) * TB],
                            start=(i == 0), stop=(i == NFH - 1)),
                            reads=[wd, GBb[(i, tb)]], writes=[pb])
                    resid_from_psum(pb, l, 40, n, tb)
                pump(2)

    WO = [carve(WREG, i * 2048, BF16, [128, KC, 128], "WO%d" % i) for i in range(2)]
    wo_rr = [0]

    def emit_wout(l, w_dram, OT, OTb):
        wos = WStream(WO, KC, lambda i, wo: P.dma(wo[:, :, :].rearrange("p k n -> p (k n)"), w_dram[i, :, :],
                                                  writes=[wo], q="pool"))
        for n in range(KC):
            wo = wos.need(n)
            pump(1)
            for tb in range(NTB):
                pb = psum()
                for kc in range(KC):
                    P.add("pe", lambda e, pb=pb, wo=wo, kc=kc, tb=tb: e.matmul(
                        pb[:, :], wo[:, kc, :], OT[:, kc, tb * TB:(tb + 1) * TB],
                        start=(kc == 0), stop=(kc == KC - 1)), reads=[wo, OTb[kc]], writes=[pb])
                resid_from_psum(pb, l, 16, n, tb)

    rb_t = nc.dram_tensor("rbpad", [480, 127], F32)
    RBPAD = P.view("rbpad", rb_t.ap())
    J2B = P.sb("J2B", [128, 64], BF16)
    P.add("dve", lambda e: e.tensor_copy(J2B[:, :], CONST[:, 128:192]), reads=[CONST], writes=[J2B])

    def emit_rbpad():
        RBP = carve(ARENA, 16384, F32, [128, 4, 127], "RBP")
        P.add("pool", lambda e: e.memset(RBP[:, :, :], 0.0), writes=[RBP])
        P.dma(RBP[0:120, :, 48:79], na_rel_bias.rearrange("j (p a) f -> p (j a) f", a=2)[:, :, :]
              if False else na_rel_bias.rearrange("j r f -> (j r) f").rearrange("(p a) f -> p a f", a=4),
              reads=[], writes=[RBP], allow_slow_non_contiguous=True)
        P.dma(rb_t.ap().rearrange("(p a) f -> p a f", a=4), RBP[0:120, :, :], reads=[RBP], writes=[RBPAD])

    def psum_of(lst, st):
        b = PS[lst[st[0] % len(lst)]]
        st[0] += 1
        return b

    def emit_na(l):
        j = l // 2
        OT = carve(ARENA, 0, BF16, [128, KC, NT], "OT")
        OTb = [OT.sub("OT%d" % c) for c in range(KC)]
        QZ = [carve(ARENA, 24576 + i * 3072, BF16, [128, NT], "QZ%d" % i) for i in range(2)]
        KT = carve(ARENA, 30720, BF16, [128, NT], "KT")
        VT = carve(ARENA, 33792, BF16, [128, 12, 128], "VT")
        VTS = carve(ARENA, 36864, BF16, [128, 7, 128], "VTS")
        KCT = carve(ARENA, 38656, BF16, [128, 256], "KCT")
        VC = carve(ARENA, 39168, BF16, [128, 2, 128], "VC")
        PTC = carve(ARENA, 39680, BF16, [128, 2, 1024], "PTC")
        PTL = [carve(ARENA, 43776 + i * 512, BF16, [128, 4, 64], "PTL%d" % i) for i in range(2)]
        PTL.append(carve(ARENA, 54016, BF16, [128, 4, 64], "PTL2"))
        PTP = [carve(ARENA, 44800 + i * 1024, BF16, [128, 2, 256], "PTP%d" % i) for i in range(2)]
        HKR = carve(ARENA, 46848, BF16, [128, 14, 2, 64], "HKR")
        HKZ = carve(ARENA, 50432, BF16, [128, 14, 2, 64], "HKZ")
        CKS = TMPN[0].alias(TMPN[0].t[:, :].rearrange("p (a b) -> p a b", a=4))
        CVS = TMPN[1].alias(TMPN[1].t[:, :].rearrange("p (a b) -> p a b", a=4))
        RD = RSTD[0].alias(RSTD[0].t[:, :])
        KCS = RSTD[1].alias(RSTD[1].t[:, 0:256].rearrange("p (a b) -> p a b", a=2))
        WQ = [carve(WREG, i * 6144, BF16, [128, KC, 3, 128], "WQ%d" % i) for i in range(2)]
        lo = [0]
        hi = [0]
        LO = [0, 1, 2, 3]
        HI = [4, 5, 6, 7]
        P.add("pool", lambda e: e.memset(QZ[0][64:128, :], 0.0), writes=[QZ[0]])
        P.add("pool", lambda e: e.memset(QZ[1][0:64, :], 0.0), writes=[QZ[1]])
        P.add("pool", lambda e: e.memset(HKZ[:, :, :, :], 0.0), writes=[HKZ])
        ptl_rr = [0]
        ptp_rr = [0]
        wqs = WStream(WQ, KC, lambda i, wq: P.dma(wq[:, :, :, :].rearrange("p k g n -> p (k g n)"),
                                                  na_w_qkv[j, i, :, :], writes=[wq], q="pool"))
        for c in range(KC):
            wq = wqs.need(c)
            pump(3)
            for tb in range(NTB):
                sl = slice(tb * TB, (tb + 1) * TB)
                pb = psum_of(LO, lo)
                for kc in range(KC):
                    P.add("pe", lambda e, pb=pb, kc=kc, sl=sl: e.matmul(
                        pb[:, :], wq[:, kc, 0, :], H[:, kc, sl], start=(kc == 0), stop=(kc == KC - 1)),
                        reads=[wq, Hb[(kc, tb)]], writes=[pb])
                P.add("act", lambda e, pb=pb, sl=sl: e.activation(QZ[0][0:64, sl], pb[0:64, :], AF.Copy, scale=0.125),
                      reads=[pb], writes=[QZ[0]])
                P.add("act", lambda e, pb=pb, sl=sl: e.activation(QZ[1][64:128, sl], pb[64:128, :], AF.Copy, scale=0.125),
                      reads=[pb], writes=[QZ[1]])
                pb = psum_of(LO, lo)
                for kc in range(KC):
                    P.add("pe", lambda e, pb=pb, kc=kc, sl=sl: e.matmul(
                        pb[:, :], wq[:, kc, 1, :], H[:, kc, sl], start=(kc == 0), stop=(kc == KC - 1)),
                        reads=[wq, Hb[(kc, tb)]], writes=[pb])
                P.add("dve", lambda e, pb=pb, sl=sl: e.tensor_copy(KT[:, sl], pb[:, :]), reads=[pb], writes=[KT])
            for g in range(3):
                pb = psum_of(LO, lo)
                for b in range(4):
                    blk = g * 4 + b
                    for kc in range(KC):
                        P.add("pe", lambda e, pb=pb, kc=kc, b=b, blk=blk: e.matmul(
                            pb[:, b * 128:(b + 1) * 128], H[:, kc, blk * 128:(blk + 1) * 128], wq[:, kc, 2, :],
                            start=(kc == 0), stop=(kc == KC - 1)),
                            reads=[wq, Hb[(kc, blk // 4)]], writes=[pb])
                P.add("dve", lambda e, pb=pb, g=g: e.tensor_copy(
                    VT[:, g * 4:(g + 1) * 4, :], pb[:, :].rearrange("p (b f) -> p b f", b=4)),
                    reads=[pb], writes=[VT])
                if g == 0:
                    P.add("act", lambda e, pb=pb: e.copy(CVS[:, :, :], pb[:, :].rearrange("p (b f) -> p b f", b=4)),
                          reads=[pb], writes=[CVS])
                    for sq in range(2):
                        P.dma(cv_out[sq, j, :, c * 128:(c + 1) * 128].rearrange("(b p) f -> p b f", p=128),
                              CVS[:, sq * 2:sq * 2 + 2, :], reads=[CVS])
            pb = psum_of(LO, lo)
            for b in range(4):
                for kc in range(KC):
                    P.add("pe", lambda e, pb=pb, kc=kc, b=b: e.matmul(
                        pb[:, b * 128:(b + 1) * 128], H[:, kc, b * 128:(b + 1) * 128], wq[:, kc, 1, :],
                        start=(kc == 0), stop=(kc == KC - 1)), reads=[wq, Hb[(kc, 0)]], writes=[pb])
            P.add("act", lambda e, pb=pb: e.copy(CKS[:, :, :], pb[:, :].rearrange("p (b f) -> p b f", b=4)),
                  reads=[pb], writes=[CKS])
            for sq in range(2):
                P.dma(ck_out[sq, j, :, c * 128:(c + 1) * 128].rearrange("(b p) f -> p b f", p=128),
                      CKS[:, sq * 2:sq * 2 + 2, :], reads=[CKS])
            for g in range(2):
                pb = psum_of(LO, lo)
                nb = 4 if g == 0 else 3
                for b in range(nb):
                    m = g * 4 + b
                    t0 = 512 + 64 + 128 * m
                    for kc in range(KC):
                        P.add("pe", lambda e, pb=pb, kc=kc, b=b, t0=t0: e.matmul(
                            pb[:, b * 128:(b + 1) * 128], H[:, kc, t0:t0 + 128], wq[:, kc, 2, :],
                            start=(kc == 0), stop=(kc == KC - 1)),
                            reads=[wq, Hb[(kc, 1)], Hb[(kc, 2)]], writes=[pb])
                P.add("dve", lambda e, pb=pb, g=g, nb=nb: e.tensor_copy(
                    VTS[:, g * 4:g * 4 + nb, :], pb[:, 0:nb * 128].rearrange("p (b f) -> p b f", b=nb)),
                    reads=[pb], writes=[VTS])
            P.dma(KCS[:, :, :], cache_k[j, :, c * 128:(c + 1) * 128].rearrange("(b p) f -> p b f", p=128),
                  writes=[KCS])
            pb = psum_of(LO, lo)
            for b in range(2):
                P.add("pe", lambda e, pb=pb, b=b: e.transpose(pb[:, b * 128:(b + 1) * 128], KCS[:, b, :], CONST[:, 0:128]),
                      reads=[KCS, CONST], writes=[pb])
            P.add("act", lambda e, pb=pb: e.copy(KCT[:, :], pb[:, 0:256]), reads=[pb], writes=[KCT])
            P.dma(VC[:, :, :], cache_v[j, :, c * 128:(c + 1) * 128].rearrange("(b p) f -> p b f", p=128),
                  writes=[VC], q="pool")
            for sq in range(2):
                pbo = psum_of(HI, hi)
                for hh in range(2):
                    hs = slice(hh * 64, hh * 64 + 64)
                    ptp = PTP[ptp_rr[0] % 2]
                    ptp_rr[0] += 1
                    pbs = psum_of(LO, lo)
                    for kb in range(2):
                        P.add("pe", lambda e, pbs=pbs, kb=kb, hh=hh, sq=sq: e.matmul(
                            pbs[:, kb * 256:(kb + 1) * 256], KT[:, sq * 256 + kb * 128:sq * 256 + (kb + 1) * 128],
                            QZ[hh][:, sq * 256:(sq + 1) * 256], start=True, stop=True),
                            reads=[KT, QZ[hh]], writes=[pbs])
                    P.add("act", lambda e, pbs=pbs, ptp=ptp: e.activation(
                        ptp[:, :, :], pbs[:, :].rearrange("p (b q) -> p b q", b=2), AF.Exp),
                        reads=[pbs], writes=[ptp])
                    for kb in range(2):
                        P.add("pe", lambda e, kb=kb, hs=hs, ptp=ptp, sq=sq: e.matmul(
                            pbo[hs, 0:256], VT[:, sq * 2 + kb, hs], ptp[:, kb, :], start=(kb == 0), stop=(kb == 1)),
                            reads=[VT, ptp], writes=[pbo])
                    for kb in range(2):
                        P.add("pe", lambda e, kb=kb, hs=hs, ptp=ptp: e.matmul(
                            pbo[hs, 256:512], ONESB[:, 0:64], ptp[:, kb, :], start=(kb == 0), stop=(kb == 1)),
                            reads=[ONESB, ptp], writes=[pbo])
                P.add("act", lambda e, pbo=pbo: e.activation(RD[:, 0:256], pbo[:, 256:512], AF.Ln), reads=[pbo], writes=[RD])
                P.add("act", lambda e: e.activation(RD[:, 0:256], RD[:, 0:256], AF.Exp, scale=-1.0), reads=[RD], writes=[RD])
                P.add("dve", lambda e, pbo=pbo, sq=sq: e.tensor_tensor(
                    OT[:, c, sq * 256:(sq + 1) * 256], pbo[:, 0:256], RD[:, 0:256], ALU.mult),
                    reads=[pbo, RD], writes=[OTb[c]])
            pbo_s = [PS[4], PS[5]]
            pbd_s = [PS[6], PS[7]]
            for hh in range(2):
                h = 2 * c + hh
                hs = slice(hh * 64, hh * 64 + 64)
                for u in range(2):
                    src = bass.AP(rb_t, (j * 240 + h * 15 + u) * 127, [[1, 64], [127, 14], [1, 64]])
                    P.dma(HKR[0:64, :, u, :], src, reads=[RBPAD], writes=[HKR], q="pool")
                P.add("pool", lambda e: e.tensor_tensor(
                    HKZ[0:64, :, :, :].rearrange("p a u k -> p (a u) k"),
                    HKR[0:64, :, :, :].rearrange("p a u k -> p (a u) k"),
                    CONST[0:64, 192:256].unsqueeze(1).to_broadcast([64, 28, 64]), ALU.add),
                    reads=[HKR, CONST], writes=[HKZ])
                for kb in range(2):
                    for qb in range(2):
                        pbs = psum_of(LO, lo)
                        P.add("pe", lambda e, pbs=pbs, kb=kb, qb=qb, hh=hh: e.matmul(
                            pbs[:, :], KCT[:, kb * 128:(kb + 1) * 128], QZ[hh][:, 512 + qb * 512:512 + (qb + 1) * 512],
                            start=True, stop=True), reads=[KCT, QZ[hh]], writes=[pbs])
                        P.add("act", lambda e, pbs=pbs, kb=kb, qb=qb: e.activation(
                            PTC[:, kb, qb * 512:(qb + 1) * 512], pbs[:, :], AF.Exp), reads=[pbs], writes=[PTC])
                def row_scores(r):
                    rs = min(max(r - 4, 0), 8)
                    ptl = PTL[ptl_rr[0] % 3]
                    ptl_rr[0] += 1
                    pbs = psum_of(LO, lo)
                    qsl = slice(512 + 64 * r, 512 + 64 * r + 64)
                    for i in range(4):
                        kr = rs + 2 * i
                        ri = kr - r + 7
                        k0 = 512 + 64 * kr
                        P.add("pe", lambda e: e.matmul(
                            pbs[:, i * 64:(i + 1) * 64], KT[:, k0:k0 + 128], QZ[hh][:, qsl], start=True, stop=False),
                            reads=[KT, QZ[hh]], writes=[pbs])
                        P.add("pe", lambda e: e.matmul(
                            pbs[:, i * 64:(i + 1) * 64], HKZ[:, ri, :, :].rearrange("p u k -> p (u k)"), J2B[:, :],
                            start=False, stop=True), reads=[HKZ, J2B], writes=[pbs])
                    P.add("act", lambda e: e.activation(
                        ptl[:, :, :], pbs[:, 0:256].rearrange("p (i q) -> p i q", i=4), AF.Exp),
                        reads=[pbs], writes=[ptl])
                    return ptl

                def row_pv(r, ptl):
                    rs = min(max(r - 4, 0), 8)
                    pbo = pbo_s[r // 8]
                    pbd = pbd_s[r // 8]
                    osl = slice((r % 8) * 64, (r % 8) * 64 + 64)
                    for pbx, isden in ((pbo, False), (pbd, True)):
                        for i in range(4):
                            kr = rs + 2 * i
                            if kr % 2 == 0:
                                vv = VT[:, 4 + kr // 2, hs]
                                vb = VT
                            else:
                                vv = VTS[:, (kr - 1) // 2, hs]
                                vb = VTS
                            lhs = ONESB[:, 0:64] if isden else vv
                            P.add("pe", lambda e: e.matmul(
                                pbx[hs, osl], lhs, ptl[:, i, :], start=(i == 0), stop=False),
                                reads=[vb, ONESB, ptl], writes=[pbx])
                        for kb in range(2):
                            lhs = ONESB[:, 0:64] if isden else VC[:, kb, hs]
                            P.add("pe", lambda e: e.matmul(
                                pbx[hs, osl], lhs, PTC[:, kb, 64 * r:64 * r + 64], start=False, stop=(kb == 1)),
                                reads=[VC, ONESB, PTC], writes=[pbx])

                ptls = {0: row_scores(0), 1: row_scores(1)}
                for r in range(2, 16):
                    ptls[r] = row_scores(r)
                    row_pv(r - 2, ptls.pop(r - 2))
                row_pv(14, ptls.pop(14))
                row_pv(15, ptls.pop(15))
            for half in range(2):
                P.add("act", lambda e, half=half: e.activation(RD[:, :], pbd_s[half][:, :], AF.Ln),
                      reads=[pbd_s[half]], writes=[RD])
                P.add("act", lambda e: e.activation(RD[:, :], RD[:, :], AF.Exp, scale=-1.0), reads=[RD], writes=[RD])
                P.add("dve", lambda e, half=half: e.tensor_tensor(
                    OT[:, c, 512 + half * 512:512 + (half + 1) * 512], pbo_s[half][:, :], RD[:, :], ALU.mult),
                    reads=[pbo_s[half], RD], writes=[OTb[c]])
        P.fence()
        emit_wout(l, na_w_out[j], OT, OTb)


    C_I = CONST[:, 0:128]
    C_TRIF = CONST[:, 256:384]
    C_TRIB = CONST[:, 384:512]
    C_EVEN = CONST[:, 512:640]
    C_ODD = CONST[:, 640:768]
    C_MINC = [CONST[:, 768:896], CONST[:, 1024:1152]]
    C_MSTR = [CONST[:, 896:1024], CONST[:, 1152:1280]]
    C_ONES = CONST[:, 1280:1408]
    C_NEG1 = CONST[:, 1408:1536]
    GCW = P.sb("GCW", [128, 2, 3, 24], F32)
    load_fm(GCW, GCW[:, :, :, :].rearrange("p j t k -> p (j t k)"),
            gdn_conv_w.rearrange("j t (k f) -> (j t k) f", f=128), 144)
    GNG = P.sb("GNG", [128, 2], F32)
    load_fm(GNG, GNG[:, :], gdn_norm_g, 2)
    ZEROC = P.sb("ZEROC", [128, 1], F32)
    P.add("dve", lambda e: e.memset(ZEROC[:, :], 0.0), writes=[ZEROC])
    ev_rr = [0]

    def evac(dst_ap, dst_bufs, src_ap, pbs, scale=None):
        use_act = True
        ev_rr[0] += 1
        if use_act:
            if scale is None:
                P.add("act", lambda e: e.copy(dst_ap, src_ap), reads=pbs, writes=dst_bufs)
            else:
                P.add("act", lambda e: e.activation(dst_ap, src_ap, AF.Copy, scale=scale), reads=pbs, writes=dst_bufs)
        else:
            if scale is None:
                P.add("dve", lambda e: e.tensor_copy(dst_ap, src_ap), reads=pbs, writes=dst_bufs)
            else:
                P.add("dve", lambda e: e.tensor_scalar(dst_ap, src_ap, scale, None, ALU.mult), reads=pbs, writes=dst_bufs)

    def emit_gdn(l):
        j = l // 2
        off = [0]

        def AR(dtype, shape, name):
            esz = 2 if dtype == BF16 else 4
            n = esz
            for d_ in shape[1:]:
                n *= d_
            v = carve(ARENA, off[0], dtype, shape, name)
            off[0] += (n + 63) // 64 * 64
            assert off[0] <= 55296, off[0]
            return v

        TK = [AR(F32, [128, 12, 16], "TK%d" % i) for i in range(12)]
        GTOK, GC, BETA, GCB, EGC, BEG, GLE, GLO, EDE, EDO, TMPA, TMPB = TK
        AB = P.view("AB", ARENA.t[:, (10 * 768) // 4:(12 * 768) // 4].rearrange("p (b c) -> p b c", b=12))
        QNb = AR(BF16, [128, NT], "QNb")
        KNb = AR(BF16, [128, NT], "KNb")
        OTh = AR(BF16, [128, NT], "OTh")
        BS = []
        BSR = []
        for d_ in range(2):
            row, rowr = [], []
            for i in range(3):
                k_ = d_ * 3 + i
                apr = CHAIN[:, k_ * 512:(k_ + 1) * 512].rearrange("p (a b) -> p a b", a=4)
                apf = CHAIN.bitcast(F32)[:, k_ * 512:(k_ + 1) * 512].rearrange("p (a b) -> p a b", a=4)
                row.append(P.view("BS%d_%d" % (d_, i), apf))
                rowr.append(apr)
            BS.append(row)
            BSR.append(rowr)
        TTbs = [AR(BF16, [128, 4, 128], "TTb%d" % d_) for d_ in range(2)]
        VBs = [AR(BF16, [128, 4, 128], "VB%d" % d_) for d_ in range(2)]
        KBGs = [AR(BF16, [128, 4, 128], "KBG%d" % d_) for d_ in range(2)]
        F = [b_.alias(b_.t[:, :].rearrange("p (a b) -> p a b", a=4)) for b_ in (RSTD[0], RSTD[1], TMPN[0], TMPN[1])]
        TC = F
        SETS = []
        for i in range(4):
            SETS.append(dict(U=AR(BF16, [128, 4, 128], "U%d" % i), NWT=AR(BF16, [128, 4, 128], "NWT%d" % i),
                             PT=AR(BF16, [128, 4, 128], "PT%d" % i), KD=[AR(BF16, [128, 4, 128], "KDe%d" % i),
                                                                        AR(BF16, [128, 4, 128], "KDo%d" % i)],
                             QG=AR(BF16, [128, 4, 128], "QG%d" % i)))
        CH = []
        for i in range(4):
            CH.append(dict(S=AR(F32, [128, 128], "S%d" % i), Sb=AR(BF16, [128, 128], "Sb%d" % i),
                           VN=AR(BF16, [128, 128], "VN%d" % i)))
        WAB = AR(BF16, [128, KC, 32], "WAB")
        DTB16 = AR(F32, [128, 16], "DTB16")
        NEGA = AR(F32, [128, 16], "NEGA")
        WI = [carve(WREG, 4096 + i * 4096, BF16, [128, KC, 2, 128], "WI%d" % i) for i in range(2)]
        WOH = [carve(WREG, i * 2048, BF16, [128, D], "WOH%d" % i) for i in range(2)]
        CTQ, CTK, CTV, CTZ = CT
        OF = CTQ
        OFb = [OF.sub("OF%d" % b) for b in range(12)]
        for ch in CH:
            P.add("pool", lambda e, ch=ch: e.memset(ch["VN"][:, :], 0.0), writes=[ch["VN"]])

        P.dma(WAB[:, :, :].rearrange("p k n -> p (k n)"), gdn_w_ab[j, :, :], writes=[WAB], q="pool")
        P.dma(DTB16[:, :], gdn_dt_bias[j:j + 1, :].to_broadcast([128, 16]), writes=[DTB16])
        P.dma(NEGA[:, :], gdn_a_log[j:j + 1, :].to_broadcast([128, 16]), writes=[NEGA])
        P.add("act", lambda e: e.activation(NEGA[:, :], NEGA[:, :], AF.Exp), reads=[NEGA], writes=[NEGA])
        pb = psum()
        for blk in range(12):
            for kc in range(KC):
                P.add("pe", lambda e, pb=pb, blk=blk, kc=kc: e.matmul(
                    pb[:, blk * 32:(blk + 1) * 32], H[:, kc, blk * 128:(blk + 1) * 128], WAB[:, kc, :],
                    start=(kc == 0), stop=(kc == KC - 1)), reads=[WAB, Hb[(kc, blk // 4)]], writes=[pb])
        P.add("dve", lambda e, pb=pb: e.tensor_copy(AB[:, :, :], pb[:, 0:384].rearrange("p (b c) -> p b c", b=12)),
              reads=[pb], writes=[TMPA, TMPB])
        bc16 = lambda t: t[:, :].unsqueeze(1).to_broadcast([128, 12, 16])
        P.add("dve", lambda e: e.tensor_tensor(GTOK[:, :, :], AB[:, :, 0:16], bc16(DTB16), ALU.add),
              reads=[TMPA, TMPB, DTB16], writes=[GTOK])
        P.add("act", lambda e: e.activation(GTOK[:, :, :], GTOK[:, :, :], AF.Exp), reads=[GTOK], writes=[GTOK])
        P.add("dve", lambda e: e.tensor_scalar(GTOK[:, :, :], GTOK[:, :, :], 1.0, None, ALU.add), reads=[GTOK], writes=[GTOK])
        P.add("act", lambda e: e.activation(GTOK[:, :, :], GTOK[:, :, :], AF.Ln), reads=[GTOK], writes=[GTOK])
        P.add("dve", lambda e: e.scalar_tensor_tensor(GTOK[:, :, :], GTOK[:, :, :], -1.0, bc16(NEGA), ALU.mult, ALU.mult),
              reads=[GTOK, NEGA], writes=[GTOK])
        P.add("act", lambda e: e.activation(BETA[:, :, :], AB[:, :, 16:32], AF.Exp, scale=-1.0),
              reads=[TMPA, TMPB], writes=[BETA])
        P.add("dve", lambda e: e.tensor_scalar(BETA[:, :, :], BETA[:, :, :], 1.0, None, ALU.add), reads=[BETA], writes=[BETA])
        P.add("act", lambda e: e.activation(GCB[:, :, :], BETA[:, :, :], AF.Ln), reads=[BETA], writes=[GCB])
        P.add("dve", lambda e: e.reciprocal(BETA[:, :, :], BETA[:, :, :]), reads=[BETA], writes=[BETA])
        pb = psum()
        for blk in range(12):
            for d in range(2):
                tri = C_TRIF if d == 0 else C_TRIB
                P.add("pe", lambda e, pb=pb, blk=blk, d=d, tri=tri: e.matmul(
                    pb[:, blk * 16 + d * 8:blk * 16 + d * 8 + 8], tri, GTOK[:, blk, d * 8:d * 8 + 8],
                    start=True, stop=True), reads=[CONST, GTOK], writes=[pb])
        P.add("dve", lambda e, pb=pb: e.tensor_copy(GC[:, :, :], pb[:, 0:192].rearrange("p (b c) -> p b c", b=12)),
              reads=[pb], writes=[GC])
        for (dst, cm) in ((GLE, C_EVEN), (GLO, C_ODD)):
            pb = psum()
            for blk in range(12):
                P.add("pe", lambda e, pb=pb, blk=blk, cm=cm: e.matmul(
                    pb[:, blk * 16:(blk + 1) * 16], cm, GTOK[:, blk, :], start=True, stop=True),
                    reads=[CONST, GTOK], writes=[pb])
            P.add("dve", lambda e, pb=pb, dst=dst: e.tensor_copy(
                dst[:, :, :], pb[:, 0:192].rearrange("p (b c) -> p b c", b=12)), reads=[pb], writes=[dst])
        P.add("dve", lambda e: e.tensor_tensor(GCB[:, :, :], GC[:, :, :], GCB[:, :, :], ALU.subtract),
              reads=[GC, GCB], writes=[GCB])
        P.add("act", lambda e: e.activation(EGC[:, :, :], GC[:, :, :], AF.Exp), reads=[GC], writes=[EGC])
        P.add("dve", lambda e: e.tensor_tensor(BEG[:, :, :], BETA[:, :, :], EGC[:, :, :], ALU.mult),
              reads=[BETA, EGC], writes=[BEG])
        for (dst, gl, mcol) in ((EDE, GLE, C_EVEN[:, 0:1]), (EDO, GLO, C_ODD[:, 0:1])):
            P.add("dve", lambda e, dst=dst, gl=gl: e.tensor_tensor(dst[:, :, :], gl[:, :, :], GC[:, :, :], ALU.subtract),
                  reads=[gl, GC], writes=[dst])
            P.add("dve", lambda e, dst=dst: e.tensor_scalar(dst[:, :, :], dst[:, :, :], 0.0, None, ALU.min),
                  reads=[dst], writes=[dst])
            P.add("act", lambda e, dst=dst: e.activation(dst[:, :, :], dst[:, :, :], AF.Exp), reads=[dst], writes=[dst])
            P.add("dve", lambda e, dst=dst, mcol=mcol: e.tensor_scalar(dst[:, :, :], dst[:, :, :], mcol, None, ALU.mult),
                  reads=[dst, CONST], writes=[dst])
        P.add("act", lambda e: e.activation(GLE[:, :, :], GLE[:, :, :], AF.Exp), reads=[GLE], writes=[GLE])
        P.add("act", lambda e: e.activation(GLO[:, :, :], GLO[:, :, :], AF.Exp), reads=[GLO], writes=[GLO])
        EGL = [GLE, GLO]
        ED = [EDE, EDO]
        NGC = TMPA
        P.add("dve", lambda e: e.tensor_scalar(NGC[:, :, :], GC[:, :, :], -1.0, None, ALU.mult), reads=[GC], writes=[NGC])

        wis_ = WStream(WI, 16, lambda i, wi: P.dma(wi[:, :, :, :].rearrange("p k g n -> p (k g n)"),
                                                   gdn_w_in[j, i // 2, i % 2, :, :], writes=[wi], q="pool"))
        whs_ = WStream(WOH, 8, lambda i, woh: P.dma(woh[:, :], gdn_w_out[j, i * 128:(i + 1) * 128, :],
                                                    writes=[woh], q="pool"))
        PRE_BANKS, SCAN_BANKS = [0, 1, 2, 3, 4], [5, 6, 7]
        pre_rr, scan_rr = [0], [0]
        for h in range(8):
            pump(3)
            for pi in range(2):
                wi = wis_.need(h * 2 + pi)
                for g in range(2):
                    t = pi * 2 + g
                    b0 = bank_group()
                    for tb in range(NTB):
                        pbk = PS[b0 + tb]
                        for kc in range(KC):
                            P.add("pe", lambda e, pbk=pbk, wi=wi, kc=kc, g=g, tb=tb: e.matmul(
                                pbk[:, :], wi[:, kc, g, :], H[:, kc, tb * TB:(tb + 1) * TB],
                                start=(kc == 0), stop=(kc == KC - 1)), reads=[wi, Hb[(kc, tb)]], writes=[pbk])
                    ct = CT[t]
                    if t < 3:
                        ch = t * 8 + h
                        conv_from_psum(b0, GCW[:, j, 0, ch:ch + 1], GCW[:, j, 1, ch:ch + 1], GCW[:, j, 2, ch:ch + 1],
                                       ZEROC[:, 0:1], ct, ct)
                        P.add("act", lambda e, ct=ct: e.activation(ct[:, :], ct[:, :], AF.Silu), reads=[ct], writes=[ct])
                    else:
                        P.add("act", lambda e, ct=ct, b0=b0: e.activation(ct[:, 0:512], PS[b0][:, :], AF.Silu),
                              reads=[PS[b0]], writes=[ct])
                        P.add("act", lambda e, ct=ct, b0=b0: e.activation(
                            ct[:, 512:NT], PSALL[:, (b0 + 1) * 512:(b0 + 3) * 512], AF.Silu),
                            reads=[PS[b0 + 1], PS[b0 + 2]], writes=[ct])
            f2 = lambda T_: T_[:, :, :].rearrange("p a b -> p (a b)")
            for (ct, dstb, keep) in ((CTQ, QNb, False), (CTK, KNb, True)):
                sls = [slice(tb * TB, (tb + 1) * TB) for tb in range(NTB)]
                for tb in range(NTB):
                    P.add("act", lambda e: e.activation(f2(TC[tb]), ct[:, sls[tb]], AF.Square), reads=[ct], writes=[TC[tb]])
                pbks = []
                for tb in range(NTB):
                    pbk = psum()
                    pbks.append(pbk)
                    P.add("pe", lambda e: e.matmul(pbk[:, :], C_ONES, f2(TC[tb]), start=True, stop=True),
                          reads=[CONST, TC[tb]], writes=[pbk])
                for tb in range(NTB):
                    P.add("act", lambda e: e.activation(f2(TC[tb]), pbks[tb][:, :], AF.Ln, bias=EPSC[:, 0:1], scale=1.0),
                          reads=[pbks[tb], EPSC], writes=[TC[tb]])
                for tb in range(NTB):
                    P.add("act", lambda e: e.activation(f2(TC[tb]), f2(TC[tb]), AF.Exp, scale=-0.5),
                          reads=[TC[tb]], writes=[TC[tb]])
                for tb in range(NTB):
                    sl = sls[tb]
                    if keep:
                        P.add("pool", lambda e: e.tensor_tensor(ct[:, sl], ct[:, sl], f2(TC[tb]), ALU.mult),
                              reads=[ct, TC[tb]], writes=[ct])
                        P.add("act", lambda e: e.copy(dstb[:, sl], ct[:, sl]), reads=[ct], writes=[dstb])
                    else:
                        P.add("pool", lambda e: e.tensor_tensor(dstb[:, sl], ct[:, sl], f2(TC[tb]), ALU.mult),
                              reads=[ct, TC[tb]], writes=[dstb])
            P.add("pool", lambda e: e.memset(OF[:, :], 0.0), writes=[OF] + OFb)

            f4 = lambda T_: T_[:, :, :].rearrange("p a b -> p (a b)")
            v4 = lambda pb_: pb_[:, :].rearrange("p (b f) -> p b f", b=4)
            Ibc = C_I.unsqueeze(1).to_broadcast([128, 4, 128])

            def pre_dir(g, d, st, pk, pv, pg, pq):
                cd = d * 8 + h
                blks = slice(g * 4, g * 4 + 4)
                tsl = slice(g * 512, (g + 1) * 512)
                col = lambda T_: T_[:, blks, cd:cd + 1].to_broadcast([128, 4, 128])
                B = BS[d]
                Fa, Fb = F[2 * d], F[2 * d + 1]
                VB, KBG = VBs[d], KBGs[d]
                P.add("dve", lambda e: e.tensor_tensor(VB[:, :, :], v4(pv), col(BETA), ALU.mult),
                      reads=[pv, BETA], writes=[VB])
                P.add("dve", lambda e: e.tensor_tensor(KBG[:, :, :], v4(pk), col(BEG), ALU.mult),
                      reads=[pk, BEG], writes=[KBG])
                for u in range(2):
                    P.add("dve", lambda e: e.tensor_tensor(st["KD"][u][:, :, :], v4(pk), col(ED[u]), ALU.mult),
                          reads=[pk, ED[u]], writes=[st["KD"][u]])
                yield
                P.add("pool", lambda e: e.tensor_tensor(Fa[:, :, :], Ibc, col(EGC), ALU.mult),
                      reads=[CONST, EGC], writes=[Fa])
                pr = psum_of(PRE_BANKS, pre_rr)
                for b in range(4):
                    P.add("pe", lambda e: e.matmul(pr[:, b * 128:(b + 1) * 128], C_ONES, Fa[:, b, :],
                                                   start=True, stop=True), reads=[CONST, Fa], writes=[pr])
                P.add("dve", lambda e: e.scalar_tensor_tensor(f4(st["QG"]), pr[:, :], 128.0 ** -0.5, QNb[:, tsl],
                                                              ALU.mult, ALU.mult), reads=[pr, QNb], writes=[st["QG"]])
                yield
                P.add("pool", lambda e: e.tensor_tensor(Fa[:, :, :], Ibc, col(GC), ALU.mult),
                      reads=[CONST, GC], writes=[Fa])
                P.add("pool", lambda e: e.tensor_tensor(Fb[:, :, :], Ibc, col(GCB), ALU.mult),
                      reads=[CONST, GCB], writes=[Fb])
                pa = psum_of(PRE_BANKS, pre_rr)
                pbb = psum_of(PRE_BANKS, pre_rr)
                for (pz, dg, msk) in ((pa, Fa, C_MINC[d]), (pbb, Fb, C_MSTR[d])):
                    for b in range(4):
                        o_ = pz[:, b * 128:(b + 1) * 128]
                        P.add("pe", lambda e: e.matmul(o_, C_ONES, dg[:, b, :], start=True, stop=False),
                              reads=[CONST, dg], writes=[pz])
                        P.add("pe", lambda e: e.matmul(o_, C_I, msk, start=False, stop=True),
                              reads=[CONST], writes=[pz])
                BT, Bs_, M_ = B
                BTr, Bsr, Mr = [x_.t for x_ in B]
                r4 = lambda ap_: ap_[:, :, :].rearrange("p a b -> p (a b)")
                for b in range(4):
                    ngc = NGC[:, g * 4 + b, cd:cd + 1]
                    P.add("act", lambda e: e.activation(Mr[:, b, :], pa[:, b * 128:(b + 1) * 128], AF.Exp, bias=ngc),
                          reads=[pa, NGC], writes=[M_])
                    P.add("act", lambda e: e.activation(BTr[:, b, :], pbb[:, b * 128:(b + 1) * 128], AF.Exp, bias=ngc),
                          reads=[pbb, NGC], writes=[BT])
                yield
                pg = psum_of(PRE_BANKS, pre_rr)
                pq = psum_of(PRE_BANKS, pre_rr)
                for b in range(4):
                    c0 = g * 512 + b * 128
                    P.add("pe", lambda e: e.matmul(pg[:, b * 128:(b + 1) * 128], KNb[:, c0:c0 + 128], KNb[:, c0:c0 + 128],
                                                   start=True, stop=True), reads=[KNb], writes=[pg])
                for b in range(4):
                    c0 = g * 512 + b * 128
                    P.add("pe", lambda e: e.matmul(pq[:, b * 128:(b + 1) * 128], KNb[:, c0:c0 + 128], QNb[:, c0:c0 + 128],
                                                   start=True, stop=True), reads=[KNb, QNb], writes=[pq])
                P.add("dve", lambda e: e.scalar_tensor_tensor(f4(st["PT"]), pq[:, :], 128.0 ** -0.5, f4(M_),
                                                              ALU.mult, ALU.mult), reads=[pq, M_], writes=[st["PT"]])
                P.add("dve", lambda e: e.scalar_tensor_tensor(r4(BTr), pg[:, :], -1.0, f4(BT), ALU.mult, ALU.mult),
                      reads=[pg, BT], writes=[BT])
                yield
                pt_ = psum_of(PRE_BANKS, pre_rr)
                for b in range(4):
                    P.add("pe", lambda e: e.transpose(pt_[:, b * 128:(b + 1) * 128], BT[:, b, :], C_I),
                          reads=[BT, CONST], writes=[pt_])
                evac(r4(Bsr), [Bs_], pt_[:, :], [pt_])
                P.add("pool", lambda e: e.tensor_tensor(Mr[:, :, :], BT[:, :, :], Ibc, ALU.add),
                      reads=[BT, CONST], writes=[M_])
                yield
                TTb = TTbs[d]
                for k in range(5):
                    if k < 4:
                        px = psum_of(PRE_BANKS, pre_rr)
                        for b in range(4):
                            P.add("pe", lambda e: e.matmul(px[:, b * 128:(b + 1) * 128], Bsr[:, b, :], BTr[:, b, :],
                                                           start=True, stop=True), reads=[Bs_, BT], writes=[px])
                    py = psum_of(PRE_BANKS, pre_rr)
                    for b in range(4):
                        P.add("pe", lambda e: e.matmul(py[:, b * 128:(b + 1) * 128], BTr[:, b, :], Bsr[:, b, :],
                                                       start=True, stop=True), reads=[Bs_, BT], writes=[py])
                    if k < 4:
                        evac(r4(BTr), [BT], px[:, :], [px])
                    evac(r4(Bsr), [Bs_], py[:, :], [py])
                    yield
                    pm_ = psum_of(PRE_BANKS, pre_rr)
                    for b in range(4):
                        o_ = pm_[:, b * 128:(b + 1) * 128]
                        P.add("pe", lambda e: e.matmul(o_, Bsr[:, b, :], Mr[:, b, :], start=True, stop=True),
                              reads=[Bs_, M_], writes=[pm_])
                    if k < 4:
                        P.add("dve", lambda e: e.tensor_tensor(r4(Mr), pm_[:, :], f4(M_), ALU.add),
                              reads=[pm_, M_], writes=[M_])
                    else:
                        P.add("dve", lambda e: e.tensor_tensor(f4(TTb), pm_[:, :], f4(M_), ALU.add),
                              reads=[pm_, M_], writes=[TTb])
                    yield
                pu = psum_of(PRE_BANKS, pre_rr)
                pw = psum_of(PRE_BANKS, pre_rr)
                for b in range(4):
                    P.add("pe", lambda e: e.matmul(pu[:, b * 128:(b + 1) * 128], TTb[:, b, :], VB[:, b, :],
                                                   start=True, stop=True), reads=[TTb, VB], writes=[pu])
                for b in range(4):
                    P.add("pe", lambda e: e.matmul(pw[:, b * 128:(b + 1) * 128], KBG[:, b, :], TTb[:, b, :],
                                                   start=True, stop=True), reads=[TTb, KBG], writes=[pw])
                evac(f4(st["U"]), [st["U"]], pu[:, :], [pu])
                evac(f4(st["NWT"]), [st["NWT"]], pw[:, :], [pw], scale=-1.0)
                yield

            def precompute2_gen(gs, sts):
                tiles = {}
                for g in sorted(set(gs)):
                    pk = psum_of(PRE_BANKS, pre_rr)
                    pv = psum_of(PRE_BANKS, pre_rr)
                    for b in range(4):
                        c0 = g * 512 + b * 128
                        P.add("pe", lambda e: e.transpose(pk[:, b * 128:(b + 1) * 128], CTK[:, c0:c0 + 128], C_I),
                              reads=[CTK, CONST], writes=[pk])
                    for b in range(4):
                        c0 = g * 512 + b * 128
                        P.add("pe", lambda e: e.transpose(pv[:, b * 128:(b + 1) * 128], CTV[:, c0:c0 + 128], C_I),
                              reads=[CTV, CONST], writes=[pv])
                    tiles[g] = (pk, pv)
                gens = [pre_dir(gs[d_], d_, sts[d_], tiles[gs[d_]][0], tiles[gs[d_]][1], None, None) for d_ in range(2)]
                while gens:
                    for gi in list(gens):
                        try:
                            next(gi)
                        except StopIteration:
                            gens.remove(gi)
                    yield

            def interleave(gen, rounds, every):
                i = 0
                r = 0
                for _ in gen:
                    i += 1
                    if i % every == 0 and r < len(rounds):
                        rounds[r]()
                        r += 1
                while r < len(rounds):
                    rounds[r]()
                    r += 1

            def scan_round(items):
                b1 = psum_of(SCAN_BANKS, scan_rr)
                b2 = psum_of(SCAN_BANKS, scan_rr)
                b3 = psum_of(SCAN_BANKS, scan_rr)
                for ci, (ch, st, d, blk, u) in enumerate(items):
                    bi = blk % 4
                    rsl = slice(u * 64, u * 64 + 64)
                    P.add("pe", lambda e: e.matmul(b1[rsl, ci * 128:(ci + 1) * 128], st["NWT"][:, bi, rsl], ch["Sb"][:, :],
                                                   start=True, stop=True), reads=[st["NWT"], ch["Sb"]], writes=[b1])
                for ci, (ch, st, d, blk, u) in enumerate(items):
                    bi = blk % 4
                    rsl = slice(u * 64, u * 64 + 64)
                    P.add("dve", lambda e: e.tensor_tensor(ch["VN"][rsl, :], st["U"][rsl, bi, :],
                                                           b1[rsl, ci * 128:(ci + 1) * 128], ALU.add),
                          reads=[st["U"], b1], writes=[ch["VN"]])
                for ci, (ch, st, d, blk, u) in enumerate(items):
                    bi = blk % 4
                    rsl = slice(u * 64, u * 64 + 64)
                    o2 = b2[:, ci * 64:(ci + 1) * 64]
                    P.add("pe", lambda e: e.matmul(o2, ch["Sb"][:, :], st["QG"][:, bi, rsl], start=True, stop=False),
                          reads=[ch["Sb"], st["QG"]], writes=[b2])
                    P.add("pe", lambda e: e.matmul(o2, ch["VN"][:, :], st["PT"][:, bi, rsl], start=False, stop=True),
                          reads=[ch["VN"], st["PT"]], writes=[b2])
                    P.add("pe", lambda e: e.matmul(b3[:, ci * 128:(ci + 1) * 128], st["KD"][u][:, bi, :], ch["VN"][:, :],
                                                   start=True, stop=True), reads=[st["KD"][u], ch["VN"]], writes=[b3])
                for ci, (ch, st, d, blk, u) in enumerate(items):
                    cd = d * 8 + h
                    S, Sb = ch["S"], ch["Sb"]
                    o3 = b3[:, ci * 128:(ci + 1) * 128]
                    P.add("dve", lambda e: e.scalar_tensor_tensor(Sb[:, :], S[:, :], EGL[u][:, blk, cd:cd + 1], o3,
                                                                  ALU.mult, ALU.add), reads=[S, EGL[u], b3], writes=[Sb])
                    P.add("dve", lambda e: e.scalar_tensor_tensor(S[:, :], S[:, :], EGL[u][:, blk, cd:cd + 1], o3,
                                                                  ALU.mult, ALU.add), reads=[S, EGL[u], b3], writes=[S])
                for ci, (ch, st, d, blk, u) in enumerate(items):
                    c0 = blk * 128 + u * 64
                    P.add("dve", lambda e: e.tensor_tensor(OF[:, c0:c0 + 64], OF[:, c0:c0 + 64],
                                                           b2[:, ci * 64:(ci + 1) * 64], ALU.add),
                          reads=[OFb[blk], b2], writes=[OFb[blk]])

            def scan_step(ch, st, d, blk, u):
                scan_round([(ch, st, d, blk, u)])

            def chain_steps(d, blocks):
                steps = [(b, u) for b in blocks for u in range(2)]
                return steps if d == 0 else steps[::-1]

            def init_chain(ch, d, seq):
                if seq < 2:
                    P.add("pool", lambda e: e.memset(ch["S"][:, :], 0.0), writes=[ch["S"]])
                else:
                    P.dma(ch["S"][:, :], state_gdn[j, d, h, :, :], writes=[ch["S"]])
                P.add("act", lambda e: e.copy(ch["Sb"][:, :], ch["S"][:, :]), reads=[ch["S"]], writes=[ch["Sb"]])

            for _ in precompute2_gen([0, 0], [SETS[2], SETS[3]]):
                pass
            chains = []
            for d in range(2):
                for seq in range(2):
                    ch = CH[d * 2 + seq]
                    init_chain(ch, d, seq)
                    chains.append((ch, SETS[2 + d], d, seq, chain_steps(d, [2 * seq, 2 * seq + 1])))

            def roundA(si):
                scan_round([(ch, st, d, steps[si][0], steps[si][1]) for (ch, st, d, seq, steps) in chains])

            interleave(precompute2_gen([1, 2], [SETS[0], SETS[1]]), [lambda si=si: roundA(si) for si in range(4)], 3)
            for (ch, st, d, seq, steps) in chains:
                P.dma(st_out[seq, j, d, h, :, :], ch["S"][:, :], reads=[ch["S"]])
            chB = [CH[0], CH[1]]
            stepsB = [chain_steps(d, list(range(4, 12))) for d in range(2)]
            for d in range(2):
                init_chain(chB[d], d, 2)
            setB = {(0, 1): SETS[0], (1, 2): SETS[1], (0, 2): SETS[2], (1, 1): SETS[3]}

            def roundB(si):
                items = []
                for d in range(2):
                    blk, u = stepsB[d][si]
                    items.append((chB[d], setB[(d, blk // 4)], d, blk, u))
                scan_round(items)

            interleave(precompute2_gen([2, 1], [SETS[2], SETS[3]]), [lambda si=si: roundB(si) for si in range(8)], 2)
            for si in range(8, 16):
                roundB(si)

            sls = [slice(tb * TB, (tb + 1) * TB) for tb in range(NTB)]
            for tb in range(NTB):
                P.add("act", lambda e: e.activation(f2(TC[tb]), OF[:, sls[tb]], AF.Square),
                      reads=OFb[tb * 4:tb * 4 + 4] + [OF], writes=[TC[tb]])
            pbks = []
            for tb in range(NTB):
                pbk = psum()
                pbks.append(pbk)
                P.add("pe", lambda e: e.matmul(pbk[:, :], C_ONES, f2(TC[tb]), start=True, stop=True),
                      reads=[CONST, TC[tb]], writes=[pbk])
            for tb in range(NTB):
                P.add("act", lambda e: e.activation(f2(TC[tb]), pbks[tb][:, :], AF.Ln, bias=EPSC[:, 0:1], scale=1.0 / 128),
                      reads=[pbks[tb], EPSC], writes=[TC[tb]])
            for tb in range(NTB):
                P.add("act", lambda e: e.activation(f2(TC[tb]), f2(TC[tb]), AF.Exp, scale=-0.5), reads=[TC[tb]], writes=[TC[tb]])
            for tb in range(NTB):
                sl = sls[tb]
                P.add("pool", lambda e: e.tensor_tensor(f2(TC[tb]), OF[:, sl], f2(TC[tb]), ALU.mult),
                      reads=OFb[tb * 4:tb * 4 + 4] + [TC[tb], OF], writes=[TC[tb]])
                P.add("dve", lambda e: e.scalar_tensor_tensor(
                    OTh[:, sl], f2(TC[tb]), GNG[:, j:j + 1], CTZ[:, sl], ALU.mult, ALU.mult),
                    reads=[TC[tb], GNG, CTZ], writes=[OTh])
            woh = whs_.need(h)
            for n in range(KC):
                for tb in range(NTB):
                    pbk = psum()
                    P.add("pe", lambda e, pbk=pbk, n=n, tb=tb, woh=woh: e.matmul(
                        pbk[:, :], woh[:, n * 128:(n + 1) * 128], OTh[:, tb * TB:(tb + 1) * TB], start=True, stop=True),
                        reads=[woh, OTh], writes=[pbk])
                    resid_from_psum(pbk, l, 16, n, tb)

    P.fence()
    emit_rbpad()
    mod_state["gen"] = mod_gen(0)
    pump(24)
    for l in range(cfg.n_layers):
        emit_norm_mod(l, 1)
        P.fence()
        if l % 2 == 1 and cfg.do_na:
            emit_na(l)
        if l % 2 == 0 and cfg.do_gdn:
            emit_gdn(l)
        pump(48)
        emit_norm_mod(l, 2)
        P.fence()
        if l + 1 < cfg.n_layers:
            mod_state["gen"] = mod_gen(l + 1)
        emit_ffn(l)
        pump(48)
    P.fence()

    FG = carve(ARENA, 8192, F32, [128, D], "FG")
    P.dma(FG[:, :], final_g.rearrange("(o d) -> o d", o=1).to_broadcast([128, D]), writes=[FG])
    YT = [carve(ARENA, i * 4096, F32, [128, D], "YT%d" % i) for i in range(2)]
    YSQ = carve(ARENA, 12288, F32, [128, D], "YSQ")
    SS = [P.sb("SS%d" % i, [128, 1], F32) for i in range(2)]
    for blk in range(NT // 128):
        yt = YT[blk % 2]
        ss = SS[blk % 2]
        tb = (blk * 128) // TB
        for half in range(2):
            pb = psum()
            for j in range(4):
                kc = half * 4 + j
                P.add("pe", lambda e, pb=pb, kc=kc, j=j, blk=blk: e.transpose(
                    pb[:, j * 128:(j + 1) * 128], X[:, kc, blk * 128:(blk + 1) * 128], CONST[:, 0:128]),
                    reads=[Xb[(kc, tb)], CONST], writes=[pb])
            P.add("act", lambda e, pb=pb, yt=yt, half=half: e.copy(yt[:, half * 512:(half + 1) * 512], pb[:, :]),
                  reads=[pb], writes=[yt])
        P.add("act", lambda e, yt=yt, ss=ss: e.activation(YSQ[:, :], yt[:, :], AF.Square, accum_out=ss[:, 0:1]),
              reads=[yt], writes=[YSQ, ss])
        P.add("act", lambda e, ss=ss: e.activation(ss[:, :], ss[:, :], AF.Ln, bias=EPSC[:, 0:1], scale=1.0 / D),
              reads=[ss, EPSC], writes=[ss])
        P.add("act", lambda e, ss=ss: e.activation(ss[:, :], ss[:, :], AF.Exp, scale=-0.5), reads=[ss], writes=[ss])
        P.add("dve", lambda e, yt=yt, ss=ss: e.scalar_tensor_tensor(
            yt[:, :], yt[:, :], ss[:, 0:1], FG[:, :], ALU.mult, ALU.mult), reads=[yt, ss, FG], writes=[yt])
        P.dma(y_out[blk * 128:(blk + 1) * 128, :], yt[:, :], reads=[yt])

    P.emit()
    return nc


def make_consts():
    c = np.zeros((128, 1536), np.float32)
    c[:, 0:128] = np.eye(128, dtype=np.float32)
    J = np.zeros((64, 64), np.float32)
    J[np.arange(64), 63 - np.arange(64)] = 1.0
    c[0:64, 128:192] = J
    c[64:128, 128:192] = J
    qc = 63 - np.arange(64)[:, None]
    kc = np.arange(64)[None, :]
    cs = np.clip(qc - 8, 0, 48)
    inside = (kc >= cs) & (kc < cs + 16)
    c[0:64, 192:256] = np.where(inside, 0.0, -30000.0)
    jj = np.arange(128)[:, None]
    ii = np.arange(128)[None, :]
    same = (jj // 64) == (ii // 64)
    c[:, 256:384] = (same & (jj <= ii)).astype(np.float32)
    c[:, 384:512] = (same & (jj >= ii)).astype(np.float32)
    c[:, 512:640] = (jj < 64).astype(np.float32) * np.ones((1, 128), np.float32)
    c[:, 640:768] = (jj >= 64).astype(np.float32) * np.ones((1, 128), np.float32)
    NEG = -30000.0
    c[:, 768:896] = np.where(same & (jj <= ii), 0.0, NEG)
    c[:, 896:1024] = np.where(same & (jj < ii), 0.0, NEG)
    c[:, 1024:1152] = np.where(same & (jj >= ii), 0.0, NEG)
    c[:, 1152:1280] = np.where(same & (jj > ii), 0.0, NEG)
    c[:, 1280:1408] = 1.0
    c[:, 1408:1536] = -1.0
    return c


_NC_CACHE = {}


def kernel(x_prompt, x_sample, state_gdn, cache_k, cache_v, c, c_ctx, w_ada, b_ada, norm1_g, norm2_g,
           gdn_w_in, gdn_conv_w, gdn_a_log, gdn_dt_bias, gdn_norm_g, gdn_w_out,
           na_w_qkv, na_rel_bias, na_w_out, ffn_w_up, ffn_conv_w, ffn_conv_b, ffn_w_down, final_g, _cfg=Cfg, _cores=8):
    f = lambda a: np.ascontiguousarray(np.asarray(a, dtype=np.float32))
    nc = build(_cfg)
    consts = make_consts()
    c_ = np.ascontiguousarray
    w_ada_t = c_(f(w_ada).reshape(4, 8, 128, 48, 128).transpose(0, 3, 2, 1, 4)).reshape(4, 48, 128, 1024)
    w_up_t = c_(f(ffn_w_up).reshape(4, 8, 128, 2, 22, 128).transpose(0, 4, 2, 1, 3, 5)).reshape(4, 22, 128, 2048)
    w_dn_t = c_(f(ffn_w_down).reshape(4, 2, 11, 128, 8, 128).transpose(0, 1, 4, 3, 2, 5)).reshape(4, 2, 8, 128, 1408)
    qkv_t = c_(f(na_w_qkv).reshape(2, 8, 128, 3, 8, 128).transpose(0, 4, 2, 1, 3, 5)).reshape(2, 8, 128, 3072)
    nwo_t = c_(f(na_w_out).reshape(2, 8, 128, 8, 128).transpose(0, 3, 2, 1, 4)).reshape(2, 8, 128, 1024)
    gin = f(gdn_w_in)
    gin_t = c_(gin[:, :, :4096].reshape(2, 8, 128, 2, 2, 8, 128).transpose(0, 5, 3, 2, 1, 4, 6)).reshape(2, 8, 2, 128, 2048)
    gab_t = c_(gin[:, :, 4096:4128].reshape(2, 8, 128, 32).transpose(0, 2, 1, 3)).reshape(2, 128, 256)
    shared = {
        "w_ada": w_ada_t, "b_ada": f(b_ada), "norm1_g": f(norm1_g), "norm2_g": f(norm2_g),
        "gdn_w_in": gin_t, "gdn_w_ab": gab_t, "gdn_conv_w": f(gdn_conv_w),
        "gdn_a_log": f(gdn_a_log).reshape(2, 16), "gdn_dt_bias": f(gdn_dt_bias).reshape(2, 16),
        "gdn_norm_g": f(gdn_norm_g), "gdn_w_out": f(gdn_w_out),
        "na_w_qkv": qkv_t, "na_rel_bias": f(na_rel_bias).reshape(2, 240, 31), "na_w_out": nwo_t,
        "ffn_w_up": w_up_t, "ffn_conv_w": f(ffn_conv_w), "ffn_conv_b": f(ffn_conv_b),
        "ffn_w_down": w_dn_t, "final_g": f(final_g), "consts": consts,
    }
    xp = f(x_prompt)
    xs = f(x_sample)
    in_maps = []
    for i in range(_cores):
        m = dict(shared)
        m["x_in"] = np.concatenate([xp[2 * i].reshape(256, D), xp[2 * i + 1].reshape(256, D), xs[i]], axis=0)
        m["state_gdn"] = f(state_gdn[i])
        m["cache_k"] = f(cache_k[i]).reshape(2, 256, 1024)
        m["cache_v"] = f(cache_v[i]).reshape(2, 256, 1024)
        m["cvec"] = np.stack([f(c_ctx), f(c[i])], axis=0)
        in_maps.append(m)
    res = run_bass_kernel_spmd(nc, in_maps, core_ids=list(range(_cores)))
    R = res.results
    y_prompt = np.zeros((16, 256, D), np.float32)
    y_sample = np.zeros((8, 1024, D), np.float32)
    new_state = np.zeros((16, 2, 2, 8, 128, 128), np.float32)
    new_k = np.zeros((16, 2, 256, 16, 64), np.float32)
    new_v = np.zeros((16, 2, 256, 16, 64), np.float32)
    for i in range(_cores):
        y = R[i]["y_out"]
        y_prompt[2 * i] = y[0:256]
        y_prompt[2 * i + 1] = y[256:512]
        y_sample[i] = y[512:]
        new_state[2 * i:2 * i + 2] = R[i]["st_out"]
        new_k[2 * i:2 * i + 2] = R[i]["ck_out"].reshape(2, 2, 256, 16, 64)
        new_v[2 * i:2 * i + 2] = R[i]["cv_out"].reshape(2, 2, 256, 16, 64)
    return (y_prompt, y_sample, new_state, new_k, new_v)
```
